# Optimizing a Trainium2 kernel written in Bass

```python
import jax, jax.numpy as jnp
from jax import lax
import numpy as np

D_MODEL = 2048
BATCH = 4
SEQ = 2048
DEPTH = 1
DEC_BATCH = 128
DEC_SEQ = 8
PAST_LEN = 8192
PAGE_SIZE = 128

HEAD_DIM = 64
N_Q_HEADS = 16
N_KV_HEADS = 4
Q_PER_KV = N_Q_HEADS // N_KV_HEADS
WINDOW = 128
ATTN_Q_WIDTH = N_Q_HEADS * HEAD_DIM
ATTN_KV_WIDTH = N_KV_HEADS * HEAD_DIM
CHUNK = 128
GMLP_GROUPS = 8
GMLP_GROUP_DIM = 128
GMLP_WIDTH = GMLP_GROUPS * GMLP_GROUP_DIM
D_FF = -(-8 * D_MODEL // (3 * 256)) * 256
IN_WIDTH = ATTN_Q_WIDTH + 2 * ATTN_KV_WIDTH + 2 * GMLP_WIDTH + 2 * D_MODEL
EPS = 1e-6
NEG = -1e30

kernel_name = "gated_parallel_swa_sink_chunk_gmlp_decoder_step"


def rms_norm(x, g):
    xf = x.astype(jnp.float32)
    y = xf * lax.rsqrt(jnp.mean(xf * xf, axis=-1, keepdims=True) + EPS)
    return (y * g.astype(jnp.float32)).astype(x.dtype)


def layer_norm_nobias(x, g):
    xf = x.astype(jnp.float32)
    xc = xf - jnp.mean(xf, axis=-1, keepdims=True)
    y = xc * lax.rsqrt(jnp.mean(xc * xc, axis=-1, keepdims=True) + EPS)
    return (y * g.astype(jnp.float32)).astype(x.dtype)


def split_in(z):
    o0 = ATTN_Q_WIDTH
    o1 = o0 + ATTN_KV_WIDTH
    o2 = o1 + ATTN_KV_WIDTH
    o3 = o2 + GMLP_WIDTH
    o4 = o3 + GMLP_WIDTH
    o5 = o4 + D_MODEL
    return (z[..., :o0], z[..., o0:o1], z[..., o1:o2], z[..., o2:o3],
            z[..., o3:o4], z[..., o4:o5], z[..., o5:])


def sink_softmax(scores, mask, sink):
    s = jnp.where(mask, scores, NEG)
    m = jnp.maximum(jnp.max(s, axis=-1, keepdims=True), sink)
    p = jnp.exp(s - m)
    return p / (jnp.sum(p, axis=-1, keepdims=True) + jnp.exp(sink - m))


def chunk_spatial_gate(u, v, w_s, b_s):
    B, L = u.shape[0], u.shape[1]
    vc = v.reshape(B, L // CHUNK, CHUNK, GMLP_GROUPS, GMLP_GROUP_DIM)
    w = w_s * jnp.tril(jnp.ones((CHUNK, CHUNK), w_s.dtype))[None]
    mixed = jnp.einsum('gts,bcsgd->bctgd', w, vc) + b_s.T[None, None, :, :, None]
    return u * mixed.reshape(B, L, GMLP_GROUPS, GMLP_GROUP_DIM)


def branch_inputs(x, norm1_g, w_in, gmlp_norm_g):
    h = rms_norm(x, norm1_g)
    q, k, v, u, gv, ga, gb = split_in(h @ w_in)
    B, L = x.shape[0], x.shape[1]
    q = q.reshape(B, L, N_KV_HEADS, Q_PER_KV, HEAD_DIM)
    k = k.reshape(B, L, N_KV_HEADS, HEAD_DIM)
    v = v.reshape(B, L, N_KV_HEADS, HEAD_DIM)
    u = jax.nn.gelu(u).reshape(B, L, GMLP_GROUPS, GMLP_GROUP_DIM)
    gv = layer_norm_nobias(jax.nn.gelu(gv), gmlp_norm_g).reshape(B, L, GMLP_GROUPS, GMLP_GROUP_DIM)
    return q, k, v, u, gv, ga, gb


def merge_and_ffn(x, attn, gmlp, ga, gb, w_pa, w_pb, w_o, norm2_g, w_ff_gate, w_ff_up, w_ff_down):
    mix = jax.nn.sigmoid(ga) * (gmlp @ w_pa) + jax.nn.sigmoid(gb) * (attn @ w_pb)
    x = x + mix @ w_o
    hh = rms_norm(x, norm2_g)
    return x + (jax.nn.silu(hh @ w_ff_gate) * (hh @ w_ff_up)) @ w_ff_down


def prompt_attention(q, k, v, sinks):
    B, S = q.shape[0], q.shape[1]
    nb = S // WINDOW
    qb = q.reshape(B, nb, WINDOW, N_KV_HEADS, Q_PER_KV, HEAD_DIM).astype(jnp.float32)
    kb = k.reshape(B, nb, WINDOW, N_KV_HEADS, HEAD_DIM)
    vb = v.reshape(B, nb, WINDOW, N_KV_HEADS, HEAD_DIM)
    kk = jnp.concatenate([jnp.concatenate([jnp.zeros_like(kb[:, :1]), kb[:, :-1]], 1), kb], 2)
    vv = jnp.concatenate([jnp.concatenate([jnp.zeros_like(vb[:, :1]), vb[:, :-1]], 1), vb], 2)
    scores = jnp.einsum('bnqhrd,bnkhd->bnhrqk', qb, kk.astype(jnp.float32)) * (HEAD_DIM ** -0.5)
    i = jnp.arange(WINDOW)[:, None]
    j = jnp.arange(2 * WINDOW)[None, :]
    diff = i + WINDOW - j
    c = jnp.arange(nb)[:, None, None]
    mask = (diff >= 0)[None] & (diff <= WINDOW)[None] & (c * WINDOW + j - WINDOW >= 0)
    sink = sinks.astype(jnp.float32).reshape(1, 1, N_KV_HEADS, Q_PER_KV, 1, 1)
    probs = sink_softmax(scores, mask[None, :, None, None], sink)
    out = jnp.einsum('bnhrqk,bnkhd->bnqhrd', probs.astype(v.dtype), vv)
    return out.reshape(B, S, ATTN_Q_WIDTH)


def sample_attention(q, k, v, cache_k, cache_v, sinks):
    B, T = q.shape[0], q.shape[1]
    wc = cache_k.shape[1]
    kk = jnp.concatenate([cache_k, k], 1)
    vv = jnp.concatenate([cache_v, v], 1)
    scores = jnp.einsum('bqhrd,bkhd->bhrqk', q.astype(jnp.float32), kk.astype(jnp.float32)) * (HEAD_DIM ** -0.5)
    i = jnp.arange(T)[:, None]
    j = jnp.arange(wc + T)[None, :]
    diff = wc + i - j
    mask = (diff >= 0) & (diff <= WINDOW)
    sink = sinks.astype(jnp.float32).reshape(1, N_KV_HEADS, Q_PER_KV, 1, 1)
    probs = sink_softmax(scores, mask, sink)
    out = jnp.einsum('bhrqk,bkhd->bqhrd', probs.astype(v.dtype), vv)
    return out.reshape(B, T, ATTN_Q_WIDTH), kk[:, -wc:], vv[:, -wc:]


def setup_inputs(seed: int = 0) -> dict:
    key = jax.random.key(seed)
    ks = jax.random.split(key, 20)
    f32 = jnp.float32
    n = lambda k, shape, s: jax.random.normal(k, shape, f32) * s
    cache_w = min(WINDOW, PAST_LEN)
    return {
        "x_prompt": n(ks[0], (BATCH, SEQ, D_MODEL), 1.0),
        "x_sample": n(ks[1], (DEC_BATCH, DEC_SEQ, D_MODEL), 1.0),
        "cache_k": n(ks[2], (DEPTH, DEC_BATCH, cache_w, N_KV_HEADS, HEAD_DIM), 1.0),
        "cache_v": n(ks[3], (DEPTH, DEC_BATCH, cache_w, N_KV_HEADS, HEAD_DIM), 1.0),
        "norm1_g": 1.0 + n(ks[4], (DEPTH, D_MODEL), 0.02),
        "w_in": n(ks[5], (DEPTH, D_MODEL, IN_WIDTH), D_MODEL ** -0.5),
        "gmlp_norm_g": 1.0 + n(ks[6], (DEPTH, GMLP_WIDTH), 0.02),
        "w_s": n(ks[7], (DEPTH, GMLP_GROUPS, CHUNK, CHUNK), CHUNK ** -0.5),
        "b_s": n(ks[8], (DEPTH, GMLP_GROUPS, CHUNK), 0.02),
        "sinks": n(ks[9], (DEPTH, N_Q_HEADS), 0.5),
        "w_pa": n(ks[10], (DEPTH, GMLP_WIDTH, D_MODEL), GMLP_WIDTH ** -0.5),
        "w_pb": n(ks[11], (DEPTH, ATTN_Q_WIDTH, D_MODEL), ATTN_Q_WIDTH ** -0.5),
        "w_o": n(ks[12], (DEPTH, D_MODEL, D_MODEL), D_MODEL ** -0.5),
        "norm2_g": 1.0 + n(ks[13], (DEPTH, D_MODEL), 0.02),
        "w_ff_gate": n(ks[14], (DEPTH, D_MODEL, D_FF), D_MODEL ** -0.5),
        "w_ff_up": n(ks[15], (DEPTH, D_MODEL, D_FF), D_MODEL ** -0.5),
        "w_ff_down": n(ks[16], (DEPTH, D_FF, D_MODEL), D_FF ** -0.5),
        "final_g": 1.0 + n(ks[17], (D_MODEL,), 0.02),
    }


def reference(x_prompt, x_sample, cache_k, cache_v, norm1_g, w_in, gmlp_norm_g, w_s, b_s, sinks,
              w_pa, w_pb, w_o, norm2_g, w_ff_gate, w_ff_up, w_ff_down, final_g):
    xp, xs = x_prompt, x_sample
    pk_list, pv_list, sk_list, sv_list, sg_list = [], [], [], [], []
    for l in range(DEPTH):
        B, S = xp.shape[0], xp.shape[1]
        q, k, v, u, gv, ga, gb = branch_inputs(xp, norm1_g[l], w_in[l], gmlp_norm_g[l])
        attn = prompt_attention(q, k, v, sinks[l])
        gm = chunk_spatial_gate(u, gv, w_s[l], b_s[l]).reshape(B, S, GMLP_WIDTH)
        xp = merge_and_ffn(xp, attn, gm, ga, gb, w_pa[l], w_pb[l], w_o[l], norm2_g[l],
                           w_ff_gate[l], w_ff_up[l], w_ff_down[l])
        pk_list.append(k[:, -WINDOW:])
        pv_list.append(v[:, -WINDOW:])

        Bd, T = xs.shape[0], xs.shape[1]
        q, k, v, u, gv, ga, gb = branch_inputs(xs, norm1_g[l], w_in[l], gmlp_norm_g[l])
        attn, k_win, v_win = sample_attention(q, k, v, cache_k[l], cache_v[l], sinks[l])
        pad = (-T) % CHUNK
        padw = ((0, 0), (0, pad), (0, 0), (0, 0))
        gm = chunk_spatial_gate(jnp.pad(u, padw), jnp.pad(gv, padw), w_s[l], b_s[l])[:, :T]
        xs = merge_and_ffn(xs, attn, gm.reshape(Bd, T, GMLP_WIDTH), ga, gb, w_pa[l], w_pb[l], w_o[l],
                           norm2_g[l], w_ff_gate[l], w_ff_up[l], w_ff_down[l])
        sk_list.append(k_win)
        sv_list.append(v_win)
        sg_list.append(gv)

    y_prompt = rms_norm(xp, final_g)
    y_sample = rms_norm(xs, final_g)
    prompt_k_win = jnp.stack(pk_list, 0)
    prompt_v_win = jnp.stack(pv_list, 0)
    sample_k_win = jnp.stack(sk_list, 0)
    sample_v_win = jnp.stack(sv_list, 0)
    sample_gmlp_v = jnp.stack(sg_list, 0)
    return (y_prompt, y_sample, prompt_k_win, prompt_v_win, sample_k_win, sample_v_win, sample_gmlp_v)
```

```python
import contextlib
import numpy as np
import concourse.bass as bass
import concourse.mybir as mybir
from concourse.bass_utils import run_bass_kernel_spmd

F32 = mybir.dt.float32
BF16 = mybir.dt.bfloat16
AF = mybir.ActivationFunctionType
ALU = mybir.AluOpType
AX = mybir.AxisListType

ENGS = ("pe", "act", "dve", "pool", "sp")

D = 2048
DFF = 5632
INW = 7680
NT = 9
TOK = NT * 128
TOKP = TOK + 128
EPS = 1e-6
NEG = -1e30
NCORES = 8


class Res:
    __slots__ = ("name", "last_w", "readers", "excl")

    def __init__(self, name, excl=False):
        self.name = name
        self.last_w = None
        self.readers = []
        self.excl = excl


class Op:
    __slots__ = ("eng", "fn", "deps", "signal", "is_dma", "key", "val", "idx")

    def __init__(self, eng, fn, is_dma, key):
        self.eng = eng
        self.fn = fn
        self.deps = []
        self.signal = False
        self.is_dma = is_dma
        self.key = key
        self.val = None
        self.idx = None


class Sched:
    def __init__(self):
        self.q = {e: [] for e in ENGS}
        self.dma_count = {}
        self.stopped = False

    def add(self, eng, fn, reads=(), writes=(), dma_key=None):
        if self.stopped:
            return None
        is_dma = dma_key is not None
        op = Op(eng, fn, is_dma, dma_key)
        deps = {}
        for r in reads:
            if r.last_w is not None:
                deps[id(r.last_w)] = r.last_w
            if r.excl:
                for rd in r.readers:
                    if rd.is_dma or rd.eng != eng:
                        deps[id(rd)] = rd
        for w in writes:
            if w.last_w is not None:
                deps[id(w.last_w)] = w.last_w
            for rd in w.readers:
                deps[id(rd)] = rd
        for d in deps.values():
            if d is op:
                continue
            if (not d.is_dma) and (not is_dma) and d.eng == "pe" and eng == "pe":
                continue
            op.deps.append(d)
            d.signal = True
        for r in reads:
            if not is_dma:
                r.readers = [x for x in r.readers if x.is_dma or x.eng != eng]
            r.readers.append(op)
        for w in writes:
            w.last_w = op
            w.readers = []
        if is_dma:
            c = self.dma_count.get(dma_key, 0) + 1
            self.dma_count[dma_key] = c
            op.val = 16 * c
        op.idx = len(self.q[eng])
        self.q[eng].append(op)
        return op

    def emit(self, nc, stack):
        for e in ENGS:
            c = 0
            for op in self.q[e]:
                if op.is_dma:
                    continue
                if op.signal:
                    c += 1
                    op.val = c
        sems = {}

        def sem_of(key):
            if key not in sems:
                sems[key] = stack.enter_context(nc.semaphore("s%d" % len(sems)))
            return sems[key]

        for e in ("pe", "act", "dve", "pool"):
            sem_of(("eng", e))
        for k in self.dma_count:
            sem_of(("dma", k))

        def tok(d):
            if d.is_dma:
                return ("dma", d.key), d.val
            return ("eng", d.eng), d.val

        block = stack.enter_context(nc.Block())
        engobj = {"pe": block.tensor, "act": block.scalar, "dve": block.vector,
                  "pool": block.gpsimd, "sp": block.sync}

        def run_queue(e, eng):
            waited = {}
            for op in self.q[e]:
                need = {}
                for d in op.deps:
                    k, v = tok(d)
                    if waited.get(k, 0) >= v:
                        continue
                    if need.get(k, 0) < v:
                        need[k] = v
                for k, v in need.items():
                    eng.wait_ge(sem_of(k), v)
                    waited[k] = v
                inst = op.fn(eng)
                if op.is_dma:
                    inst.then_inc(sem_of(("dma", op.key)), 16)
                elif op.signal:
                    inst.then_inc(sem_of(("eng", e)), 1)
            if e == "sp":
                for k, c in self.dma_count.items():
                    eng.wait_ge(sem_of(("dma", k)), 16 * c)

        for e in ENGS:
            def body(eng, e=e):
                run_queue(e, eng)
            engobj[e](body)


def _drain(gen):
    for _ in gen:
        pass


def _interleave(main, filler, k=1):
    fdone = False
    for _ in main:
        for _ in range(k):
            if not fdone:
                try:
                    next(filler)
                except StopIteration:
                    fdone = True
    if not fdone:
        _drain(filler)


def build_program(stop=None):
    nc = bass.Bass("TRN2", target_bir_lowering=False)

    def din(name, shape):
        return nc.dram_tensor(name, list(shape), F32, kind="ExternalInput").ap()

    def dout(name, shape):
        return nc.dram_tensor(name, list(shape), F32, kind="ExternalOutput").ap()

    xc = din("xc", [TOKP, D])
    ckT = din("ckT", [2, 128, 16, 128])
    ck = din("ck", [16, 128, 256])
    cv = din("cv", [16, 128, 256])
    w_in = din("w_in", [D, INW])
    w_pa = din("w_pa", [1024, D])
    w_pb = din("w_pb", [1024, D])
    w_o = din("w_o", [D, D])
    w_g = din("w_g", [D, DFF])
    w_u = din("w_u", [D, DFF])
    w_d = din("w_d", [DFF, D])
    g1T = din("g1T", [128, 16])
    g2T = din("g2T", [128, 16])
    gfB = din("gfB", [128, D])
    gnB = din("gnB", [128, 1024])
    wsT = din("wsT", [8, 128, 128])
    wsTr = din("wsTr", [8, 128, 128])
    trilT = din("trilT", [128, 128])
    blkm = din("blkm", [128, 128])
    bsB = din("bsB", [128, 8, 128])
    bsBs = din("bsBs", [128, 8, 128])
    sinksB = din("sinksB", [128, 16])
    sinkrep = din("sinkrep", [128, 1])
    maskN = din("maskN", [128, 256])
    mask0 = din("mask0", [128, 256])
    maskSc = din("maskSc", [128, 128])
    maskSn = din("maskSn", [128, 16, 128])
    identd = din("ident", [128, 128])
    sinksB2 = din("sinksB2", [128, 32])
    sinkrep2 = din("sinkrep2", [128, 2])

    y_out = dout("y", [TOK, D])
    kv7_out = dout("kv7", [128, 512])
    skw_out = dout("skw", [16, 128, 256])
    svw_out = dout("svw", [16, 128, 256])
    sg_out = dout("sg", [128, 1024])

    S = Sched()

    with contextlib.ExitStack() as st:
        R1 = 112640
        R2 = 36864
        NSLOT = 4
        WS = 8192
        SS = 12288
        SMALL = 7168
        TOTAL = R1 + R2 + NSLOT * WS + SS + SMALL
        arena = st.enter_context(nc.sbuf_tensor("arena", [128, TOTAL // 4], F32))

        def view(off, nbytes, dt=F32, pat=None, **kw):
            assert off % 4 == 0 and nbytes % 4 == 0
            ap = arena[:, off // 4:(off + nbytes) // 4]
            if dt is not F32:
                ap = ap.bitcast(dt)
            if pat is not None:
                ap = ap.rearrange(pat, **kw)
            return ap

        hT = view(0, 40960, BF16, "p (c t) -> p c t", c=16)
        QT = view(40960, 18432, BF16, "p (c t) -> p c t", c=8)
        uT = view(59392, 18432, BF16, "p (c t) -> p c t", c=8)
        VLN_OFF = 77824
        KTd = view(96256, 10240, BF16, "p (h t) -> p h t", h=4)
        Vb = view(106496, 5120, BF16, "p (b c) -> p b c", b=10)
        gvf = view(59392, 36864, F32, "p (t c) -> p t c", t=NT)
        x1 = view(0, 73728, F32, "p (t c) -> p t c", t=NT)
        hhT = view(73728, 36864, BF16, "p (c t) -> p c t", c=16)
        r2 = R1
        mixT = view(r2, 36864, BF16, "p (c t) -> p c t", c=16)
        w0 = R1 + R2
        wslot = [w0 + i * WS for i in range(NSLOT)]
        wres = [Res("w%d" % i) for i in range(NSLOT)]
        s0 = w0 + NSLOT * WS
        hb = [view(s0 + i * 4096, 4096, BF16) for i in range(3)]
        hbres = [Res("hb%d" % i) for i in range(3)]
        m0 = s0 + SS
        _sm = [m0]

        def small(nbytes, dt=F32, pat=None, **kw):
            off = _sm[0]
            _sm[0] += (nbytes + 31) // 32 * 32
            assert _sm[0] <= m0 + SMALL, "small region overflow"
            return view(off, nbytes, dt, pat, **kw)

        ident = small(256, BF16)
        maskNb = small(512, BF16)
        mask0b = small(512, BF16)
        g1t = small(64)
        g2t = small(64)
        sinks_t = small(64)
        sinkrep_t = small(4)
        sinks_b2 = small(64, BF16)
        sinkrep_b2 = small(4, BF16)
        statA = small(4 * 64)
        st_att = small(4 * 6 * 2 * 16, F32, "p (q b h) -> p q b h", q=6, b=2)
        bnst = small(4 * 12 * 2, F32, "p (b k) -> p b k", b=2)
        bnmv = small(4 * 4 * 2, F32, "p (b k) -> p b k", b=2)
        st2 = small(4 * 64)
        SMALL_EXTRA = _sm[0]
        _sm[0] += 160 + 448
        assert _sm[0] <= m0 + SMALL

        banks = [st.enter_context(nc.psum_tensor("bank%d" % i, [128, 512], F32)) for i in range(8)]
        bres = [Res("bank%d" % i, excl=True) for i in range(8)]

        def bankbf(i):
            return banks[i][:].bitcast(BF16)

        def pe_T(bk, j, in_ap, reads):
            S.add("pe", lambda e: e.matmul(banks[bk][:, j * 128:(j + 1) * 128], lhsT=in_ap, rhs=ident,
                                           start=True, stop=True), reads=reads + CONST, writes=[bres[bk]])

        cnt = {"ev": 0}

        def tcols(i):
            return slice(i * 128, (i + 1) * 128)

        def kblk(i):
            return 0 if i == 9 else i + 1

        def ev_engine():
            cnt["ev"] += 1
            return "act" if cnt["ev"] % 2 else "dve"

        def copy_op(eng_name, out, in_, reads, writes, scale=None):
            if eng_name == "act":
                if scale is None:
                    S.add("act", lambda e: e.copy(out=out, in_=in_), reads=reads, writes=writes)
                else:
                    S.add("act", lambda e: e.mul(out=out, in_=in_, mul=scale), reads=reads, writes=writes)
            else:
                if scale is None:
                    S.add(eng_name, lambda e: e.tensor_copy(out=out, in_=in_), reads=reads, writes=writes)
                else:
                    S.add(eng_name, lambda e: e.tensor_scalar_mul(out=out, in0=in_, scalar1=scale),
                          reads=reads, writes=writes)

        wq = []
        wstate = {"issued": 0, "consumed": 0}

        def wblock(parts):
            wq.append(parts)
            return len(wq) - 1

        def w_issue_upto(n):
            while wstate["issued"] < min(n, len(wq)):
                b = wstate["issued"]
                s = b % NSLOT
                for (vf, src) in wq[b]:
                    dst = vf(wslot[s])
                    S.add("pool", lambda e, dst=dst, src=src: e.dma_start(out=dst, in_=src),
                          writes=[wres[s]], dma_key="w%d" % s)
                wstate["issued"] += 1

        def w_use(b):
            assert b < wstate["consumed"] + NSLOT, (b, wstate)
            w_issue_upto(wstate["consumed"] + NSLOT)
            return wslot[b % NSLOT], wres[b % NSLOT]

        def w_done(b):
            wstate["consumed"] = b + 1
            w_issue_upto(wstate["consumed"] + NSLOT)

        def v_k16(off):
            return view(off, 8192, BF16, "p (k c) -> p k c", k=16)

        def blk_k16(w, c0):
            return wblock([(v_k16, w[:, c0:c0 + 256].rearrange("(k p) c -> p k c", p=128))])

        B_K = blk_k16(w_in, 1024)
        B_V = blk_k16(w_in, 1280)
        B_Q = [blk_k16(w_in, 0 + 256 * i) for i in range(4)]
        B_GV = [blk_k16(w_in, 2560 + 256 * i) for i in range(4)]
        B_U = [blk_k16(w_in, 1536 + 256 * i) for i in range(4)]
        B_GATE = []
        for i in range(8):
            ga = blk_k16(w_in, 3584 + 256 * i)
            gb = blk_k16(w_in, 5632 + 256 * i)
            pab = wblock([
                (lambda off: view(off, 4096, BF16, "p (k c) -> p k c", k=8),
                 w_pa[:, 256 * i:256 * i + 256].rearrange("(k p) c -> p k c", p=128)),
                (lambda off: view(off + 4096, 4096, BF16, "p (k c) -> p k c", k=8),
                 w_pb[:, 256 * i:256 * i + 256].rearrange("(k p) c -> p k c", p=128)),
            ])
            B_GATE.append((ga, gb, pab))
        B_WO = [blk_k16(w_o, 256 * i) for i in range(8)]
        NG = 11
        B_FF = []
        for gi in range(NG):
            f0 = gi * 4
            gu = []
            for hf in range(2):
                c0 = (f0 + 2 * hf) * 128
                gu.append((blk_k16(w_g, c0), blk_k16(w_u, c0)))
            dn = []
            for ch in range(2):
                dn.append(wblock([(lambda off: view(off, 8192, BF16, "p (f c) -> p f c", f=4),
                                   w_d[f0 * 128:(f0 + 4) * 128, ch * 1024:(ch + 1) * 1024]
                                   .rearrange("(f p) c -> p f c", p=128))]))
            B_FF.append((gu, dn))

        def cload(dst, src, cast=False):
            if cast:
                S.add("pool", lambda e: e.dma_start(out=dst, in_=src), dma_key="constc")
            else:
                S.add("sp", lambda e: e.dma_start(out=dst, in_=src), dma_key="const")

        cload(ident, identd[:, :], cast=True)
        cload(maskNb, maskN[:, :], cast=True)
        cload(mask0b, mask0[:, :], cast=True)
        cload(g1t, g1T[:, :])
        cload(g2t, g2T[:, :])
        cload(sinks_t, sinksB[:, :])
        cload(sinkrep_t, sinkrep[:, :])
        cload(sinks_b2, sinksB2[:, :], cast=True)
        cload(sinkrep_b2, sinkrep2[:, :], cast=True)
        constA = Res("constA")
        constB = Res("constB")
        constA.last_w = [op for op in S.q["pool"] if op.key == "constc"][-1]
        constB.last_w = [op for op in S.q["sp"] if op.key == "const"][-1]
        CONST = [constA, constB]

        xtv = [view(r2 + i * 8192, 8192) for i in range(2)]
        xtres = [Res("xt%d" % i) for i in range(2)]
        hTres = [[Res("hT%d_%d" % (i, c)) for c in range(16)] for i in range(10)]
        order = [9] + list(range(9))
        KVB = r2 + 16384
        kd = [view(KVB + i * 1024, 1024, BF16, "p (h u d) -> p h u d", h=4, u=2) for i in range(2)]
        kdres = [Res("kd%d" % i) for i in range(2)]
        kvf = [view(KVB + 2048 + i * 2048, 2048) for i in range(2)]
        kvfres = [Res("kvf%d" % i) for i in range(2)]
        Kb8 = view(KVB + 6144, 512, BF16)
        KT8 = view(KVB + 6656, 512, BF16, "p (c t) -> p c t", c=2)
        Kb8res = Res("Kb8")
        KT8res = Res("KT8")
        KTres = [Res("KT%d" % i) for i in range(10)]
        Vres = [Res("V%d" % i) for i in range(10)]

        def stageA1(n, i):
            b = n % 2
            r0 = TOK if i == 9 else i * 128
            S.add("sp", lambda e: e.dma_start(out=xtv[b], in_=xc[r0:r0 + 128, :]),
                  writes=[xtres[b]], dma_key="xt%d" % b)
            ssr = Res("ss")
            ss = statA[:, 2 * n:2 * n + 1]
            rs = statA[:, 2 * n + 1:2 * n + 2]
            S.add("act", lambda e: e.activation(out=hb[b], in_=xtv[b], func=AF.Square, accum_out=ss),
                  reads=[xtres[b]], writes=[hbres[b], ssr])
            S.add("dve", lambda e: e.tensor_scalar(out=rs, in0=ss, scalar1=1.0 / D, scalar2=EPS,
                                                   op0=ALU.mult, op1=ALU.add), reads=[ssr], writes=[ssr])
            S.add("act", lambda e: e.activation(out=rs, in_=rs, func=AF.Sqrt), reads=[ssr], writes=[ssr])
            S.add("dve", lambda e: e.reciprocal(out=rs, in_=rs), reads=[ssr], writes=[ssr])
            S.add("dve", lambda e: e.tensor_scalar_mul(out=hb[b], in0=xtv[b], scalar1=rs),
                  reads=[ssr, xtres[b], hbres[b]], writes=[hbres[b]])

        def stageA2(n, i):
            b = n % 2
            for half in range(4):
                bk = 4 + half
                for j in range(4):
                    c = half * 4 + j
                    pe_T(bk, j, hb[b][:, c * 128:(c + 1) * 128], [hbres[b]])
                en = "act" if half % 2 == 0 else "dve"
                for j in range(4):
                    c = half * 4 + j
                    src = banks[bk][:, j * 128:(j + 1) * 128]
                    dst = hT[:, c, tcols(i)]
                    if en == "act":
                        S.add("act", lambda e, dst=dst, src=src, c=c: e.activation(
                            out=dst, in_=src, func=AF.Copy, scale=g1t[:, c:c + 1]),
                            reads=[bres[bk]] + CONST, writes=[hTres[i][c]])
                    else:
                        S.add("dve", lambda e, dst=dst, src=src, c=c: e.tensor_scalar_mul(
                            out=dst, in0=src, scalar1=g1t[:, c:c + 1]),
                            reads=[bres[bk]] + CONST, writes=[hTres[i][c]])

        offK, rK = w_use(B_K)
        offV, rV = w_use(B_V)
        wk = v_k16(offK)
        wv = v_k16(offV)

        def stageKV(n, i):
            bk = n % 2
            for kc in range(16):
                S.add("pe", lambda e, kc=kc: e.matmul(
                    banks[bk][:, 0:256], lhsT=hT[:, kc, tcols(i)], rhs=wk[:, kc, :], start=(kc == 0), stop=(kc == 15)),
                    reads=[hTres[i][kc], rK], writes=[bres[bk]])
            for kc in range(16):
                S.add("pe", lambda e, kc=kc: e.matmul(
                    banks[bk][:, 256:512], lhsT=hT[:, kc, tcols(i)], rhs=wv[:, kc, :], start=(kc == 0), stop=(kc == 15)),
                    reads=[hTres[i][kc], rV], writes=[bres[bk]])
            kdb = kd[n % 2]
            kdr = kdres[n % 2]
            kin = banks[bk][:, 0:256].rearrange("p (h d) -> p h d", h=4)
            S.add("dve", lambda e: e.tensor_copy(out=kdb[:, :, 0, :], in_=kin), reads=[bres[bk]], writes=[kdr])
            S.add("dve", lambda e: e.tensor_copy(out=kdb[:, :, 1, :], in_=kin), reads=[bres[bk], kdr], writes=[kdr])
            vdst = Vb[:, kblk(i), :]
            S.add("dve", lambda e: e.tensor_copy(out=vdst, in_=banks[bk][:, 256:512]), reads=[bres[bk]], writes=[Vres[i]])
            if i in (7, 8):
                kvb = kvf[i - 7]
                kvr = kvfres[i - 7]
                S.add("dve", lambda e: e.tensor_copy(out=kvb, in_=banks[bk][:, :]), reads=[bres[bk]], writes=[kvr])
                if i == 7:
                    S.add("sp", lambda e: e.dma_start(out=kv7_out[:, :], in_=kvb), reads=[kvr], dma_key="okv7")
                else:
                    for s in range(16):
                        S.add("sp", lambda e, s=s: e.dma_start(
                            out=skw_out[s, 120:128, :], in_=kvb[s * 8:(s + 1) * 8, 0:256]), reads=[kvr], dma_key="oskw")
                        S.add("sp", lambda e, s=s: e.dma_start(
                            out=svw_out[s, 120:128, :], in_=kvb[s * 8:(s + 1) * 8, 256:512]), reads=[kvr], dma_key="osvw")
                    S.add("dve", lambda e: e.tensor_copy(out=Kb8, in_=banks[bk][:, 0:256]), reads=[bres[bk]], writes=[Kb8res])
            tb = 2 + n % 2
            for h in range(4):
                pe_T(tb, h, kdb[:, h, :, :].rearrange("p u d -> p (u d)"), [kdr])
            kdst = KTd[:, :, kblk(i) * 128:(kblk(i) + 1) * 128]
            S.add("act", lambda e: e.copy(out=kdst, in_=banks[tb][:, 0:512].rearrange("p (h t) -> p h t", h=4)),
                  reads=[bres[tb]], writes=[KTres[i]])
            if i == 8:
                for c in range(2):
                    pe_T(tb, c, Kb8[:, c * 128:(c + 1) * 128], [Kb8res])
                S.add("act", lambda e: e.copy(
                    out=KT8, in_=banks[tb][:, 0:256].rearrange("p (c t) -> p c t", c=2)),
                    reads=[bres[tb]], writes=[KT8res])

        stageA1(0, order[0])
        for n, i in enumerate(order):
            if n + 1 < len(order):
                stageA1(n + 1, order[n + 1])
            stageA2(n, i)
            stageKV(n, i)
        w_done(B_V)
        S.add("sp", lambda e: e.dma_start(out=skw_out[:, 0:120, :], in_=ck[:, 8:128, :]), dma_key="ockw")
        S.add("sp", lambda e: e.dma_start(out=svw_out[:, 0:120, :], in_=cv[:, 8:128, :]), dma_key="ocvw")

        TG = [(0, 384), (384, 768), (768, 1152)]
        QTres = [Res("QT%d" % i) for i in range(NT)]

        def tiles_of_tg(tg):
            return [tg * 3, tg * 3 + 1, tg * 3 + 2]

        sa = 59392
        Qz = view(sa, 4096, BF16, "p (h c) -> p h c", h=16)
        QTz = view(sa + 4096, 8192, BF16, "p (c s h t) -> p c s h t", c=2, s=16, h=16)
        ckTb = view(sa + 12288, 8192, BF16, "p (c s k) -> p c s k", c=2, s=16)
        cvb = view(sa + 20480, 8192, BF16, "p (s c) -> p s c", s=16)
        mSn = view(sa + 28672, 4096, BF16, "p (s k) -> p s k", s=16)
        mSc = view(sa + 32768, 256, BF16)
        Osm1 = view(sa + 33024, 2048, BF16, "p (s d) -> p s d", s=16)
        SAres = Res("sample_attn_bufs")
        Qzres = Res("Qz")
        QTzres = Res("QTz")
        S.add("pool", lambda e: e.memset(view(sa, 4096, BF16), 0.0), writes=[Qzres])
        S.add("pool", lambda e: e.memset(view(sa + 4096, 8192, BF16), 0.0), writes=[QTzres])
        S.add("pool", lambda e: e.dma_start(out=ckTb, in_=ckT.rearrange("c p s k -> p c s k")),
              writes=[SAres], dma_key="sa")
        S.add("pool", lambda e: e.dma_start(out=cvb, in_=cv.rearrange("s k c -> k s c")),
              writes=[SAres], dma_key="sa")
        S.add("pool", lambda e: e.dma_start(out=mSn, in_=maskSn[:, :, :]), writes=[SAres], dma_key="sa")
        S.add("pool", lambda e: e.dma_start(out=mSc, in_=maskSc[:, :]), writes=[SAres], dma_key="sa")

        fm_bank = {"n": 0}

        def fm_group(wt, cl, K, src, srcres_fn, tg, extra_reads, banklist):
            bk = banklist[fm_bank["n"] % len(banklist)]
            fm_bank["n"] += 1
            lo, hi = TG[tg]
            for kc in range(K):
                S.add("pe", lambda e, kc=kc: e.matmul(
                    banks[bk][:, 0:384], lhsT=wt[:, kc, cl * 128:(cl + 1) * 128], rhs=src[:, kc, lo:hi],
                    start=(kc == 0), stop=(kc == K - 1)),
                    reads=[srcres_fn(t, kc) for t in tiles_of_tg(tg)] + extra_reads, writes=[bres[bk]])
            return bk, lo, hi

        hTr = lambda t, kc: hTres[t][kc]
        q_w = []
        for wb in range(4):
            off, rw = w_use(B_Q[wb])
            q_w.append((v_k16(off), rw))

        def q_group(wb, cl, tg, banklist):
            wt, rw = q_w[wb]
            m = wb * 2 + cl
            bk, lo, hi = fm_group(wt, cl, 16, hT, hTr, tg, [rw], banklist)
            copy_op(ev_engine(), QT[:, m, lo:hi], banks[bk][:, 0:384], [bres[bk]],
                    [QTres[t] for t in tiles_of_tg(tg)], scale=0.125)

        def gen_q_rest():
            for tg in (1, 2):
                for wb in range(4):
                    for cl in range(2):
                        q_group(wb, cl, tg, [7])
                        if tg == 2 and cl == 1:
                            w_done(B_Q[wb])
                        yield

        for wb in range(4):
            wt, rw = q_w[wb]
            for cl in range(2):
                q_group(wb, cl, 0, [0, 1, 2, 3, 4, 5])
            for kc in range(16):
                S.add("pe", lambda e, kc=kc, wt=wt: e.matmul(
                    banks[6][:, 0:256], lhsT=hT[:, kc, tcols(8)], rhs=wt[:, kc, :], start=(kc == 0), stop=(kc == 15)),
                    reads=[hTres[8][kc], rw], writes=[bres[6]])
            eh = wb % 2
            S.add("dve", lambda e, wb=wb, eh=eh: e.tensor_scalar_mul(
                out=Qz[:, 4 * wb:4 * wb + 4, eh * 64:(eh + 1) * 64],
                in0=banks[6][:, 0:256].rearrange("p (h d) -> p h d", h=4), scalar1=0.125),
                reads=[bres[6], Qzres], writes=[Qzres])

        ATB = r2 + 24576
        Pb = [view(ATB + i * 512, 512, BF16) for i in range(2)]
        PTb = [view(ATB + 1024 + i * 512, 512, BF16) for i in range(2)]
        atm = [view(ATB + 2048 + i * 2048, 2048, BF16) for i in range(2)]
        Osm = view(ATB + 6144, 4096, BF16, "p (s u d) -> p s u d", s=16, u=2)
        Pres = [Res("P%d" % i) for i in range(2)]
        PTres = [Res("PT%d" % i) for i in range(2)]
        atmres = [Res("atm%d" % i) for i in range(2)]
        stres = {}

        def sres(q, b, h):
            k = (q, b, h)
            if k not in stres:
                stres[k] = Res("st%s" % (k,))
            return stres[k]

        att_n = {"n": 0}

        def softmax_head(sbank, b2, h, sink_ap, per_head_stats):
            n = att_n["n"]
            att_n["n"] += 1
            pb = n % 2
            mx = st_att[:, 0, b2, h:h + 1]
            ngm = st_att[:, 1, b2, h:h + 1]
            rsum = st_att[:, 2, b2, h:h + 1]
            R = [sres(q, b2, h) for q in range(6)]
            S.add("dve", lambda e: e.reduce_max(out=ngm, in_=banks[sbank][:, 0:258], axis=AX.X, negate=True),
                  reads=[bres[sbank]], writes=[R[1]])
            S.add("act", lambda e: e.activation(out=Pb[pb], in_=banks[sbank][:, 0:256], func=AF.Exp, bias=ngm,
                                                scale=1.0, accum_out=rsum),
                  reads=[bres[sbank], R[1]], writes=[Pres[pb], R[2]])
            if per_head_stats:
                es = st_att[:, 3, b2, h:h + 1]
                den = st_att[:, 4, b2, h:h + 1]
                rden = st_att[:, 5, b2, h:h + 1]
                S.add("act", lambda e: e.activation(out=es, in_=ngm, func=AF.Exp, bias=sink_ap, scale=1.0),
                      reads=[R[1]] + CONST, writes=[R[3]])
                S.add("dve", lambda e: e.tensor_tensor(out=den, in0=rsum, in1=es, op=ALU.add),
                      reads=[R[2], R[3]], writes=[R[4]])
                S.add("dve", lambda e: e.reciprocal(out=rden, in_=den), reads=[R[4]], writes=[R[5]])
            return pb

        def transpose_probs(pb):
            tb = 2 + pb
            for c in range(2):
                pe_T(tb, c, Pb[pb][:, c * 128:(c + 1) * 128], [Pres[pb]])
            copy_op("act", PTb[pb], banks[tb][:, 0:256], [bres[tb]], [PTres[pb]])

        for h in range(16):
            qb = 6 + (h // 4) % 2
            pe_T(qb, h % 4, Qz[:, h, :], [Qzres])
            if h % 4 == 3:
                for hh in range(h - 3, h + 1):
                    cc = (hh // 4) // 2
                    copy_op("dve" if qb == 6 else "act", QTz[:, cc, :, hh, :],
                            banks[qb][:, (hh % 4) * 128:(hh % 4 + 1) * 128].rearrange("p (s t) -> p s t", s=16),
                            [bres[qb], QTzres], [QTzres])
        Osres = Res("Osm1")

        def sample_scores(s):
            sbank = s % 2
            S.add("pe", lambda e: e.matmul(banks[sbank][:, 256:258], lhsT=ident, rhs=sinkrep_b2,
                                           start=True, stop=True), reads=CONST, writes=[bres[sbank]])
            for c in range(2):
                S.add("pe", lambda e, c=c: e.matmul(
                    banks[sbank][:, 0:128], lhsT=QTz[:, c, s, :, :].rearrange("p h t -> p (h t)"), rhs=ckTb[:, c, s, :],
                    start=(c == 0), stop=False), reads=[QTzres, SAres], writes=[bres[sbank]])
            S.add("pe", lambda e: e.matmul(banks[sbank][:, 0:128], lhsT=ident, rhs=mSc, start=False, stop=True),
                  reads=CONST + [SAres], writes=[bres[sbank]])
            for c in range(2):
                S.add("pe", lambda e, c=c: e.matmul(
                    banks[sbank][:, 128:256], lhsT=QTz[:, c, s, :, :].rearrange("p h t -> p (h t)"), rhs=KT8[:, c, :],
                    start=(c == 0), stop=False), reads=[QTzres, KT8res], writes=[bres[sbank]])
            S.add("pe", lambda e: e.matmul(banks[sbank][:, 128:256], lhsT=ident, rhs=mSn[:, s, :], start=False, stop=True),
                  reads=CONST + [SAres], writes=[bres[sbank]])

        spb = {}

        def sample_E1(s):
            spb[s] = softmax_head(s % 2, s % 2, s, sinkrep_t[:, 0:1], True)

        def sample_V(s):
            b2 = s % 2
            pb = spb[s]
            ob = 4 + s % 2
            S.add("pe", lambda e, ob=ob, pb=pb, s=s: e.matmul(
                banks[ob][:, 0:256], lhsT=PTb[pb][:, 0:128], rhs=cvb[:, s, :], start=True, stop=False),
                reads=[PTres[pb], SAres], writes=[bres[ob]])
            S.add("pe", lambda e, ob=ob, pb=pb: e.matmul(
                banks[ob][:, 0:256], lhsT=PTb[pb][:, 128:256], rhs=Vb[:, 9, :], start=False, stop=True),
                reads=[PTres[pb], Vres[8]], writes=[bres[ob]])
            for g in range(4):
                S.add("dve", lambda e, ob=ob, g=g, s=s, b2=b2: e.tensor_scalar_mul(
                    out=Osm1[32 * g:32 * g + 32, s, :], in0=banks[ob][32 * g:32 * g + 32, 64 * g:64 * g + 64],
                    scalar1=st_att[32 * g:32 * g + 32, 5, b2, s:s + 1]),
                    reads=[bres[ob], sres(5, b2, s)], writes=[Osres])

        def gen_sample():
            sample_scores(0)
            sample_scores(1)
            sample_E1(0)
            sample_scores(2)
            sample_E1(1)
            transpose_probs(spb[0])
            for s in range(16):
                if s + 3 < 16:
                    sample_scores(s + 3)
                if s + 2 < 16:
                    sample_E1(s + 2)
                if s + 1 < 16:
                    transpose_probs(spb[s + 1])
                sample_V(s)
                yield

        qrest = gen_q_rest()
        for _ in gen_sample():
            try:
                next(qrest)
            except StopIteration:
                pass
        _drain(qrest)
        Osmres = Res("Osm")
        for u in range(2):
            S.add("dve", lambda e, u=u: e.tensor_copy(out=Osm[:, :, u, :], in_=Osm1), reads=[Osres, Osmres], writes=[Osmres])
        for s in range(16):
            tb = 6 + s % 2
            pe_T(tb, 0, Osm[:, s, :, :].rearrange("p u d -> p (u d)"), [Osmres])
            for eh in range(2):
                srcv = banks[tb][eh * 64:(eh + 1) * 64, 0:128].rearrange("p (j u t) -> p j u t", j=8, u=2)[:, :, eh, :]
                dstv = QT[eh * 64:(eh + 1) * 64, :, 1024 + s * 8:1024 + (s + 1) * 8]
                copy_op("act" if s % 2 else "dve", dstv, srcv, [bres[tb]], [QTres[8]])
        attnT = QT
        attnres = QTres

        sample_dead = [SAres, Qzres, QTzres, Osres]

        def prompt_scores(t, h):
            kvh = h // 4
            base = (h % 2) * 64
            ch = h // 2
            sbank = h % 2
            mk = mask0b if t == 0 else maskNb
            kcols = slice(t * 128, t * 128 + 256)
            prev_i = 9 if t == 0 else t - 1
            S.add("pe", lambda e: e.matmul(banks[sbank][:, 256:258], lhsT=ident, rhs=sinks_b2[:, 2 * h:2 * h + 2],
                                           start=True, stop=True), reads=CONST, writes=[bres[sbank]])
            S.add("pe", lambda e: e.matmul(
                banks[sbank][:, 0:256], lhsT=QT[base:base + 64, ch, tcols(t)], rhs=KTd[base:base + 64, kvh, kcols],
                start=True, stop=False),
                reads=[QTres[t], KTres[prev_i], KTres[t]], writes=[bres[sbank]])
            S.add("pe", lambda e: e.matmul(banks[sbank][:, 0:256], lhsT=ident, rhs=mk, start=False, stop=True),
                  reads=CONST, writes=[bres[sbank]])

        ATT_HEADS = [(t, h) for t in range(8) for h in range(16)]
        NH = len(ATT_HEADS)
        att_pb = {}

        def att_E1(idx):
            t, h = ATT_HEADS[idx]
            att_pb[idx] = softmax_head(h % 2, t % 2, h, sinks_t[:, h:h + 1], False)

        def att_E2(idx):
            pb = att_pb[idx]
            tb = 2 + pb
            for c in range(2):
                pe_T(tb, c, Pb[pb][:, c * 128:(c + 1) * 128], [Pres[pb]])
            copy_op("dve" if idx % 2 else "act", PTb[pb], banks[tb][:, 0:256], [bres[tb]], [PTres[pb]])

        def att_V(idx):
            t, h = ATT_HEADS[idx]
            pb = att_pb[idx]
            b2 = t % 2
            ab = atm[b2]
            kvh = h // 4
            prev_i = 9 if t == 0 else t - 1
            ob = 4 + h // 8
            oc = (h % 8) * 64
            S.add("pe", lambda e: e.matmul(
                banks[ob][:, oc:oc + 64], lhsT=PTb[pb][:, 0:128], rhs=Vb[:, t, kvh * 64:(kvh + 1) * 64],
                start=True, stop=False), reads=[PTres[pb], Vres[prev_i]], writes=[bres[ob]])
            S.add("pe", lambda e: e.matmul(
                banks[ob][:, oc:oc + 64], lhsT=PTb[pb][:, 128:256], rhs=Vb[:, t + 1, kvh * 64:(kvh + 1) * 64],
                start=False, stop=True), reads=[PTres[pb], Vres[t]], writes=[bres[ob]])
            if h % 8 == 7:
                h0 = h - 7
                hs = slice(h0, h0 + 8)
                Rg = [sres(q, b2, hh) for q in (1, 2) for hh in range(h0, h0 + 8)]
                Wg = [sres(q, b2, hh) for q in (3, 4, 5) for hh in range(h0, h0 + 8)]
                S.add("dve", lambda e: e.tensor_tensor(out=st_att[:, 3, b2, hs], in0=st_att[:, 1, b2, hs],
                                                       in1=sinks_t[:, hs], op=ALU.add),
                      reads=Rg + CONST, writes=Wg)
                S.add("act", lambda e: e.activation(out=st_att[:, 3, b2, hs], in_=st_att[:, 3, b2, hs], func=AF.Exp),
                      reads=Wg, writes=Wg)
                S.add("dve", lambda e: e.tensor_tensor(out=st_att[:, 4, b2, hs], in0=st_att[:, 2, b2, hs],
                                                       in1=st_att[:, 3, b2, hs], op=ALU.add), reads=Rg + Wg, writes=Wg)
                S.add("dve", lambda e: e.reciprocal(out=st_att[:, 5, b2, hs], in_=st_att[:, 4, b2, hs]),
                      reads=Wg, writes=Wg)
                rdb = st_att[:, 5, b2, hs].unsqueeze(2).broadcast_to([128, 8, 64])
                S.add("dve", lambda e: e.tensor_tensor(
                    out=ab[:, h0 * 64:(h0 + 8) * 64].rearrange("p (h d) -> p h d", h=8),
                    in0=banks[ob][:, :].rearrange("p (h d) -> p h d", h=8), in1=rdb, op=ALU.mult),
                    reads=[bres[ob]] + Wg, writes=[atmres[b2]])
            if h == 15:
                for p4 in range(2):
                    for j in range(4):
                        pe_T(6, j, ab[:, (4 * p4 + j) * 128:(4 * p4 + j + 1) * 128], [atmres[b2]])
                    copy_op("act" if p4 == 0 else "dve", QT[:, 4 * p4:4 * p4 + 4, tcols(t)],
                            banks[6][:, 0:512].rearrange("p (c t) -> p c t", c=4), [bres[6]], [QTres[t]])

        def gen_att_prompt():
            prompt_scores(*ATT_HEADS[0])
            prompt_scores(*ATT_HEADS[1])
            att_E1(0)
            prompt_scores(*ATT_HEADS[2])
            att_E1(1)
            att_E2(0)
            for k in range(NH):
                if k + 3 < NH:
                    prompt_scores(*ATT_HEADS[k + 3])
                if k + 2 < NH:
                    att_E1(k + 2)
                if k + 1 < NH:
                    att_E2(k + 1)
                att_V(k)
                yield

        gvres = [Res("gvf%d" % i) for i in range(NT)]
        vlnb = view(r2, 18432, BF16, "p (t c) -> p t c", t=NT)
        gnb = view(r2 + 18432, 4096)
        vlnres = [Res("vln%d" % i) for i in range(NT)]
        gnres = Res("gnB")
        uTres = [Res("uT%d" % i) for i in range(NT)]
        lnres = [Res("ln0"), Res("ln1")]

        bnall = view(SMALL_EXTRA, 4 * NT * 4, F32, "p (t k) -> p t k", t=NT)
        bnst9 = view(SMALL_EXTRA + 160, 4 * NT * 12, F32, "p (t k) -> p t k", t=NT)

        def gen_gv():
            r2old = [xtres[0], xtres[1], KT8res, Kb8res] + kdres + kvfres
            S.add("sp", lambda e: e.dma_start(out=gnb, in_=gnB[:, :]), writes=[gnres] + r2old, dma_key="gn")
            for wb in range(4):
                off, rw = w_use(B_GV[wb])
                wt = v_k16(off)
                for t in range(NT):
                    gv_group(wb, t, wt, rw)
                    yield
                w_done(B_GV[wb])

        def gv_group(wb, t, wt, rw):
            bk = 7
            for kc in range(16):
                S.add("pe", lambda e, kc=kc: e.matmul(
                    banks[bk][:, 0:256], lhsT=hT[:, kc, tcols(t)], rhs=wt[:, kc, :], start=(kc == 0), stop=(kc == 15)),
                    reads=[hTres[t][kc], rw], writes=[bres[bk]])
            S.add("dve", lambda e: e.tensor_copy(out=gvf[:, t, wb * 256:(wb + 1) * 256], in_=banks[bk][:, 0:256]),
                  reads=[bres[bk]], writes=[gvres[t]] + (sample_dead if wb == 0 else []))

        def u_group(wb, cl, tg, wt, rw):
            m = wb * 2 + cl
            bk, lo, hi = fm_group(wt, cl, 16, hT, hTr, tg, [rw], [0, 1, 2, 3, 4, 5, 6, 7])
            S.add("act", lambda e: e.activation(out=uT[:, m, lo:hi], in_=banks[bk][:, 0:384], func=AF.Gelu_apprx_tanh),
                  reads=[bres[bk]],
                  writes=[uTres[t] for t in tiles_of_tg(tg)]
                  + [gvres[t] for t in range((m * 2304) // 4096, ((m + 1) * 2304 - 1) // 4096 + 1)])

        def ln_part1():
            R = lnres[0]
            for t in range(NT):
                S.add("act", lambda e, t=t: e.activation(out=gvf[:, t, :], in_=gvf[:, t, :], func=AF.Gelu_apprx_tanh),
                      reads=[gvres[t]], writes=[gvres[t]])
                for c in range(2):
                    S.add("dve", lambda e, t=t, c=c: e.bn_stats(out=bnst9[:, t, c * 6:(c + 1) * 6],
                                                                in_=gvf[:, t, c * 512:(c + 1) * 512]),
                          reads=[gvres[t]], writes=[R])
                S.add("dve", lambda e, t=t: e.bn_aggr(out=bnall[:, t, 0:2], in_=bnst9[:, t, :]), reads=[R], writes=[R])

        def ln_part2():
            R = lnres[0]
            S.add("dve", lambda e: e.tensor_scalar_add(out=bnall[:, :, 2], in0=bnall[:, :, 1], scalar1=EPS),
                  reads=[R], writes=[R])
            S.add("act", lambda e: e.activation(out=bnall[:, :, 2], in_=bnall[:, :, 2], func=AF.Sqrt), reads=[R], writes=[R])
            S.add("dve", lambda e: e.reciprocal(out=bnall[:, :, 3], in_=bnall[:, :, 2]), reads=[R], writes=[R])
            for t in range(NT):
                S.add("dve", lambda e, t=t: e.tensor_scalar(out=gvf[:, t, :], in0=gvf[:, t, :], scalar1=bnall[:, t, 0:1],
                                                            scalar2=bnall[:, t, 3:4], op0=ALU.subtract, op1=ALU.mult),
                      reads=[R, gvres[t]], writes=[gvres[t]])
                S.add("dve", lambda e, t=t: e.tensor_tensor(out=vlnb[:, t, :], in0=gvf[:, t, :], in1=gnb, op=ALU.mult),
                      reads=[gvres[t], gnres], writes=[vlnres[t]])
                if t == 8:
                    S.add("dve", lambda e, t=t: e.tensor_tensor(out=gvf[:, t, :], in0=gvf[:, t, :], in1=gnb, op=ALU.mult),
                          reads=[gvres[t], gnres, vlnres[t]], writes=[gvres[t]])
                    S.add("sp", lambda e, t=t: e.dma_start(out=sg_out[:, :], in_=gvf[:, t, :]), reads=[gvres[t]], dma_key="osg")

        _interleave(gen_att_prompt(), gen_gv(), k=1)
        ln_part1()
        ln_part2()
        for wb in range(4):
            off, rw = w_use(B_U[wb])
            wt = v_k16(off)
            for cl in range(2):
                for tg in range(3):
                    u_group(wb, cl, tg, wt, rw)
            w_done(B_U[wb])

        SPB = r2 + 22528
        wst = view(SPB, 4096, F32, "p (g t) -> p g t", g=8)
        wsm = view(SPB + 4096, 2048, BF16, "p (g t) -> p g t", g=8)
        wsms = view(SPB + 6144, 2048, BF16, "p (g t) -> p g t", g=8)
        msk = view(SPB + 8192, 512)
        bsb = view(SPB + 8704, 4096, F32, "p (g t) -> p g t", g=8)
        wres_sp = Res("wst")
        mres = Res("msk")
        wsmres = Res("wsm")
        bsres = Res("bsb")
        att_bufs = Pres + PTres + atmres + [Osmres]
        for which in range(2):
            srcw = wsT if which == 0 else wsTr
            srcm = trilT if which == 0 else blkm
            dstw = wsm if which == 0 else wsms
            S.add("sp", lambda e, srcw=srcw: e.dma_start(out=wst, in_=srcw.rearrange("g s t -> s g t")),
                  writes=[wres_sp] + (att_bufs if which == 0 else []), dma_key="wst")
            S.add("sp", lambda e, srcm=srcm: e.dma_start(out=msk, in_=srcm[:, :]),
                  writes=[mres] + (att_bufs if which == 0 else []), dma_key="msk")
            for g in range(8):
                S.add("dve", lambda e, g=g, dstw=dstw: e.tensor_tensor(out=dstw[:, g, :], in0=wst[:, g, :], in1=msk,
                                                                      op=ALU.mult),
                      reads=[wres_sp, mres], writes=[wsmres])
        S.add("sp", lambda e: e.dma_start(out=bsb, in_=bsB[:, :, :]), writes=[bsres] + att_bufs, dma_key="bsb")
        bsbs = wst
        S.add("sp", lambda e: e.dma_start(out=bsbs, in_=bsBs[:, :, :]), writes=[wres_sp], dma_key="wst")
        sptmp = [view(96256 + i * 2048, 2048) for i in range(2)]
        sptres = [Res("sptmp%d" % i) for i in range(2)]
        kv_dead = KTres + Vres
        for t in range(NT):
            wm = wsms if t == 8 else wsm
            bb = bsbs if t == 8 else bsb
            for half in range(2):
                n = t * 2 + half
                bk = n % 4
                for gl in range(4):
                    g = half * 4 + gl
                    S.add("pe", lambda e, bk=bk, gl=gl, g=g, t=t, wm=wm: e.matmul(
                        banks[bk][:, gl * 128:(gl + 1) * 128], lhsT=vlnb[:, t, g * 128:(g + 1) * 128], rhs=wm[:, g, :],
                        start=True, stop=True), reads=[vlnres[t], wsmres], writes=[bres[bk]])
                tp = sptmp[n % 2]
                S.add("dve", lambda e, bk=bk, tp=tp, bb=bb, half=half: e.tensor_tensor(
                    out=tp, in0=banks[bk][:, :], in1=bb[:, half * 4:(half + 1) * 4, :].rearrange("p g t -> p (g t)"),
                    op=ALU.add), reads=[bres[bk], bsres, wres_sp], writes=[sptres[n % 2]] + (kv_dead if n < 2 else []))
                S.add("pool" if n % 2 else "dve", lambda e, tp=tp, half=half, t=t: e.tensor_tensor(
                    out=uT[:, half * 4:(half + 1) * 4, tcols(t)], in0=uT[:, half * 4:(half + 1) * 4, tcols(t)],
                    in1=tp.rearrange("p (g t) -> p g t", g=4), op=ALU.mult),
                    reads=[sptres[n % 2], uTres[t]], writes=[uTres[t]])
        aT = uT
        aTres = uTres

        mixres = [Res("mix%d" % i) for i in range(NT)]
        sga = [view(VLN_OFF + i * 1536, 1536) for i in range(6)]
        sgb = [view(VLN_OFF + 9216 + i * 1536, 1536) for i in range(6)]
        sgares = [Res("sga%d" % i) for i in range(6)]
        sgbres = [Res("sgb%d" % i) for i in range(6)]
        R2old = vlnres + [gnres, wres_sp, mres, wsmres, bsres]
        gate_first = {"a": True, "m": True}
        ALLB = list(range(8))
        for i in range(8):
            ga, gb, pab = B_GATE[i]
            for (blk, dst, dres) in ((ga, sga, sgares), (gb, sgb, sgbres)):
                off, rw = w_use(blk)
                wt = v_k16(off)
                for cl in range(2):
                    for tg in range(3):
                        q = cl * 3 + tg
                        bk, lo, hi = fm_group(wt, cl, 16, hT, hTr, tg, [rw], ALLB)
                        old = (gvres + sptres) if gate_first["a"] else []
                        S.add("act", lambda e, q=q, bk=bk, dst=dst: e.activation(out=dst[q], in_=banks[bk][:, 0:384],
                                                                                  func=AF.Sigmoid),
                              reads=[bres[bk]], writes=[dres[q]] + old)
                gate_first["a"] = False
                w_done(blk)
            offp, rp = w_use(pab)
            wpa = view(offp, 4096, BF16, "p (k c) -> p k c", k=8)
            wpb = view(offp + 4096, 4096, BF16, "p (k c) -> p k c", k=8)
            for cl in range(2):
                m = i * 2 + cl
                for tg in range(3):
                    q = cl * 3 + tg
                    tl = tiles_of_tg(tg)
                    bka, lo, hi = fm_group(wpa, cl, 8, aT, lambda t, kc: aTres[t], tg, [rp], ALLB)
                    S.add("dve", lambda e, q=q, bka=bka: e.tensor_tensor(out=sga[q], in0=sga[q], in1=banks[bka][:, 0:384],
                                                                          op=ALU.mult),
                          reads=[sgares[q], bres[bka]], writes=[sgares[q]])
                    bkb, lo, hi = fm_group(wpb, cl, 8, attnT, lambda t, kc: attnres[t], tg, [rp], ALLB)
                    S.add("dve", lambda e, q=q, bkb=bkb: e.tensor_tensor(out=sgb[q], in0=sgb[q], in1=banks[bkb][:, 0:384],
                                                                          op=ALU.mult),
                          reads=[sgbres[q], bres[bkb]], writes=[sgbres[q]])
                    S.add("pool", lambda e, q=q, m=m, lo=lo, hi=hi: e.tensor_tensor(
                        out=mixT[:, m, lo:hi], in0=sga[q], in1=sgb[q], op=ALU.add),
                        reads=[sgares[q], sgbres[q]], writes=[mixres[t] for t in tl] + (R2old if gate_first["m"] else []))
                    gate_first["m"] = False
            w_done(pab)

        x1res = [Res("x1_%d" % i) for i in range(NT)]
        R1B_all = [r for l in hTres for r in l] + QTres + uTres + gvres + KTres + Vres + sptres + sgares + sgbres
        hT_all = [r for l in hTres for r in l]
        rest_all = QTres + uTres + gvres + KTres + Vres + sptres + sgares + sgbres
        for t in range(NT):
            guard = hT_all if t < 5 else (rest_all if t == 5 else [])
            S.add("sp", lambda e, t=t: e.dma_start(out=x1[:, t, :], in_=xc[t * 128:(t + 1) * 128, :]),
                  writes=[x1res[t]] + guard, dma_key="x1_%d" % (t % 3))
        hhres = [Res("hh%d" % i) for i in range(NT)]

        rms_state = {}

        def rms_a(t, n):
            b = n % 3
            ssr = Res("ss2")
            ss = st2[:, 2 * (n % 16):2 * (n % 16) + 1]
            rs = st2[:, 2 * (n % 16) + 1:2 * (n % 16) + 2]
            S.add("act", lambda e: e.activation(out=hb[b], in_=x1[:, t, :], func=AF.Square, accum_out=ss),
                  reads=[x1res[t]], writes=[hbres[b], ssr])
            rms_state[n] = (ss, rs, ssr)

        def rms_b(n):
            ss, rs, ssr = rms_state[n]
            S.add("dve", lambda e: e.tensor_scalar(out=rs, in0=ss, scalar1=1.0 / D, scalar2=EPS, op0=ALU.mult, op1=ALU.add),
                  reads=[ssr], writes=[ssr])
            S.add("act", lambda e: e.activation(out=rs, in_=rs, func=AF.Sqrt), reads=[ssr], writes=[ssr])
            S.add("dve", lambda e: e.reciprocal(out=rs, in_=rs), reads=[ssr], writes=[ssr])
            return rs, ssr

        def norm_stage1b(t, n):
            b = n % 3
            rs, ssr = rms_b(n)
            S.add("dve", lambda e: e.tensor_scalar_mul(out=hb[b], in0=x1[:, t, :], scalar1=rs),
                  reads=[ssr, x1res[t], hbres[b]], writes=[hbres[b]])

        def norm_stage2(t, n, gt, dstT, dres, tbanks):
            b = n % 3
            for half in range(4):
                bk = tbanks[half]
                for j in range(4):
                    c = half * 4 + j
                    pe_T(bk, j, hb[b][:, c * 128:(c + 1) * 128], [hbres[b]])
                en_ = "act" if half % 2 == 0 else "dve"
                for j in range(4):
                    c = half * 4 + j
                    src = banks[bk][:, j * 128:(j + 1) * 128]
                    dst = dstT[:, c, tcols(t)]
                    if en_ == "act":
                        S.add("act", lambda e, dst=dst, src=src, c=c: e.activation(
                            out=dst, in_=src, func=AF.Copy, scale=gt[:, c:c + 1]),
                            reads=[bres[bk]] + CONST, writes=[dres])
                    else:
                        S.add("dve", lambda e, dst=dst, src=src, c=c: e.tensor_scalar_mul(
                            out=dst, in0=src, scalar1=gt[:, c:c + 1]),
                            reads=[bres[bk]] + CONST, writes=[dres])

        wo_n = {"n": 0}

        def wo_group(cg, t, wt, rw):
            bk = wo_n["n"] % 4
            wo_n["n"] += 1
            for kc in range(16):
                S.add("pe", lambda e, kc=kc: e.matmul(
                    banks[bk][:, 0:256], lhsT=mixT[:, kc, tcols(t)], rhs=wt[:, kc, :], start=(kc == 0), stop=(kc == 15)),
                    reads=[mixres[t], rw], writes=[bres[bk]])
            S.add("dve", lambda e: e.tensor_tensor(
                out=x1[:, t, cg * 256:(cg + 1) * 256], in0=banks[bk][:, 0:256], in1=x1[:, t, cg * 256:(cg + 1) * 256],
                op=ALU.add), reads=[bres[bk], x1res[t]], writes=[x1res[t]])

        NCGO = 6
        for cg in range(NCGO):
            off, rw = w_use(B_WO[cg])
            wt = v_k16(off)
            for t in range(NT):
                wo_group(cg, t, wt, rw)
            w_done(B_WO[cg])
        tail_w = []
        for cg in range(NCGO, 8):
            off, rw = w_use(B_WO[cg])
            tail_w.append((cg, v_k16(off), rw))
        for t in range(NT + 1):
            if t < NT:
                for (cg, wt, rw) in tail_w:
                    wo_group(cg, t, wt, rw)
                rms_a(t, t)
            if 1 <= t <= NT:
                norm_stage1b(t - 1, t - 1)
            if 2 <= t:
                norm_stage2(t - 2, t - 2, g2t, hhT, hhres[t - 2], [4, 5, 6, 7])
        w_done(B_WO[7])

        def norm_drain():
            norm_stage2(NT - 1, NT - 1, g2t, hhT, hhres[NT - 1], [4, 5, 6, 7])

        actT = [view(r2 + i * 9216, 9216, BF16, "p (f t) -> p f t", f=4) for i in range(2)]
        actres = [[Res("act%d_%d" % (i, t)) for t in range(NT)] for i in range(2)]
        silt = [view(r2 + 18432 + i * 1536, 1536) for i in range(2)]
        silres = [Res("sil%d" % i) for i in range(2)]
        gfbv = view(r2 + 21504, 8192)
        gfres = Res("gfb")
        S.add("sp", lambda e: e.dma_start(out=gfbv, in_=gfB[:, :]), writes=[gfres] + mixres, dma_key="gf")
        ffn_n = {"n": 0}
        dn_n = {"n": 0}
        for gi in range(NG):
            gu, dn = B_FF[gi]
            ab = gi % 2
            for hf in range(2):
                offg, rg = w_use(gu[hf][0])
                offu, ru = w_use(gu[hf][1])
                wg_ = v_k16(offg)
                wu_ = v_k16(offu)
                cltg = [(cl, tg) for cl in range(2) for tg in range(3)]
                if gi == 0 and hf == 0:
                    cltg = [(0, 0), (0, 1), (1, 0), (1, 1), None, (0, 2), (1, 2)]
                for ct in cltg:
                    if ct is None:
                        norm_drain()
                        continue
                    cl, tg = ct
                    f = hf * 2 + cl
                    if True:
                        lo, hi = TG[tg]
                        n = ffn_n["n"]
                        ffn_n["n"] += 1
                        bg = (n % 2) * 2
                        bu = bg + 1
                        tl = tiles_of_tg(tg)
                        for (bk, wt, rr) in ((bg, wg_, rg), (bu, wu_, ru)):
                            for kc in range(16):
                                S.add("pe", lambda e, bk=bk, kc=kc, wt=wt, lo=lo, hi=hi, cl=cl: e.matmul(
                                    banks[bk][:, 0:384], lhsT=wt[:, kc, cl * 128:(cl + 1) * 128], rhs=hhT[:, kc, lo:hi],
                                    start=(kc == 0), stop=(kc == 15)),
                                    reads=[hhres[t] for t in tl] + [rr], writes=[bres[bk]])
                        sl = silt[n % 2]
                        old = mixres if n < 2 else []
                        S.add("act", lambda e, sl=sl, bg=bg: e.activation(out=sl, in_=banks[bg][:, 0:384], func=AF.Silu),
                              reads=[bres[bg]], writes=[silres[n % 2]] + old)
                        S.add("dve", lambda e, sl=sl, bu=bu, ab=ab, f=f, lo=lo, hi=hi: e.tensor_tensor(
                            out=actT[ab][:, f, lo:hi], in0=sl, in1=banks[bu][:, 0:384], op=ALU.mult),
                            reads=[silres[n % 2], bres[bu]], writes=[actres[ab][t] for t in tl] + old)
                w_done(gu[hf][1])
            last = (gi == NG - 1)
            dnw = []
            if last:
                for ch in range(2):
                    offd, rd_ = w_use(dn[ch])
                    dnw.append((view(offd, 8192, BF16, "p (f c) -> p f c", f=4), rd_))
            for ch_t in ([(ch, None) for ch in range(2)] if not last else [(ch, t) for t in range(NT) for ch in range(2)]):
                ch = ch_t[0]
                if not last:
                    offd, rd_ = w_use(dn[ch])
                    wd_ = view(offd, 8192, BF16, "p (f c) -> p f c", f=4)
                    tiles = range(NT)
                else:
                    wd_, rd_ = dnw[ch]
                    tiles = [ch_t[1]]
                for t in tiles:
                    for c2 in range(2):
                        bk = 4 + dn_n["n"] % 4
                        dn_n["n"] += 1
                        for f in range(4):
                            S.add("pe", lambda e, bk=bk, f=f, t=t, c2=c2, ab=ab, wd_=wd_: e.matmul(
                                banks[bk][:, 0:512], lhsT=actT[ab][:, f, tcols(t)], rhs=wd_[:, f, c2 * 512:(c2 + 1) * 512],
                                start=(f == 0), stop=(f == 3)), reads=[actres[ab][t], rd_], writes=[bres[bk]])
                        col = ch * 1024 + c2 * 512
                        S.add("dve", lambda e, bk=bk, t=t, col=col: e.tensor_tensor(
                            out=x1[:, t, col:col + 512], in0=banks[bk][:, 0:512], in1=x1[:, t, col:col + 512], op=ALU.add),
                            reads=[bres[bk], x1res[t]], writes=[x1res[t]])
                    if last and ch == 1:
                        rms_a(t, NT + t)
                    for tf in (([t - 1] if t >= 1 else []) + ([t] if t == NT - 1 else [])) if (last and ch == 1) else []:
                        rs, ssr = rms_b(NT + tf)
                        yA = Res("yA")
                        yB = Res("yB")
                        S.add("dve", lambda e, t=tf, rs=rs: e.scalar_tensor_tensor(
                            out=x1[:, t, 0:1024], in0=x1[:, t, 0:1024], scalar=rs, in1=gfbv[:, 0:1024],
                            op0=ALU.mult, op1=ALU.mult), reads=[ssr, x1res[tf], gfres], writes=[yA])
                        S.add("act", lambda e, t=tf, rs=rs: e.activation(out=x1[:, t, 1024:2048], in_=x1[:, t, 1024:2048],
                                                                         func=AF.Copy, scale=rs),
                              reads=[ssr, x1res[tf]], writes=[yB])
                        S.add("pool", lambda e, t=tf: e.tensor_tensor(out=x1[:, t, 1024:2048], in0=x1[:, t, 1024:2048],
                                                                      in1=gfbv[:, 1024:2048], op=ALU.mult),
                              reads=[yB, gfres], writes=[yB])
                        S.add("sp", lambda e, t=tf: e.dma_start(out=y_out[t * 128:(t + 1) * 128, :], in_=x1[:, t, :]),
                              reads=[yA, yB], writes=[x1res[tf]], dma_key="oy%d" % (tf % 3))
                if not last:
                    w_done(dn[ch])
            if last:
                w_done(dn[1])
        S.emit(nc, st)
    return nc


def _consts():
    i = np.arange(128)[:, None]
    j = np.arange(128)[None, :]
    prev = np.where(j >= i, 0.0, NEG).astype(np.float32)
    cur = np.where(j <= i, 0.0, NEG).astype(np.float32)
    maskN = np.concatenate([prev, cur], 1)
    mask0_first = np.concatenate([np.full((128, 128), NEG, np.float32), cur], 1)
    t_row = (np.arange(128) % 8)[:, None]
    maskSc = np.where(j >= t_row, 0.0, NEG).astype(np.float32)
    maskSn = np.full((128, 16, 128), NEG, np.float32)
    for s in range(16):
        for tp in range(8):
            maskSn[:, s, s * 8 + tp] = np.where(tp <= t_row[:, 0], 0.0, NEG)
    trilT = (i <= j).astype(np.float32)
    si, ti = np.arange(128)[:, None], np.arange(128)[None, :]
    blkm = ((si // 8 == ti // 8) & (si % 8 <= ti % 8)).astype(np.float32)
    ident = np.eye(128, dtype=np.float32)
    return dict(maskN=maskN, mask0_first=mask0_first, maskSc=maskSc, maskSn=maskSn, trilT=trilT, blkm=blkm, ident=ident)


_NC_CACHE = {}


def make_in_maps(x_prompt, x_sample, cache_k, cache_v, norm1_g, w_in, gmlp_norm_g, w_s, b_s, sinks,
                 w_pa, w_pb, w_o, norm2_g, w_ff_gate, w_ff_up, w_ff_down, final_g):
    f = lambda a: np.ascontiguousarray(np.asarray(a, dtype=np.float32))
    x_prompt, x_sample, cache_k, cache_v = f(x_prompt), f(x_sample), f(cache_k), f(cache_v)
    C = _consts()

    ws = f(w_s)[0]
    wsT = np.ascontiguousarray(ws.transpose(0, 2, 1))
    wsTr = np.ascontiguousarray(np.tile(wsT[:, 0:8, 0:8], (1, 16, 16)))
    bs = f(b_s)[0]
    bsB = np.ascontiguousarray(np.broadcast_to(bs[None], (128, 8, 128)))
    bsBs = np.ascontiguousarray(np.broadcast_to(np.tile(bs[:, 0:8], (1, 16))[None], (128, 8, 128)))
    sk = f(sinks)[0]
    shared = dict(
        w_in=f(w_in)[0], w_pa=f(w_pa)[0], w_pb=f(w_pb)[0], w_o=f(w_o)[0],
        w_g=f(w_ff_gate)[0], w_u=f(w_ff_up)[0], w_d=f(w_ff_down)[0],
        g1T=np.ascontiguousarray(f(norm1_g)[0].reshape(16, 128).T),
        g2T=np.ascontiguousarray(f(norm2_g)[0].reshape(16, 128).T),
        gfB=np.ascontiguousarray(np.broadcast_to(f(final_g)[None], (128, D))),
        gnB=np.ascontiguousarray(np.broadcast_to(f(gmlp_norm_g)[0][None], (128, 1024))),
        wsT=wsT, wsTr=wsTr, trilT=C["trilT"], blkm=C["blkm"], bsB=bsB, bsBs=bsBs,
        sinksB=np.ascontiguousarray(np.broadcast_to(sk[None], (128, 16))),
        sinkrep=np.ascontiguousarray(np.repeat(sk, 8)[:, None]),
        sinksB2=np.ascontiguousarray(np.broadcast_to(np.repeat(sk, 2)[None], (128, 32))),
        sinkrep2=np.ascontiguousarray(np.repeat(np.repeat(sk, 8)[:, None], 2, axis=1)),
        maskN=C["maskN"], maskSc=C["maskSc"], maskSn=C["maskSn"], ident=C["ident"],
    )
    in_maps = []
    for c in range(NCORES):
        b, half = c // 2, c % 2
        xs = x_sample[c * 16:(c + 1) * 16].reshape(128, D)
        xp = x_prompt[b, half * 1024:(half + 1) * 1024]
        if half == 1:
            prev = x_prompt[b, 896:1024]
            mask0 = C["maskN"]
        else:
            prev = np.zeros((128, D), np.float32)
            mask0 = C["mask0_first"]
        xcat = np.ascontiguousarray(np.concatenate([xp, xs, prev], 0))
        ckc = cache_k[0, c * 16:(c + 1) * 16].reshape(16, 128, 256)
        cvc = cache_v[0, c * 16:(c + 1) * 16].reshape(16, 128, 256)
        ckT = np.ascontiguousarray(ckc.reshape(16, 128, 2, 128).transpose(2, 3, 0, 1))
        m = dict(shared)
        m.update(xc=xcat, ckT=ckT, ck=np.ascontiguousarray(ckc), cv=np.ascontiguousarray(cvc), mask0=mask0)
        in_maps.append(m)
    return in_maps


def kernel(**inputs):
    in_maps = make_in_maps(**inputs)
    if "nc" not in _NC_CACHE:
        _NC_CACHE["nc"] = build_program()
    nc = _NC_CACHE["nc"]
    res = run_bass_kernel_spmd(nc, in_maps, core_ids=list(range(NCORES)))
    return assemble(res.results)


def assemble(R):
    y_prompt = np.empty((4, 2048, D), np.float32)
    y_sample = np.empty((128, 8, D), np.float32)
    pk = np.empty((1, 4, 128, 4, 64), np.float32)
    pv = np.empty((1, 4, 128, 4, 64), np.float32)
    skw = np.empty((1, 128, 128, 4, 64), np.float32)
    svw = np.empty((1, 128, 128, 4, 64), np.float32)
    sg = np.empty((1, 128, 8, 8, 128), np.float32)
    for c in range(NCORES):
        b, half = c // 2, c % 2
        y = R[c]["y"]
        y_prompt[b, half * 1024:(half + 1) * 1024] = y[0:1024]
        y_sample[c * 16:(c + 1) * 16] = y[1024:1152].reshape(16, 8, D)
        if half == 1:
            pk[0, b] = R[c]["kv7"][:, 0:256].reshape(128, 4, 64)
            pv[0, b] = R[c]["kv7"][:, 256:512].reshape(128, 4, 64)
        skw[0, c * 16:(c + 1) * 16] = R[c]["skw"].reshape(16, 128, 4, 64)
        svw[0, c * 16:(c + 1) * 16] = R[c]["svw"].reshape(16, 128, 4, 64)
        sg[0, c * 16:(c + 1) * 16] = R[c]["sg"].reshape(16, 8, 8, 128)
    return (y_prompt, y_sample, pk, pv, skw, svw, sg)
```

```python
import contextlib
import numpy as np
import concourse.bass as bass
import concourse.mybir as mybir
from concourse.bass_utils import run_bass_kernel_spmd

F32 = mybir.dt.float32
BF16 = mybir.dt.bfloat16
AF = mybir.ActivationFunctionType
ALU = mybir.AluOpType
AX = mybir.AxisListType

ENGS = ("pe", "act", "dve", "pool", "sp")

D = 2048
DFF = 5632
INW = 7680
NT = 9
TOK = NT * 128
TOKP = TOK + 128
EPS = 1e-6
NEG = -1e30
NCORES = 8


class Res:
    __slots__ = ("name", "last_w", "readers", "excl")

    def __init__(self, name, excl=False):
        self.name = name
        self.last_w = None
        self.readers = []
        self.excl = excl


class Op:
    __slots__ = ("eng", "fn", "deps", "signal", "is_dma", "key", "val", "idx")

    def __init__(self, eng, fn, is_dma, key):
        self.eng = eng
        self.fn = fn
        self.deps = []
        self.signal = False
        self.is_dma = is_dma
        self.key = key
        self.val = None
        self.idx = None


class Sched:
    def __init__(self):
        self.q = {e: [] for e in ENGS}
        self.dma_count = {}
        self.stopped = False

    def add(self, eng, fn, reads=(), writes=(), dma_key=None):
        if self.stopped:
            return None
        is_dma = dma_key is not None
        op = Op(eng, fn, is_dma, dma_key)
        deps = {}
        for r in reads:
            if r.last_w is not None:
                deps[id(r.last_w)] = r.last_w
            if r.excl:
                for rd in r.readers:
                    if rd.is_dma or rd.eng != eng:
                        deps[id(rd)] = rd
        for w in writes:
            if w.last_w is not None:
                deps[id(w.last_w)] = w.last_w
            for rd in w.readers:
                deps[id(rd)] = rd
        for d in deps.values():
            if d is op:
                continue
            if (not d.is_dma) and (not is_dma) and d.eng == "pe" and eng == "pe":
                continue
            op.deps.append(d)
            d.signal = True
        for r in reads:
            if not is_dma:
                r.readers = [x for x in r.readers if x.is_dma or x.eng != eng]
            r.readers.append(op)
        for w in writes:
            w.last_w = op
            w.readers = []
        if is_dma:
            c = self.dma_count.get(dma_key, 0) + 1
            self.dma_count[dma_key] = c
            op.val = 16 * c
        op.idx = len(self.q[eng])
        self.q[eng].append(op)
        return op

    def emit(self, nc, stack):
        for e in ENGS:
            c = 0
            for op in self.q[e]:
                if op.is_dma:
                    continue
                if op.signal:
                    c += 1
                    op.val = c
        sems = {}

        def sem_of(key):
            if key not in sems:
                sems[key] = stack.enter_context(nc.semaphore("s%d" % len(sems)))
            return sems[key]

        for e in ("pe", "act", "dve", "pool"):
            sem_of(("eng", e))
        for k in self.dma_count:
            sem_of(("dma", k))

        def tok(d):
            if d.is_dma:
                return ("dma", d.key), d.val
            return ("eng", d.eng), d.val

        block = stack.enter_context(nc.Block())
        engobj = {"pe": block.tensor, "act": block.scalar, "dve": block.vector,
                  "pool": block.gpsimd, "sp": block.sync}

        def run_queue(e, eng):
            waited = {}
            for op in self.q[e]:
                need = {}
                for d in op.deps:
                    k, v = tok(d)
                    if waited.get(k, 0) >= v:
                        continue
                    if need.get(k, 0) < v:
                        need[k] = v
                for k, v in need.items():
                    eng.wait_ge(sem_of(k), v)
                    waited[k] = v
                inst = op.fn(eng)
                if op.is_dma:
                    inst.then_inc(sem_of(("dma", op.key)), 16)
                elif op.signal:
                    inst.then_inc(sem_of(("eng", e)), 1)
            if e == "sp":
                for k, c in self.dma_count.items():
                    eng.wait_ge(sem_of(("dma", k)), 16 * c)

        for e in ENGS:
            def body(eng, e=e):
                run_queue(e, eng)
            engobj[e](body)


def _drain(gen):
    for _ in gen:
        pass


def _interleave(main, filler, k=1):
    fdone = False
    for _ in main:
        for _ in range(k):
            if not fdone:
                try:
                    next(filler)
                except StopIteration:
                    fdone = True
    if not fdone:
        _drain(filler)


def build_program(stop=None):
    nc = bass.Bass("TRN2", target_bir_lowering=False)

    def din(name, shape):
        return nc.dram_tensor(name, list(shape), F32, kind="ExternalInput").ap()

    def dout(name, shape):
        return nc.dram_tensor(name, list(shape), F32, kind="ExternalOutput").ap()

    xc = din("xc", [TOKP, D])
    ckT = din("ckT", [2, 128, 16, 128])
    ck = din("ck", [16, 128, 256])
    cv = din("cv", [16, 128, 256])
    w_in = din("w_in", [D, INW])
    w_pa = din("w_pa", [1024, D])
    w_pb = din("w_pb", [1024, D])
    w_o = din("w_o", [D, D])
    w_g = din("w_g", [D, DFF])
    w_u = din("w_u", [D, DFF])
    w_d = din("w_d", [DFF, D])
    g1T = din("g1T", [128, 16])
    g2T = din("g2T", [128, 16])
    gfB = din("gfB", [128, D])
    gnB = din("gnB", [128, 1024])
    wsT = din("wsT", [8, 128, 128])
    wsTr = din("wsTr", [8, 128, 128])
    trilT = din("trilT", [128, 128])
    blkm = din("blkm", [128, 128])
    bsB = din("bsB", [128, 8, 128])
    bsBs = din("bsBs", [128, 8, 128])
    sinksB = din("sinksB", [128, 16])
    sinkrep = din("sinkrep", [128, 1])
    maskN = din("maskN", [128, 256])
    mask0 = din("mask0", [128, 256])
    maskSc = din("maskSc", [128, 128])
    maskSn = din("maskSn", [128, 16, 128])
    identd = din("ident", [128, 128])
    sinksB2 = din("sinksB2", [128, 32])
    sinkrep2 = din("sinkrep2", [128, 2])

    y_out = dout("y", [TOK, D])
    kv7_out = dout("kv7", [128, 512])
    skw_out = dout("skw", [16, 128, 256])
    svw_out = dout("svw", [16, 128, 256])
    sg_out = dout("sg", [128, 1024])

    S = Sched()

    with contextlib.ExitStack() as st:
        R1 = 112640
        R2 = 36864
        NSLOT = 4
        WS = 8192
        SS = 12288
        SMALL = 7168
        TOTAL = R1 + R2 + NSLOT * WS + SS + SMALL
        arena = st.enter_context(nc.sbuf_tensor("arena", [128, TOTAL // 4], F32))

        def view(off, nbytes, dt=F32, pat=None, **kw):
            assert off % 4 == 0 and nbytes % 4 == 0
            ap = arena[:, off // 4:(off + nbytes) // 4]
            if dt is not F32:
                ap = ap.bitcast(dt)
            if pat is not None:
                ap = ap.rearrange(pat, **kw)
            return ap

        hT = view(0, 40960, BF16, "p (c t) -> p c t", c=16)
        QT = view(40960, 18432, BF16, "p (c t) -> p c t", c=8)
        uT = view(59392, 18432, BF16, "p (c t) -> p c t", c=8)
        VLN_OFF = 77824
        KTd = view(96256, 10240, BF16, "p (h t) -> p h t", h=4)
        Vb = view(106496, 5120, BF16, "p (b c) -> p b c", b=10)
        gvf = view(59392, 36864, F32, "p (t c) -> p t c", t=NT)
        x1 = view(0, 73728, F32, "p (t c) -> p t c", t=NT)
        hhT = view(73728, 36864, BF16, "p (c t) -> p c t", c=16)
        r2 = R1
        mixT = view(r2, 36864, BF16, "p (c t) -> p c t", c=16)
        w0 = R1 + R2
        wslot = [w0 + i * WS for i in range(NSLOT)]
        wres = [Res("w%d" % i) for i in range(NSLOT)]
        s0 = w0 + NSLOT * WS
        hb = [view(s0 + i * 4096, 4096, BF16) for i in range(3)]
        hbres = [Res("hb%d" % i) for i in range(3)]
        m0 = s0 + SS
        _sm = [m0]

        def small(nbytes, dt=F32, pat=None, **kw):
            off = _sm[0]
            _sm[0] += (nbytes + 31) // 32 * 32
            assert _sm[0] <= m0 + SMALL, "small region overflow"
            return view(off, nbytes, dt, pat, **kw)

        ident = small(256, BF16)
        maskNb = small(512, BF16)
        mask0b = small(512, BF16)
        g1t = small(64)
        g2t = small(64)
        sinks_t = small(64)
        sinkrep_t = small(4)
        sinks_b2 = small(64, BF16)
        sinkrep_b2 = small(4, BF16)
        statA = small(4 * 64)
        st_att = small(4 * 6 * 2 * 16, F32, "p (q b h) -> p q b h", q=6, b=2)
        bnst = small(4 * 12 * 2, F32, "p (b k) -> p b k", b=2)
        bnmv = small(4 * 4 * 2, F32, "p (b k) -> p b k", b=2)
        st2 = small(4 * 64)
        SMALL_EXTRA = _sm[0]
        _sm[0] += 160 + 448
        assert _sm[0] <= m0 + SMALL

        banks = [st.enter_context(nc.psum_tensor("bank%d" % i, [128, 512], F32)) for i in range(8)]
        bres = [Res("bank%d" % i, excl=True) for i in range(8)]

        def bankbf(i):
            return banks[i][:].bitcast(BF16)

        cnt = {"ev": 0}

        def tcols(i):
            return slice(i * 128, (i + 1) * 128)

        def kblk(i):
            return 0 if i == 9 else i + 1

        def ev_engine():
            cnt["ev"] += 1
            return "act" if cnt["ev"] % 2 else "dve"

        def copy_op(eng_name, out, in_, reads, writes, scale=None):
            if eng_name == "act":
                if scale is None:
                    S.add("act", lambda e: e.copy(out=out, in_=in_), reads=reads, writes=writes)
                else:
                    S.add("act", lambda e: e.mul(out=out, in_=in_, mul=scale), reads=reads, writes=writes)
            else:
                if scale is None:
                    S.add(eng_name, lambda e: e.tensor_copy(out=out, in_=in_), reads=reads, writes=writes)
                else:
                    S.add(eng_name, lambda e: e.tensor_scalar_mul(out=out, in0=in_, scalar1=scale),
                          reads=reads, writes=writes)

        wq = []
        wstate = {"issued": 0, "consumed": 0}

        def wblock(parts):
            wq.append(parts)
            return len(wq) - 1

        def w_issue_upto(n):
            while wstate["issued"] < min(n, len(wq)):
                b = wstate["issued"]
                s = b % NSLOT
                for (vf, src) in wq[b]:
                    dst = vf(wslot[s])
                    S.add("pool", lambda e, dst=dst, src=src: e.dma_start(out=dst, in_=src),
                          writes=[wres[s]], dma_key="w%d" % s)
                wstate["issued"] += 1

        def w_use(b):
            assert b < wstate["consumed"] + NSLOT, (b, wstate)
            w_issue_upto(wstate["consumed"] + NSLOT)
            return wslot[b % NSLOT], wres[b % NSLOT]

        def w_done(b):
            wstate["consumed"] = b + 1
            w_issue_upto(wstate["consumed"] + NSLOT)

        def v_k16(off):
            return view(off, 8192, BF16, "p (k c) -> p k c", k=16)

        def blk_k16(w, c0):
            return wblock([(v_k16, w[:, c0:c0 + 256].rearrange("(k p) c -> p k c", p=128))])

        B_K = blk_k16(w_in, 1024)
        B_V = blk_k16(w_in, 1280)
        B_Q = [blk_k16(w_in, 0 + 256 * i) for i in range(4)]
        B_GV = [blk_k16(w_in, 2560 + 256 * i) for i in range(4)]
        B_U = [blk_k16(w_in, 1536 + 256 * i) for i in range(4)]
        B_GATE = []
        for i in range(8):
            ga = blk_k16(w_in, 3584 + 256 * i)
            gb = blk_k16(w_in, 5632 + 256 * i)
            pab = wblock([
                (lambda off: view(off, 4096, BF16, "p (k c) -> p k c", k=8),
                 w_pa[:, 256 * i:256 * i + 256].rearrange("(k p) c -> p k c", p=128)),
                (lambda off: view(off + 4096, 4096, BF16, "p (k c) -> p k c", k=8),
                 w_pb[:, 256 * i:256 * i + 256].rearrange("(k p) c -> p k c", p=128)),
            ])
            B_GATE.append((ga, gb, pab))
        B_WO = [blk_k16(w_o, 256 * i) for i in range(8)]
        NG = 11
        B_FF = []
        for gi in range(NG):
            f0 = gi * 4
            gu = []
            for hf in range(2):
                c0 = (f0 + 2 * hf) * 128
                gu.append((blk_k16(w_g, c0), blk_k16(w_u, c0)))
            dn = []
            for ch in range(2):
                dn.append(wblock([(lambda off: view(off, 8192, BF16, "p (f c) -> p f c", f=4),
                                   w_d[f0 * 128:(f0 + 4) * 128, ch * 1024:(ch + 1) * 1024]
                                   .rearrange("(f p) c -> p f c", p=128))]))
            B_FF.append((gu, dn))

        def cload(dst, src, cast=False):
            if cast:
                S.add("pool", lambda e: e.dma_start(out=dst, in_=src), dma_key="constc")
            else:
                S.add("sp", lambda e: e.dma_start(out=dst, in_=src), dma_key="const")

        cload(ident, identd[:, :], cast=True)
        cload(maskNb, maskN[:, :], cast=True)
        cload(mask0b, mask0[:, :], cast=True)
        cload(g1t, g1T[:, :])
        cload(g2t, g2T[:, :])
        cload(sinks_t, sinksB[:, :])
        cload(sinkrep_t, sinkrep[:, :])
        cload(sinks_b2, sinksB2[:, :], cast=True)
        cload(sinkrep_b2, sinkrep2[:, :], cast=True)
        constA = Res("constA")
        constB = Res("constB")
        constA.last_w = [op for op in S.q["pool"] if op.key == "constc"][-1]
        constB.last_w = [op for op in S.q["sp"] if op.key == "const"][-1]
        CONST = [constA, constB]
        eps_t = small(4)
        epsR = Res("eps")
        S.add("pool", lambda e: e.memset(eps_t, EPS), writes=[epsR])

        xtv = [view(r2 + i * 8192, 8192) for i in range(2)]
        xtres = [Res("xt%d" % i) for i in range(2)]
        hTres = [[Res("hT%d_%d" % (i, c)) for c in range(16)] for i in range(10)]
        order = [9] + list(range(9))
        KVB = r2 + 16384
        kd = [view(KVB + i * 1024, 1024, BF16, "p (h u d) -> p h u d", h=4, u=2) for i in range(2)]
        kdres = [Res("kd%d" % i) for i in range(2)]
        kvf = [view(KVB + 2048 + i * 2048, 2048) for i in range(2)]
        kvfres = [Res("kvf%d" % i) for i in range(2)]
        Kb8 = view(KVB + 6144, 512, BF16)
        KT8 = view(KVB + 6656, 512, BF16, "p (c t) -> p c t", c=2)
        Kb8res = Res("Kb8")
        KT8res = Res("KT8")
        KTres = [Res("KT%d" % i) for i in range(10)]
        Vres = [Res("V%d" % i) for i in range(10)]

        def stageA1(n, i):
            b = n % 2
            r0 = TOK if i == 9 else i * 128
            S.add("sp", lambda e: e.dma_start(out=xtv[b], in_=xc[r0:r0 + 128, :]),
                  writes=[xtres[b]], dma_key="xt%d" % b)
            ssr = Res("ss")
            ss = statA[:, 2 * n:2 * n + 1]
            rs = statA[:, 2 * n + 1:2 * n + 2]
            S.add("act", lambda e: e.activation(out=hb[b], in_=xtv[b], func=AF.Square, accum_out=ss),
                  reads=[xtres[b]], writes=[hbres[b], ssr])
            S.add("act", lambda e: e.activation(out=rs, in_=ss, func=AF.Sqrt, bias=eps_t[:, 0:1], scale=1.0 / D),
                  reads=[ssr, epsR], writes=[ssr])
            S.add("dve", lambda e: e.reciprocal(out=rs, in_=rs), reads=[ssr], writes=[ssr])
            S.add("dve", lambda e: e.tensor_scalar_mul(out=hb[b], in0=xtv[b], scalar1=rs),
                  reads=[ssr, xtres[b], hbres[b]], writes=[hbres[b]])

        def stageA2(n, i):
            b = n % 2
            for half in range(2):
                bk = 4 + (2 * n + half) % 4
                for j in range(8):
                    c = half * 8 + j
                    S.add("pe", lambda e, j=j, c=c, bk=bk: e.transpose(
                        out=bankbf(bk)[:, j * 128:(j + 1) * 128], in_=hb[b][:, c * 128:(c + 1) * 128],
                        identity=ident), reads=[hbres[b]] + CONST, writes=[bres[bk]])
                en = "act" if half == 0 else "dve"
                for j in range(8):
                    c = half * 8 + j
                    src = bankbf(bk)[:, j * 128:(j + 1) * 128]
                    dst = hT[:, c, tcols(i)]
                    if en == "act":
                        S.add("act", lambda e, dst=dst, src=src, c=c: e.activation(
                            out=dst, in_=src, func=AF.Copy, scale=g1t[:, c:c + 1]),
                            reads=[bres[bk]] + CONST, writes=[hTres[i][c]])
                    else:
                        S.add("dve", lambda e, dst=dst, src=src, c=c: e.tensor_scalar_mul(
                            out=dst, in0=src, scalar1=g1t[:, c:c + 1]),
                            reads=[bres[bk]] + CONST, writes=[hTres[i][c]])

        offK, rK = w_use(B_K)
        offV, rV = w_use(B_V)
        wk = v_k16(offK)
        wv = v_k16(offV)

        def stageKV(n, i):
            bk = n % 2
            for kc in range(16):
                S.add("pe", lambda e, kc=kc: e.matmul(
                    banks[bk][:, 0:256], lhsT=hT[:, kc, tcols(i)], rhs=wk[:, kc, :], start=(kc == 0), stop=(kc == 15)),
                    reads=[hTres[i][kc], rK], writes=[bres[bk]])
            for kc in range(16):
                S.add("pe", lambda e, kc=kc: e.matmul(
                    banks[bk][:, 256:512], lhsT=hT[:, kc, tcols(i)], rhs=wv[:, kc, :], start=(kc == 0), stop=(kc == 15)),
                    reads=[hTres[i][kc], rV], writes=[bres[bk]])
            kdb = kd[n % 2]
            kdr = kdres[n % 2]
            kin = banks[bk][:, 0:256].rearrange("p (h d) -> p h d", h=4)
            S.add("dve", lambda e: e.tensor_copy(out=kdb[:, :, 0, :], in_=kin), reads=[bres[bk]], writes=[kdr])
            S.add("dve", lambda e: e.tensor_copy(out=kdb[:, :, 1, :], in_=kin), reads=[bres[bk], kdr], writes=[kdr])
            vdst = Vb[:, kblk(i), :]
            S.add("dve", lambda e: e.tensor_copy(out=vdst, in_=banks[bk][:, 256:512]), reads=[bres[bk]], writes=[Vres[i]])
            if i in (7, 8):
                kvb = kvf[i - 7]
                kvr = kvfres[i - 7]
                S.add("dve", lambda e: e.tensor_copy(out=kvb, in_=banks[bk][:, :]), reads=[bres[bk]], writes=[kvr])
                if i == 7:
                    S.add("sp", lambda e: e.dma_start(out=kv7_out[:, :], in_=kvb), reads=[kvr], dma_key="okv7")
                else:
                    for s in range(16):
                        S.add("sp", lambda e, s=s: e.dma_start(
                            out=skw_out[s, 120:128, :], in_=kvb[s * 8:(s + 1) * 8, 0:256]), reads=[kvr], dma_key="oskw")
                        S.add("sp", lambda e, s=s: e.dma_start(
                            out=svw_out[s, 120:128, :], in_=kvb[s * 8:(s + 1) * 8, 256:512]), reads=[kvr], dma_key="osvw")
                    S.add("dve", lambda e: e.tensor_copy(out=Kb8, in_=banks[bk][:, 0:256]), reads=[bres[bk]], writes=[Kb8res])
            tb = 2 + n % 2
            for h in range(4):
                S.add("pe", lambda e, h=h: e.transpose(
                    out=bankbf(tb)[:, h * 128:(h + 1) * 128],
                    in_=kdb[:, h, :, :].rearrange("p u d -> p (u d)"), identity=ident),
                    reads=[kdr] + CONST, writes=[bres[tb]])
            kdst = KTd[:, :, kblk(i) * 128:(kblk(i) + 1) * 128]
            S.add("act", lambda e: e.copy(out=kdst, in_=bankbf(tb)[:, 0:512].rearrange("p (h t) -> p h t", h=4)),
                  reads=[bres[tb]], writes=[KTres[i]])
            if i == 8:
                for c in range(2):
                    S.add("pe", lambda e, c=c: e.transpose(
                        out=bankbf(tb)[:, 512 + c * 128:512 + (c + 1) * 128], in_=Kb8[:, c * 128:(c + 1) * 128],
                        identity=ident), reads=[Kb8res] + CONST, writes=[bres[tb]])
                S.add("act", lambda e: e.copy(
                    out=KT8, in_=bankbf(tb)[:, 512:768].rearrange("p (c t) -> p c t", c=2)),
                    reads=[bres[tb]], writes=[KT8res])

        stageA1(0, order[0])
        for n, i in enumerate(order):
            if n + 1 < len(order):
                stageA1(n + 1, order[n + 1])
            stageA2(n, i)
            stageKV(n, i)
        w_done(B_V)
        S.add("sp", lambda e: e.dma_start(out=skw_out[:, 0:120, :], in_=ck[:, 8:128, :]), dma_key="ockw")
        S.add("sp", lambda e: e.dma_start(out=svw_out[:, 0:120, :], in_=cv[:, 8:128, :]), dma_key="ocvw")

        TG = [(0, 384), (384, 768), (768, 1152)]
        QTres = [Res("QT%d" % i) for i in range(NT)]

        def tiles_of_tg(tg):
            return [tg * 3, tg * 3 + 1, tg * 3 + 2]

        sa = 59392
        Qz = view(sa, 4096, BF16, "p (h c) -> p h c", h=16)
        QTz = view(sa + 4096, 8192, BF16, "p (c s h t) -> p c s h t", c=2, s=16, h=16)
        ckTb = view(sa + 12288, 8192, BF16, "p (c s k) -> p c s k", c=2, s=16)
        cvb = view(sa + 20480, 8192, BF16, "p (s c) -> p s c", s=16)
        mSn = view(sa + 28672, 4096, BF16, "p (s k) -> p s k", s=16)
        mSc = view(sa + 32768, 256, BF16)
        Osm1 = view(sa + 33024, 2048, BF16, "p (s d) -> p s d", s=16)
        SAres = Res("sample_attn_bufs")
        Qzres = Res("Qz")
        QTzres = Res("QTz")
        S.add("pool", lambda e: e.memset(view(sa, 4096, BF16), 0.0), writes=[Qzres])
        S.add("pool", lambda e: e.memset(view(sa + 4096, 8192, BF16), 0.0), writes=[QTzres])
        S.add("pool", lambda e: e.dma_start(out=ckTb, in_=ckT.rearrange("c p s k -> p c s k")),
              writes=[SAres], dma_key="sa")
        S.add("pool", lambda e: e.dma_start(out=cvb, in_=cv.rearrange("s k c -> k s c")),
              writes=[SAres], dma_key="sa")
        S.add("pool", lambda e: e.dma_start(out=mSn, in_=maskSn[:, :, :]), writes=[SAres], dma_key="sa")
        S.add("pool", lambda e: e.dma_start(out=mSc, in_=maskSc[:, :]), writes=[SAres], dma_key="sa")

        fm_bank = {"n": 0}

        def fm_group(wt, cl, K, src, srcres_fn, tg, extra_reads, banklist):
            bk = banklist[fm_bank["n"] % len(banklist)]
            fm_bank["n"] += 1
            lo, hi = TG[tg]
            for kc in range(K):
                S.add("pe", lambda e, kc=kc: e.matmul(
                    banks[bk][:, 0:384], lhsT=wt[:, kc, cl * 128:(cl + 1) * 128], rhs=src[:, kc, lo:hi],
                    start=(kc == 0), stop=(kc == K - 1)),
                    reads=[srcres_fn(t, kc) for t in tiles_of_tg(tg)] + extra_reads, writes=[bres[bk]])
            return bk, lo, hi

        hTr = lambda t, kc: hTres[t][kc]
        q_w = []
        for wb in range(4):
            off, rw = w_use(B_Q[wb])
            q_w.append((v_k16(off), rw))

        def q_group(wb, cl, tg, banklist):
            wt, rw = q_w[wb]
            m = wb * 2 + cl
            bk, lo, hi = fm_group(wt, cl, 16, hT, hTr, tg, [rw], banklist)
            copy_op(ev_engine(), QT[:, m, lo:hi], banks[bk][:, 0:384], [bres[bk]],
                    [QTres[t] for t in tiles_of_tg(tg)], scale=0.125)

        def gen_q_rest():
            for tg in (1, 2):
                for wb in range(4):
                    for cl in range(2):
                        q_group(wb, cl, tg, [7])
                        if tg == 2 and cl == 1:
                            w_done(B_Q[wb])
                        yield

        for wb in range(4):
            wt, rw = q_w[wb]
            for cl in range(2):
                q_group(wb, cl, 0, [0, 1, 2, 3, 4, 5])
            for kc in range(16):
                S.add("pe", lambda e, kc=kc, wt=wt: e.matmul(
                    banks[6][:, 0:256], lhsT=hT[:, kc, tcols(8)], rhs=wt[:, kc, :], start=(kc == 0), stop=(kc == 15)),
                    reads=[hTres[8][kc], rw], writes=[bres[6]])
            eh = wb % 2
            S.add("dve", lambda e, wb=wb, eh=eh: e.tensor_scalar_mul(
                out=Qz[:, 4 * wb:4 * wb + 4, eh * 64:(eh + 1) * 64],
                in0=banks[6][:, 0:256].rearrange("p (h d) -> p h d", h=4), scalar1=0.125),
                reads=[bres[6], Qzres], writes=[Qzres])
            for hh in range(4 * wb, 4 * wb + 4):
                S.add("pe", lambda e, hh=hh: e.transpose(out=bankbf(7)[:, (hh % 4) * 128:(hh % 4 + 1) * 128],
                                                         in_=Qz[:, hh, :], identity=ident),
                      reads=[Qzres] + CONST, writes=[bres[7]])
            for hh in range(4 * wb, 4 * wb + 4):
                copy_op("act" if hh % 2 else "dve", QTz[:, wb // 2, :, hh, :],
                        bankbf(7)[:, (hh % 4) * 128:(hh % 4 + 1) * 128].rearrange("p (s t) -> p s t", s=16),
                        [bres[7], QTzres], [QTzres])

        ATB = r2 + 24576
        Pb = [view(ATB + i * 512, 512, BF16) for i in range(2)]
        PTb = [view(ATB + 1024 + i * 512, 512, BF16) for i in range(2)]
        atm = [view(ATB + 2048 + i * 2048, 2048, BF16) for i in range(2)]
        Osm = view(ATB + 6144, 4096, BF16, "p (s u d) -> p s u d", s=16, u=2)
        Pres = [Res("P%d" % i) for i in range(2)]
        PTres = [Res("PT%d" % i) for i in range(2)]
        atmres = [Res("atm%d" % i) for i in range(2)]
        stres = {}

        def sres(q, b, h):
            k = (q, b, h)
            if k not in stres:
                stres[k] = Res("st%s" % (k,))
            return stres[k]

        att_n = {"n": 0}

        def softmax_head(sbank, b2, h, sink_ap, per_head_stats):
            n = att_n["n"]
            att_n["n"] += 1
            pb = n % 2
            mx = st_att[:, 0, b2, h:h + 1]
            ngm = st_att[:, 1, b2, h:h + 1]
            rsum = st_att[:, 2, b2, h:h + 1]
            R = [sres(q, b2, h) for q in range(6)]
            S.add("dve", lambda e: e.reduce_max(out=ngm, in_=banks[sbank][:, 0:258], axis=AX.X, negate=True),
                  reads=[bres[sbank]], writes=[R[1]])
            S.add("act", lambda e: e.activation(out=Pb[pb], in_=banks[sbank][:, 0:256], func=AF.Exp, bias=ngm,
                                                scale=1.0, accum_out=rsum),
                  reads=[bres[sbank], R[1]], writes=[Pres[pb], R[2]])
            if per_head_stats:
                es = st_att[:, 3, b2, h:h + 1]
                den = st_att[:, 4, b2, h:h + 1]
                rden = st_att[:, 5, b2, h:h + 1]
                S.add("act", lambda e: e.activation(out=es, in_=ngm, func=AF.Exp, bias=sink_ap, scale=1.0),
                      reads=[R[1]] + CONST, writes=[R[3]])
                S.add("dve", lambda e: e.tensor_tensor(out=den, in0=rsum, in1=es, op=ALU.add),
                      reads=[R[2], R[3]], writes=[R[4]])
                S.add("dve", lambda e: e.reciprocal(out=rden, in_=den), reads=[R[4]], writes=[R[5]])
            return pb

        def transpose_probs(pb):
            tb = 2 + pb
            for c in range(2):
                S.add("pe", lambda e, c=c: e.transpose(out=bankbf(tb)[:, c * 128:(c + 1) * 128],
                                                       in_=Pb[pb][:, c * 128:(c + 1) * 128], identity=ident),
                      reads=[Pres[pb]] + CONST, writes=[bres[tb]])
            copy_op("act", PTb[pb], bankbf(tb)[:, 0:256], [bres[tb]], [PTres[pb]])

        Osres = Res("Osm1")

        def sample_scores(s):
            sbank = s % 2
            S.add("pe", lambda e: e.matmul(banks[sbank][:, 256:258], lhsT=ident, rhs=sinkrep_b2,
                                           start=True, stop=True), reads=CONST, writes=[bres[sbank]])
            for c in range(2):
                S.add("pe", lambda e, c=c: e.matmul(
                    banks[sbank][:, 0:128], lhsT=QTz[:, c, s, :, :].rearrange("p h t -> p (h t)"), rhs=ckTb[:, c, s, :],
                    start=(c == 0), stop=False), reads=[QTzres, SAres], writes=[bres[sbank]])
            S.add("pe", lambda e: e.matmul(banks[sbank][:, 0:128], lhsT=ident, rhs=mSc, start=False, stop=True),
                  reads=CONST + [SAres], writes=[bres[sbank]])
            for c in range(2):
                S.add("pe", lambda e, c=c: e.matmul(
                    banks[sbank][:, 128:256], lhsT=QTz[:, c, s, :, :].rearrange("p h t -> p (h t)"), rhs=KT8[:, c, :],
                    start=(c == 0), stop=False), reads=[QTzres, KT8res], writes=[bres[sbank]])
            S.add("pe", lambda e: e.matmul(banks[sbank][:, 128:256], lhsT=ident, rhs=mSn[:, s, :], start=False, stop=True),
                  reads=CONST + [SAres], writes=[bres[sbank]])

        spb = {}

        def sample_E1(s):
            spb[s] = softmax_head(s % 2, s % 2, s, sinkrep_t[:, 0:1], True)

        def sample_V(s):
            b2 = s % 2
            pb = spb[s]
            ob = 4 + s % 2
            S.add("pe", lambda e, ob=ob, pb=pb, s=s: e.matmul(
                banks[ob][:, 0:256], lhsT=PTb[pb][:, 0:128], rhs=cvb[:, s, :], start=True, stop=False),
                reads=[PTres[pb], SAres], writes=[bres[ob]])
            S.add("pe", lambda e, ob=ob, pb=pb: e.matmul(
                banks[ob][:, 0:256], lhsT=PTb[pb][:, 128:256], rhs=Vb[:, 9, :], start=False, stop=True),
                reads=[PTres[pb], Vres[8]], writes=[bres[ob]])
            for g in range(4):
                S.add("dve", lambda e, ob=ob, g=g, s=s, b2=b2: e.tensor_scalar_mul(
                    out=Osm1[32 * g:32 * g + 32, s, :], in0=banks[ob][32 * g:32 * g + 32, 64 * g:64 * g + 64],
                    scalar1=st_att[32 * g:32 * g + 32, 5, b2, s:s + 1]),
                    reads=[bres[ob], sres(5, b2, s)], writes=[Osres])

        def gen_sample():
            sample_scores(0)
            sample_scores(1)
            sample_E1(0)
            sample_scores(2)
            sample_E1(1)
            transpose_probs(spb[0])
            for s in range(16):
                if s + 3 < 16:
                    sample_scores(s + 3)
                if s + 2 < 16:
                    sample_E1(s + 2)
                if s + 1 < 16:
                    transpose_probs(spb[s + 1])
                sample_V(s)
                yield

        qrest = gen_q_rest()
        for _ in gen_sample():
            try:
                next(qrest)
            except StopIteration:
                pass
        _drain(qrest)
        Osmres = Res("Osm")
        for u in range(2):
            S.add("dve", lambda e, u=u: e.tensor_copy(out=Osm[:, :, u, :], in_=Osm1), reads=[Osres, Osmres], writes=[Osmres])
        for s in range(16):
            tb = 6 + s % 2
            S.add("pe", lambda e, tb=tb, s=s: e.transpose(out=bankbf(tb)[:, 0:128],
                                                          in_=Osm[:, s, :, :].rearrange("p u d -> p (u d)"), identity=ident),
                  reads=[Osmres] + CONST, writes=[bres[tb]])
            for eh in range(2):
                srcv = bankbf(tb)[eh * 64:(eh + 1) * 64, 0:128].rearrange("p (j u t) -> p j u t", j=8, u=2)[:, :, eh, :]
                dstv = QT[eh * 64:(eh + 1) * 64, :, 1024 + s * 8:1024 + (s + 1) * 8]
                copy_op("act" if s % 2 else "dve", dstv, srcv, [bres[tb]], [QTres[8]])
        attnT = QT
        attnres = QTres

        sample_dead = [SAres, Qzres, QTzres, Osres]

        def prompt_scores(t, h):
            kvh = h // 4
            base = (h % 2) * 64
            ch = h // 2
            sbank = h % 2
            mk = mask0b if t == 0 else maskNb
            kcols = slice(t * 128, t * 128 + 256)
            prev_i = 9 if t == 0 else t - 1
            S.add("pe", lambda e: e.matmul(banks[sbank][:, 256:258], lhsT=ident, rhs=sinks_b2[:, 2 * h:2 * h + 2],
                                           start=True, stop=True), reads=CONST, writes=[bres[sbank]])
            S.add("pe", lambda e: e.matmul(
                banks[sbank][:, 0:256], lhsT=QT[base:base + 64, ch, tcols(t)], rhs=KTd[base:base + 64, kvh, kcols],
                start=True, stop=False),
                reads=[QTres[t], KTres[prev_i], KTres[t]], writes=[bres[sbank]])
            S.add("pe", lambda e: e.matmul(banks[sbank][:, 0:256], lhsT=ident, rhs=mk, start=False, stop=True),
                  reads=CONST, writes=[bres[sbank]])

        ATT_HEADS = [(t, h) for t in range(8) for h in range(16)]
        NH = len(ATT_HEADS)
        att_pb = {}

        def att_E1(idx):
            t, h = ATT_HEADS[idx]
            att_pb[idx] = softmax_head(h % 2, t % 2, h, sinks_t[:, h:h + 1], False)

        def att_E2(idx):
            pb = att_pb[idx]
            tb = 2 + pb
            for c in range(2):
                S.add("pe", lambda e, c=c: e.transpose(out=bankbf(tb)[:, c * 128:(c + 1) * 128],
                                                       in_=Pb[pb][:, c * 128:(c + 1) * 128], identity=ident),
                      reads=[Pres[pb]] + CONST, writes=[bres[tb]])
            copy_op("dve" if idx % 2 else "act", PTb[pb], bankbf(tb)[:, 0:256], [bres[tb]], [PTres[pb]])

        def att_V(idx):
            t, h = ATT_HEADS[idx]
            pb = att_pb[idx]
            b2 = t % 2
            ab = atm[b2]
            kvh = h // 4
            prev_i = 9 if t == 0 else t - 1
            ob = 4 + h // 8
            oc = (h % 8) * 64
            S.add("pe", lambda e: e.matmul(
                banks[ob][:, oc:oc + 64], lhsT=PTb[pb][:, 0:128], rhs=Vb[:, t, kvh * 64:(kvh + 1) * 64],
                start=True, stop=False), reads=[PTres[pb], Vres[prev_i]], writes=[bres[ob]])
            S.add("pe", lambda e: e.matmul(
                banks[ob][:, oc:oc + 64], lhsT=PTb[pb][:, 128:256], rhs=Vb[:, t + 1, kvh * 64:(kvh + 1) * 64],
                start=False, stop=True), reads=[PTres[pb], Vres[t]], writes=[bres[ob]])
            if h % 8 == 7:
                h0 = h - 7
                hs = slice(h0, h0 + 8)
                Rg = [sres(q, b2, hh) for q in (1, 2) for hh in range(h0, h0 + 8)]
                Wg = [sres(q, b2, hh) for q in (3, 4, 5) for hh in range(h0, h0 + 8)]
                S.add("dve", lambda e: e.tensor_tensor(out=st_att[:, 3, b2, hs], in0=st_att[:, 1, b2, hs],
                                                       in1=sinks_t[:, hs], op=ALU.add),
                      reads=Rg + CONST, writes=Wg)
                S.add("act", lambda e: e.activation(out=st_att[:, 3, b2, hs], in_=st_att[:, 3, b2, hs], func=AF.Exp),
                      reads=Wg, writes=Wg)
                S.add("dve", lambda e: e.tensor_tensor(out=st_att[:, 4, b2, hs], in0=st_att[:, 2, b2, hs],
                                                       in1=st_att[:, 3, b2, hs], op=ALU.add), reads=Rg + Wg, writes=Wg)
                S.add("dve", lambda e: e.reciprocal(out=st_att[:, 5, b2, hs], in_=st_att[:, 4, b2, hs]),
                      reads=Wg, writes=Wg)
                rdb = st_att[:, 5, b2, hs].unsqueeze(2).broadcast_to([128, 8, 64])
                S.add("dve", lambda e: e.tensor_tensor(
                    out=ab[:, h0 * 64:(h0 + 8) * 64].rearrange("p (h d) -> p h d", h=8),
                    in0=banks[ob][:, :].rearrange("p (h d) -> p h d", h=8), in1=rdb, op=ALU.mult),
                    reads=[bres[ob]] + Wg, writes=[atmres[b2]])
            if h == 15:
                for j in range(8):
                    S.add("pe", lambda e, j=j: e.transpose(out=bankbf(6)[:, j * 128:(j + 1) * 128],
                                                           in_=ab[:, j * 128:(j + 1) * 128], identity=ident),
                          reads=[atmres[b2]] + CONST, writes=[bres[6]])
                copy_op("act", QT[:, :, tcols(t)], bankbf(6)[:, 0:1024].rearrange("p (c t) -> p c t", c=8),
                        [bres[6]], [QTres[t]])

        def gen_att_prompt():
            prompt_scores(*ATT_HEADS[0])
            prompt_scores(*ATT_HEADS[1])
            att_E1(0)
            prompt_scores(*ATT_HEADS[2])
            att_E1(1)
            att_E2(0)
            for k in range(NH):
                if k + 3 < NH:
                    prompt_scores(*ATT_HEADS[k + 3])
                if k + 2 < NH:
                    att_E1(k + 2)
                if k + 1 < NH:
                    att_E2(k + 1)
                att_V(k)
                yield

        gvres = [Res("gvf%d" % i) for i in range(NT)]
        vlnb = view(r2, 18432, BF16, "p (t c) -> p t c", t=NT)
        gnb = view(r2 + 18432, 4096)
        vlnres = [Res("vln%d" % i) for i in range(NT)]
        gnres = Res("gnB")
        uTres = [Res("uT%d" % i) for i in range(NT)]
        lnres = [Res("ln0"), Res("ln1")]

        bnall = view(SMALL_EXTRA, 4 * NT * 4, F32, "p (t k) -> p t k", t=NT)
        bnst9 = view(SMALL_EXTRA + 160, 4 * NT * 12, F32, "p (t k) -> p t k", t=NT)

        def gen_gv():
            r2old = [xtres[0], xtres[1], KT8res, Kb8res] + kdres + kvfres
            S.add("sp", lambda e: e.dma_start(out=gnb, in_=gnB[:, :]), writes=[gnres] + r2old, dma_key="gn")
            for wb in range(4):
                off, rw = w_use(B_GV[wb])
                wt = v_k16(off)
                for t in range(NT):
                    gv_group(wb, t, wt, rw)
                    yield
                w_done(B_GV[wb])

        def gv_group(wb, t, wt, rw):
            bk = 7
            for kc in range(16):
                S.add("pe", lambda e, kc=kc: e.matmul(
                    banks[bk][:, 0:256], lhsT=hT[:, kc, tcols(t)], rhs=wt[:, kc, :], start=(kc == 0), stop=(kc == 15)),
                    reads=[hTres[t][kc], rw], writes=[bres[bk]])
            S.add("dve", lambda e: e.tensor_copy(out=gvf[:, t, wb * 256:(wb + 1) * 256], in_=banks[bk][:, 0:256]),
                  reads=[bres[bk]], writes=[gvres[t]] + (sample_dead if wb == 0 else []))

        def u_group(wb, cl, tg, wt, rw):
            m = wb * 2 + cl
            bk, lo, hi = fm_group(wt, cl, 16, hT, hTr, tg, [rw], [0, 1, 2, 3, 4, 5, 6, 7])
            S.add("act", lambda e: e.activation(out=uT[:, m, lo:hi], in_=banks[bk][:, 0:384], func=AF.Gelu_apprx_tanh),
                  reads=[bres[bk]],
                  writes=[uTres[t] for t in tiles_of_tg(tg)]
                  + [gvres[t] for t in range((m * 2304) // 4096, ((m + 1) * 2304 - 1) // 4096 + 1)])

        def ln_part1():
            R = lnres[0]
            for t in range(NT):
                S.add("act", lambda e, t=t: e.activation(out=gvf[:, t, :], in_=gvf[:, t, :], func=AF.Gelu_apprx_tanh),
                      reads=[gvres[t]], writes=[gvres[t]])
                for c in range(2):
                    S.add("dve", lambda e, t=t, c=c: e.bn_stats(out=bnst9[:, t, c * 6:(c + 1) * 6],
                                                                in_=gvf[:, t, c * 512:(c + 1) * 512]),
                          reads=[gvres[t]], writes=[R])
                S.add("dve", lambda e, t=t: e.bn_aggr(out=bnall[:, t, 0:2], in_=bnst9[:, t, :]), reads=[R], writes=[R])

        def ln_part2():
            R = lnres[0]
            S.add("act", lambda e: e.activation(out=bnall[:, :, 2], in_=bnall[:, :, 1], func=AF.Sqrt, bias=eps_t[:, 0:1],
                                                scale=1.0), reads=[R, epsR], writes=[R])
            S.add("dve", lambda e: e.reciprocal(out=bnall[:, :, 3], in_=bnall[:, :, 2]), reads=[R], writes=[R])
            for t in range(NT):
                S.add("dve", lambda e, t=t: e.tensor_scalar(out=gvf[:, t, :], in0=gvf[:, t, :], scalar1=bnall[:, t, 0:1],
                                                            scalar2=bnall[:, t, 3:4], op0=ALU.subtract, op1=ALU.mult),
                      reads=[R, gvres[t]], writes=[gvres[t]])
                S.add("pool", lambda e, t=t: e.tensor_tensor(out=vlnb[:, t, :], in0=gvf[:, t, :], in1=gnb, op=ALU.mult),
                      reads=[gvres[t], gnres], writes=[vlnres[t]])
                if t == 8:
                    S.add("dve", lambda e, t=t: e.tensor_tensor(out=gvf[:, t, :], in0=gvf[:, t, :], in1=gnb, op=ALU.mult),
                          reads=[gvres[t], gnres, vlnres[t]], writes=[gvres[t]])
                    S.add("sp", lambda e, t=t: e.dma_start(out=sg_out[:, :], in_=gvf[:, t, :]), reads=[gvres[t]], dma_key="osg")

        _interleave(gen_att_prompt(), gen_gv(), k=1)
        ln_part1()
        ln_part2()
        for wb in range(4):
            off, rw = w_use(B_U[wb])
            wt = v_k16(off)
            for cl in range(2):
                for tg in range(3):
                    u_group(wb, cl, tg, wt, rw)
            w_done(B_U[wb])

        SPB = r2 + 22528
        wst = view(SPB, 4096, F32, "p (g t) -> p g t", g=8)
        wsm = view(SPB + 4096, 2048, BF16, "p (g t) -> p g t", g=8)
        wsms = view(SPB + 6144, 2048, BF16, "p (g t) -> p g t", g=8)
        msk = view(SPB + 8192, 512)
        bsb = view(SPB + 8704, 4096, F32, "p (g t) -> p g t", g=8)
        wres_sp = Res("wst")
        mres = Res("msk")
        wsmres = Res("wsm")
        bsres = Res("bsb")
        att_bufs = Pres + PTres + atmres + [Osmres]
        for which in range(2):
            srcw = wsT if which == 0 else wsTr
            srcm = trilT if which == 0 else blkm
            dstw = wsm if which == 0 else wsms
            S.add("sp", lambda e, srcw=srcw: e.dma_start(out=wst, in_=srcw.rearrange("g s t -> s g t")),
                  writes=[wres_sp] + (att_bufs if which == 0 else []), dma_key="wst")
            S.add("sp", lambda e, srcm=srcm: e.dma_start(out=msk, in_=srcm[:, :]),
                  writes=[mres] + (att_bufs if which == 0 else []), dma_key="msk")
            for g in range(8):
                S.add("dve", lambda e, g=g, dstw=dstw: e.tensor_tensor(out=dstw[:, g, :], in0=wst[:, g, :], in1=msk,
                                                                      op=ALU.mult),
                      reads=[wres_sp, mres], writes=[wsmres])
        S.add("sp", lambda e: e.dma_start(out=bsb, in_=bsB[:, :, :]), writes=[bsres] + att_bufs, dma_key="bsb")
        bsbs = wst
        S.add("sp", lambda e: e.dma_start(out=bsbs, in_=bsBs[:, :, :]), writes=[wres_sp], dma_key="wst")
        sptmp = [view(96256 + i * 2048, 2048) for i in range(2)]
        sptres = [Res("sptmp%d" % i) for i in range(2)]
        kv_dead = KTres + Vres
        for t in range(NT):
            wm = wsms if t == 8 else wsm
            bb = bsbs if t == 8 else bsb
            for half in range(2):
                n = t * 2 + half
                bk = n % 4
                for gl in range(4):
                    g = half * 4 + gl
                    S.add("pe", lambda e, bk=bk, gl=gl, g=g, t=t, wm=wm: e.matmul(
                        banks[bk][:, gl * 128:(gl + 1) * 128], lhsT=vlnb[:, t, g * 128:(g + 1) * 128], rhs=wm[:, g, :],
                        start=True, stop=True), reads=[vlnres[t], wsmres], writes=[bres[bk]])
                tp = sptmp[n % 2]
                S.add("dve", lambda e, bk=bk, tp=tp, bb=bb, half=half: e.tensor_tensor(
                    out=tp, in0=banks[bk][:, :], in1=bb[:, half * 4:(half + 1) * 4, :].rearrange("p g t -> p (g t)"),
                    op=ALU.add), reads=[bres[bk], bsres, wres_sp], writes=[sptres[n % 2]] + (kv_dead if n < 2 else []))
                S.add("pool", lambda e, tp=tp, half=half, t=t: e.tensor_tensor(
                    out=uT[:, half * 4:(half + 1) * 4, tcols(t)], in0=uT[:, half * 4:(half + 1) * 4, tcols(t)],
                    in1=tp.rearrange("p (g t) -> p g t", g=4), op=ALU.mult),
                    reads=[sptres[n % 2], uTres[t]], writes=[uTres[t]])
        aT = uT
        aTres = uTres

        mixres = [Res("mix%d" % i) for i in range(NT)]
        sga = [view(VLN_OFF + i * 1536, 1536) for i in range(6)]
        sgb = [view(VLN_OFF + 9216 + i * 1536, 1536) for i in range(6)]
        sgares = [Res("sga%d" % i) for i in range(6)]
        sgbres = [Res("sgb%d" % i) for i in range(6)]
        R2old = vlnres + [gnres, wres_sp, mres, wsmres, bsres]
        gate_first = {"a": True, "m": True}
        ALLB = list(range(8))
        for i in range(8):
            ga, gb, pab = B_GATE[i]
            for (blk, dst, dres) in ((ga, sga, sgares), (gb, sgb, sgbres)):
                off, rw = w_use(blk)
                wt = v_k16(off)
                for cl in range(2):
                    for tg in range(3):
                        q = cl * 3 + tg
                        bk, lo, hi = fm_group(wt, cl, 16, hT, hTr, tg, [rw], ALLB)
                        old = (gvres + sptres) if gate_first["a"] else []
                        S.add("act", lambda e, q=q, bk=bk, dst=dst: e.activation(out=dst[q], in_=banks[bk][:, 0:384],
                                                                                  func=AF.Sigmoid),
                              reads=[bres[bk]], writes=[dres[q]] + old)
                gate_first["a"] = False
                w_done(blk)
            offp, rp = w_use(pab)
            wpa = view(offp, 4096, BF16, "p (k c) -> p k c", k=8)
            wpb = view(offp + 4096, 4096, BF16, "p (k c) -> p k c", k=8)
            for cl in range(2):
                m = i * 2 + cl
                for tg in range(3):
                    q = cl * 3 + tg
                    tl = tiles_of_tg(tg)
                    bka, lo, hi = fm_group(wpa, cl, 8, aT, lambda t, kc: aTres[t], tg, [rp], ALLB)
                    S.add("dve", lambda e, q=q, bka=bka: e.tensor_tensor(out=sga[q], in0=sga[q], in1=banks[bka][:, 0:384],
                                                                          op=ALU.mult),
                          reads=[sgares[q], bres[bka]], writes=[sgares[q]])
                    bkb, lo, hi = fm_group(wpb, cl, 8, attnT, lambda t, kc: attnres[t], tg, [rp], ALLB)
                    S.add("dve", lambda e, q=q, bkb=bkb: e.tensor_tensor(out=sgb[q], in0=sgb[q], in1=banks[bkb][:, 0:384],
                                                                          op=ALU.mult),
                          reads=[sgbres[q], bres[bkb]], writes=[sgbres[q]])
                    S.add("pool", lambda e, q=q, m=m, lo=lo, hi=hi: e.tensor_tensor(
                        out=mixT[:, m, lo:hi], in0=sga[q], in1=sgb[q], op=ALU.add),
                        reads=[sgares[q], sgbres[q]], writes=[mixres[t] for t in tl] + (R2old if gate_first["m"] else []))
                    gate_first["m"] = False
            w_done(pab)

        x1res = [Res("x1_%d" % i) for i in range(NT)]
        R1B_all = [r for l in hTres for r in l] + QTres + uTres + gvres + KTres + Vres + sptres + sgares + sgbres
        hT_all = [r for l in hTres for r in l]
        rest_all = QTres + uTres + gvres + KTres + Vres + sptres + sgares + sgbres
        for t in range(NT):
            guard = hT_all if t < 5 else (rest_all if t == 5 else [])
            S.add("sp", lambda e, t=t: e.dma_start(out=x1[:, t, :], in_=xc[t * 128:(t + 1) * 128, :]),
                  writes=[x1res[t]] + guard, dma_key="x1_%d" % (t % 3))
        hhres = [Res("hh%d" % i) for i in range(NT)]

        rms_state = {}

        def rms_a(t, n):
            b = n % 3
            ssr = Res("ss2")
            ss = st2[:, 2 * (n % 16):2 * (n % 16) + 1]
            rs = st2[:, 2 * (n % 16) + 1:2 * (n % 16) + 2]
            S.add("act", lambda e: e.activation(out=hb[b], in_=x1[:, t, :], func=AF.Square, accum_out=ss),
                  reads=[x1res[t]], writes=[hbres[b], ssr])
            rms_state[n] = (ss, rs, ssr)

        def rms_b1(n):
            ss, rs, ssr = rms_state[n]
            S.add("act", lambda e: e.activation(out=rs, in_=ss, func=AF.Sqrt, bias=eps_t[:, 0:1], scale=1.0 / D),
                  reads=[ssr, epsR], writes=[ssr])

        def rms_b(n):
            ss, rs, ssr = rms_state[n]
            S.add("dve", lambda e: e.reciprocal(out=rs, in_=rs), reads=[ssr], writes=[ssr])
            return rs, ssr

        def norm_stage1b(t, n):
            b = n % 3
            rs, ssr = rms_b(n)
            S.add("dve", lambda e: e.tensor_scalar_mul(out=hb[b], in0=x1[:, t, :], scalar1=rs),
                  reads=[ssr, x1res[t], hbres[b]], writes=[hbres[b]])

        def norm_stage2(t, n, gt, dstT, dres, tbanks):
            b = n % 3
            for half in range(2):
                bk = tbanks[(2 * n + half) % len(tbanks)]
                for j in range(8):
                    c = half * 8 + j
                    S.add("pe", lambda e, bk=bk, j=j, c=c: e.transpose(
                        out=bankbf(bk)[:, j * 128:(j + 1) * 128], in_=hb[b][:, c * 128:(c + 1) * 128], identity=ident),
                        reads=[hbres[b]] + CONST, writes=[bres[bk]])
                en_ = "act" if half == 0 else "dve"
                for j in range(8):
                    c = half * 8 + j
                    src = bankbf(bk)[:, j * 128:(j + 1) * 128]
                    dst = dstT[:, c, tcols(t)]
                    if en_ == "act":
                        S.add("act", lambda e, dst=dst, src=src, c=c: e.activation(
                            out=dst, in_=src, func=AF.Copy, scale=gt[:, c:c + 1]),
                            reads=[bres[bk]] + CONST, writes=[dres])
                    else:
                        S.add("dve", lambda e, dst=dst, src=src, c=c: e.tensor_scalar_mul(
                            out=dst, in0=src, scalar1=gt[:, c:c + 1]),
                            reads=[bres[bk]] + CONST, writes=[dres])

        wo_n = {"n": 0}

        def wo_group(cg, t, wt, rw):
            bk = wo_n["n"] % 4
            wo_n["n"] += 1
            for kc in range(16):
                S.add("pe", lambda e, kc=kc: e.matmul(
                    banks[bk][:, 0:256], lhsT=mixT[:, kc, tcols(t)], rhs=wt[:, kc, :], start=(kc == 0), stop=(kc == 15)),
                    reads=[mixres[t], rw], writes=[bres[bk]])
            S.add("dve", lambda e: e.tensor_tensor(
                out=x1[:, t, cg * 256:(cg + 1) * 256], in0=banks[bk][:, 0:256], in1=x1[:, t, cg * 256:(cg + 1) * 256],
                op=ALU.add), reads=[bres[bk], x1res[t]], writes=[x1res[t]])

        NCGO = 6
        for cg in range(NCGO):
            off, rw = w_use(B_WO[cg])
            wt = v_k16(off)
            for t in range(NT):
                wo_group(cg, t, wt, rw)
            w_done(B_WO[cg])
        tail_w = []
        for cg in range(NCGO, 8):
            off, rw = w_use(B_WO[cg])
            tail_w.append((cg, v_k16(off), rw))
        for t in range(NT + 1):
            if t < NT:
                for (cg, wt, rw) in tail_w:
                    wo_group(cg, t, wt, rw)
                rms_a(t, t)
                rms_b1(t)
            if 1 <= t <= NT:
                norm_stage1b(t - 1, t - 1)
            if 2 <= t:
                norm_stage2(t - 2, t - 2, g2t, hhT, hhres[t - 2], [4, 5, 6, 7])
        w_done(B_WO[7])

        def norm_drain():
            norm_stage2(NT - 1, NT - 1, g2t, hhT, hhres[NT - 1], [4, 5, 6, 7])

        actT = [view(r2 + i * 9216, 9216, BF16, "p (f t) -> p f t", f=4) for i in range(2)]
        actres = [[Res("act%d_%d" % (i, t)) for t in range(NT)] for i in range(2)]
        silt = [view(r2 + 18432 + i * 1536, 1536) for i in range(2)]
        silres = [Res("sil%d" % i) for i in range(2)]
        gfbv = view(r2 + 21504, 8192)
        gfres = Res("gfb")
        S.add("sp", lambda e: e.dma_start(out=gfbv, in_=gfB[:, :]), writes=[gfres] + mixres, dma_key="gf")
        ffn_n = {"n": 0}
        dn_n = {"n": 0}
        for gi in range(NG):
            gu, dn = B_FF[gi]
            ab = gi % 2
            for hf in range(2):
                offg, rg = w_use(gu[hf][0])
                offu, ru = w_use(gu[hf][1])
                wg_ = v_k16(offg)
                wu_ = v_k16(offu)
                cltg = [(cl, tg) for cl in range(2) for tg in range(3)]
                if gi == 0 and hf == 0:
                    cltg = [(0, 0), (0, 1), (1, 0), (1, 1), None, (0, 2), (1, 2)]
                for ct in cltg:
                    if ct is None:
                        norm_drain()
                        continue
                    cl, tg = ct
                    f = hf * 2 + cl
                    if True:
                        lo, hi = TG[tg]
                        n = ffn_n["n"]
                        ffn_n["n"] += 1
                        bg = (n % 2) * 2
                        bu = bg + 1
                        tl = tiles_of_tg(tg)
                        for (bk, wt, rr) in ((bg, wg_, rg), (bu, wu_, ru)):
                            for kc in range(16):
                                S.add("pe", lambda e, bk=bk, kc=kc, wt=wt, lo=lo, hi=hi, cl=cl: e.matmul(
                                    banks[bk][:, 0:384], lhsT=wt[:, kc, cl * 128:(cl + 1) * 128], rhs=hhT[:, kc, lo:hi],
                                    start=(kc == 0), stop=(kc == 15)),
                                    reads=[hhres[t] for t in tl] + [rr], writes=[bres[bk]])
                        sl = silt[n % 2]
                        old = mixres if n < 2 else []
                        S.add("act", lambda e, sl=sl, bg=bg: e.activation(out=sl, in_=banks[bg][:, 0:384], func=AF.Silu),
                              reads=[bres[bg]], writes=[silres[n % 2]] + old)
                        S.add("dve", lambda e, sl=sl, bu=bu, ab=ab, f=f, lo=lo, hi=hi: e.tensor_tensor(
                            out=actT[ab][:, f, lo:hi], in0=sl, in1=banks[bu][:, 0:384], op=ALU.mult),
                            reads=[silres[n % 2], bres[bu]], writes=[actres[ab][t] for t in tl] + old)
                w_done(gu[hf][1])
            last = (gi == NG - 1)
            dnw = []
            if last:
                for ch in range(2):
                    offd, rd_ = w_use(dn[ch])
                    dnw.append((view(offd, 8192, BF16, "p (f c) -> p f c", f=4), rd_))
            for ch_t in ([(ch, None) for ch in range(2)] if not last else [(ch, t) for t in range(NT) for ch in range(2)]):
                ch = ch_t[0]
                if not last:
                    offd, rd_ = w_use(dn[ch])
                    wd_ = view(offd, 8192, BF16, "p (f c) -> p f c", f=4)
                    tiles = range(NT)
                else:
                    wd_, rd_ = dnw[ch]
                    tiles = [ch_t[1]]
                for t in tiles:
                    for c2 in range(2):
                        bk = 4 + dn_n["n"] % 4
                        dn_n["n"] += 1
                        for f in range(4):
                            S.add("pe", lambda e, bk=bk, f=f, t=t, c2=c2, ab=ab, wd_=wd_: e.matmul(
                                banks[bk][:, 0:512], lhsT=actT[ab][:, f, tcols(t)], rhs=wd_[:, f, c2 * 512:(c2 + 1) * 512],
                                start=(f == 0), stop=(f == 3)), reads=[actres[ab][t], rd_], writes=[bres[bk]])
                        col = ch * 1024 + c2 * 512
                        S.add("dve", lambda e, bk=bk, t=t, col=col: e.tensor_tensor(
                            out=x1[:, t, col:col + 512], in0=banks[bk][:, 0:512], in1=x1[:, t, col:col + 512], op=ALU.add),
                            reads=[bres[bk], x1res[t]], writes=[x1res[t]])
                    if last and ch == 1:
                        rms_a(t, NT + t)
                        rms_b1(NT + t)
                    for tf in (([t - 1] if t >= 1 else []) + ([t] if t == NT - 1 else [])) if (last and ch == 1) else []:
                        rs, ssr = rms_b(NT + tf)
                        yA = Res("yA")
                        yB = Res("yB")
                        S.add("dve", lambda e, t=tf, rs=rs: e.scalar_tensor_tensor(
                            out=x1[:, t, 0:1024], in0=x1[:, t, 0:1024], scalar=rs, in1=gfbv[:, 0:1024],
                            op0=ALU.mult, op1=ALU.mult), reads=[ssr, x1res[tf], gfres], writes=[yA])
                        S.add("act", lambda e, t=tf, rs=rs: e.activation(out=x1[:, t, 1024:2048], in_=x1[:, t, 1024:2048],
                                                                         func=AF.Copy, scale=rs),
                              reads=[ssr, x1res[tf]], writes=[yB])
                        S.add("pool", lambda e, t=tf: e.tensor_tensor(out=x1[:, t, 1024:2048], in0=x1[:, t, 1024:2048],
                                                                      in1=gfbv[:, 1024:2048], op=ALU.mult),
                              reads=[yB, gfres], writes=[yB])
                        S.add("sp", lambda e, t=tf: e.dma_start(out=y_out[t * 128:(t + 1) * 128, :], in_=x1[:, t, :]),
                              reads=[yA, yB], writes=[x1res[tf]], dma_key="oy%d" % (tf % 3))
                if not last:
                    w_done(dn[ch])
            if last:
                w_done(dn[1])
        S.emit(nc, st)
    return nc


def _consts():
    i = np.arange(128)[:, None]
    j = np.arange(128)[None, :]
    prev = np.where(j >= i, 0.0, NEG).astype(np.float32)
    cur = np.where(j <= i, 0.0, NEG).astype(np.float32)
    maskN = np.concatenate([prev, cur], 1)
    mask0_first = np.concatenate([np.full((128, 128), NEG, np.float32), cur], 1)
    t_row = (np.arange(128) % 8)[:, None]
    maskSc = np.where(j >= t_row, 0.0, NEG).astype(np.float32)
    maskSn = np.full((128, 16, 128), NEG, np.float32)
    for s in range(16):
        for tp in range(8):
            maskSn[:, s, s * 8 + tp] = np.where(tp <= t_row[:, 0], 0.0, NEG)
    trilT = (i <= j).astype(np.float32)
    si, ti = np.arange(128)[:, None], np.arange(128)[None, :]
    blkm = ((si // 8 == ti // 8) & (si % 8 <= ti % 8)).astype(np.float32)
    ident = np.eye(128, dtype=np.float32)
    return dict(maskN=maskN, mask0_first=mask0_first, maskSc=maskSc, maskSn=maskSn, trilT=trilT, blkm=blkm, ident=ident)


_NC_CACHE = {}


def make_in_maps(x_prompt, x_sample, cache_k, cache_v, norm1_g, w_in, gmlp_norm_g, w_s, b_s, sinks,
                 w_pa, w_pb, w_o, norm2_g, w_ff_gate, w_ff_up, w_ff_down, final_g):
    f = lambda a: np.ascontiguousarray(np.asarray(a, dtype=np.float32))
    x_prompt, x_sample, cache_k, cache_v = f(x_prompt), f(x_sample), f(cache_k), f(cache_v)
    C = _consts()

    ws = f(w_s)[0]
    wsT = np.ascontiguousarray(ws.transpose(0, 2, 1))
    wsTr = np.ascontiguousarray(np.tile(wsT[:, 0:8, 0:8], (1, 16, 16)))
    bs = f(b_s)[0]
    bsB = np.ascontiguousarray(np.broadcast_to(bs[None], (128, 8, 128)))
    bsBs = np.ascontiguousarray(np.broadcast_to(np.tile(bs[:, 0:8], (1, 16))[None], (128, 8, 128)))
    sk = f(sinks)[0]
    shared = dict(
        w_in=f(w_in)[0], w_pa=f(w_pa)[0], w_pb=f(w_pb)[0], w_o=f(w_o)[0],
        w_g=f(w_ff_gate)[0], w_u=f(w_ff_up)[0], w_d=f(w_ff_down)[0],
        g1T=np.ascontiguousarray(f(norm1_g)[0].reshape(16, 128).T),
        g2T=np.ascontiguousarray(f(norm2_g)[0].reshape(16, 128).T),
        gfB=np.ascontiguousarray(np.broadcast_to(f(final_g)[None], (128, D))),
        gnB=np.ascontiguousarray(np.broadcast_to(f(gmlp_norm_g)[0][None], (128, 1024))),
        wsT=wsT, wsTr=wsTr, trilT=C["trilT"], blkm=C["blkm"], bsB=bsB, bsBs=bsBs,
        sinksB=np.ascontiguousarray(np.broadcast_to(sk[None], (128, 16))),
        sinkrep=np.ascontiguousarray(np.repeat(sk, 8)[:, None]),
        sinksB2=np.ascontiguousarray(np.broadcast_to(np.repeat(sk, 2)[None], (128, 32))),
        sinkrep2=np.ascontiguousarray(np.repeat(np.repeat(sk, 8)[:, None], 2, axis=1)),
        maskN=C["maskN"], maskSc=C["maskSc"], maskSn=C["maskSn"], ident=C["ident"],
    )
    in_maps = []
    for c in range(NCORES):
        b, half = c // 2, c % 2
        xs = x_sample[c * 16:(c + 1) * 16].reshape(128, D)
        xp = x_prompt[b, half * 1024:(half + 1) * 1024]
        if half == 1:
            prev = x_prompt[b, 896:1024]
            mask0 = C["maskN"]
        else:
            prev = np.zeros((128, D), np.float32)
            mask0 = C["mask0_first"]
        xcat = np.ascontiguousarray(np.concatenate([xp, xs, prev], 0))
        ckc = cache_k[0, c * 16:(c + 1) * 16].reshape(16, 128, 256)
        cvc = cache_v[0, c * 16:(c + 1) * 16].reshape(16, 128, 256)
        ckT = np.ascontiguousarray(ckc.reshape(16, 128, 2, 128).transpose(2, 3, 0, 1))
        m = dict(shared)
        m.update(xc=xcat, ckT=ckT, ck=np.ascontiguousarray(ckc), cv=np.ascontiguousarray(cvc), mask0=mask0)
        in_maps.append(m)
    return in_maps


def kernel(**inputs):
    in_maps = make_in_maps(**inputs)
    if "nc" not in _NC_CACHE:
        _NC_CACHE["nc"] = build_program()
    nc = _NC_CACHE["nc"]
    res = run_bass_kernel_spmd(nc, in_maps, core_ids=list(range(NCORES)))
    return assemble(res.results)


def assemble(R):
    y_prompt = np.empty((4, 2048, D), np.float32)
    y_sample = np.empty((128, 8, D), np.float32)
    pk = np.empty((1, 4, 128, 4, 64), np.float32)
    pv = np.empty((1, 4, 128, 4, 64), np.float32)
    skw = np.empty((1, 128, 128, 4, 64), np.float32)
    svw = np.empty((1, 128, 128, 4, 64), np.float32)
    sg = np.empty((1, 128, 8, 8, 128), np.float32)
    for c in range(NCORES):
        b, half = c // 2, c % 2
        y = R[c]["y"]
        y_prompt[b, half * 1024:(half + 1) * 1024] = y[0:1024]
        y_sample[c * 16:(c + 1) * 16] = y[1024:1152].reshape(16, 8, D)
        if half == 1:
            pk[0, b] = R[c]["kv7"][:, 0:256].reshape(128, 4, 64)
            pv[0, b] = R[c]["kv7"][:, 256:512].reshape(128, 4, 64)
        skw[0, c * 16:(c + 1) * 16] = R[c]["skw"].reshape(16, 128, 4, 64)
        svw[0, c * 16:(c + 1) * 16] = R[c]["svw"].reshape(16, 128, 4, 64)
        sg[0, c * 16:(c + 1) * 16] = R[c]["sg"].reshape(16, 8, 8, 128)
    return (y_prompt, y_sample, pk, pv, skw, svw, sg)
```

```python
import contextlib
import numpy as np
import concourse.bass as bass
import concourse.mybir as mybir
from concourse.bass_utils import run_bass_kernel_spmd

F32 = mybir.dt.float32
BF16 = mybir.dt.bfloat16
AF = mybir.ActivationFunctionType
ALU = mybir.AluOpType
AX = mybir.AxisListType

ENGS = ("pe", "act", "dve", "pool", "sp")

D = 2048
DFF = 5632
INW = 7680
NT = 9
TOK = NT * 128
TOKP = TOK + 128
EPS = 1e-6
NEG = -1e30
NCORES = 8


class Res:
    __slots__ = ("name", "last_w", "readers", "excl")

    def __init__(self, name, excl=False):
        self.name = name
        self.last_w = None
        self.readers = []
        self.excl = excl


class Op:
    __slots__ = ("eng", "fn", "deps", "signal", "is_dma", "key", "val", "idx")

    def __init__(self, eng, fn, is_dma, key):
        self.eng = eng
        self.fn = fn
        self.deps = []
        self.signal = False
        self.is_dma = is_dma
        self.key = key
        self.val = None
        self.idx = None


class Sched:
    def __init__(self):
        self.q = {e: [] for e in ENGS}
        self.dma_count = {}
        self.stopped = False

    def add(self, eng, fn, reads=(), writes=(), dma_key=None):
        if self.stopped:
            return None
        is_dma = dma_key is not None
        op = Op(eng, fn, is_dma, dma_key)
        deps = {}
        for r in reads:
            if r.last_w is not None:
                deps[id(r.last_w)] = r.last_w
            if r.excl:
                for rd in r.readers:
                    if rd.is_dma or rd.eng != eng:
                        deps[id(rd)] = rd
        for w in writes:
            if w.last_w is not None:
                deps[id(w.last_w)] = w.last_w
            for rd in w.readers:
                deps[id(rd)] = rd
        for d in deps.values():
            if d is op:
                continue
            if (not d.is_dma) and (not is_dma) and d.eng == "pe" and eng == "pe":
                continue
            op.deps.append(d)
            d.signal = True
        for r in reads:
            if not is_dma:
                r.readers = [x for x in r.readers if x.is_dma or x.eng != eng]
            r.readers.append(op)
        for w in writes:
            w.last_w = op
            w.readers = []
        if is_dma:
            c = self.dma_count.get(dma_key, 0) + 1
            self.dma_count[dma_key] = c
            op.val = 16 * c
        op.idx = len(self.q[eng])
        self.q[eng].append(op)
        return op

    def emit(self, nc, stack):
        for e in ENGS:
            c = 0
            for op in self.q[e]:
                if op.is_dma:
                    continue
                if op.signal:
                    c += 1
                    op.val = c
        sems = {}

        def sem_of(key):
            if key not in sems:
                sems[key] = stack.enter_context(nc.semaphore("s%d" % len(sems)))
            return sems[key]

        for e in ("pe", "act", "dve", "pool"):
            sem_of(("eng", e))
        for k in self.dma_count:
            sem_of(("dma", k))

        def tok(d):
            if d.is_dma:
                return ("dma", d.key), d.val
            return ("eng", d.eng), d.val

        block = stack.enter_context(nc.Block())
        engobj = {"pe": block.tensor, "act": block.scalar, "dve": block.vector,
                  "pool": block.gpsimd, "sp": block.sync}

        def run_queue(e, eng):
            waited = {}
            for op in self.q[e]:
                need = {}
                for d in op.deps:
                    k, v = tok(d)
                    if waited.get(k, 0) >= v:
                        continue
                    if need.get(k, 0) < v:
                        need[k] = v
                for k, v in need.items():
                    eng.wait_ge(sem_of(k), v)
                    waited[k] = v
                inst = op.fn(eng)
                if op.is_dma:
                    inst.then_inc(sem_of(("dma", op.key)), 16)
                elif op.signal:
                    inst.then_inc(sem_of(("eng", e)), 1)
            if e == "sp":
                for k, c in self.dma_count.items():
                    eng.wait_ge(sem_of(("dma", k)), 16 * c)

        for e in ENGS:
            def body(eng, e=e):
                run_queue(e, eng)
            engobj[e](body)


def _drain(gen):
    for _ in gen:
        pass


def _interleave(main, filler, k=1):
    fdone = False
    for _ in main:
        for _ in range(k):
            if not fdone:
                try:
                    next(filler)
                except StopIteration:
                    fdone = True
    if not fdone:
        _drain(filler)


def build_program(stop=None):
    nc = bass.Bass("TRN2", target_bir_lowering=False)

    def din(name, shape):
        return nc.dram_tensor(name, list(shape), F32, kind="ExternalInput").ap()

    def dout(name, shape):
        return nc.dram_tensor(name, list(shape), F32, kind="ExternalOutput").ap()

    xc = din("xc", [TOKP, D])
    ckT = din("ckT", [2, 128, 16, 128])
    ck = din("ck", [16, 128, 256])
    cv = din("cv", [16, 128, 256])
    w_in = din("w_in", [D, INW])
    w_pa = din("w_pa", [1024, D])
    w_pb = din("w_pb", [1024, D])
    w_o = din("w_o", [D, D])
    w_g = din("w_g", [D, DFF])
    w_u = din("w_u", [D, DFF])
    w_d = din("w_d", [DFF, D])
    g1T = din("g1T", [128, 16])
    g2T = din("g2T", [128, 16])
    gfB = din("gfB", [128, D])
    gnB = din("gnB", [128, 1024])
    wsT = din("wsT", [8, 128, 128])
    wsTr = din("wsTr", [8, 128, 128])
    trilT = din("trilT", [128, 128])
    blkm = din("blkm", [128, 128])
    bsB = din("bsB", [128, 8, 128])
    bsBs = din("bsBs", [128, 8, 128])
    sinksB = din("sinksB", [128, 16])
    sinkrep = din("sinkrep", [128, 1])
    maskN = din("maskN", [128, 256])
    mask0 = din("mask0", [128, 256])
    maskSc = din("maskSc", [128, 128])
    maskSn = din("maskSn", [128, 16, 128])
    identd = din("ident", [128, 128])
    sinksB2 = din("sinksB2", [128, 32])
    sinkrep2 = din("sinkrep2", [128, 2])

    y_out = dout("y", [TOK, D])
    kv7_out = dout("kv7", [128, 512])
    skw_out = dout("skw", [16, 128, 256])
    svw_out = dout("svw", [16, 128, 256])
    sg_out = dout("sg", [128, 1024])

    S = Sched()

    with contextlib.ExitStack() as st:
        R1 = 112640
        R2 = 36864
        NSLOT = 4
        WS = 8192
        SS = 12288
        SMALL = 7168
        TOTAL = R1 + R2 + NSLOT * WS + SS + SMALL
        arena = st.enter_context(nc.sbuf_tensor("arena", [128, TOTAL // 4], F32))

        def view(off, nbytes, dt=F32, pat=None, **kw):
            assert off % 4 == 0 and nbytes % 4 == 0
            ap = arena[:, off // 4:(off + nbytes) // 4]
            if dt is not F32:
                ap = ap.bitcast(dt)
            if pat is not None:
                ap = ap.rearrange(pat, **kw)
            return ap

        hT = view(0, 40960, BF16, "p (c t) -> p c t", c=16)
        QT = view(40960, 18432, BF16, "p (c t) -> p c t", c=8)
        uT = view(59392, 18432, BF16, "p (c t) -> p c t", c=8)
        VLN_OFF = 77824
        KTd = view(96256, 10240, BF16, "p (h t) -> p h t", h=4)
        Vb = view(106496, 5120, BF16, "p (b c) -> p b c", b=10)
        gvf = view(59392, 36864, F32, "p (t c) -> p t c", t=NT)
        x1 = view(0, 73728, F32, "p (t c) -> p t c", t=NT)
        hhT = view(73728, 36864, BF16, "p (c t) -> p c t", c=16)
        r2 = R1
        mixT = view(r2, 36864, BF16, "p (c t) -> p c t", c=16)
        w0 = R1 + R2
        wslot = [w0 + i * WS for i in range(NSLOT)]
        wres = [Res("w%d" % i) for i in range(NSLOT)]
        s0 = w0 + NSLOT * WS
        hb = [view(s0 + i * 4096, 4096, BF16) for i in range(3)]
        hbres = [Res("hb%d" % i) for i in range(3)]
        m0 = s0 + SS
        _sm = [m0]

        def small(nbytes, dt=F32, pat=None, **kw):
            off = _sm[0]
            _sm[0] += (nbytes + 31) // 32 * 32
            assert _sm[0] <= m0 + SMALL, "small region overflow"
            return view(off, nbytes, dt, pat, **kw)

        ident = small(256, BF16)
        maskNb = small(512, BF16)
        mask0b = small(512, BF16)
        g1t = small(64)
        g2t = small(64)
        sinks_t = small(64)
        sinkrep_t = small(4)
        sinks_b2 = small(64, BF16)
        sinkrep_b2 = small(4, BF16)
        statA = small(4 * 64)
        st_att = small(4 * 6 * 2 * 16, F32, "p (q b h) -> p q b h", q=6, b=2)
        bnst = small(4 * 12 * 2, F32, "p (b k) -> p b k", b=2)
        bnmv = small(4 * 4 * 2, F32, "p (b k) -> p b k", b=2)
        st2 = small(4 * 64)
        SMALL_EXTRA = _sm[0]
        _sm[0] += 160 + 448
        assert _sm[0] <= m0 + SMALL

        banks = [st.enter_context(nc.psum_tensor("bank%d" % i, [128, 512], F32)) for i in range(8)]
        bres = [Res("bank%d" % i, excl=True) for i in range(8)]

        def bankbf(i):
            return banks[i][:].bitcast(BF16)

        cnt = {"ev": 0}

        def tcols(i):
            return slice(i * 128, (i + 1) * 128)

        def kblk(i):
            return 0 if i == 9 else i + 1

        def ev_engine():
            cnt["ev"] += 1
            return "act" if cnt["ev"] % 2 else "dve"

        def copy_op(eng_name, out, in_, reads, writes, scale=None):
            if eng_name == "act":
                if scale is None:
                    S.add("act", lambda e: e.copy(out=out, in_=in_), reads=reads, writes=writes)
                else:
                    S.add("act", lambda e: e.mul(out=out, in_=in_, mul=scale), reads=reads, writes=writes)
            else:
                if scale is None:
                    S.add(eng_name, lambda e: e.tensor_copy(out=out, in_=in_), reads=reads, writes=writes)
                else:
                    S.add(eng_name, lambda e: e.tensor_scalar_mul(out=out, in0=in_, scalar1=scale),
                          reads=reads, writes=writes)

        wq = []
        wstate = {"issued": 0, "consumed": 0}

        def wblock(parts):
            wq.append(parts)
            return len(wq) - 1

        def w_issue_upto(n):
            while wstate["issued"] < min(n, len(wq)):
                b = wstate["issued"]
                s = b % NSLOT
                for (vf, src) in wq[b]:
                    dst = vf(wslot[s])
                    S.add("pool", lambda e, dst=dst, src=src: e.dma_start(out=dst, in_=src),
                          writes=[wres[s]], dma_key="w%d" % s)
                wstate["issued"] += 1

        def w_use(b):
            assert b < wstate["consumed"] + NSLOT, (b, wstate)
            w_issue_upto(wstate["consumed"] + NSLOT)
            return wslot[b % NSLOT], wres[b % NSLOT]

        def w_done(b):
            wstate["consumed"] = b + 1
            w_issue_upto(wstate["consumed"] + NSLOT)

        def v_k16(off):
            return view(off, 8192, BF16, "p (k c) -> p k c", k=16)

        def blk_k16(w, c0):
            return wblock([(v_k16, w[:, c0:c0 + 256].rearrange("(k p) c -> p k c", p=128))])

        B_K = blk_k16(w_in, 1024)
        B_V = blk_k16(w_in, 1280)
        B_Q = [blk_k16(w_in, 0 + 256 * i) for i in range(4)]
        B_GV = [blk_k16(w_in, 2560 + 256 * i) for i in range(4)]
        B_U = [blk_k16(w_in, 1536 + 256 * i) for i in range(4)]
        B_GATE = []
        for i in range(8):
            ga = blk_k16(w_in, 3584 + 256 * i)
            gb = blk_k16(w_in, 5632 + 256 * i)
            pab = wblock([
                (lambda off: view(off, 4096, BF16, "p (k c) -> p k c", k=8),
                 w_pa[:, 256 * i:256 * i + 256].rearrange("(k p) c -> p k c", p=128)),
                (lambda off: view(off + 4096, 4096, BF16, "p (k c) -> p k c", k=8),
                 w_pb[:, 256 * i:256 * i + 256].rearrange("(k p) c -> p k c", p=128)),
            ])
            B_GATE.append((ga, gb, pab))
        B_WO = [blk_k16(w_o, 256 * i) for i in range(8)]
        NG = 11
        B_FF = []
        for gi in range(NG):
            f0 = gi * 4
            gu = []
            for hf in range(2):
                c0 = (f0 + 2 * hf) * 128
                gu.append((blk_k16(w_g, c0), blk_k16(w_u, c0)))
            dn = []
            for ch in range(2):
                dn.append(wblock([(lambda off: view(off, 8192, BF16, "p (f c) -> p f c", f=4),
                                   w_d[f0 * 128:(f0 + 4) * 128, ch * 1024:(ch + 1) * 1024]
                                   .rearrange("(f p) c -> p f c", p=128))]))
            B_FF.append((gu, dn))

        def cload(dst, src, cast=False):
            if cast:
                S.add("pool", lambda e: e.dma_start(out=dst, in_=src), dma_key="constc")
            else:
                S.add("sp", lambda e: e.dma_start(out=dst, in_=src), dma_key="const")

        cload(ident, identd[:, :], cast=True)
        cload(maskNb, maskN[:, :], cast=True)
        cload(mask0b, mask0[:, :], cast=True)
        cload(g1t, g1T[:, :])
        cload(g2t, g2T[:, :])
        cload(sinks_t, sinksB[:, :])
        cload(sinkrep_t, sinkrep[:, :])
        cload(sinks_b2, sinksB2[:, :], cast=True)
        cload(sinkrep_b2, sinkrep2[:, :], cast=True)
        constA = Res("constA")
        constB = Res("constB")
        constA.last_w = [op for op in S.q["pool"] if op.key == "constc"][-1]
        constB.last_w = [op for op in S.q["sp"] if op.key == "const"][-1]
        CONST = [constA, constB]
        eps_t = small(4)
        epsR = Res("eps")
        S.add("pool", lambda e: e.memset(eps_t, EPS), writes=[epsR])

        sa = 59392
        Qz = view(sa, 4096, BF16, "p (h c) -> p h c", h=16)
        QTz = view(sa + 4096, 8192, BF16, "p (c s h t) -> p c s h t", c=2, s=16, h=16)
        ckTb = view(sa + 12288, 8192, BF16, "p (c s k) -> p c s k", c=2, s=16)
        cvb = view(sa + 20480, 8192, BF16, "p (s c) -> p s c", s=16)
        mSn = view(sa + 28672, 4096, BF16, "p (s k) -> p s k", s=16)
        mSc = view(sa + 32768, 256, BF16)
        Osm1 = view(sa + 33024, 2048, BF16, "p (s d) -> p s d", s=16)
        SAres = Res("sample_attn_bufs")
        Qzres = Res("Qz")
        QTzres = Res("QTz")
        S.add("pool", lambda e: e.memset(view(sa, 4096, BF16), 0.0), writes=[Qzres])
        S.add("pool", lambda e: e.memset(view(sa + 4096, 8192, BF16), 0.0), writes=[QTzres])
        S.add("pool", lambda e: e.dma_start(out=ckTb, in_=ckT.rearrange("c p s k -> p c s k")),
              writes=[SAres], dma_key="sa")
        S.add("pool", lambda e: e.dma_start(out=cvb, in_=cv.rearrange("s k c -> k s c")),
              writes=[SAres], dma_key="sa")
        S.add("pool", lambda e: e.dma_start(out=mSn, in_=maskSn[:, :, :]), writes=[SAres], dma_key="sa")
        S.add("pool", lambda e: e.dma_start(out=mSc, in_=maskSc[:, :]), writes=[SAres], dma_key="sa")

        xtv = [view(r2 + i * 8192, 8192) for i in range(2)]
        xtres = [Res("xt%d" % i) for i in range(2)]
        hTres = [[Res("hT%d_%d" % (i, c)) for c in range(16)] for i in range(10)]
        order = [9] + list(range(9))
        KVB = r2 + 16384
        kd = [view(KVB + i * 1024, 1024, BF16, "p (h u d) -> p h u d", h=4, u=2) for i in range(2)]
        kdres = [Res("kd%d" % i) for i in range(2)]
        kvf = [view(KVB + 2048 + i * 2048, 2048) for i in range(2)]
        kvfres = [Res("kvf%d" % i) for i in range(2)]
        Kb8 = view(KVB + 6144, 512, BF16)
        KT8 = view(KVB + 6656, 512, BF16, "p (c t) -> p c t", c=2)
        Kb8res = Res("Kb8")
        KT8res = Res("KT8")
        KTres = [Res("KT%d" % i) for i in range(10)]
        Vres = [Res("V%d" % i) for i in range(10)]

        def stageA1(n, i):
            b = n % 2
            r0 = TOK if i == 9 else i * 128
            S.add("sp", lambda e: e.dma_start(out=xtv[b], in_=xc[r0:r0 + 128, :]),
                  writes=[xtres[b]], dma_key="xt%d" % b)
            ssr = Res("ss")
            ss = statA[:, 2 * n:2 * n + 1]
            rs = statA[:, 2 * n + 1:2 * n + 2]
            S.add("act", lambda e: e.activation(out=hb[b], in_=xtv[b], func=AF.Square, accum_out=ss),
                  reads=[xtres[b]], writes=[hbres[b], ssr])
            S.add("act", lambda e: e.activation(out=rs, in_=ss, func=AF.Sqrt, bias=eps_t[:, 0:1], scale=1.0 / D),
                  reads=[ssr, epsR], writes=[ssr])
            S.add("dve", lambda e: e.reciprocal(out=rs, in_=rs), reads=[ssr], writes=[ssr])
            S.add("dve", lambda e: e.tensor_scalar_mul(out=hb[b], in0=xtv[b], scalar1=rs),
                  reads=[ssr, xtres[b], hbres[b]], writes=[hbres[b]])

        def stageA2(n, i):
            b = n % 2
            for half in range(2):
                bk = 4 + (2 * n + half) % 4
                for j in range(8):
                    c = half * 8 + j
                    S.add("pe", lambda e, j=j, c=c, bk=bk: e.transpose(
                        out=bankbf(bk)[:, j * 128:(j + 1) * 128], in_=hb[b][:, c * 128:(c + 1) * 128],
                        identity=ident), reads=[hbres[b]] + CONST, writes=[bres[bk]])
                en = "act" if half == 0 else "dve"
                for j in range(8):
                    c = half * 8 + j
                    src = bankbf(bk)[:, j * 128:(j + 1) * 128]
                    dst = hT[:, c, tcols(i)]
                    if en == "act":
                        S.add("act", lambda e, dst=dst, src=src, c=c: e.activation(
                            out=dst, in_=src, func=AF.Copy, scale=g1t[:, c:c + 1]),
                            reads=[bres[bk]] + CONST, writes=[hTres[i][c]])
                    else:
                        S.add("dve", lambda e, dst=dst, src=src, c=c: e.tensor_scalar_mul(
                            out=dst, in0=src, scalar1=g1t[:, c:c + 1]),
                            reads=[bres[bk]] + CONST, writes=[hTres[i][c]])

        offK, rK = w_use(B_K)
        offV, rV = w_use(B_V)
        wk = v_k16(offK)
        wv = v_k16(offV)

        def stageKV(n, i):
            bk = n % 2
            for kc in range(16):
                S.add("pe", lambda e, kc=kc: e.matmul(
                    banks[bk][:, 0:256], lhsT=hT[:, kc, tcols(i)], rhs=wk[:, kc, :], start=(kc == 0), stop=(kc == 15)),
                    reads=[hTres[i][kc], rK], writes=[bres[bk]])
            for kc in range(16):
                S.add("pe", lambda e, kc=kc: e.matmul(
                    banks[bk][:, 256:512], lhsT=hT[:, kc, tcols(i)], rhs=wv[:, kc, :], start=(kc == 0), stop=(kc == 15)),
                    reads=[hTres[i][kc], rV], writes=[bres[bk]])
            kdb = kd[n % 2]
            kdr = kdres[n % 2]
            kin = banks[bk][:, 0:256].rearrange("p (h d) -> p h d", h=4)
            S.add("dve", lambda e: e.tensor_copy(out=kdb[:, :, 0, :], in_=kin), reads=[bres[bk]], writes=[kdr])
            S.add("dve", lambda e: e.tensor_copy(out=kdb[:, :, 1, :], in_=kin), reads=[bres[bk], kdr], writes=[kdr])
            vdst = Vb[:, kblk(i), :]
            S.add("dve", lambda e: e.tensor_copy(out=vdst, in_=banks[bk][:, 256:512]), reads=[bres[bk]], writes=[Vres[i]])
            if i in (7, 8):
                kvb = kvf[i - 7]
                kvr = kvfres[i - 7]
                S.add("dve", lambda e: e.tensor_copy(out=kvb, in_=banks[bk][:, :]), reads=[bres[bk]], writes=[kvr])
                if i == 7:
                    S.add("sp", lambda e: e.dma_start(out=kv7_out[:, :], in_=kvb), reads=[kvr], dma_key="okv7")
                else:
                    for s in range(16):
                        S.add("sp", lambda e, s=s: e.dma_start(
                            out=skw_out[s, 120:128, :], in_=kvb[s * 8:(s + 1) * 8, 0:256]), reads=[kvr], dma_key="oskw")
                        S.add("sp", lambda e, s=s: e.dma_start(
                            out=svw_out[s, 120:128, :], in_=kvb[s * 8:(s + 1) * 8, 256:512]), reads=[kvr], dma_key="osvw")
                    S.add("dve", lambda e: e.tensor_copy(out=Kb8, in_=banks[bk][:, 0:256]), reads=[bres[bk]], writes=[Kb8res])
            tb = 2 + n % 2
            for h in range(4):
                S.add("pe", lambda e, h=h: e.transpose(
                    out=bankbf(tb)[:, h * 128:(h + 1) * 128],
                    in_=kdb[:, h, :, :].rearrange("p u d -> p (u d)"), identity=ident),
                    reads=[kdr] + CONST, writes=[bres[tb]])
            kdst = KTd[:, :, kblk(i) * 128:(kblk(i) + 1) * 128]
            S.add("act", lambda e: e.copy(out=kdst, in_=bankbf(tb)[:, 0:512].rearrange("p (h t) -> p h t", h=4)),
                  reads=[bres[tb]], writes=[KTres[i]])
            if i == 8:
                for c in range(2):
                    S.add("pe", lambda e, c=c: e.transpose(
                        out=bankbf(tb)[:, 512 + c * 128:512 + (c + 1) * 128], in_=Kb8[:, c * 128:(c + 1) * 128],
                        identity=ident), reads=[Kb8res] + CONST, writes=[bres[tb]])
                S.add("act", lambda e: e.copy(
                    out=KT8, in_=bankbf(tb)[:, 512:768].rearrange("p (c t) -> p c t", c=2)),
                    reads=[bres[tb]], writes=[KT8res])

        stageA1(0, order[0])
        for n, i in enumerate(order):
            if n + 1 < len(order):
                stageA1(n + 1, order[n + 1])
            stageA2(n, i)
            if n >= 1:
                stageKV(n - 1, order[n - 1])
        stageKV(len(order) - 1, order[-1])
        w_done(B_V)
        S.add("sp", lambda e: e.dma_start(out=skw_out[:, 0:120, :], in_=ck[:, 8:128, :]), dma_key="ockw")
        S.add("sp", lambda e: e.dma_start(out=svw_out[:, 0:120, :], in_=cv[:, 8:128, :]), dma_key="ocvw")

        TG = [(0, 384), (384, 768), (768, 1152)]
        QTres = [Res("QT%d" % i) for i in range(NT)]

        def tiles_of_tg(tg):
            return [tg * 3, tg * 3 + 1, tg * 3 + 2]

        fm_bank = {"n": 0}

        def fm_group(wt, cl, K, src, srcres_fn, tg, extra_reads, banklist):
            bk = banklist[fm_bank["n"] % len(banklist)]
            fm_bank["n"] += 1
            lo, hi = TG[tg]
            for kc in range(K):
                S.add("pe", lambda e, kc=kc: e.matmul(
                    banks[bk][:, 0:384], lhsT=wt[:, kc, cl * 128:(cl + 1) * 128], rhs=src[:, kc, lo:hi],
                    start=(kc == 0), stop=(kc == K - 1)),
                    reads=[srcres_fn(t, kc) for t in tiles_of_tg(tg)] + extra_reads, writes=[bres[bk]])
            return bk, lo, hi

        hTr = lambda t, kc: hTres[t][kc]
        q_w = []
        for wb in range(4):
            off, rw = w_use(B_Q[wb])
            q_w.append((v_k16(off), rw))

        def q_group(wb, cl, tg, banklist):
            wt, rw = q_w[wb]
            m = wb * 2 + cl
            bk, lo, hi = fm_group(wt, cl, 16, hT, hTr, tg, [rw], banklist)
            copy_op(ev_engine(), QT[:, m, lo:hi], banks[bk][:, 0:384], [bres[bk]],
                    [QTres[t] for t in tiles_of_tg(tg)], scale=0.125)

        def gen_q_rest():
            for tg in (1, 2):
                for wb in range(4):
                    for cl in range(2):
                        q_group(wb, cl, tg, [7])
                        if tg == 2 and cl == 1:
                            w_done(B_Q[wb])
                        yield

        for wb in range(4):
            wt, rw = q_w[wb]
            for cl in range(2):
                q_group(wb, cl, 0, [0, 1, 2, 3, 4, 5])
            for kc in range(16):
                S.add("pe", lambda e, kc=kc, wt=wt: e.matmul(
                    banks[6][:, 0:256], lhsT=hT[:, kc, tcols(8)], rhs=wt[:, kc, :], start=(kc == 0), stop=(kc == 15)),
                    reads=[hTres[8][kc], rw], writes=[bres[6]])
            eh = wb % 2
            S.add("dve", lambda e, wb=wb, eh=eh: e.tensor_scalar_mul(
                out=Qz[:, 4 * wb:4 * wb + 4, eh * 64:(eh + 1) * 64],
                in0=banks[6][:, 0:256].rearrange("p (h d) -> p h d", h=4), scalar1=0.125),
                reads=[bres[6], Qzres], writes=[Qzres])
            for hh in range(4 * wb, 4 * wb + 4):
                S.add("pe", lambda e, hh=hh: e.transpose(out=bankbf(7)[:, (hh % 4) * 128:(hh % 4 + 1) * 128],
                                                         in_=Qz[:, hh, :], identity=ident),
                      reads=[Qzres] + CONST, writes=[bres[7]])
            for hh in range(4 * wb, 4 * wb + 4):
                copy_op("act" if hh % 2 else "dve", QTz[:, wb // 2, :, hh, :],
                        bankbf(7)[:, (hh % 4) * 128:(hh % 4 + 1) * 128].rearrange("p (s t) -> p s t", s=16),
                        [bres[7], QTzres], [QTzres])

        ATB = r2 + 24576
        Pb = [view(ATB + i * 512, 512, BF16) for i in range(2)]
        PTb = [view(ATB + 1024 + i * 512, 512, BF16) for i in range(2)]
        atm = [view(ATB + 2048 + i * 2048, 2048, BF16) for i in range(2)]
        Osm = view(ATB + 6144, 4096, BF16, "p (s u d) -> p s u d", s=16, u=2)
        Pres = [Res("P%d" % i) for i in range(2)]
        PTres = [Res("PT%d" % i) for i in range(2)]
        atmres = [Res("atm%d" % i) for i in range(2)]
        stres = {}

        def sres(q, b, h):
            k = (q, b, h)
            if k not in stres:
                stres[k] = Res("st%s" % (k,))
            return stres[k]

        att_n = {"n": 0}

        def softmax_head(sbank, b2, h, sink_ap, per_head_stats):
            n = att_n["n"]
            att_n["n"] += 1
            pb = n % 2
            mx = st_att[:, 0, b2, h:h + 1]
            ngm = st_att[:, 1, b2, h:h + 1]
            rsum = st_att[:, 2, b2, h:h + 1]
            R = [sres(q, b2, h) for q in range(6)]
            S.add("dve", lambda e: e.reduce_max(out=ngm, in_=banks[sbank][:, 0:258], axis=AX.X, negate=True),
                  reads=[bres[sbank]], writes=[R[1]])
            S.add("act", lambda e: e.activation(out=Pb[pb], in_=banks[sbank][:, 0:256], func=AF.Exp, bias=ngm,
                                                scale=1.0, accum_out=rsum),
                  reads=[bres[sbank], R[1]], writes=[Pres[pb], R[2]])
            if per_head_stats:
                es = st_att[:, 3, b2, h:h + 1]
                den = st_att[:, 4, b2, h:h + 1]
                rden = st_att[:, 5, b2, h:h + 1]
                S.add("act", lambda e: e.activation(out=es, in_=ngm, func=AF.Exp, bias=sink_ap, scale=1.0),
                      reads=[R[1]] + CONST, writes=[R[3]])
                S.add("dve", lambda e: e.tensor_tensor(out=den, in0=rsum, in1=es, op=ALU.add),
                      reads=[R[2], R[3]], writes=[R[4]])
                S.add("dve", lambda e: e.reciprocal(out=rden, in_=den), reads=[R[4]], writes=[R[5]])
            return pb

        def transpose_probs(pb):
            tb = 2 + pb
            for c in range(2):
                S.add("pe", lambda e, c=c: e.transpose(out=bankbf(tb)[:, c * 128:(c + 1) * 128],
                                                       in_=Pb[pb][:, c * 128:(c + 1) * 128], identity=ident),
                      reads=[Pres[pb]] + CONST, writes=[bres[tb]])
            copy_op("act", PTb[pb], bankbf(tb)[:, 0:256], [bres[tb]], [PTres[pb]])

        Osres = Res("Osm1")

        def sample_scores(s):
            sbank = s % 2
            S.add("pe", lambda e: e.matmul(banks[sbank][:, 256:258], lhsT=ident, rhs=sinkrep_b2,
                                           start=True, stop=True), reads=CONST, writes=[bres[sbank]])
            for c in range(2):
                S.add("pe", lambda e, c=c: e.matmul(
                    banks[sbank][:, 0:128], lhsT=QTz[:, c, s, :, :].rearrange("p h t -> p (h t)"), rhs=ckTb[:, c, s, :],
                    start=(c == 0), stop=False), reads=[QTzres, SAres], writes=[bres[sbank]])
            S.add("pe", lambda e: e.matmul(banks[sbank][:, 0:128], lhsT=ident, rhs=mSc, start=False, stop=True),
                  reads=CONST + [SAres], writes=[bres[sbank]])
            for c in range(2):
                S.add("pe", lambda e, c=c: e.matmul(
                    banks[sbank][:, 128:256], lhsT=QTz[:, c, s, :, :].rearrange("p h t -> p (h t)"), rhs=KT8[:, c, :],
                    start=(c == 0), stop=False), reads=[QTzres, KT8res], writes=[bres[sbank]])
            S.add("pe", lambda e: e.matmul(banks[sbank][:, 128:256], lhsT=ident, rhs=mSn[:, s, :], start=False, stop=True),
                  reads=CONST + [SAres], writes=[bres[sbank]])

        spb = {}

        def sample_E1(s):
            spb[s] = softmax_head(s % 2, s % 2, s, sinkrep_t[:, 0:1], True)

        def sample_V(s):
            b2 = s % 2
            pb = spb[s]
            ob = 4 + s % 2
            S.add("pe", lambda e, ob=ob, pb=pb, s=s: e.matmul(
                banks[ob][:, 0:256], lhsT=PTb[pb][:, 0:128], rhs=cvb[:, s, :], start=True, stop=False),
                reads=[PTres[pb], SAres], writes=[bres[ob]])
            S.add("pe", lambda e, ob=ob, pb=pb: e.matmul(
                banks[ob][:, 0:256], lhsT=PTb[pb][:, 128:256], rhs=Vb[:, 9, :], start=False, stop=True),
                reads=[PTres[pb], Vres[8]], writes=[bres[ob]])
            for g in range(4):
                S.add("dve", lambda e, ob=ob, g=g, s=s, b2=b2: e.tensor_scalar_mul(
                    out=Osm1[32 * g:32 * g + 32, s, :], in0=banks[ob][32 * g:32 * g + 32, 64 * g:64 * g + 64],
                    scalar1=st_att[32 * g:32 * g + 32, 5, b2, s:s + 1]),
                    reads=[bres[ob], sres(5, b2, s)], writes=[Osres])

        def gen_sample():
            sample_scores(0)
            sample_scores(1)
            sample_E1(0)
            sample_scores(2)
            sample_E1(1)
            transpose_probs(spb[0])
            for s in range(16):
                if s + 3 < 16:
                    sample_scores(s + 3)
                if s + 2 < 16:
                    sample_E1(s + 2)
                if s + 1 < 16:
                    transpose_probs(spb[s + 1])
                sample_V(s)
                yield

        qrest = gen_q_rest()
        for _ in gen_sample():
            try:
                next(qrest)
            except StopIteration:
                pass
        _drain(qrest)
        Osmres = Res("Osm")
        for u in range(2):
            S.add("dve", lambda e, u=u: e.tensor_copy(out=Osm[:, :, u, :], in_=Osm1), reads=[Osres, Osmres], writes=[Osmres])
        for s in range(16):
            tb = 6 + s % 2
            S.add("pe", lambda e, tb=tb, s=s: e.transpose(out=bankbf(tb)[:, 0:128],
                                                          in_=Osm[:, s, :, :].rearrange("p u d -> p (u d)"), identity=ident),
                  reads=[Osmres] + CONST, writes=[bres[tb]])
            for eh in range(2):
                srcv = bankbf(tb)[eh * 64:(eh + 1) * 64, 0:128].rearrange("p (j u t) -> p j u t", j=8, u=2)[:, :, eh, :]
                dstv = QT[eh * 64:(eh + 1) * 64, :, 1024 + s * 8:1024 + (s + 1) * 8]
                copy_op("act" if s % 2 else "dve", dstv, srcv, [bres[tb]], [QTres[8]])
        attnT = QT
        attnres = QTres

        sample_dead = [SAres, Qzres, QTzres, Osres]

        def prompt_scores(t, h):
            kvh = h // 4
            base = (h % 2) * 64
            ch = h // 2
            sbank = h % 2
            mk = mask0b if t == 0 else maskNb
            kcols = slice(t * 128, t * 128 + 256)
            prev_i = 9 if t == 0 else t - 1
            S.add("pe", lambda e: e.matmul(banks[sbank][:, 256:258], lhsT=ident, rhs=sinks_b2[:, 2 * h:2 * h + 2],
                                           start=True, stop=True), reads=CONST, writes=[bres[sbank]])
            S.add("pe", lambda e: e.matmul(
                banks[sbank][:, 0:256], lhsT=QT[base:base + 64, ch, tcols(t)], rhs=KTd[base:base + 64, kvh, kcols],
                start=True, stop=False),
                reads=[QTres[t], KTres[prev_i], KTres[t]], writes=[bres[sbank]])
            S.add("pe", lambda e: e.matmul(banks[sbank][:, 0:256], lhsT=ident, rhs=mk, start=False, stop=True),
                  reads=CONST, writes=[bres[sbank]])

        ATT_HEADS = [(t, h) for t in range(8) for h in range(16)]
        NH = len(ATT_HEADS)
        att_pb = {}

        def att_E1(idx):
            t, h = ATT_HEADS[idx]
            att_pb[idx] = softmax_head(h % 2, t % 2, h, sinks_t[:, h:h + 1], False)

        def att_E2(idx):
            pb = att_pb[idx]
            tb = 2 + pb
            for c in range(2):
                S.add("pe", lambda e, c=c: e.transpose(out=bankbf(tb)[:, c * 128:(c + 1) * 128],
                                                       in_=Pb[pb][:, c * 128:(c + 1) * 128], identity=ident),
                      reads=[Pres[pb]] + CONST, writes=[bres[tb]])
            copy_op("dve" if idx % 2 else "act", PTb[pb], bankbf(tb)[:, 0:256], [bres[tb]], [PTres[pb]])

        def att_V(idx):
            t, h = ATT_HEADS[idx]
            pb = att_pb[idx]
            b2 = t % 2
            ab = atm[b2]
            kvh = h // 4
            prev_i = 9 if t == 0 else t - 1
            ob = 4 + h // 8
            oc = (h % 8) * 64
            S.add("pe", lambda e: e.matmul(
                banks[ob][:, oc:oc + 64], lhsT=PTb[pb][:, 0:128], rhs=Vb[:, t, kvh * 64:(kvh + 1) * 64],
                start=True, stop=False), reads=[PTres[pb], Vres[prev_i]], writes=[bres[ob]])
            S.add("pe", lambda e: e.matmul(
                banks[ob][:, oc:oc + 64], lhsT=PTb[pb][:, 128:256], rhs=Vb[:, t + 1, kvh * 64:(kvh + 1) * 64],
                start=False, stop=True), reads=[PTres[pb], Vres[t]], writes=[bres[ob]])
            if h % 8 == 7:
                h0 = h - 7
                hs = slice(h0, h0 + 8)
                Rg = [sres(q, b2, hh) for q in (1, 2) for hh in range(h0, h0 + 8)]
                Wg = [sres(q, b2, hh) for q in (3, 4, 5) for hh in range(h0, h0 + 8)]
                S.add("dve", lambda e: e.tensor_tensor(out=st_att[:, 3, b2, hs], in0=st_att[:, 1, b2, hs],
                                                       in1=sinks_t[:, hs], op=ALU.add),
                      reads=Rg + CONST, writes=Wg)
                S.add("act", lambda e: e.activation(out=st_att[:, 3, b2, hs], in_=st_att[:, 3, b2, hs], func=AF.Exp),
                      reads=Wg, writes=Wg)
                S.add("dve", lambda e: e.tensor_tensor(out=st_att[:, 4, b2, hs], in0=st_att[:, 2, b2, hs],
                                                       in1=st_att[:, 3, b2, hs], op=ALU.add), reads=Rg + Wg, writes=Wg)
                S.add("dve", lambda e: e.reciprocal(out=st_att[:, 5, b2, hs], in_=st_att[:, 4, b2, hs]),
                      reads=Wg, writes=Wg)
                rdb = st_att[:, 5, b2, hs].unsqueeze(2).broadcast_to([128, 8, 64])
                S.add("dve", lambda e: e.tensor_tensor(
                    out=ab[:, h0 * 64:(h0 + 8) * 64].rearrange("p (h d) -> p h d", h=8),
                    in0=banks[ob][:, :].rearrange("p (h d) -> p h d", h=8), in1=rdb, op=ALU.mult),
                    reads=[bres[ob]] + Wg, writes=[atmres[b2]])
            if h == 15:
                for j in range(8):
                    S.add("pe", lambda e, j=j: e.transpose(out=bankbf(6)[:, j * 128:(j + 1) * 128],
                                                           in_=ab[:, j * 128:(j + 1) * 128], identity=ident),
                          reads=[atmres[b2]] + CONST, writes=[bres[6]])
                copy_op("act", QT[:, :, tcols(t)], bankbf(6)[:, 0:1024].rearrange("p (c t) -> p c t", c=8),
                        [bres[6]], [QTres[t]])

        def gen_att_prompt():
            prompt_scores(*ATT_HEADS[0])
            prompt_scores(*ATT_HEADS[1])
            att_E1(0)
            prompt_scores(*ATT_HEADS[2])
            att_E1(1)
            att_E2(0)
            for k in range(NH):
                if k + 3 < NH:
                    prompt_scores(*ATT_HEADS[k + 3])
                if k + 2 < NH:
                    att_E1(k + 2)
                if k + 1 < NH:
                    att_E2(k + 1)
                att_V(k)
                yield

        gvres = [Res("gvf%d" % i) for i in range(NT)]
        vlnb = view(r2, 18432, BF16, "p (t c) -> p t c", t=NT)
        gnb = view(r2 + 18432, 4096)
        vlnres = [Res("vln%d" % i) for i in range(NT)]
        gnres = Res("gnB")
        uTres = [Res("uT%d" % i) for i in range(NT)]
        lnres = [Res("ln0"), Res("ln1")]

        bnall = view(SMALL_EXTRA, 4 * NT * 4, F32, "p (t k) -> p t k", t=NT)
        bnst9 = view(SMALL_EXTRA + 160, 4 * NT * 12, F32, "p (t k) -> p t k", t=NT)

        def gen_gv():
            r2old = [xtres[0], xtres[1], KT8res, Kb8res] + kdres + kvfres
            S.add("sp", lambda e: e.dma_start(out=gnb, in_=gnB[:, :]), writes=[gnres] + r2old, dma_key="gn")
            for wb in range(4):
                off, rw = w_use(B_GV[wb])
                wt = v_k16(off)
                for t in range(NT):
                    gv_group(wb, t, wt, rw)
                    yield
                w_done(B_GV[wb])

        def gv_group(wb, t, wt, rw):
            bk = 7
            for kc in range(16):
                S.add("pe", lambda e, kc=kc: e.matmul(
                    banks[bk][:, 0:256], lhsT=hT[:, kc, tcols(t)], rhs=wt[:, kc, :], start=(kc == 0), stop=(kc == 15)),
                    reads=[hTres[t][kc], rw], writes=[bres[bk]])
            S.add("dve", lambda e: e.tensor_copy(out=gvf[:, t, wb * 256:(wb + 1) * 256], in_=banks[bk][:, 0:256]),
                  reads=[bres[bk]], writes=[gvres[t]] + (sample_dead if wb == 0 else []))

        def u_group(wb, cl, tg, wt, rw):
            m = wb * 2 + cl
            bk, lo, hi = fm_group(wt, cl, 16, hT, hTr, tg, [rw], [0, 1, 2, 3, 4, 5, 6, 7])
            S.add("act", lambda e: e.activation(out=uT[:, m, lo:hi], in_=banks[bk][:, 0:384], func=AF.Gelu_apprx_tanh),
                  reads=[bres[bk]],
                  writes=[uTres[t] for t in tiles_of_tg(tg)]
                  + [gvres[t] for t in range((m * 2304) // 4096, ((m + 1) * 2304 - 1) // 4096 + 1)])

        def ln_part1():
            R = lnres[0]
            for t in range(NT):
                S.add("act", lambda e, t=t: e.activation(out=gvf[:, t, :], in_=gvf[:, t, :], func=AF.Gelu_apprx_tanh),
                      reads=[gvres[t]], writes=[gvres[t]])
                for c in range(2):
                    S.add("dve", lambda e, t=t, c=c: e.bn_stats(out=bnst9[:, t, c * 6:(c + 1) * 6],
                                                                in_=gvf[:, t, c * 512:(c + 1) * 512]),
                          reads=[gvres[t]], writes=[R])
                S.add("dve", lambda e, t=t: e.bn_aggr(out=bnall[:, t, 0:2], in_=bnst9[:, t, :]), reads=[R], writes=[R])

        def ln_part2():
            R = lnres[0]
            S.add("act", lambda e: e.activation(out=bnall[:, :, 2], in_=bnall[:, :, 1], func=AF.Sqrt, bias=eps_t[:, 0:1],
                                                scale=1.0), reads=[R, epsR], writes=[R])
            S.add("dve", lambda e: e.reciprocal(out=bnall[:, :, 3], in_=bnall[:, :, 2]), reads=[R], writes=[R])
            for t in range(NT):
                S.add("dve", lambda e, t=t: e.tensor_scalar(out=gvf[:, t, :], in0=gvf[:, t, :], scalar1=bnall[:, t, 0:1],
                                                            scalar2=bnall[:, t, 3:4], op0=ALU.subtract, op1=ALU.mult),
                      reads=[R, gvres[t]], writes=[gvres[t]])
                S.add("dve", lambda e, t=t: e.tensor_tensor(out=vlnb[:, t, :], in0=gvf[:, t, :], in1=gnb, op=ALU.mult),
                      reads=[gvres[t], gnres], writes=[vlnres[t]])
                if t == 8:
                    S.add("dve", lambda e, t=t: e.tensor_tensor(out=gvf[:, t, :], in0=gvf[:, t, :], in1=gnb, op=ALU.mult),
                          reads=[gvres[t], gnres, vlnres[t]], writes=[gvres[t]])
                    S.add("sp", lambda e, t=t: e.dma_start(out=sg_out[:, :], in_=gvf[:, t, :]), reads=[gvres[t]], dma_key="osg")

        _interleave(gen_att_prompt(), gen_gv(), k=1)
        ln_part1()
        ln_part2()
        for wb in range(4):
            off, rw = w_use(B_U[wb])
            wt = v_k16(off)
            for cl in range(2):
                for tg in range(3):
                    u_group(wb, cl, tg, wt, rw)
            w_done(B_U[wb])

        SPB = r2 + 22528
        wst = view(SPB, 4096, F32, "p (g t) -> p g t", g=8)
        wsm = view(SPB + 4096, 2048, BF16, "p (g t) -> p g t", g=8)
        wsms = view(SPB + 6144, 2048, BF16, "p (g t) -> p g t", g=8)
        msk = view(SPB + 8192, 512)
        bsb = view(SPB + 8704, 4096, F32, "p (g t) -> p g t", g=8)
        wres_sp = Res("wst")
        mres = Res("msk")
        wsmres = Res("wsm")
        bsres = Res("bsb")
        att_bufs = Pres + PTres + atmres + [Osmres]
        for which in range(2):
            srcw = wsT if which == 0 else wsTr
            srcm = trilT if which == 0 else blkm
            dstw = wsm if which == 0 else wsms
            S.add("sp", lambda e, srcw=srcw: e.dma_start(out=wst, in_=srcw.rearrange("g s t -> s g t")),
                  writes=[wres_sp] + (att_bufs if which == 0 else []), dma_key="wst")
            S.add("sp", lambda e, srcm=srcm: e.dma_start(out=msk, in_=srcm[:, :]),
                  writes=[mres] + (att_bufs if which == 0 else []), dma_key="msk")
            for g in range(8):
                S.add("dve", lambda e, g=g, dstw=dstw: e.tensor_tensor(out=dstw[:, g, :], in0=wst[:, g, :], in1=msk,
                                                                      op=ALU.mult),
                      reads=[wres_sp, mres], writes=[wsmres])
        S.add("sp", lambda e: e.dma_start(out=bsb, in_=bsB[:, :, :]), writes=[bsres] + att_bufs, dma_key="bsb")
        bsbs = wst
        S.add("sp", lambda e: e.dma_start(out=bsbs, in_=bsBs[:, :, :]), writes=[wres_sp], dma_key="wst")
        sptmp = [view(96256 + i * 2048, 2048) for i in range(2)]
        sptres = [Res("sptmp%d" % i) for i in range(2)]
        kv_dead = KTres + Vres
        for t in range(NT):
            wm = wsms if t == 8 else wsm
            bb = bsbs if t == 8 else bsb
            for half in range(2):
                n = t * 2 + half
                bk = n % 4
                for gl in range(4):
                    g = half * 4 + gl
                    S.add("pe", lambda e, bk=bk, gl=gl, g=g, t=t, wm=wm: e.matmul(
                        banks[bk][:, gl * 128:(gl + 1) * 128], lhsT=vlnb[:, t, g * 128:(g + 1) * 128], rhs=wm[:, g, :],
                        start=True, stop=True), reads=[vlnres[t], wsmres], writes=[bres[bk]])
                tp = sptmp[n % 2]
                S.add("dve", lambda e, bk=bk, tp=tp, bb=bb, half=half: e.tensor_tensor(
                    out=tp, in0=banks[bk][:, :], in1=bb[:, half * 4:(half + 1) * 4, :].rearrange("p g t -> p (g t)"),
                    op=ALU.add), reads=[bres[bk], bsres, wres_sp], writes=[sptres[n % 2]] + (kv_dead if n < 2 else []))
                S.add("pool" if n % 2 else "dve", lambda e, tp=tp, half=half, t=t: e.tensor_tensor(
                    out=uT[:, half * 4:(half + 1) * 4, tcols(t)], in0=uT[:, half * 4:(half + 1) * 4, tcols(t)],
                    in1=tp.rearrange("p (g t) -> p g t", g=4), op=ALU.mult),
                    reads=[sptres[n % 2], uTres[t]], writes=[uTres[t]])
        aT = uT
        aTres = uTres

        mixres = [Res("mix%d" % i) for i in range(NT)]
        sga = [view(VLN_OFF + i * 1536, 1536) for i in range(6)]
        sgb = [view(VLN_OFF + 9216 + i * 1536, 1536) for i in range(6)]
        sgares = [Res("sga%d" % i) for i in range(6)]
        sgbres = [Res("sgb%d" % i) for i in range(6)]
        R2old = vlnres + [gnres, wres_sp, mres, wsmres, bsres]
        gate_first = {"a": True, "m": True}
        ALLB = list(range(8))
        for i in range(8):
            ga, gb, pab = B_GATE[i]
            for (blk, dst, dres) in ((ga, sga, sgares), (gb, sgb, sgbres)):
                off, rw = w_use(blk)
                wt = v_k16(off)
                for cl in range(2):
                    for tg in range(3):
                        q = cl * 3 + tg
                        bk, lo, hi = fm_group(wt, cl, 16, hT, hTr, tg, [rw], ALLB)
                        old = (gvres + sptres) if gate_first["a"] else []
                        S.add("act", lambda e, q=q, bk=bk, dst=dst: e.activation(out=dst[q], in_=banks[bk][:, 0:384],
                                                                                  func=AF.Sigmoid),
                              reads=[bres[bk]], writes=[dres[q]] + old)
                gate_first["a"] = False
                w_done(blk)
            offp, rp = w_use(pab)
            wpa = view(offp, 4096, BF16, "p (k c) -> p k c", k=8)
            wpb = view(offp + 4096, 4096, BF16, "p (k c) -> p k c", k=8)
            for cl in range(2):
                m = i * 2 + cl
                for tg in range(3):
                    q = cl * 3 + tg
                    tl = tiles_of_tg(tg)
                    bka, lo, hi = fm_group(wpa, cl, 8, aT, lambda t, kc: aTres[t], tg, [rp], ALLB)
                    S.add("dve", lambda e, q=q, bka=bka: e.tensor_tensor(out=sga[q], in0=sga[q], in1=banks[bka][:, 0:384],
                                                                          op=ALU.mult),
                          reads=[sgares[q], bres[bka]], writes=[sgares[q]])
                    bkb, lo, hi = fm_group(wpb, cl, 8, attnT, lambda t, kc: attnres[t], tg, [rp], ALLB)
                    S.add("dve", lambda e, q=q, bkb=bkb: e.tensor_tensor(out=sgb[q], in0=sgb[q], in1=banks[bkb][:, 0:384],
                                                                          op=ALU.mult),
                          reads=[sgbres[q], bres[bkb]], writes=[sgbres[q]])
                    S.add("pool", lambda e, q=q, m=m, lo=lo, hi=hi: e.tensor_tensor(
                        out=mixT[:, m, lo:hi], in0=sga[q], in1=sgb[q], op=ALU.add),
                        reads=[sgares[q], sgbres[q]], writes=[mixres[t] for t in tl] + (R2old if gate_first["m"] else []))
                    gate_first["m"] = False
            w_done(pab)

        x1res = [Res("x1_%d" % i) for i in range(NT)]
        R1B_all = [r for l in hTres for r in l] + QTres + uTres + gvres + KTres + Vres + sptres + sgares + sgbres
        hT_all = [r for l in hTres for r in l]
        rest_all = QTres + uTres + gvres + KTres + Vres + sptres + sgares + sgbres
        for t in range(NT):
            guard = hT_all if t < 5 else (rest_all if t == 5 else [])
            S.add("sp", lambda e, t=t: e.dma_start(out=x1[:, t, :], in_=xc[t * 128:(t + 1) * 128, :]),
                  writes=[x1res[t]] + guard, dma_key="x1_%d" % (t % 3))
        hhres = [Res("hh%d" % i) for i in range(NT)]

        rms_state = {}

        def rms_a(t, n):
            b = n % 3
            ssr = Res("ss2")
            ss = st2[:, 2 * (n % 16):2 * (n % 16) + 1]
            rs = st2[:, 2 * (n % 16) + 1:2 * (n % 16) + 2]
            S.add("act", lambda e: e.activation(out=hb[b], in_=x1[:, t, :], func=AF.Square, accum_out=ss),
                  reads=[x1res[t]], writes=[hbres[b], ssr])
            rms_state[n] = (ss, rs, ssr)

        def rms_b1(n):
            ss, rs, ssr = rms_state[n]
            S.add("act", lambda e: e.activation(out=rs, in_=ss, func=AF.Sqrt, bias=eps_t[:, 0:1], scale=1.0 / D),
                  reads=[ssr, epsR], writes=[ssr])

        def rms_b(n):
            ss, rs, ssr = rms_state[n]
            S.add("dve", lambda e: e.reciprocal(out=rs, in_=rs), reads=[ssr], writes=[ssr])
            return rs, ssr

        def norm_stage1b(t, n):
            b = n % 3
            rs, ssr = rms_b(n)
            S.add("dve", lambda e: e.tensor_scalar_mul(out=hb[b], in0=x1[:, t, :], scalar1=rs),
                  reads=[ssr, x1res[t], hbres[b]], writes=[hbres[b]])

        def norm_stage2(t, n, gt, dstT, dres, tbanks):
            b = n % 3
            for half in range(2):
                bk = tbanks[(2 * n + half) % len(tbanks)]
                for j in range(8):
                    c = half * 8 + j
                    S.add("pe", lambda e, bk=bk, j=j, c=c: e.transpose(
                        out=bankbf(bk)[:, j * 128:(j + 1) * 128], in_=hb[b][:, c * 128:(c + 1) * 128], identity=ident),
                        reads=[hbres[b]] + CONST, writes=[bres[bk]])
                en_ = "act" if half == 0 else "dve"
                for j in range(8):
                    c = half * 8 + j
                    src = bankbf(bk)[:, j * 128:(j + 1) * 128]
                    dst = dstT[:, c, tcols(t)]
                    if en_ == "act":
                        S.add("act", lambda e, dst=dst, src=src, c=c: e.activation(
                            out=dst, in_=src, func=AF.Copy, scale=gt[:, c:c + 1]),
                            reads=[bres[bk]] + CONST, writes=[dres])
                    else:
                        S.add("dve", lambda e, dst=dst, src=src, c=c: e.tensor_scalar_mul(
                            out=dst, in0=src, scalar1=gt[:, c:c + 1]),
                            reads=[bres[bk]] + CONST, writes=[dres])

        wo_n = {"n": 0}

        def wo_group(cg, t, wt, rw):
            bk = wo_n["n"] % 4
            wo_n["n"] += 1
            for kc in range(16):
                S.add("pe", lambda e, kc=kc: e.matmul(
                    banks[bk][:, 0:256], lhsT=mixT[:, kc, tcols(t)], rhs=wt[:, kc, :], start=(kc == 0), stop=(kc == 15)),
                    reads=[mixres[t], rw], writes=[bres[bk]])
            S.add("dve", lambda e: e.tensor_tensor(
                out=x1[:, t, cg * 256:(cg + 1) * 256], in0=banks[bk][:, 0:256], in1=x1[:, t, cg * 256:(cg + 1) * 256],
                op=ALU.add), reads=[bres[bk], x1res[t]], writes=[x1res[t]])

        NCGO = 6
        for cg in range(NCGO):
            off, rw = w_use(B_WO[cg])
            wt = v_k16(off)
            for t in range(NT):
                wo_group(cg, t, wt, rw)
            w_done(B_WO[cg])
        tail_w = []
        for cg in range(NCGO, 8):
            off, rw = w_use(B_WO[cg])
            tail_w.append((cg, v_k16(off), rw))
        for t in range(NT + 1):
            if t < NT:
                for (cg, wt, rw) in tail_w:
                    wo_group(cg, t, wt, rw)
                rms_a(t, t)
                rms_b1(t)
            if 1 <= t <= NT:
                norm_stage1b(t - 1, t - 1)
            if 2 <= t:
                norm_stage2(t - 2, t - 2, g2t, hhT, hhres[t - 2], [4, 5, 6, 7])
        w_done(B_WO[7])

        def norm_drain():
            norm_stage2(NT - 1, NT - 1, g2t, hhT, hhres[NT - 1], [4, 5, 6, 7])

        actT = [view(r2 + i * 9216, 9216, BF16, "p (f t) -> p f t", f=4) for i in range(2)]
        actres = [[Res("act%d_%d" % (i, t)) for t in range(NT)] for i in range(2)]
        silt = [view(r2 + 18432 + i * 1536, 1536) for i in range(2)]
        silres = [Res("sil%d" % i) for i in range(2)]
        gfbv = view(r2 + 21504, 8192)
        gfres = Res("gfb")
        S.add("sp", lambda e: e.dma_start(out=gfbv, in_=gfB[:, :]), writes=[gfres] + mixres, dma_key="gf")
        ffn_n = {"n": 0}
        dn_n = {"n": 0}
        for gi in range(NG):
            gu, dn = B_FF[gi]
            ab = gi % 2
            for hf in range(2):
                offg, rg = w_use(gu[hf][0])
                offu, ru = w_use(gu[hf][1])
                wg_ = v_k16(offg)
                wu_ = v_k16(offu)
                cltg = [(cl, tg) for cl in range(2) for tg in range(3)]
                if gi == 0 and hf == 0:
                    cltg = [(0, 0), (0, 1), (1, 0), (1, 1), None, (0, 2), (1, 2)]
                for ct in cltg:
                    if ct is None:
                        norm_drain()
                        continue
                    cl, tg = ct
                    f = hf * 2 + cl
                    if True:
                        lo, hi = TG[tg]
                        n = ffn_n["n"]
                        ffn_n["n"] += 1
                        bg = (n % 2) * 2
                        bu = bg + 1
                        tl = tiles_of_tg(tg)
                        for (bk, wt, rr) in ((bg, wg_, rg), (bu, wu_, ru)):
                            for kc in range(16):
                                S.add("pe", lambda e, bk=bk, kc=kc, wt=wt, lo=lo, hi=hi, cl=cl: e.matmul(
                                    banks[bk][:, 0:384], lhsT=wt[:, kc, cl * 128:(cl + 1) * 128], rhs=hhT[:, kc, lo:hi],
                                    start=(kc == 0), stop=(kc == 15)),
                                    reads=[hhres[t] for t in tl] + [rr], writes=[bres[bk]])
                        sl = silt[n % 2]
                        old = mixres if n < 2 else []
                        S.add("act", lambda e, sl=sl, bg=bg: e.activation(out=sl, in_=banks[bg][:, 0:384], func=AF.Silu),
                              reads=[bres[bg]], writes=[silres[n % 2]] + old)
                        S.add("dve", lambda e, sl=sl, bu=bu, ab=ab, f=f, lo=lo, hi=hi: e.tensor_tensor(
                            out=actT[ab][:, f, lo:hi], in0=sl, in1=banks[bu][:, 0:384], op=ALU.mult),
                            reads=[silres[n % 2], bres[bu]], writes=[actres[ab][t] for t in tl] + old)
                w_done(gu[hf][1])
            last = (gi == NG - 1)
            dnw = []
            if last:
                for ch in range(2):
                    offd, rd_ = w_use(dn[ch])
                    dnw.append((view(offd, 8192, BF16, "p (f c) -> p f c", f=4), rd_))
            for ch_t in ([(ch, None) for ch in range(2)] if not last else [(ch, t) for t in range(NT) for ch in range(2)]):
                ch = ch_t[0]
                if not last:
                    offd, rd_ = w_use(dn[ch])
                    wd_ = view(offd, 8192, BF16, "p (f c) -> p f c", f=4)
                    tiles = range(NT)
                else:
                    wd_, rd_ = dnw[ch]
                    tiles = [ch_t[1]]
                for t in tiles:
                    for c2 in range(2):
                        bk = 4 + dn_n["n"] % 4
                        dn_n["n"] += 1
                        for f in range(4):
                            S.add("pe", lambda e, bk=bk, f=f, t=t, c2=c2, ab=ab, wd_=wd_: e.matmul(
                                banks[bk][:, 0:512], lhsT=actT[ab][:, f, tcols(t)], rhs=wd_[:, f, c2 * 512:(c2 + 1) * 512],
                                start=(f == 0), stop=(f == 3)), reads=[actres[ab][t], rd_], writes=[bres[bk]])
                        col = ch * 1024 + c2 * 512
                        S.add("dve", lambda e, bk=bk, t=t, col=col: e.tensor_tensor(
                            out=x1[:, t, col:col + 512], in0=banks[bk][:, 0:512], in1=x1[:, t, col:col + 512], op=ALU.add),
                            reads=[bres[bk], x1res[t]], writes=[x1res[t]])
                    if last and ch == 1:
                        rms_a(t, NT + t)
                        rms_b1(NT + t)
                    for tf in (([t - 1] if t >= 1 else []) + ([t] if t == NT - 1 else [])) if (last and ch == 1) else []:
                        rs, ssr = rms_b(NT + tf)
                        yA = Res("yA")
                        yB = Res("yB")
                        S.add("dve", lambda e, t=tf, rs=rs: e.scalar_tensor_tensor(
                            out=x1[:, t, 0:1024], in0=x1[:, t, 0:1024], scalar=rs, in1=gfbv[:, 0:1024],
                            op0=ALU.mult, op1=ALU.mult), reads=[ssr, x1res[tf], gfres], writes=[yA])
                        S.add("act", lambda e, t=tf, rs=rs: e.activation(out=x1[:, t, 1024:2048], in_=x1[:, t, 1024:2048],
                                                                         func=AF.Copy, scale=rs),
                              reads=[ssr, x1res[tf]], writes=[yB])
                        S.add("pool", lambda e, t=tf: e.tensor_tensor(out=x1[:, t, 1024:2048], in0=x1[:, t, 1024:2048],
                                                                      in1=gfbv[:, 1024:2048], op=ALU.mult),
                              reads=[yB, gfres], writes=[yB])
                        S.add("sp", lambda e, t=tf: e.dma_start(out=y_out[t * 128:(t + 1) * 128, :], in_=x1[:, t, :]),
                              reads=[yA, yB], writes=[x1res[tf]], dma_key="oy%d" % (tf % 3))
                if not last:
                    w_done(dn[ch])
            if last:
                w_done(dn[1])
        S.emit(nc, st)
    return nc


def _consts():
    i = np.arange(128)[:, None]
    j = np.arange(128)[None, :]
    prev = np.where(j >= i, 0.0, NEG).astype(np.float32)
    cur = np.where(j <= i, 0.0, NEG).astype(np.float32)
    maskN = np.concatenate([prev, cur], 1)
    mask0_first = np.concatenate([np.full((128, 128), NEG, np.float32), cur], 1)
    t_row = (np.arange(128) % 8)[:, None]
    maskSc = np.where(j >= t_row, 0.0, NEG).astype(np.float32)
    maskSn = np.full((128, 16, 128), NEG, np.float32)
    for s in range(16):
        for tp in range(8):
            maskSn[:, s, s * 8 + tp] = np.where(tp <= t_row[:, 0], 0.0, NEG)
    trilT = (i <= j).astype(np.float32)
    si, ti = np.arange(128)[:, None], np.arange(128)[None, :]
    blkm = ((si // 8 == ti // 8) & (si % 8 <= ti % 8)).astype(np.float32)
    ident = np.eye(128, dtype=np.float32)
    return dict(maskN=maskN, mask0_first=mask0_first, maskSc=maskSc, maskSn=maskSn, trilT=trilT, blkm=blkm, ident=ident)


_NC_CACHE = {}


def make_in_maps(x_prompt, x_sample, cache_k, cache_v, norm1_g, w_in, gmlp_norm_g, w_s, b_s, sinks,
                 w_pa, w_pb, w_o, norm2_g, w_ff_gate, w_ff_up, w_ff_down, final_g):
    f = lambda a: np.ascontiguousarray(np.asarray(a, dtype=np.float32))
    x_prompt, x_sample, cache_k, cache_v = f(x_prompt), f(x_sample), f(cache_k), f(cache_v)
    C = _consts()

    ws = f(w_s)[0]
    wsT = np.ascontiguousarray(ws.transpose(0, 2, 1))
    wsTr = np.ascontiguousarray(np.tile(wsT[:, 0:8, 0:8], (1, 16, 16)))
    bs = f(b_s)[0]
    bsB = np.ascontiguousarray(np.broadcast_to(bs[None], (128, 8, 128)))
    bsBs = np.ascontiguousarray(np.broadcast_to(np.tile(bs[:, 0:8], (1, 16))[None], (128, 8, 128)))
    sk = f(sinks)[0]
    shared = dict(
        w_in=f(w_in)[0], w_pa=f(w_pa)[0], w_pb=f(w_pb)[0], w_o=f(w_o)[0],
        w_g=f(w_ff_gate)[0], w_u=f(w_ff_up)[0], w_d=f(w_ff_down)[0],
        g1T=np.ascontiguousarray(f(norm1_g)[0].reshape(16, 128).T),
        g2T=np.ascontiguousarray(f(norm2_g)[0].reshape(16, 128).T),
        gfB=np.ascontiguousarray(np.broadcast_to(f(final_g)[None], (128, D))),
        gnB=np.ascontiguousarray(np.broadcast_to(f(gmlp_norm_g)[0][None], (128, 1024))),
        wsT=wsT, wsTr=wsTr, trilT=C["trilT"], blkm=C["blkm"], bsB=bsB, bsBs=bsBs,
        sinksB=np.ascontiguousarray(np.broadcast_to(sk[None], (128, 16))),
        sinkrep=np.ascontiguousarray(np.repeat(sk, 8)[:, None]),
        sinksB2=np.ascontiguousarray(np.broadcast_to(np.repeat(sk, 2)[None], (128, 32))),
        sinkrep2=np.ascontiguousarray(np.repeat(np.repeat(sk, 8)[:, None], 2, axis=1)),
        maskN=C["maskN"], maskSc=C["maskSc"], maskSn=C["maskSn"], ident=C["ident"],
    )
    in_maps = []
    for c in range(NCORES):
        b, half = c // 2, c % 2
        xs = x_sample[c * 16:(c + 1) * 16].reshape(128, D)
        xp = x_prompt[b, half * 1024:(half + 1) * 1024]
        if half == 1:
            prev = x_prompt[b, 896:1024]
            mask0 = C["maskN"]
        else:
            prev = np.zeros((128, D), np.float32)
            mask0 = C["mask0_first"]
        xcat = np.ascontiguousarray(np.concatenate([xp, xs, prev], 0))
        ckc = cache_k[0, c * 16:(c + 1) * 16].reshape(16, 128, 256)
        cvc = cache_v[0, c * 16:(c + 1) * 16].reshape(16, 128, 256)
        ckT = np.ascontiguousarray(ckc.reshape(16, 128, 2, 128).transpose(2, 3, 0, 1))
        m = dict(shared)
        m.update(xc=xcat, ckT=ckT, ck=np.ascontiguousarray(ckc), cv=np.ascontiguousarray(cvc), mask0=mask0)
        in_maps.append(m)
    return in_maps


def kernel(**inputs):
    in_maps = make_in_maps(**inputs)
    if "nc" not in _NC_CACHE:
        _NC_CACHE["nc"] = build_program()
    nc = _NC_CACHE["nc"]
    res = run_bass_kernel_spmd(nc, in_maps, core_ids=list(range(NCORES)))
    return assemble(res.results)


def assemble(R):
    y_prompt = np.empty((4, 2048, D), np.float32)
    y_sample = np.empty((128, 8, D), np.float32)
    pk = np.empty((1, 4, 128, 4, 64), np.float32)
    pv = np.empty((1, 4, 128, 4, 64), np.float32)
    skw = np.empty((1, 128, 128, 4, 64), np.float32)
    svw = np.empty((1, 128, 128, 4, 64), np.float32)
    sg = np.empty((1, 128, 8, 8, 128), np.float32)
    for c in range(NCORES):
        b, half = c // 2, c % 2
        y = R[c]["y"]
        y_prompt[b, half * 1024:(half + 1) * 1024] = y[0:1024]
        y_sample[c * 16:(c + 1) * 16] = y[1024:1152].reshape(16, 8, D)
        if half == 1:
            pk[0, b] = R[c]["kv7"][:, 0:256].reshape(128, 4, 64)
            pv[0, b] = R[c]["kv7"][:, 256:512].reshape(128, 4, 64)
        skw[0, c * 16:(c + 1) * 16] = R[c]["skw"].reshape(16, 128, 4, 64)
        svw[0, c * 16:(c + 1) * 16] = R[c]["svw"].reshape(16, 128, 4, 64)
        sg[0, c * 16:(c + 1) * 16] = R[c]["sg"].reshape(16, 8, 8, 128)
    return (y_prompt, y_sample, pk, pv, skw, svw, sg)
```

```python
import contextlib
import numpy as np
import concourse.bass as bass
import concourse.mybir as mybir
from concourse.bass_utils import run_bass_kernel_spmd

F32 = mybir.dt.float32
BF16 = mybir.dt.bfloat16
AF = mybir.ActivationFunctionType
ALU = mybir.AluOpType
AX = mybir.AxisListType

ENGS = ("pe", "act", "dve", "pool", "sp")

D = 2048
DFF = 5632
INW = 7680
NT = 9
TOK = NT * 128
TOKP = TOK + 128
EPS = 1e-6
NEG = -1e30
NCORES = 8


class Res:
    __slots__ = ("name", "last_w", "readers", "excl")

    def __init__(self, name, excl=False):
        self.name = name
        self.last_w = None
        self.readers = []
        self.excl = excl


class Op:
    __slots__ = ("eng", "fn", "deps", "signal", "is_dma", "key", "val", "idx")

    def __init__(self, eng, fn, is_dma, key):
        self.eng = eng
        self.fn = fn
        self.deps = []
        self.signal = False
        self.is_dma = is_dma
        self.key = key
        self.val = None
        self.idx = None


class Sched:
    def __init__(self):
        self.q = {e: [] for e in ENGS}
        self.dma_count = {}
        self.stopped = False

    def add(self, eng, fn, reads=(), writes=(), dma_key=None):
        if self.stopped:
            return None
        is_dma = dma_key is not None
        op = Op(eng, fn, is_dma, dma_key)
        deps = {}
        for r in reads:
            if r.last_w is not None:
                deps[id(r.last_w)] = r.last_w
            if r.excl:
                for rd in r.readers:
                    if rd.is_dma or rd.eng != eng:
                        deps[id(rd)] = rd
        for w in writes:
            if w.last_w is not None:
                deps[id(w.last_w)] = w.last_w
            for rd in w.readers:
                deps[id(rd)] = rd
        for d in deps.values():
            if d is op:
                continue
            if (not d.is_dma) and (not is_dma) and d.eng == "pe" and eng == "pe":
                continue
            op.deps.append(d)
            d.signal = True
        for r in reads:
            if not is_dma:
                r.readers = [x for x in r.readers if x.is_dma or x.eng != eng]
            r.readers.append(op)
        for w in writes:
            w.last_w = op
            w.readers = []
        if is_dma:
            c = self.dma_count.get(dma_key, 0) + 1
            self.dma_count[dma_key] = c
            op.val = 16 * c
        op.idx = len(self.q[eng])
        self.q[eng].append(op)
        return op

    def emit(self, nc, stack):
        for e in ENGS:
            c = 0
            for op in self.q[e]:
                if op.is_dma:
                    continue
                if op.signal:
                    c += 1
                    op.val = c
        sems = {}

        def sem_of(key):
            if key not in sems:
                sems[key] = stack.enter_context(nc.semaphore("s%d" % len(sems)))
            return sems[key]

        for e in ("pe", "act", "dve", "pool"):
            sem_of(("eng", e))
        for k in self.dma_count:
            sem_of(("dma", k))

        def tok(d):
            if d.is_dma:
                return ("dma", d.key), d.val
            return ("eng", d.eng), d.val

        block = stack.enter_context(nc.Block())
        engobj = {"pe": block.tensor, "act": block.scalar, "dve": block.vector,
                  "pool": block.gpsimd, "sp": block.sync}

        def run_queue(e, eng):
            waited = {}
            for op in self.q[e]:
                need = {}
                for d in op.deps:
                    k, v = tok(d)
                    if waited.get(k, 0) >= v:
                        continue
                    if need.get(k, 0) < v:
                        need[k] = v
                for k, v in need.items():
                    eng.wait_ge(sem_of(k), v)
                    waited[k] = v
                inst = op.fn(eng)
                if op.is_dma:
                    inst.then_inc(sem_of(("dma", op.key)), 16)
                elif op.signal:
                    inst.then_inc(sem_of(("eng", e)), 1)
            if e == "sp":
                for k, c in self.dma_count.items():
                    eng.wait_ge(sem_of(("dma", k)), 16 * c)

        for e in ENGS:
            def body(eng, e=e):
                run_queue(e, eng)
            engobj[e](body)


def _drain(gen):
    for _ in gen:
        pass


def _interleave(main, filler, k=1):
    fdone = False
    for _ in main:
        for _ in range(k):
            if not fdone:
                try:
                    next(filler)
                except StopIteration:
                    fdone = True
    if not fdone:
        _drain(filler)


def build_program(stop=None):
    nc = bass.Bass("TRN2", target_bir_lowering=False)

    def din(name, shape):
        return nc.dram_tensor(name, list(shape), F32, kind="ExternalInput").ap()

    def dout(name, shape):
        return nc.dram_tensor(name, list(shape), F32, kind="ExternalOutput").ap()

    xc = din("xc", [TOKP, D])
    ckT = din("ckT", [2, 128, 16, 128])
    ck = din("ck", [16, 128, 256])
    cv = din("cv", [16, 128, 256])
    w_in = din("w_in", [D, INW])
    w_pa = din("w_pa", [1024, D])
    w_pb = din("w_pb", [1024, D])
    w_o = din("w_o", [D, D])
    w_g = din("w_g", [D, DFF])
    w_u = din("w_u", [D, DFF])
    w_d = din("w_d", [DFF, D])
    g1T = din("g1T", [128, 16])
    g2T = din("g2T", [128, 16])
    gfB = din("gfB", [128, D])
    gnB = din("gnB", [128, 1024])
    wsT = din("wsT", [8, 128, 128])
    wsTr = din("wsTr", [8, 128, 128])
    trilT = din("trilT", [128, 128])
    blkm = din("blkm", [128, 128])
    bsB = din("bsB", [128, 8, 128])
    bsBs = din("bsBs", [128, 8, 128])
    sinksB = din("sinksB", [128, 16])
    sinkrep = din("sinkrep", [128, 1])
    maskN = din("maskN", [128, 256])
    mask0 = din("mask0", [128, 256])
    maskSc = din("maskSc", [128, 128])
    maskSn = din("maskSn", [128, 16, 128])
    identd = din("ident", [128, 128])
    sinksB2 = din("sinksB2", [128, 32])
    sinkrep2 = din("sinkrep2", [128, 2])

    y_out = dout("y", [TOK, D])
    kv7_out = dout("kv7", [128, 512])
    skw_out = dout("skw", [16, 128, 256])
    svw_out = dout("svw", [16, 128, 256])
    sg_out = dout("sg", [128, 1024])

    S = Sched()

    with contextlib.ExitStack() as st:
        R1 = 112640
        R2 = 36864
        NSLOT = 4
        WS = 8192
        SS = 12288
        SMALL = 7168
        TOTAL = R1 + R2 + NSLOT * WS + SS + SMALL
        arena = st.enter_context(nc.sbuf_tensor("arena", [128, TOTAL // 4], F32))

        def view(off, nbytes, dt=F32, pat=None, **kw):
            assert off % 4 == 0 and nbytes % 4 == 0
            ap = arena[:, off // 4:(off + nbytes) // 4]
            if dt is not F32:
                ap = ap.bitcast(dt)
            if pat is not None:
                ap = ap.rearrange(pat, **kw)
            return ap

        hT = view(0, 40960, BF16, "p (c t) -> p c t", c=16)
        QT = view(40960, 18432, BF16, "p (c t) -> p c t", c=8)
        uT = view(59392, 18432, BF16, "p (c t) -> p c t", c=8)
        VLN_OFF = 77824
        KTd = view(96256, 10240, BF16, "p (h t) -> p h t", h=4)
        Vb = view(106496, 5120, BF16, "p (b c) -> p b c", b=10)
        gvf = view(59392, 36864, F32, "p (t c) -> p t c", t=NT)
        x1 = view(0, 73728, F32, "p (t c) -> p t c", t=NT)
        hhT = view(73728, 36864, BF16, "p (c t) -> p c t", c=16)
        r2 = R1
        mixT = view(r2, 36864, BF16, "p (c t) -> p c t", c=16)
        w0 = R1 + R2
        wslot = [w0 + i * WS for i in range(NSLOT)]
        wres = [Res("w%d" % i) for i in range(NSLOT)]
        s0 = w0 + NSLOT * WS
        hb = [view(s0 + i * 4096, 4096, BF16) for i in range(3)]
        hbres = [Res("hb%d" % i) for i in range(3)]
        m0 = s0 + SS
        _sm = [m0]

        def small(nbytes, dt=F32, pat=None, **kw):
            off = _sm[0]
            _sm[0] += (nbytes + 31) // 32 * 32
            assert _sm[0] <= m0 + SMALL, "small region overflow"
            return view(off, nbytes, dt, pat, **kw)

        ident = small(256, BF16)
        maskNb = small(512, BF16)
        mask0b = small(512, BF16)
        g1t = small(64)
        g2t = small(64)
        sinks_t = small(64)
        sinkrep_t = small(4)
        sinks_b2 = small(64, BF16)
        sinkrep_b2 = small(4, BF16)
        statA = small(4 * 64)
        st_att = small(4 * 6 * 2 * 16, F32, "p (q b h) -> p q b h", q=6, b=2)
        bnst = small(4 * 12 * 2, F32, "p (b k) -> p b k", b=2)
        bnmv = small(4 * 4 * 2, F32, "p (b k) -> p b k", b=2)
        st2 = small(4 * 64)
        SMALL_EXTRA = _sm[0]
        _sm[0] += 160 + 448
        assert _sm[0] <= m0 + SMALL

        banks = [st.enter_context(nc.psum_tensor("bank%d" % i, [128, 512], F32)) for i in range(8)]
        bres = [Res("bank%d" % i, excl=True) for i in range(8)]

        def bankbf(i):
            return banks[i][:].bitcast(BF16)

        cnt = {"ev": 0}

        def tcols(i):
            return slice(i * 128, (i + 1) * 128)

        def kblk(i):
            return 0 if i == 9 else i + 1

        def ev_engine():
            cnt["ev"] += 1
            return "act" if cnt["ev"] % 2 else "dve"

        def copy_op(eng_name, out, in_, reads, writes, scale=None):
            if eng_name == "act":
                if scale is None:
                    S.add("act", lambda e: e.copy(out=out, in_=in_), reads=reads, writes=writes)
                else:
                    S.add("act", lambda e: e.mul(out=out, in_=in_, mul=scale), reads=reads, writes=writes)
            else:
                if scale is None:
                    S.add(eng_name, lambda e: e.tensor_copy(out=out, in_=in_), reads=reads, writes=writes)
                else:
                    S.add(eng_name, lambda e: e.tensor_scalar_mul(out=out, in0=in_, scalar1=scale),
                          reads=reads, writes=writes)

        wq = []
        wstate = {"issued": 0, "consumed": 0}

        def wblock(parts):
            wq.append(parts)
            return len(wq) - 1

        def w_issue_upto(n):
            while wstate["issued"] < min(n, len(wq)):
                b = wstate["issued"]
                s = b % NSLOT
                for (vf, src) in wq[b]:
                    dst = vf(wslot[s])
                    S.add("pool", lambda e, dst=dst, src=src: e.dma_start(out=dst, in_=src),
                          writes=[wres[s]], dma_key="w%d" % s)
                wstate["issued"] += 1

        def w_use(b):
            assert b < wstate["consumed"] + NSLOT, (b, wstate)
            w_issue_upto(wstate["consumed"] + NSLOT)
            return wslot[b % NSLOT], wres[b % NSLOT]

        def w_done(b):
            wstate["consumed"] = b + 1
            w_issue_upto(wstate["consumed"] + NSLOT)

        def v_k16(off):
            return view(off, 8192, BF16, "p (k c) -> p k c", k=16)

        def blk_k16(w, c0):
            return wblock([(v_k16, w[:, c0:c0 + 256].rearrange("(k p) c -> p k c", p=128))])

        B_K = blk_k16(w_in, 1024)
        B_V = blk_k16(w_in, 1280)
        B_Q = [blk_k16(w_in, 0 + 256 * i) for i in range(4)]
        B_GV = [blk_k16(w_in, 2560 + 256 * i) for i in range(4)]
        B_U = [blk_k16(w_in, 1536 + 256 * i) for i in range(4)]
        B_GATE = []
        for i in range(8):
            ga = blk_k16(w_in, 3584 + 256 * i)
            gb = blk_k16(w_in, 5632 + 256 * i)
            pab = wblock([
                (lambda off: view(off, 4096, BF16, "p (k c) -> p k c", k=8),
                 w_pa[:, 256 * i:256 * i + 256].rearrange("(k p) c -> p k c", p=128)),
                (lambda off: view(off + 4096, 4096, BF16, "p (k c) -> p k c", k=8),
                 w_pb[:, 256 * i:256 * i + 256].rearrange("(k p) c -> p k c", p=128)),
            ])
            B_GATE.append((ga, gb, pab))
        B_WO = [blk_k16(w_o, 256 * i) for i in range(8)]
        NG = 11
        B_FF = []
        for gi in range(NG):
            f0 = gi * 4
            gu = []
            for hf in range(2):
                c0 = (f0 + 2 * hf) * 128
                gu.append((blk_k16(w_g, c0), blk_k16(w_u, c0)))
            dn = []
            for ch in range(2):
                dn.append(wblock([(lambda off: view(off, 8192, BF16, "p (f c) -> p f c", f=4),
                                   w_d[f0 * 128:(f0 + 4) * 128, ch * 1024:(ch + 1) * 1024]
                                   .rearrange("(f p) c -> p f c", p=128))]))
            B_FF.append((gu, dn))

        def cload(dst, src, cast=False):
            if cast:
                S.add("pool", lambda e: e.dma_start(out=dst, in_=src), dma_key="constc")
            else:
                S.add("sp", lambda e: e.dma_start(out=dst, in_=src), dma_key="const")

        cload(ident, identd[:, :], cast=True)
        cload(maskNb, maskN[:, :], cast=True)
        cload(mask0b, mask0[:, :], cast=True)
        cload(g1t, g1T[:, :])
        cload(g2t, g2T[:, :])
        cload(sinks_t, sinksB[:, :])
        cload(sinkrep_t, sinkrep[:, :])
        cload(sinks_b2, sinksB2[:, :], cast=True)
        cload(sinkrep_b2, sinkrep2[:, :], cast=True)
        constA = Res("constA")
        constB = Res("constB")
        constA.last_w = [op for op in S.q["pool"] if op.key == "constc"][-1]
        constB.last_w = [op for op in S.q["sp"] if op.key == "const"][-1]
        CONST = [constA, constB]
        eps_t = small(4)
        epsR = Res("eps")
        S.add("pool", lambda e: e.memset(eps_t, EPS), writes=[epsR])

        xtv = [view(r2 + i * 8192, 8192) for i in range(2)]
        xtres = [Res("xt%d" % i) for i in range(2)]
        hTres = [[Res("hT%d_%d" % (i, c)) for c in range(16)] for i in range(10)]
        order = [9] + list(range(9))
        KVB = r2 + 16384
        kd = [view(KVB + i * 1024, 1024, BF16, "p (h u d) -> p h u d", h=4, u=2) for i in range(2)]
        kdres = [Res("kd%d" % i) for i in range(2)]
        kvf = [view(KVB + 2048 + i * 2048, 2048) for i in range(2)]
        kvfres = [Res("kvf%d" % i) for i in range(2)]
        Kb8 = view(KVB + 6144, 512, BF16)
        KT8 = view(KVB + 6656, 512, BF16, "p (c t) -> p c t", c=2)
        Kb8res = Res("Kb8")
        KT8res = Res("KT8")
        KTres = [Res("KT%d" % i) for i in range(10)]
        Vres = [Res("V%d" % i) for i in range(10)]

        def stageA1(n, i):
            b = n % 2
            r0 = TOK if i == 9 else i * 128
            S.add("sp", lambda e: e.dma_start(out=xtv[b], in_=xc[r0:r0 + 128, :]),
                  writes=[xtres[b]], dma_key="xt%d" % b)
            ssr = Res("ss")
            ss = statA[:, 2 * n:2 * n + 1]
            rs = statA[:, 2 * n + 1:2 * n + 2]
            S.add("act", lambda e: e.activation(out=hb[b], in_=xtv[b], func=AF.Square, accum_out=ss),
                  reads=[xtres[b]], writes=[hbres[b], ssr])
            S.add("act", lambda e: e.activation(out=rs, in_=ss, func=AF.Sqrt, bias=eps_t[:, 0:1], scale=1.0 / D),
                  reads=[ssr, epsR], writes=[ssr])
            S.add("dve", lambda e: e.reciprocal(out=rs, in_=rs), reads=[ssr], writes=[ssr])
            S.add("dve", lambda e: e.tensor_scalar_mul(out=hb[b], in0=xtv[b], scalar1=rs),
                  reads=[ssr, xtres[b], hbres[b]], writes=[hbres[b]])

        def stageA2(n, i):
            b = n % 2
            for half in range(2):
                bk = 4 + (2 * n + half) % 4
                for j in range(8):
                    c = half * 8 + j
                    S.add("pe", lambda e, j=j, c=c, bk=bk: e.transpose(
                        out=bankbf(bk)[:, j * 128:(j + 1) * 128], in_=hb[b][:, c * 128:(c + 1) * 128],
                        identity=ident), reads=[hbres[b]] + CONST, writes=[bres[bk]])
                en = "act" if half == 0 else "dve"
                for j in range(8):
                    c = half * 8 + j
                    src = bankbf(bk)[:, j * 128:(j + 1) * 128]
                    dst = hT[:, c, tcols(i)]
                    if en == "act":
                        S.add("act", lambda e, dst=dst, src=src, c=c: e.activation(
                            out=dst, in_=src, func=AF.Copy, scale=g1t[:, c:c + 1]),
                            reads=[bres[bk]] + CONST, writes=[hTres[i][c]])
                    else:
                        S.add("dve", lambda e, dst=dst, src=src, c=c: e.tensor_scalar_mul(
                            out=dst, in0=src, scalar1=g1t[:, c:c + 1]),
                            reads=[bres[bk]] + CONST, writes=[hTres[i][c]])

        offK, rK = w_use(B_K)
        offV, rV = w_use(B_V)
        wk = v_k16(offK)
        wv = v_k16(offV)
        sa = 59392
        Qz = view(sa, 4096, BF16, "p (h c) -> p h c", h=16)
        QTz = view(sa + 4096, 8192, BF16, "p (c s h t) -> p c s h t", c=2, s=16, h=16)
        ckTb = view(sa + 12288, 8192, BF16, "p (c s k) -> p c s k", c=2, s=16)
        cvb = view(sa + 20480, 8192, BF16, "p (s c) -> p s c", s=16)
        mSn = view(sa + 28672, 4096, BF16, "p (s k) -> p s k", s=16)
        mSc = view(sa + 32768, 256, BF16)
        Osm1 = view(sa + 33024, 2048, BF16, "p (s d) -> p s d", s=16)
        SAres = Res("sample_attn_bufs")
        Qzres = Res("Qz")
        QTzres = Res("QTz")
        S.add("pool", lambda e: e.memset(view(sa, 4096, BF16), 0.0), writes=[Qzres])
        S.add("pool", lambda e: e.memset(view(sa + 4096, 8192, BF16), 0.0), writes=[QTzres])
        S.add("pool", lambda e: e.dma_start(out=ckTb, in_=ckT.rearrange("c p s k -> p c s k")),
              writes=[SAres], dma_key="sa")
        S.add("pool", lambda e: e.dma_start(out=cvb, in_=cv.rearrange("s k c -> k s c")),
              writes=[SAres], dma_key="sa")
        S.add("pool", lambda e: e.dma_start(out=mSn, in_=maskSn[:, :, :]), writes=[SAres], dma_key="sa")
        S.add("pool", lambda e: e.dma_start(out=mSc, in_=maskSc[:, :]), writes=[SAres], dma_key="sa")


        def stageKV(n, i):
            bk = n % 2
            for kc in range(16):
                S.add("pe", lambda e, kc=kc: e.matmul(
                    banks[bk][:, 0:256], lhsT=hT[:, kc, tcols(i)], rhs=wk[:, kc, :], start=(kc == 0), stop=(kc == 15)),
                    reads=[hTres[i][kc], rK], writes=[bres[bk]])
            for kc in range(16):
                S.add("pe", lambda e, kc=kc: e.matmul(
                    banks[bk][:, 256:512], lhsT=hT[:, kc, tcols(i)], rhs=wv[:, kc, :], start=(kc == 0), stop=(kc == 15)),
                    reads=[hTres[i][kc], rV], writes=[bres[bk]])
            kdb = kd[n % 2]
            kdr = kdres[n % 2]
            kin = banks[bk][:, 0:256].rearrange("p (h d) -> p h d", h=4)
            S.add("dve", lambda e: e.tensor_copy(out=kdb[:, :, 0, :], in_=kin), reads=[bres[bk]], writes=[kdr])
            S.add("dve", lambda e: e.tensor_copy(out=kdb[:, :, 1, :], in_=kin), reads=[bres[bk], kdr], writes=[kdr])
            vdst = Vb[:, kblk(i), :]
            S.add("dve", lambda e: e.tensor_copy(out=vdst, in_=banks[bk][:, 256:512]), reads=[bres[bk]], writes=[Vres[i]])
            if i in (7, 8):
                kvb = kvf[i - 7]
                kvr = kvfres[i - 7]
                S.add("dve", lambda e: e.tensor_copy(out=kvb, in_=banks[bk][:, :]), reads=[bres[bk]], writes=[kvr])
                if i == 7:
                    S.add("sp", lambda e: e.dma_start(out=kv7_out[:, :], in_=kvb), reads=[kvr], dma_key="okv7")
                else:
                    for s in range(16):
                        S.add("sp", lambda e, s=s: e.dma_start(
                            out=skw_out[s, 120:128, :], in_=kvb[s * 8:(s + 1) * 8, 0:256]), reads=[kvr], dma_key="oskw")
                        S.add("sp", lambda e, s=s: e.dma_start(
                            out=svw_out[s, 120:128, :], in_=kvb[s * 8:(s + 1) * 8, 256:512]), reads=[kvr], dma_key="osvw")
                    S.add("dve", lambda e: e.tensor_copy(out=Kb8, in_=banks[bk][:, 0:256]), reads=[bres[bk]], writes=[Kb8res])
            tb = 2 + n % 2
            for h in range(4):
                S.add("pe", lambda e, h=h: e.transpose(
                    out=bankbf(tb)[:, h * 128:(h + 1) * 128],
                    in_=kdb[:, h, :, :].rearrange("p u d -> p (u d)"), identity=ident),
                    reads=[kdr] + CONST, writes=[bres[tb]])
            kdst = KTd[:, :, kblk(i) * 128:(kblk(i) + 1) * 128]
            S.add("act", lambda e: e.copy(out=kdst, in_=bankbf(tb)[:, 0:512].rearrange("p (h t) -> p h t", h=4)),
                  reads=[bres[tb]], writes=[KTres[i]])
            if i == 8:
                for c in range(2):
                    S.add("pe", lambda e, c=c: e.transpose(
                        out=bankbf(tb)[:, 512 + c * 128:512 + (c + 1) * 128], in_=Kb8[:, c * 128:(c + 1) * 128],
                        identity=ident), reads=[Kb8res] + CONST, writes=[bres[tb]])
                S.add("act", lambda e: e.copy(
                    out=KT8, in_=bankbf(tb)[:, 512:768].rearrange("p (c t) -> p c t", c=2)),
                    reads=[bres[tb]], writes=[KT8res])

        stageA1(0, order[0])
        for n, i in enumerate(order):
            if n + 1 < len(order):
                stageA1(n + 1, order[n + 1])
            stageA2(n, i)
            if n >= 1:
                stageKV(n - 1, order[n - 1])
        stageKV(len(order) - 1, order[-1])
        w_done(B_V)
        S.add("sp", lambda e: e.dma_start(out=skw_out[:, 0:120, :], in_=ck[:, 8:128, :]), dma_key="ockw")
        S.add("sp", lambda e: e.dma_start(out=svw_out[:, 0:120, :], in_=cv[:, 8:128, :]), dma_key="ocvw")

        TG = [(0, 384), (384, 768), (768, 1152)]
        QTres = [Res("QT%d" % i) for i in range(NT)]

        def tiles_of_tg(tg):
            return [tg * 3, tg * 3 + 1, tg * 3 + 2]

        fm_bank = {"n": 0}

        def fm_group(wt, cl, K, src, srcres_fn, tg, extra_reads, banklist):
            bk = banklist[fm_bank["n"] % len(banklist)]
            fm_bank["n"] += 1
            lo, hi = TG[tg]
            for kc in range(K):
                S.add("pe", lambda e, kc=kc: e.matmul(
                    banks[bk][:, 0:384], lhsT=wt[:, kc, cl * 128:(cl + 1) * 128], rhs=src[:, kc, lo:hi],
                    start=(kc == 0), stop=(kc == K - 1)),
                    reads=[srcres_fn(t, kc) for t in tiles_of_tg(tg)] + extra_reads, writes=[bres[bk]])
            return bk, lo, hi

        hTr = lambda t, kc: hTres[t][kc]
        q_w = []
        for wb in range(4):
            off, rw = w_use(B_Q[wb])
            q_w.append((v_k16(off), rw))

        def q_group(wb, cl, tg, banklist):
            wt, rw = q_w[wb]
            m = wb * 2 + cl
            bk, lo, hi = fm_group(wt, cl, 16, hT, hTr, tg, [rw], banklist)
            copy_op(ev_engine(), QT[:, m, lo:hi], banks[bk][:, 0:384], [bres[bk]],
                    [QTres[t] for t in tiles_of_tg(tg)], scale=0.125)

        def gen_q_rest():
            for tg in (1, 2):
                for wb in range(4):
                    for cl in range(2):
                        q_group(wb, cl, tg, [7])
                        if tg == 2 and cl == 1:
                            w_done(B_Q[wb])
                        yield

        for wb in range(4):
            wt, rw = q_w[wb]
            for cl in range(2):
                q_group(wb, cl, 0, [0, 1, 2, 3, 4, 5])
            for kc in range(16):
                S.add("pe", lambda e, kc=kc, wt=wt: e.matmul(
                    banks[6][:, 0:256], lhsT=hT[:, kc, tcols(8)], rhs=wt[:, kc, :], start=(kc == 0), stop=(kc == 15)),
                    reads=[hTres[8][kc], rw], writes=[bres[6]])
            eh = wb % 2
            S.add("dve", lambda e, wb=wb, eh=eh: e.tensor_scalar_mul(
                out=Qz[:, 4 * wb:4 * wb + 4, eh * 64:(eh + 1) * 64],
                in0=banks[6][:, 0:256].rearrange("p (h d) -> p h d", h=4), scalar1=0.125),
                reads=[bres[6], Qzres], writes=[Qzres])
            for hh in range(4 * wb, 4 * wb + 4):
                S.add("pe", lambda e, hh=hh: e.transpose(out=bankbf(7)[:, (hh % 4) * 128:(hh % 4 + 1) * 128],
                                                         in_=Qz[:, hh, :], identity=ident),
                      reads=[Qzres] + CONST, writes=[bres[7]])
            for hh in range(4 * wb, 4 * wb + 4):
                copy_op("act" if hh % 2 else "dve", QTz[:, wb // 2, :, hh, :],
                        bankbf(7)[:, (hh % 4) * 128:(hh % 4 + 1) * 128].rearrange("p (s t) -> p s t", s=16),
                        [bres[7], QTzres], [QTzres])

        ATB = r2 + 24576
        Pb = [view(ATB + i * 512, 512, BF16) for i in range(2)]
        PTb = [view(ATB + 1024 + i * 512, 512, BF16) for i in range(2)]
        atm = [view(ATB + 2048 + i * 2048, 2048, BF16) for i in range(2)]
        Osm = view(ATB + 6144, 4096, BF16, "p (s u d) -> p s u d", s=16, u=2)
        Pres = [Res("P%d" % i) for i in range(2)]
        PTres = [Res("PT%d" % i) for i in range(2)]
        atmres = [Res("atm%d" % i) for i in range(2)]
        stres = {}

        def sres(q, b, h):
            k = (q, b, h)
            if k not in stres:
                stres[k] = Res("st%s" % (k,))
            return stres[k]

        att_n = {"n": 0}

        def softmax_head(sbank, b2, h, sink_ap, per_head_stats):
            n = att_n["n"]
            att_n["n"] += 1
            pb = n % 2
            mx = st_att[:, 0, b2, h:h + 1]
            ngm = st_att[:, 1, b2, h:h + 1]
            rsum = st_att[:, 2, b2, h:h + 1]
            R = [sres(q, b2, h) for q in range(6)]
            S.add("dve", lambda e: e.reduce_max(out=ngm, in_=banks[sbank][:, 0:258], axis=AX.X, negate=True),
                  reads=[bres[sbank]], writes=[R[1]])
            S.add("act", lambda e: e.activation(out=Pb[pb], in_=banks[sbank][:, 0:256], func=AF.Exp, bias=ngm,
                                                scale=1.0, accum_out=rsum),
                  reads=[bres[sbank], R[1]], writes=[Pres[pb], R[2]])
            if per_head_stats:
                es = st_att[:, 3, b2, h:h + 1]
                den = st_att[:, 4, b2, h:h + 1]
                rden = st_att[:, 5, b2, h:h + 1]
                S.add("act", lambda e: e.activation(out=es, in_=ngm, func=AF.Exp, bias=sink_ap, scale=1.0),
                      reads=[R[1]] + CONST, writes=[R[3]])
                S.add("dve", lambda e: e.tensor_tensor(out=den, in0=rsum, in1=es, op=ALU.add),
                      reads=[R[2], R[3]], writes=[R[4]])
                S.add("dve", lambda e: e.reciprocal(out=rden, in_=den), reads=[R[4]], writes=[R[5]])
            return pb

        def transpose_probs(pb):
            tb = 2 + pb
            for c in range(2):
                S.add("pe", lambda e, c=c: e.transpose(out=bankbf(tb)[:, c * 128:(c + 1) * 128],
                                                       in_=Pb[pb][:, c * 128:(c + 1) * 128], identity=ident),
                      reads=[Pres[pb]] + CONST, writes=[bres[tb]])
            copy_op("act", PTb[pb], bankbf(tb)[:, 0:256], [bres[tb]], [PTres[pb]])

        Osres = Res("Osm1")

        def sample_scores(s):
            sbank = s % 2
            S.add("pe", lambda e: e.matmul(banks[sbank][:, 256:258], lhsT=ident, rhs=sinkrep_b2,
                                           start=True, stop=True), reads=CONST, writes=[bres[sbank]])
            for c in range(2):
                S.add("pe", lambda e, c=c: e.matmul(
                    banks[sbank][:, 0:128], lhsT=QTz[:, c, s, :, :].rearrange("p h t -> p (h t)"), rhs=ckTb[:, c, s, :],
                    start=(c == 0), stop=False), reads=[QTzres, SAres], writes=[bres[sbank]])
            S.add("pe", lambda e: e.matmul(banks[sbank][:, 0:128], lhsT=ident, rhs=mSc, start=False, stop=True),
                  reads=CONST + [SAres], writes=[bres[sbank]])
            for c in range(2):
                S.add("pe", lambda e, c=c: e.matmul(
                    banks[sbank][:, 128:256], lhsT=QTz[:, c, s, :, :].rearrange("p h t -> p (h t)"), rhs=KT8[:, c, :],
                    start=(c == 0), stop=False), reads=[QTzres, KT8res], writes=[bres[sbank]])
            S.add("pe", lambda e: e.matmul(banks[sbank][:, 128:256], lhsT=ident, rhs=mSn[:, s, :], start=False, stop=True),
                  reads=CONST + [SAres], writes=[bres[sbank]])

        spb = {}

        def sample_E1(s):
            spb[s] = softmax_head(s % 2, s % 2, s, sinkrep_t[:, 0:1], True)

        def sample_V(s):
            b2 = s % 2
            pb = spb[s]
            ob = 4 + s % 2
            S.add("pe", lambda e, ob=ob, pb=pb, s=s: e.matmul(
                banks[ob][:, 0:256], lhsT=PTb[pb][:, 0:128], rhs=cvb[:, s, :], start=True, stop=False),
                reads=[PTres[pb], SAres], writes=[bres[ob]])
            S.add("pe", lambda e, ob=ob, pb=pb: e.matmul(
                banks[ob][:, 0:256], lhsT=PTb[pb][:, 128:256], rhs=Vb[:, 9, :], start=False, stop=True),
                reads=[PTres[pb], Vres[8]], writes=[bres[ob]])
            for g in range(4):
                S.add("dve", lambda e, ob=ob, g=g, s=s, b2=b2: e.tensor_scalar_mul(
                    out=Osm1[32 * g:32 * g + 32, s, :], in0=banks[ob][32 * g:32 * g + 32, 64 * g:64 * g + 64],
                    scalar1=st_att[32 * g:32 * g + 32, 5, b2, s:s + 1]),
                    reads=[bres[ob], sres(5, b2, s)], writes=[Osres])

        def gen_sample():
            sample_scores(0)
            sample_scores(1)
            sample_E1(0)
            sample_scores(2)
            sample_E1(1)
            transpose_probs(spb[0])
            for s in range(16):
                if s + 3 < 16:
                    sample_scores(s + 3)
                if s + 2 < 16:
                    sample_E1(s + 2)
                if s + 1 < 16:
                    transpose_probs(spb[s + 1])
                sample_V(s)
                yield

        qrest = gen_q_rest()
        for _ in gen_sample():
            try:
                next(qrest)
            except StopIteration:
                pass
        _drain(qrest)
        Osmres = Res("Osm")
        for u in range(2):
            S.add("dve", lambda e, u=u: e.tensor_copy(out=Osm[:, :, u, :], in_=Osm1), reads=[Osres, Osmres], writes=[Osmres])
        for s in range(16):
            tb = 6 + s % 2
            S.add("pe", lambda e, tb=tb, s=s: e.transpose(out=bankbf(tb)[:, 0:128],
                                                          in_=Osm[:, s, :, :].rearrange("p u d -> p (u d)"), identity=ident),
                  reads=[Osmres] + CONST, writes=[bres[tb]])
            for eh in range(2):
                srcv = bankbf(tb)[eh * 64:(eh + 1) * 64, 0:128].rearrange("p (j u t) -> p j u t", j=8, u=2)[:, :, eh, :]
                dstv = QT[eh * 64:(eh + 1) * 64, :, 1024 + s * 8:1024 + (s + 1) * 8]
                copy_op("act" if s % 2 else "dve", dstv, srcv, [bres[tb]], [QTres[8]])
        attnT = QT
        attnres = QTres

        sample_dead = [SAres, Qzres, QTzres, Osres]

        def prompt_scores(t, h):
            kvh = h // 4
            base = (h % 2) * 64
            ch = h // 2
            sbank = h % 2
            mk = mask0b if t == 0 else maskNb
            kcols = slice(t * 128, t * 128 + 256)
            prev_i = 9 if t == 0 else t - 1
            S.add("pe", lambda e: e.matmul(banks[sbank][:, 256:258], lhsT=ident, rhs=sinks_b2[:, 2 * h:2 * h + 2],
                                           start=True, stop=True), reads=CONST, writes=[bres[sbank]])
            S.add("pe", lambda e: e.matmul(
                banks[sbank][:, 0:256], lhsT=QT[base:base + 64, ch, tcols(t)], rhs=KTd[base:base + 64, kvh, kcols],
                start=True, stop=False),
                reads=[QTres[t], KTres[prev_i], KTres[t]], writes=[bres[sbank]])
            S.add("pe", lambda e: e.matmul(banks[sbank][:, 0:256], lhsT=ident, rhs=mk, start=False, stop=True),
                  reads=CONST, writes=[bres[sbank]])

        ATT_HEADS = [(t, h) for t in range(8) for h in range(16)]
        NH = len(ATT_HEADS)
        att_pb = {}

        def att_E1(idx):
            t, h = ATT_HEADS[idx]
            att_pb[idx] = softmax_head(h % 2, t % 2, h, sinks_t[:, h:h + 1], False)

        def att_E2(idx):
            pb = att_pb[idx]
            tb = 2 + pb
            for c in range(2):
                S.add("pe", lambda e, c=c: e.transpose(out=bankbf(tb)[:, c * 128:(c + 1) * 128],
                                                       in_=Pb[pb][:, c * 128:(c + 1) * 128], identity=ident),
                      reads=[Pres[pb]] + CONST, writes=[bres[tb]])
            copy_op("dve" if idx % 2 else "act", PTb[pb], bankbf(tb)[:, 0:256], [bres[tb]], [PTres[pb]])

        def att_V(idx):
            t, h = ATT_HEADS[idx]
            pb = att_pb[idx]
            b2 = t % 2
            ab = atm[b2]
            kvh = h // 4
            prev_i = 9 if t == 0 else t - 1
            ob = 4 + h // 8
            oc = (h % 8) * 64
            S.add("pe", lambda e: e.matmul(
                banks[ob][:, oc:oc + 64], lhsT=PTb[pb][:, 0:128], rhs=Vb[:, t, kvh * 64:(kvh + 1) * 64],
                start=True, stop=False), reads=[PTres[pb], Vres[prev_i]], writes=[bres[ob]])
            S.add("pe", lambda e: e.matmul(
                banks[ob][:, oc:oc + 64], lhsT=PTb[pb][:, 128:256], rhs=Vb[:, t + 1, kvh * 64:(kvh + 1) * 64],
                start=False, stop=True), reads=[PTres[pb], Vres[t]], writes=[bres[ob]])
            if h % 8 == 7:
                h0 = h - 7
                hs = slice(h0, h0 + 8)
                Rg = [sres(q, b2, hh) for q in (1, 2) for hh in range(h0, h0 + 8)]
                Wg = [sres(q, b2, hh) for q in (3, 4, 5) for hh in range(h0, h0 + 8)]
                S.add("dve", lambda e: e.tensor_tensor(out=st_att[:, 3, b2, hs], in0=st_att[:, 1, b2, hs],
                                                       in1=sinks_t[:, hs], op=ALU.add),
                      reads=Rg + CONST, writes=Wg)
                S.add("act", lambda e: e.activation(out=st_att[:, 3, b2, hs], in_=st_att[:, 3, b2, hs], func=AF.Exp),
                      reads=Wg, writes=Wg)
                S.add("dve", lambda e: e.tensor_tensor(out=st_att[:, 4, b2, hs], in0=st_att[:, 2, b2, hs],
                                                       in1=st_att[:, 3, b2, hs], op=ALU.add), reads=Rg + Wg, writes=Wg)
                S.add("dve", lambda e: e.reciprocal(out=st_att[:, 5, b2, hs], in_=st_att[:, 4, b2, hs]),
                      reads=Wg, writes=Wg)
                rdb = st_att[:, 5, b2, hs].unsqueeze(2).broadcast_to([128, 8, 64])
                S.add("dve", lambda e: e.tensor_tensor(
                    out=ab[:, h0 * 64:(h0 + 8) * 64].rearrange("p (h d) -> p h d", h=8),
                    in0=banks[ob][:, :].rearrange("p (h d) -> p h d", h=8), in1=rdb, op=ALU.mult),
                    reads=[bres[ob]] + Wg, writes=[atmres[b2]])
            if h == 15:
                for j in range(8):
                    S.add("pe", lambda e, j=j: e.transpose(out=bankbf(6)[:, j * 128:(j + 1) * 128],
                                                           in_=ab[:, j * 128:(j + 1) * 128], identity=ident),
                          reads=[atmres[b2]] + CONST, writes=[bres[6]])
                copy_op("act", QT[:, :, tcols(t)], bankbf(6)[:, 0:1024].rearrange("p (c t) -> p c t", c=8),
                        [bres[6]], [QTres[t]])

        def gen_att_prompt():
            prompt_scores(*ATT_HEADS[0])
            prompt_scores(*ATT_HEADS[1])
            att_E1(0)
            prompt_scores(*ATT_HEADS[2])
            att_E1(1)
            att_E2(0)
            for k in range(NH):
                if k + 3 < NH:
                    prompt_scores(*ATT_HEADS[k + 3])
                if k + 2 < NH:
                    att_E1(k + 2)
                if k + 1 < NH:
                    att_E2(k + 1)
                att_V(k)
                yield

        gvres = [Res("gvf%d" % i) for i in range(NT)]
        vlnb = view(r2, 18432, BF16, "p (t c) -> p t c", t=NT)
        gnb = view(r2 + 18432, 4096)
        vlnres = [Res("vln%d" % i) for i in range(NT)]
        gnres = Res("gnB")
        uTres = [Res("uT%d" % i) for i in range(NT)]
        lnres = [Res("ln0"), Res("ln1")]

        bnall = view(SMALL_EXTRA, 4 * NT * 4, F32, "p (t k) -> p t k", t=NT)
        bnst9 = view(SMALL_EXTRA + 160, 4 * NT * 12, F32, "p (t k) -> p t k", t=NT)

        def gen_gv():
            r2old = [xtres[0], xtres[1], KT8res, Kb8res] + kdres + kvfres
            S.add("sp", lambda e: e.dma_start(out=gnb, in_=gnB[:, :]), writes=[gnres] + r2old, dma_key="gn")
            for wb in range(4):
                off, rw = w_use(B_GV[wb])
                wt = v_k16(off)
                for t in range(NT):
                    gv_group(wb, t, wt, rw)
                    yield
                w_done(B_GV[wb])

        def gv_group(wb, t, wt, rw):
            bk = 7
            for kc in range(16):
                S.add("pe", lambda e, kc=kc: e.matmul(
                    banks[bk][:, 0:256], lhsT=hT[:, kc, tcols(t)], rhs=wt[:, kc, :], start=(kc == 0), stop=(kc == 15)),
                    reads=[hTres[t][kc], rw], writes=[bres[bk]])
            S.add("dve", lambda e: e.tensor_copy(out=gvf[:, t, wb * 256:(wb + 1) * 256], in_=banks[bk][:, 0:256]),
                  reads=[bres[bk]], writes=[gvres[t]] + (sample_dead if wb == 0 else []))

        def u_group(wb, cl, tg, wt, rw):
            m = wb * 2 + cl
            bk, lo, hi = fm_group(wt, cl, 16, hT, hTr, tg, [rw], [0, 1, 2, 3, 4, 5, 6, 7])
            S.add("act", lambda e: e.activation(out=uT[:, m, lo:hi], in_=banks[bk][:, 0:384], func=AF.Gelu_apprx_tanh),
                  reads=[bres[bk]],
                  writes=[uTres[t] for t in tiles_of_tg(tg)]
                  + [gvres[t] for t in range((m * 2304) // 4096, ((m + 1) * 2304 - 1) // 4096 + 1)])

        def ln_part1():
            R = lnres[0]
            for t in range(NT):
                S.add("act", lambda e, t=t: e.activation(out=gvf[:, t, :], in_=gvf[:, t, :], func=AF.Gelu_apprx_tanh),
                      reads=[gvres[t]], writes=[gvres[t]])
                for c in range(2):
                    S.add("dve", lambda e, t=t, c=c: e.bn_stats(out=bnst9[:, t, c * 6:(c + 1) * 6],
                                                                in_=gvf[:, t, c * 512:(c + 1) * 512]),
                          reads=[gvres[t]], writes=[R])
                S.add("dve", lambda e, t=t: e.bn_aggr(out=bnall[:, t, 0:2], in_=bnst9[:, t, :]), reads=[R], writes=[R])

        def ln_part2():
            R = lnres[0]
            S.add("act", lambda e: e.activation(out=bnall[:, :, 2], in_=bnall[:, :, 1], func=AF.Sqrt, bias=eps_t[:, 0:1],
                                                scale=1.0), reads=[R, epsR], writes=[R])
            S.add("dve", lambda e: e.reciprocal(out=bnall[:, :, 3], in_=bnall[:, :, 2]), reads=[R], writes=[R])
            for t in range(NT):
                S.add("dve", lambda e, t=t: e.tensor_scalar(out=gvf[:, t, :], in0=gvf[:, t, :], scalar1=bnall[:, t, 0:1],
                                                            scalar2=bnall[:, t, 3:4], op0=ALU.subtract, op1=ALU.mult),
                      reads=[R, gvres[t]], writes=[gvres[t]])
                S.add("dve", lambda e, t=t: e.tensor_tensor(out=vlnb[:, t, :], in0=gvf[:, t, :], in1=gnb, op=ALU.mult),
                      reads=[gvres[t], gnres], writes=[vlnres[t]])
                if t == 8:
                    S.add("dve", lambda e, t=t: e.tensor_tensor(out=gvf[:, t, :], in0=gvf[:, t, :], in1=gnb, op=ALU.mult),
                          reads=[gvres[t], gnres, vlnres[t]], writes=[gvres[t]])
                    S.add("sp", lambda e, t=t: e.dma_start(out=sg_out[:, :], in_=gvf[:, t, :]), reads=[gvres[t]], dma_key="osg")

        _interleave(gen_att_prompt(), gen_gv(), k=1)
        ln_part1()
        ln_part2()
        for wb in range(4):
            off, rw = w_use(B_U[wb])
            wt = v_k16(off)
            for cl in range(2):
                for tg in range(3):
                    u_group(wb, cl, tg, wt, rw)
            w_done(B_U[wb])

        SPB = r2 + 22528
        wst = view(SPB, 4096, F32, "p (g t) -> p g t", g=8)
        wsm = view(SPB + 4096, 2048, BF16, "p (g t) -> p g t", g=8)
        wsms = view(SPB + 6144, 2048, BF16, "p (g t) -> p g t", g=8)
        msk = view(SPB + 8192, 512)
        bsb = view(SPB + 8704, 4096, F32, "p (g t) -> p g t", g=8)
        wres_sp = Res("wst")
        mres = Res("msk")
        wsmres = Res("wsm")
        bsres = Res("bsb")
        att_bufs = Pres + PTres + atmres + [Osmres]
        for which in range(2):
            srcw = wsT if which == 0 else wsTr
            srcm = trilT if which == 0 else blkm
            dstw = wsm if which == 0 else wsms
            S.add("sp", lambda e, srcw=srcw: e.dma_start(out=wst, in_=srcw.rearrange("g s t -> s g t")),
                  writes=[wres_sp] + (att_bufs if which == 0 else []), dma_key="wst")
            S.add("sp", lambda e, srcm=srcm: e.dma_start(out=msk, in_=srcm[:, :]),
                  writes=[mres] + (att_bufs if which == 0 else []), dma_key="msk")
            for g in range(8):
                S.add("dve", lambda e, g=g, dstw=dstw: e.tensor_tensor(out=dstw[:, g, :], in0=wst[:, g, :], in1=msk,
                                                                      op=ALU.mult),
                      reads=[wres_sp, mres], writes=[wsmres])
        S.add("sp", lambda e: e.dma_start(out=bsb, in_=bsB[:, :, :]), writes=[bsres] + att_bufs, dma_key="bsb")
        bsbs = wst
        S.add("sp", lambda e: e.dma_start(out=bsbs, in_=bsBs[:, :, :]), writes=[wres_sp], dma_key="wst")
        sptmp = [view(96256 + i * 2048, 2048) for i in range(2)]
        sptres = [Res("sptmp%d" % i) for i in range(2)]
        kv_dead = KTres + Vres
        for t in range(NT):
            wm = wsms if t == 8 else wsm
            bb = bsbs if t == 8 else bsb
            for half in range(2):
                n = t * 2 + half
                bk = n % 4
                for gl in range(4):
                    g = half * 4 + gl
                    S.add("pe", lambda e, bk=bk, gl=gl, g=g, t=t, wm=wm: e.matmul(
                        banks[bk][:, gl * 128:(gl + 1) * 128], lhsT=vlnb[:, t, g * 128:(g + 1) * 128], rhs=wm[:, g, :],
                        start=True, stop=True), reads=[vlnres[t], wsmres], writes=[bres[bk]])
                tp = sptmp[n % 2]
                S.add("dve", lambda e, bk=bk, tp=tp, bb=bb, half=half: e.tensor_tensor(
                    out=tp, in0=banks[bk][:, :], in1=bb[:, half * 4:(half + 1) * 4, :].rearrange("p g t -> p (g t)"),
                    op=ALU.add), reads=[bres[bk], bsres, wres_sp], writes=[sptres[n % 2]] + (kv_dead if n < 2 else []))
                S.add("pool" if n % 2 else "dve", lambda e, tp=tp, half=half, t=t: e.tensor_tensor(
                    out=uT[:, half * 4:(half + 1) * 4, tcols(t)], in0=uT[:, half * 4:(half + 1) * 4, tcols(t)],
                    in1=tp.rearrange("p (g t) -> p g t", g=4), op=ALU.mult),
                    reads=[sptres[n % 2], uTres[t]], writes=[uTres[t]])
        aT = uT
        aTres = uTres

        mixres = [Res("mix%d" % i) for i in range(NT)]
        sga = [view(VLN_OFF + i * 1536, 1536) for i in range(6)]
        sgb = [view(VLN_OFF + 9216 + i * 1536, 1536) for i in range(6)]
        sgares = [Res("sga%d" % i) for i in range(6)]
        sgbres = [Res("sgb%d" % i) for i in range(6)]
        R2old = vlnres + [gnres, wres_sp, mres, wsmres, bsres]
        gate_first = {"a": True, "m": True}
        ALLB = list(range(8))
        for i in range(8):
            ga, gb, pab = B_GATE[i]
            for (blk, dst, dres) in ((ga, sga, sgares), (gb, sgb, sgbres)):
                off, rw = w_use(blk)
                wt = v_k16(off)
                for cl in range(2):
                    for tg in range(3):
                        q = cl * 3 + tg
                        bk, lo, hi = fm_group(wt, cl, 16, hT, hTr, tg, [rw], ALLB)
                        old = (gvres + sptres) if gate_first["a"] else []
                        S.add("act", lambda e, q=q, bk=bk, dst=dst: e.activation(out=dst[q], in_=banks[bk][:, 0:384],
                                                                                  func=AF.Sigmoid),
                              reads=[bres[bk]], writes=[dres[q]] + old)
                gate_first["a"] = False
                w_done(blk)
            offp, rp = w_use(pab)
            wpa = view(offp, 4096, BF16, "p (k c) -> p k c", k=8)
            wpb = view(offp + 4096, 4096, BF16, "p (k c) -> p k c", k=8)
            for cl in range(2):
                m = i * 2 + cl
                for tg in range(3):
                    q = cl * 3 + tg
                    tl = tiles_of_tg(tg)
                    bka, lo, hi = fm_group(wpa, cl, 8, aT, lambda t, kc: aTres[t], tg, [rp], ALLB)
                    S.add("dve", lambda e, q=q, bka=bka: e.tensor_tensor(out=sga[q], in0=sga[q], in1=banks[bka][:, 0:384],
                                                                          op=ALU.mult),
                          reads=[sgares[q], bres[bka]], writes=[sgares[q]])
                    bkb, lo, hi = fm_group(wpb, cl, 8, attnT, lambda t, kc: attnres[t], tg, [rp], ALLB)
                    S.add("dve", lambda e, q=q, bkb=bkb: e.tensor_tensor(out=sgb[q], in0=sgb[q], in1=banks[bkb][:, 0:384],
                                                                          op=ALU.mult),
                          reads=[sgbres[q], bres[bkb]], writes=[sgbres[q]])
                    S.add("pool", lambda e, q=q, m=m, lo=lo, hi=hi: e.tensor_tensor(
                        out=mixT[:, m, lo:hi], in0=sga[q], in1=sgb[q], op=ALU.add),
                        reads=[sgares[q], sgbres[q]], writes=[mixres[t] for t in tl] + (R2old if gate_first["m"] else []))
                    gate_first["m"] = False
            w_done(pab)

        x1res = [Res("x1_%d" % i) for i in range(NT)]
        R1B_all = [r for l in hTres for r in l] + QTres + uTres + gvres + KTres + Vres + sptres + sgares + sgbres
        hT_all = [r for l in hTres for r in l]
        rest_all = QTres + uTres + gvres + KTres + Vres + sptres + sgares + sgbres
        for t in range(NT):
            guard = hT_all if t < 5 else (rest_all if t == 5 else [])
            S.add("sp", lambda e, t=t: e.dma_start(out=x1[:, t, :], in_=xc[t * 128:(t + 1) * 128, :]),
                  writes=[x1res[t]] + guard, dma_key="x1_%d" % (t % 3))
        hhres = [Res("hh%d" % i) for i in range(NT)]

        rms_state = {}

        def rms_a(t, n):
            b = n % 3
            ssr = Res("ss2")
            ss = st2[:, 2 * (n % 16):2 * (n % 16) + 1]
            rs = st2[:, 2 * (n % 16) + 1:2 * (n % 16) + 2]
            S.add("act", lambda e: e.activation(out=hb[b], in_=x1[:, t, :], func=AF.Square, accum_out=ss),
                  reads=[x1res[t]], writes=[hbres[b], ssr])
            rms_state[n] = (ss, rs, ssr)

        def rms_b1(n):
            ss, rs, ssr = rms_state[n]
            S.add("act", lambda e: e.activation(out=rs, in_=ss, func=AF.Sqrt, bias=eps_t[:, 0:1], scale=1.0 / D),
                  reads=[ssr, epsR], writes=[ssr])

        def rms_b(n):
            ss, rs, ssr = rms_state[n]
            S.add("dve", lambda e: e.reciprocal(out=rs, in_=rs), reads=[ssr], writes=[ssr])
            return rs, ssr

        def norm_stage1b(t, n):
            b = n % 3
            rs, ssr = rms_b(n)
            S.add("dve", lambda e: e.tensor_scalar_mul(out=hb[b], in0=x1[:, t, :], scalar1=rs),
                  reads=[ssr, x1res[t], hbres[b]], writes=[hbres[b]])

        def norm_stage2(t, n, gt, dstT, dres, tbanks):
            b = n % 3
            for half in range(2):
                bk = tbanks[(2 * n + half) % len(tbanks)]
                for j in range(8):
                    c = half * 8 + j
                    S.add("pe", lambda e, bk=bk, j=j, c=c: e.transpose(
                        out=bankbf(bk)[:, j * 128:(j + 1) * 128], in_=hb[b][:, c * 128:(c + 1) * 128], identity=ident),
                        reads=[hbres[b]] + CONST, writes=[bres[bk]])
                en_ = "act" if half == 0 else "dve"
                for j in range(8):
                    c = half * 8 + j
                    src = bankbf(bk)[:, j * 128:(j + 1) * 128]
                    dst = dstT[:, c, tcols(t)]
                    if en_ == "act":
                        S.add("act", lambda e, dst=dst, src=src, c=c: e.activation(
                            out=dst, in_=src, func=AF.Copy, scale=gt[:, c:c + 1]),
                            reads=[bres[bk]] + CONST, writes=[dres])
                    else:
                        S.add("dve", lambda e, dst=dst, src=src, c=c: e.tensor_scalar_mul(
                            out=dst, in0=src, scalar1=gt[:, c:c + 1]),
                            reads=[bres[bk]] + CONST, writes=[dres])

        wo_n = {"n": 0}

        def wo_group(cg, t, wt, rw):
            bk = wo_n["n"] % 4
            wo_n["n"] += 1
            for kc in range(16):
                S.add("pe", lambda e, kc=kc: e.matmul(
                    banks[bk][:, 0:256], lhsT=mixT[:, kc, tcols(t)], rhs=wt[:, kc, :], start=(kc == 0), stop=(kc == 15)),
                    reads=[mixres[t], rw], writes=[bres[bk]])
            S.add("dve", lambda e: e.tensor_tensor(
                out=x1[:, t, cg * 256:(cg + 1) * 256], in0=banks[bk][:, 0:256], in1=x1[:, t, cg * 256:(cg + 1) * 256],
                op=ALU.add), reads=[bres[bk], x1res[t]], writes=[x1res[t]])

        NCGO = 6
        for cg in range(NCGO):
            off, rw = w_use(B_WO[cg])
            wt = v_k16(off)
            for t in range(NT):
                wo_group(cg, t, wt, rw)
            w_done(B_WO[cg])
        tail_w = []
        for cg in range(NCGO, 8):
            off, rw = w_use(B_WO[cg])
            tail_w.append((cg, v_k16(off), rw))
        for t in range(NT + 1):
            if t < NT:
                for (cg, wt, rw) in tail_w:
                    wo_group(cg, t, wt, rw)
                rms_a(t, t)
                rms_b1(t)
            if 1 <= t <= NT:
                norm_stage1b(t - 1, t - 1)
            if 2 <= t:
                norm_stage2(t - 2, t - 2, g2t, hhT, hhres[t - 2], [4, 5, 6, 7])
        w_done(B_WO[7])

        def norm_drain():
            norm_stage2(NT - 1, NT - 1, g2t, hhT, hhres[NT - 1], [4, 5, 6, 7])

        actT = [view(r2 + i * 9216, 9216, BF16, "p (f t) -> p f t", f=4) for i in range(2)]
        actres = [[Res("act%d_%d" % (i, t)) for t in range(NT)] for i in range(2)]
        silt = [view(r2 + 18432 + i * 1536, 1536) for i in range(2)]
        silres = [Res("sil%d" % i) for i in range(2)]
        gfbv = view(r2 + 21504, 8192)
        gfres = Res("gfb")
        S.add("sp", lambda e: e.dma_start(out=gfbv, in_=gfB[:, :]), writes=[gfres] + mixres, dma_key="gf")
        ffn_n = {"n": 0}
        dn_n = {"n": 0}
        for gi in range(NG):
            gu, dn = B_FF[gi]
            ab = gi % 2
            for hf in range(2):
                offg, rg = w_use(gu[hf][0])
                offu, ru = w_use(gu[hf][1])
                wg_ = v_k16(offg)
                wu_ = v_k16(offu)
                cltg = [(cl, tg) for cl in range(2) for tg in range(3)]
                if gi == 0 and hf == 0:
                    cltg = [(0, 0), (0, 1), (1, 0), (1, 1), None, (0, 2), (1, 2)]
                for ct in cltg:
                    if ct is None:
                        norm_drain()
                        continue
                    cl, tg = ct
                    f = hf * 2 + cl
                    if True:
                        lo, hi = TG[tg]
                        n = ffn_n["n"]
                        ffn_n["n"] += 1
                        bg = (n % 2) * 2
                        bu = bg + 1
                        tl = tiles_of_tg(tg)
                        for (bk, wt, rr) in ((bg, wg_, rg), (bu, wu_, ru)):
                            for kc in range(16):
                                S.add("pe", lambda e, bk=bk, kc=kc, wt=wt, lo=lo, hi=hi, cl=cl: e.matmul(
                                    banks[bk][:, 0:384], lhsT=wt[:, kc, cl * 128:(cl + 1) * 128], rhs=hhT[:, kc, lo:hi],
                                    start=(kc == 0), stop=(kc == 15)),
                                    reads=[hhres[t] for t in tl] + [rr], writes=[bres[bk]])
                        sl = silt[n % 2]
                        old = mixres if n < 2 else []
                        S.add("act", lambda e, sl=sl, bg=bg: e.activation(out=sl, in_=banks[bg][:, 0:384], func=AF.Silu),
                              reads=[bres[bg]], writes=[silres[n % 2]] + old)
                        S.add("dve", lambda e, sl=sl, bu=bu, ab=ab, f=f, lo=lo, hi=hi: e.tensor_tensor(
                            out=actT[ab][:, f, lo:hi], in0=sl, in1=banks[bu][:, 0:384], op=ALU.mult),
                            reads=[silres[n % 2], bres[bu]], writes=[actres[ab][t] for t in tl] + old)
                w_done(gu[hf][1])
            last = (gi == NG - 1)
            dnw = []
            if last:
                for ch in range(2):
                    offd, rd_ = w_use(dn[ch])
                    dnw.append((view(offd, 8192, BF16, "p (f c) -> p f c", f=4), rd_))
            for ch_t in ([(ch, None) for ch in range(2)] if not last else [(ch, t) for t in range(NT) for ch in range(2)]):
                ch = ch_t[0]
                if not last:
                    offd, rd_ = w_use(dn[ch])
                    wd_ = view(offd, 8192, BF16, "p (f c) -> p f c", f=4)
                    tiles = range(NT)
                else:
                    wd_, rd_ = dnw[ch]
                    tiles = [ch_t[1]]
                for t in tiles:
                    for c2 in range(2):
                        bk = 4 + dn_n["n"] % 4
                        dn_n["n"] += 1
                        for f in range(4):
                            S.add("pe", lambda e, bk=bk, f=f, t=t, c2=c2, ab=ab, wd_=wd_: e.matmul(
                                banks[bk][:, 0:512], lhsT=actT[ab][:, f, tcols(t)], rhs=wd_[:, f, c2 * 512:(c2 + 1) * 512],
                                start=(f == 0), stop=(f == 3)), reads=[actres[ab][t], rd_], writes=[bres[bk]])
                        col = ch * 1024 + c2 * 512
                        S.add("dve", lambda e, bk=bk, t=t, col=col: e.tensor_tensor(
                            out=x1[:, t, col:col + 512], in0=banks[bk][:, 0:512], in1=x1[:, t, col:col + 512], op=ALU.add),
                            reads=[bres[bk], x1res[t]], writes=[x1res[t]])
                    if last and ch == 1:
                        rms_a(t, NT + t)
                        rms_b1(NT + t)
                    for tf in (([t - 1] if t >= 1 else []) + ([t] if t == NT - 1 else [])) if (last and ch == 1) else []:
                        rs, ssr = rms_b(NT + tf)
                        yA = Res("yA")
                        yB = Res("yB")
                        S.add("dve", lambda e, t=tf, rs=rs: e.scalar_tensor_tensor(
                            out=x1[:, t, 0:1024], in0=x1[:, t, 0:1024], scalar=rs, in1=gfbv[:, 0:1024],
                            op0=ALU.mult, op1=ALU.mult), reads=[ssr, x1res[tf], gfres], writes=[yA])
                        S.add("act", lambda e, t=tf, rs=rs: e.activation(out=x1[:, t, 1024:2048], in_=x1[:, t, 1024:2048],
                                                                         func=AF.Copy, scale=rs),
                              reads=[ssr, x1res[tf]], writes=[yB])
                        S.add("pool", lambda e, t=tf: e.tensor_tensor(out=x1[:, t, 1024:2048], in0=x1[:, t, 1024:2048],
                                                                      in1=gfbv[:, 1024:2048], op=ALU.mult),
                              reads=[yB, gfres], writes=[yB])
                        S.add("sp", lambda e, t=tf: e.dma_start(out=y_out[t * 128:(t + 1) * 128, :], in_=x1[:, t, :]),
                              reads=[yA, yB], writes=[x1res[tf]], dma_key="oy%d" % (tf % 3))
                if not last:
                    w_done(dn[ch])
            if last:
                w_done(dn[1])
        S.emit(nc, st)
    return nc


def _consts():
    i = np.arange(128)[:, None]
    j = np.arange(128)[None, :]
    prev = np.where(j >= i, 0.0, NEG).astype(np.float32)
    cur = np.where(j <= i, 0.0, NEG).astype(np.float32)
    maskN = np.concatenate([prev, cur], 1)
    mask0_first = np.concatenate([np.full((128, 128), NEG, np.float32), cur], 1)
    t_row = (np.arange(128) % 8)[:, None]
    maskSc = np.where(j >= t_row, 0.0, NEG).astype(np.float32)
    maskSn = np.full((128, 16, 128), NEG, np.float32)
    for s in range(16):
        for tp in range(8):
            maskSn[:, s, s * 8 + tp] = np.where(tp <= t_row[:, 0], 0.0, NEG)
    trilT = (i <= j).astype(np.float32)
    si, ti = np.arange(128)[:, None], np.arange(128)[None, :]
    blkm = ((si // 8 == ti // 8) & (si % 8 <= ti % 8)).astype(np.float32)
    ident = np.eye(128, dtype=np.float32)
    return dict(maskN=maskN, mask0_first=mask0_first, maskSc=maskSc, maskSn=maskSn, trilT=trilT, blkm=blkm, ident=ident)


_NC_CACHE = {}


def make_in_maps(x_prompt, x_sample, cache_k, cache_v, norm1_g, w_in, gmlp_norm_g, w_s, b_s, sinks,
                 w_pa, w_pb, w_o, norm2_g, w_ff_gate, w_ff_up, w_ff_down, final_g):
    f = lambda a: np.ascontiguousarray(np.asarray(a, dtype=np.float32))
    x_prompt, x_sample, cache_k, cache_v = f(x_prompt), f(x_sample), f(cache_k), f(cache_v)
    C = _consts()

    ws = f(w_s)[0]
    wsT = np.ascontiguousarray(ws.transpose(0, 2, 1))
    wsTr = np.ascontiguousarray(np.tile(wsT[:, 0:8, 0:8], (1, 16, 16)))
    bs = f(b_s)[0]
    bsB = np.ascontiguousarray(np.broadcast_to(bs[None], (128, 8, 128)))
    bsBs = np.ascontiguousarray(np.broadcast_to(np.tile(bs[:, 0:8], (1, 16))[None], (128, 8, 128)))
    sk = f(sinks)[0]
    shared = dict(
        w_in=f(w_in)[0], w_pa=f(w_pa)[0], w_pb=f(w_pb)[0], w_o=f(w_o)[0],
        w_g=f(w_ff_gate)[0], w_u=f(w_ff_up)[0], w_d=f(w_ff_down)[0],
        g1T=np.ascontiguousarray(f(norm1_g)[0].reshape(16, 128).T),
        g2T=np.ascontiguousarray(f(norm2_g)[0].reshape(16, 128).T),
        gfB=np.ascontiguousarray(np.broadcast_to(f(final_g)[None], (128, D))),
        gnB=np.ascontiguousarray(np.broadcast_to(f(gmlp_norm_g)[0][None], (128, 1024))),
        wsT=wsT, wsTr=wsTr, trilT=C["trilT"], blkm=C["blkm"], bsB=bsB, bsBs=bsBs,
        sinksB=np.ascontiguousarray(np.broadcast_to(sk[None], (128, 16))),
        sinkrep=np.ascontiguousarray(np.repeat(sk, 8)[:, None]),
        sinksB2=np.ascontiguousarray(np.broadcast_to(np.repeat(sk, 2)[None], (128, 32))),
        sinkrep2=np.ascontiguousarray(np.repeat(np.repeat(sk, 8)[:, None], 2, axis=1)),
        maskN=C["maskN"], maskSc=C["maskSc"], maskSn=C["maskSn"], ident=C["ident"],
    )
    in_maps = []
    for c in range(NCORES):
        b, half = c // 2, c % 2
        xs = x_sample[c * 16:(c + 1) * 16].reshape(128, D)
        xp = x_prompt[b, half * 1024:(half + 1) * 1024]
        if half == 1:
            prev = x_prompt[b, 896:1024]
            mask0 = C["maskN"]
        else:
            prev = np.zeros((128, D), np.float32)
            mask0 = C["mask0_first"]
        xcat = np.ascontiguousarray(np.concatenate([xp, xs, prev], 0))
        ckc = cache_k[0, c * 16:(c + 1) * 16].reshape(16, 128, 256)
        cvc = cache_v[0, c * 16:(c + 1) * 16].reshape(16, 128, 256)
        ckT = np.ascontiguousarray(ckc.reshape(16, 128, 2, 128).transpose(2, 3, 0, 1))
        m = dict(shared)
        m.update(xc=xcat, ckT=ckT, ck=np.ascontiguousarray(ckc), cv=np.ascontiguousarray(cvc), mask0=mask0)
        in_maps.append(m)
    return in_maps


def kernel(**inputs):
    in_maps = make_in_maps(**inputs)
    if "nc" not in _NC_CACHE:
        _NC_CACHE["nc"] = build_program()
    nc = _NC_CACHE["nc"]
    res = run_bass_kernel_spmd(nc, in_maps, core_ids=list(range(NCORES)))
    return assemble(res.results)


def assemble(R):
    y_prompt = np.empty((4, 2048, D), np.float32)
    y_sample = np.empty((128, 8, D), np.float32)
    pk = np.empty((1, 4, 128, 4, 64), np.float32)
    pv = np.empty((1, 4, 128, 4, 64), np.float32)
    skw = np.empty((1, 128, 128, 4, 64), np.float32)
    svw = np.empty((1, 128, 128, 4, 64), np.float32)
    sg = np.empty((1, 128, 8, 8, 128), np.float32)
    for c in range(NCORES):
        b, half = c // 2, c % 2
        y = R[c]["y"]
        y_prompt[b, half * 1024:(half + 1) * 1024] = y[0:1024]
        y_sample[c * 16:(c + 1) * 16] = y[1024:1152].reshape(16, 8, D)
        if half == 1:
            pk[0, b] = R[c]["kv7"][:, 0:256].reshape(128, 4, 64)
            pv[0, b] = R[c]["kv7"][:, 256:512].reshape(128, 4, 64)
        skw[0, c * 16:(c + 1) * 16] = R[c]["skw"].reshape(16, 128, 4, 64)
        svw[0, c * 16:(c + 1) * 16] = R[c]["svw"].reshape(16, 128, 4, 64)
        sg[0, c * 16:(c + 1) * 16] = R[c]["sg"].reshape(16, 8, 8, 128)
    return (y_prompt, y_sample, pk, pv, skw, svw, sg)
```

```python
import contextlib
import numpy as np
import concourse.bass as bass
import concourse.mybir as mybir
from concourse.bass_utils import run_bass_kernel_spmd

F32 = mybir.dt.float32
BF16 = mybir.dt.bfloat16
AF = mybir.ActivationFunctionType
ALU = mybir.AluOpType
AX = mybir.AxisListType

ENGS = ("pe", "act", "dve", "pool", "sp")

D = 2048
DFF = 5632
INW = 7680
NT = 9
TOK = NT * 128
TOKP = TOK + 128
EPS = 1e-6
NEG = -1e30
NCORES = 8


class Res:
    __slots__ = ("name", "last_w", "readers", "excl")

    def __init__(self, name, excl=False):
        self.name = name
        self.last_w = None
        self.readers = []
        self.excl = excl


class Op:
    __slots__ = ("eng", "fn", "deps", "signal", "is_dma", "key", "val", "idx")

    def __init__(self, eng, fn, is_dma, key):
        self.eng = eng
        self.fn = fn
        self.deps = []
        self.signal = False
        self.is_dma = is_dma
        self.key = key
        self.val = None
        self.idx = None


class Sched:
    def __init__(self):
        self.q = {e: [] for e in ENGS}
        self.dma_count = {}
        self.stopped = False

    def add(self, eng, fn, reads=(), writes=(), dma_key=None):
        if self.stopped:
            return None
        is_dma = dma_key is not None
        op = Op(eng, fn, is_dma, dma_key)
        deps = {}
        for r in reads:
            if r.last_w is not None:
                deps[id(r.last_w)] = r.last_w
            if r.excl:
                for rd in r.readers:
                    if rd.is_dma or rd.eng != eng:
                        deps[id(rd)] = rd
        for w in writes:
            if w.last_w is not None:
                deps[id(w.last_w)] = w.last_w
            for rd in w.readers:
                deps[id(rd)] = rd
        for d in deps.values():
            if d is op:
                continue
            if (not d.is_dma) and (not is_dma) and d.eng == "pe" and eng == "pe":
                continue
            op.deps.append(d)
            d.signal = True
        for r in reads:
            if not is_dma:
                r.readers = [x for x in r.readers if x.is_dma or x.eng != eng]
            r.readers.append(op)
        for w in writes:
            w.last_w = op
            w.readers = []
        if is_dma:
            c = self.dma_count.get(dma_key, 0) + 1
            self.dma_count[dma_key] = c
            op.val = 16 * c
        op.idx = len(self.q[eng])
        self.q[eng].append(op)
        return op

    def emit(self, nc, stack):
        for e in ENGS:
            c = 0
            for op in self.q[e]:
                if op.is_dma:
                    continue
                if op.signal:
                    c += 1
                    op.val = c
        sems = {}

        def sem_of(key):
            if key not in sems:
                sems[key] = stack.enter_context(nc.semaphore("s%d" % len(sems)))
            return sems[key]

        for e in ("pe", "act", "dve", "pool"):
            sem_of(("eng", e))
        for k in self.dma_count:
            sem_of(("dma", k))

        def tok(d):
            if d.is_dma:
                return ("dma", d.key), d.val
            return ("eng", d.eng), d.val

        block = stack.enter_context(nc.Block())
        engobj = {"pe": block.tensor, "act": block.scalar, "dve": block.vector,
                  "pool": block.gpsimd, "sp": block.sync}

        def run_queue(e, eng):
            waited = {}
            for op in self.q[e]:
                need = {}
                for d in op.deps:
                    k, v = tok(d)
                    if waited.get(k, 0) >= v:
                        continue
                    if need.get(k, 0) < v:
                        need[k] = v
                for k, v in need.items():
                    eng.wait_ge(sem_of(k), v)
                    waited[k] = v
                inst = op.fn(eng)
                if op.is_dma:
                    inst.then_inc(sem_of(("dma", op.key)), 16)
                elif op.signal:
                    inst.then_inc(sem_of(("eng", e)), 1)
            if e == "sp":
                for k, c in self.dma_count.items():
                    eng.wait_ge(sem_of(("dma", k)), 16 * c)

        for e in ENGS:
            def body(eng, e=e):
                run_queue(e, eng)
            engobj[e](body)


def _drain(gen):
    for _ in gen:
        pass


def _interleave(main, filler, k=1):
    fdone = False
    for _ in main:
        for _ in range(k):
            if not fdone:
                try:
                    next(filler)
                except StopIteration:
                    fdone = True
    if not fdone:
        _drain(filler)


def build_program(stop=None):
    nc = bass.Bass("TRN2", target_bir_lowering=False)

    def din(name, shape):
        return nc.dram_tensor(name, list(shape), F32, kind="ExternalInput").ap()

    def dout(name, shape):
        return nc.dram_tensor(name, list(shape), F32, kind="ExternalOutput").ap()

    xc = din("xc", [TOKP, D])
    ckT = din("ckT", [2, 128, 16, 128])
    ck = din("ck", [16, 128, 256])
    cv = din("cv", [16, 128, 256])
    w_in = din("w_in", [D, INW])
    w_pa = din("w_pa", [1024, D])
    w_pb = din("w_pb", [1024, D])
    w_o = din("w_o", [D, D])
    w_g = din("w_g", [D, DFF])
    w_u = din("w_u", [D, DFF])
    w_d = din("w_d", [DFF, D])
    g1T = din("g1T", [128, 16])
    g2T = din("g2T", [128, 16])
    gfB = din("gfB", [128, D])
    gnB = din("gnB", [128, 1024])
    wsT = din("wsT", [8, 128, 128])
    wsTr = din("wsTr", [8, 128, 128])
    trilT = din("trilT", [128, 128])
    blkm = din("blkm", [128, 128])
    bsB = din("bsB", [128, 8, 128])
    bsBs = din("bsBs", [128, 8, 128])
    sinksB = din("sinksB", [128, 16])
    sinkrep = din("sinkrep", [128, 1])
    maskN = din("maskN", [128, 256])
    mask0 = din("mask0", [128, 256])
    maskSc = din("maskSc", [128, 128])
    maskSn = din("maskSn", [128, 16, 128])
    identd = din("ident", [128, 128])
    sinksB2 = din("sinksB2", [128, 32])
    sinkrep2 = din("sinkrep2", [128, 2])

    y_out = dout("y", [TOK, D])
    kv7_out = dout("kv7", [128, 512])
    skw_out = dout("skw", [16, 128, 256])
    svw_out = dout("svw", [16, 128, 256])
    sg_out = dout("sg", [128, 1024])

    S = Sched()

    with contextlib.ExitStack() as st:
        R1 = 112640
        R2 = 36864
        NSLOT = 4
        WS = 8192
        SS = 12288
        SMALL = 7168
        TOTAL = R1 + R2 + NSLOT * WS + SS + SMALL
        arena = st.enter_context(nc.sbuf_tensor("arena", [128, TOTAL // 4], F32))

        def view(off, nbytes, dt=F32, pat=None, **kw):
            assert off % 4 == 0 and nbytes % 4 == 0
            ap = arena[:, off // 4:(off + nbytes) // 4]
            if dt is not F32:
                ap = ap.bitcast(dt)
            if pat is not None:
                ap = ap.rearrange(pat, **kw)
            return ap

        hT = view(0, 40960, BF16, "p (c t) -> p c t", c=16)
        QT = view(40960, 18432, BF16, "p (c t) -> p c t", c=8)
        uT = view(59392, 18432, BF16, "p (c t) -> p c t", c=8)
        VLN_OFF = 77824
        KTd = view(96256, 10240, BF16, "p (h t) -> p h t", h=4)
        Vb = view(106496, 5120, BF16, "p (b c) -> p b c", b=10)
        gvf = view(59392, 36864, F32, "p (t c) -> p t c", t=NT)
        x1 = view(0, 73728, F32, "p (t c) -> p t c", t=NT)
        hhT = view(73728, 36864, BF16, "p (c t) -> p c t", c=16)
        r2 = R1
        mixT = view(r2, 36864, BF16, "p (c t) -> p c t", c=16)
        w0 = R1 + R2
        wslot = [w0 + i * WS for i in range(NSLOT)]
        wres = [Res("w%d" % i) for i in range(NSLOT)]
        s0 = w0 + NSLOT * WS
        hb = [view(s0 + i * 4096, 4096, BF16) for i in range(3)]
        hbres = [Res("hb%d" % i) for i in range(3)]
        m0 = s0 + SS
        _sm = [m0]

        def small(nbytes, dt=F32, pat=None, **kw):
            off = _sm[0]
            _sm[0] += (nbytes + 31) // 32 * 32
            assert _sm[0] <= m0 + SMALL, "small region overflow"
            return view(off, nbytes, dt, pat, **kw)

        ident = small(256, BF16)
        maskNb = small(512, BF16)
        mask0b = small(512, BF16)
        g1t = small(64)
        g2t = small(64)
        sinks_t = small(64)
        sinkrep_t = small(4)
        sinks_b2 = small(64, BF16)
        sinkrep_b2 = small(4, BF16)
        statA = small(4 * 64)
        st_att = small(4 * 6 * 2 * 16, F32, "p (q b h) -> p q b h", q=6, b=2)
        bnst = small(4 * 12 * 2, F32, "p (b k) -> p b k", b=2)
        bnmv = small(4 * 4 * 2, F32, "p (b k) -> p b k", b=2)
        st2 = small(4 * 64)
        SMALL_EXTRA = _sm[0]
        _sm[0] += 160 + 448
        assert _sm[0] <= m0 + SMALL

        banks = [st.enter_context(nc.psum_tensor("bank%d" % i, [128, 512], F32)) for i in range(8)]
        bres = [Res("bank%d" % i, excl=True) for i in range(8)]

        def bankbf(i):
            return banks[i][:].bitcast(BF16)

        cnt = {"ev": 0}

        def tcols(i):
            return slice(i * 128, (i + 1) * 128)

        def kblk(i):
            return 0 if i == 9 else i + 1

        def ev_engine():
            cnt["ev"] += 1
            return "act" if cnt["ev"] % 2 else "dve"

        def copy_op(eng_name, out, in_, reads, writes, scale=None):
            if eng_name == "act":
                if scale is None:
                    S.add("act", lambda e: e.copy(out=out, in_=in_), reads=reads, writes=writes)
                else:
                    S.add("act", lambda e: e.mul(out=out, in_=in_, mul=scale), reads=reads, writes=writes)
            else:
                if scale is None:
                    S.add(eng_name, lambda e: e.tensor_copy(out=out, in_=in_), reads=reads, writes=writes)
                else:
                    S.add(eng_name, lambda e: e.tensor_scalar_mul(out=out, in0=in_, scalar1=scale),
                          reads=reads, writes=writes)

        wq = []
        wstate = {"issued": 0, "consumed": 0}

        def wblock(parts):
            wq.append(parts)
            return len(wq) - 1

        def w_issue_upto(n):
            while wstate["issued"] < min(n, len(wq)):
                b = wstate["issued"]
                s = b % NSLOT
                for (vf, src) in wq[b]:
                    dst = vf(wslot[s])
                    S.add("pool", lambda e, dst=dst, src=src: e.dma_start(out=dst, in_=src),
                          writes=[wres[s]], dma_key="w%d" % s)
                wstate["issued"] += 1

        def w_use(b):
            assert b < wstate["consumed"] + NSLOT, (b, wstate)
            w_issue_upto(wstate["consumed"] + NSLOT)
            return wslot[b % NSLOT], wres[b % NSLOT]

        def w_done(b):
            wstate["consumed"] = b + 1
            w_issue_upto(wstate["consumed"] + NSLOT)

        def v_k16(off):
            return view(off, 8192, BF16, "p (k c) -> p k c", k=16)

        def blk_k16(w, c0):
            return wblock([(v_k16, w[:, c0:c0 + 256].rearrange("(k p) c -> p k c", p=128))])

        B_K = blk_k16(w_in, 1024)
        B_V = blk_k16(w_in, 1280)
        B_Q = [blk_k16(w_in, 0 + 256 * i) for i in range(4)]
        B_GV = [blk_k16(w_in, 2560 + 256 * i) for i in range(4)]
        B_U = [blk_k16(w_in, 1536 + 256 * i) for i in range(4)]
        B_GATE = []
        for i in range(8):
            ga = blk_k16(w_in, 3584 + 256 * i)
            gb = blk_k16(w_in, 5632 + 256 * i)
            pab = wblock([
                (lambda off: view(off, 4096, BF16, "p (k c) -> p k c", k=8),
                 w_pa[:, 256 * i:256 * i + 256].rearrange("(k p) c -> p k c", p=128)),
                (lambda off: view(off + 4096, 4096, BF16, "p (k c) -> p k c", k=8),
                 w_pb[:, 256 * i:256 * i + 256].rearrange("(k p) c -> p k c", p=128)),
            ])
            B_GATE.append((ga, gb, pab))
        B_WO = [blk_k16(w_o, 256 * i) for i in range(8)]
        NG = 11
        B_FF = []
        for gi in range(NG):
            f0 = gi * 4
            gu = []
            for hf in range(2):
                c0 = (f0 + 2 * hf) * 128
                gu.append((blk_k16(w_g, c0), blk_k16(w_u, c0)))
            dn = []
            for ch in range(2):
                dn.append(wblock([(lambda off: view(off, 8192, BF16, "p (f c) -> p f c", f=4),
                                   w_d[f0 * 128:(f0 + 4) * 128, ch * 1024:(ch + 1) * 1024]
                                   .rearrange("(f p) c -> p f c", p=128))]))
            B_FF.append((gu, dn))

        def cload(dst, src, cast=False):
            if cast:
                S.add("pool", lambda e: e.dma_start(out=dst, in_=src), dma_key="constc")
            else:
                S.add("sp", lambda e: e.dma_start(out=dst, in_=src), dma_key="const")

        cload(ident, identd[:, :], cast=True)
        cload(maskNb, maskN[:, :], cast=True)
        cload(mask0b, mask0[:, :], cast=True)
        cload(g1t, g1T[:, :])
        cload(g2t, g2T[:, :])
        cload(sinks_t, sinksB[:, :])
        cload(sinkrep_t, sinkrep[:, :])
        cload(sinks_b2, sinksB2[:, :], cast=True)
        cload(sinkrep_b2, sinkrep2[:, :], cast=True)
        constA = Res("constA")
        constB = Res("constB")
        constA.last_w = [op for op in S.q["pool"] if op.key == "constc"][-1]
        constB.last_w = [op for op in S.q["sp"] if op.key == "const"][-1]
        CONST = [constA, constB]
        eps_t = small(4)
        epsR = Res("eps")
        S.add("pool", lambda e: e.memset(eps_t, EPS), writes=[epsR])

        xtv = [view(r2 + i * 8192, 8192) for i in range(2)]
        xtres = [Res("xt%d" % i) for i in range(2)]
        hTres = [[Res("hT%d_%d" % (i, c)) for c in range(16)] for i in range(10)]
        order = [9] + list(range(9))
        KVB = r2 + 16384
        kd = [view(KVB + i * 1024, 1024, BF16, "p (h u d) -> p h u d", h=4, u=2) for i in range(2)]
        kdres = [Res("kd%d" % i) for i in range(2)]
        kvf = [view(KVB + 2048 + i * 2048, 2048) for i in range(2)]
        kvfres = [Res("kvf%d" % i) for i in range(2)]
        Kb8 = view(KVB + 6144, 512, BF16)
        KT8 = view(KVB + 6656, 512, BF16, "p (c t) -> p c t", c=2)
        Kb8res = Res("Kb8")
        KT8res = Res("KT8")
        KTres = [Res("KT%d" % i) for i in range(10)]
        Vres = [Res("V%d" % i) for i in range(10)]

        def stageA1(n, i):
            b = n % 2
            r0 = TOK if i == 9 else i * 128
            S.add("sp", lambda e: e.dma_start(out=xtv[b], in_=xc[r0:r0 + 128, :]),
                  writes=[xtres[b]], dma_key="xt%d" % b)
            ssr = Res("ss")
            ss = statA[:, 2 * n:2 * n + 1]
            rs = statA[:, 2 * n + 1:2 * n + 2]
            S.add("act", lambda e: e.activation(out=hb[b], in_=xtv[b], func=AF.Square, accum_out=ss),
                  reads=[xtres[b]], writes=[hbres[b], ssr])
            S.add("act", lambda e: e.activation(out=rs, in_=ss, func=AF.Sqrt, bias=eps_t[:, 0:1], scale=1.0 / D),
                  reads=[ssr, epsR], writes=[ssr])
            S.add("dve", lambda e: e.reciprocal(out=rs, in_=rs), reads=[ssr], writes=[ssr])
            S.add("dve", lambda e: e.tensor_scalar_mul(out=hb[b], in0=xtv[b], scalar1=rs),
                  reads=[ssr, xtres[b], hbres[b]], writes=[hbres[b]])

        def stageA2(n, i):
            b = n % 2
            for half in range(2):
                bk = 4 + (2 * n + half) % 4
                for j in range(8):
                    c = half * 8 + j
                    S.add("pe", lambda e, j=j, c=c, bk=bk: e.transpose(
                        out=bankbf(bk)[:, j * 128:(j + 1) * 128], in_=hb[b][:, c * 128:(c + 1) * 128],
                        identity=ident), reads=[hbres[b]] + CONST, writes=[bres[bk]])
                en = "act" if half == 0 else "dve"
                for j in range(8):
                    c = half * 8 + j
                    src = bankbf(bk)[:, j * 128:(j + 1) * 128]
                    dst = hT[:, c, tcols(i)]
                    if en == "act":
                        S.add("act", lambda e, dst=dst, src=src, c=c: e.activation(
                            out=dst, in_=src, func=AF.Copy, scale=g1t[:, c:c + 1]),
                            reads=[bres[bk]] + CONST, writes=[hTres[i][c]])
                    else:
                        S.add("dve", lambda e, dst=dst, src=src, c=c: e.tensor_scalar_mul(
                            out=dst, in0=src, scalar1=g1t[:, c:c + 1]),
                            reads=[bres[bk]] + CONST, writes=[hTres[i][c]])

        offK, rK = w_use(B_K)
        offV, rV = w_use(B_V)
        wk = v_k16(offK)
        wv = v_k16(offV)
        sa = 59392
        Qz = view(sa, 4096, BF16, "p (h c) -> p h c", h=16)
        QTz = view(sa + 4096, 8192, BF16, "p (c s h t) -> p c s h t", c=2, s=16, h=16)
        ckTb = view(sa + 12288, 8192, BF16, "p (c s k) -> p c s k", c=2, s=16)
        cvb = view(sa + 20480, 8192, BF16, "p (s c) -> p s c", s=16)
        mSn = view(sa + 28672, 4096, BF16, "p (s k) -> p s k", s=16)
        mSc = view(sa + 32768, 256, BF16)
        Osm1 = view(sa + 33024, 2048, BF16, "p (s d) -> p s d", s=16)
        SAres = Res("sample_attn_bufs")
        Qzres = Res("Qz")
        QTzres = Res("QTz")
        S.add("pool", lambda e: e.memset(view(sa, 4096, BF16), 0.0), writes=[Qzres])
        S.add("pool", lambda e: e.memset(view(sa + 4096, 8192, BF16), 0.0), writes=[QTzres])
        S.add("pool", lambda e: e.dma_start(out=ckTb, in_=ckT.rearrange("c p s k -> p c s k")),
              writes=[SAres], dma_key="sa")
        S.add("pool", lambda e: e.dma_start(out=cvb, in_=cv.rearrange("s k c -> k s c")),
              writes=[SAres], dma_key="sa")
        S.add("pool", lambda e: e.dma_start(out=mSn, in_=maskSn[:, :, :]), writes=[SAres], dma_key="sa")
        S.add("pool", lambda e: e.dma_start(out=mSc, in_=maskSc[:, :]), writes=[SAres], dma_key="sa")


        def stageKV(n, i):
            bk = n % 2
            for kc in range(16):
                S.add("pe", lambda e, kc=kc: e.matmul(
                    banks[bk][:, 0:256], lhsT=hT[:, kc, tcols(i)], rhs=wk[:, kc, :], start=(kc == 0), stop=(kc == 15)),
                    reads=[hTres[i][kc], rK], writes=[bres[bk]])
            for kc in range(16):
                S.add("pe", lambda e, kc=kc: e.matmul(
                    banks[bk][:, 256:512], lhsT=hT[:, kc, tcols(i)], rhs=wv[:, kc, :], start=(kc == 0), stop=(kc == 15)),
                    reads=[hTres[i][kc], rV], writes=[bres[bk]])
            kdb = kd[n % 2]
            kdr = kdres[n % 2]
            kin = banks[bk][:, 0:256].rearrange("p (h d) -> p h d", h=4)
            S.add("dve", lambda e: e.tensor_copy(out=kdb[:, :, 0, :], in_=kin), reads=[bres[bk]], writes=[kdr])
            S.add("dve", lambda e: e.tensor_copy(out=kdb[:, :, 1, :], in_=kin), reads=[bres[bk], kdr], writes=[kdr])
            vdst = Vb[:, kblk(i), :]
            S.add("dve", lambda e: e.tensor_copy(out=vdst, in_=banks[bk][:, 256:512]), reads=[bres[bk]], writes=[Vres[i]])
            if i in (7, 8):
                kvb = kvf[i - 7]
                kvr = kvfres[i - 7]
                S.add("dve", lambda e: e.tensor_copy(out=kvb, in_=banks[bk][:, :]), reads=[bres[bk]], writes=[kvr])
                if i == 7:
                    S.add("sp", lambda e: e.dma_start(out=kv7_out[:, :], in_=kvb), reads=[kvr], dma_key="okv7")
                else:
                    for s in range(16):
                        S.add("sp", lambda e, s=s: e.dma_start(
                            out=skw_out[s, 120:128, :], in_=kvb[s * 8:(s + 1) * 8, 0:256]), reads=[kvr], dma_key="oskw")
                        S.add("sp", lambda e, s=s: e.dma_start(
                            out=svw_out[s, 120:128, :], in_=kvb[s * 8:(s + 1) * 8, 256:512]), reads=[kvr], dma_key="osvw")
                    S.add("dve", lambda e: e.tensor_copy(out=Kb8, in_=banks[bk][:, 0:256]), reads=[bres[bk]], writes=[Kb8res])
            tb = 2 + n % 2
            for h in range(4):
                S.add("pe", lambda e, h=h: e.transpose(
                    out=bankbf(tb)[:, h * 128:(h + 1) * 128],
                    in_=kdb[:, h, :, :].rearrange("p u d -> p (u d)"), identity=ident),
                    reads=[kdr] + CONST, writes=[bres[tb]])
            kdst = KTd[:, :, kblk(i) * 128:(kblk(i) + 1) * 128]
            S.add("act", lambda e: e.copy(out=kdst, in_=bankbf(tb)[:, 0:512].rearrange("p (h t) -> p h t", h=4)),
                  reads=[bres[tb]], writes=[KTres[i]])
            if i == 8:
                for c in range(2):
                    S.add("pe", lambda e, c=c: e.transpose(
                        out=bankbf(tb)[:, 512 + c * 128:512 + (c + 1) * 128], in_=Kb8[:, c * 128:(c + 1) * 128],
                        identity=ident), reads=[Kb8res] + CONST, writes=[bres[tb]])
                S.add("act", lambda e: e.copy(
                    out=KT8, in_=bankbf(tb)[:, 512:768].rearrange("p (c t) -> p c t", c=2)),
                    reads=[bres[tb]], writes=[KT8res])

        stageA1(0, order[0])
        for n, i in enumerate(order):
            if n + 1 < len(order):
                stageA1(n + 1, order[n + 1])
            stageA2(n, i)
            if n >= 1:
                stageKV(n - 1, order[n - 1])
        stageKV(len(order) - 1, order[-1])
        w_done(B_V)
        S.add("sp", lambda e: e.dma_start(out=skw_out[:, 0:120, :], in_=ck[:, 8:128, :]), dma_key="ockw")
        S.add("sp", lambda e: e.dma_start(out=svw_out[:, 0:120, :], in_=cv[:, 8:128, :]), dma_key="ocvw")

        TG = [(0, 384), (384, 768), (768, 1152)]
        QTres = [Res("QT%d" % i) for i in range(NT)]

        def tiles_of_tg(tg):
            return [tg * 3, tg * 3 + 1, tg * 3 + 2]

        fm_bank = {"n": 0}

        def fm_group(wt, cl, K, src, srcres_fn, tg, extra_reads, banklist):
            bk = banklist[fm_bank["n"] % len(banklist)]
            fm_bank["n"] += 1
            lo, hi = TG[tg]
            for kc in range(K):
                S.add("pe", lambda e, kc=kc: e.matmul(
                    banks[bk][:, 0:384], lhsT=wt[:, kc, cl * 128:(cl + 1) * 128], rhs=src[:, kc, lo:hi],
                    start=(kc == 0), stop=(kc == K - 1)),
                    reads=[srcres_fn(t, kc) for t in tiles_of_tg(tg)] + extra_reads, writes=[bres[bk]])
            return bk, lo, hi

        hTr = lambda t, kc: hTres[t][kc]
        q_w = []
        for wb in range(4):
            off, rw = w_use(B_Q[wb])
            q_w.append((v_k16(off), rw))

        def q_group(wb, cl, tg, banklist):
            wt, rw = q_w[wb]
            m = wb * 2 + cl
            bk, lo, hi = fm_group(wt, cl, 16, hT, hTr, tg, [rw], banklist)
            copy_op(ev_engine(), QT[:, m, lo:hi], banks[bk][:, 0:384], [bres[bk]],
                    [QTres[t] for t in tiles_of_tg(tg)], scale=0.125)

        def gen_q_rest():
            for tg in (1, 2):
                for wb in range(4):
                    for cl in range(2):
                        q_group(wb, cl, tg, [7])
                        if tg == 2 and cl == 1:
                            w_done(B_Q[wb])
                        yield

        for wb in range(4):
            wt, rw = q_w[wb]
            for cl in range(2):
                q_group(wb, cl, 0, [0, 1, 2, 3, 4, 5])
            for kc in range(16):
                S.add("pe", lambda e, kc=kc, wt=wt: e.matmul(
                    banks[6][:, 0:256], lhsT=hT[:, kc, tcols(8)], rhs=wt[:, kc, :], start=(kc == 0), stop=(kc == 15)),
                    reads=[hTres[8][kc], rw], writes=[bres[6]])
            eh = wb % 2
            S.add("dve", lambda e, wb=wb, eh=eh: e.tensor_scalar_mul(
                out=Qz[:, 4 * wb:4 * wb + 4, eh * 64:(eh + 1) * 64],
                in0=banks[6][:, 0:256].rearrange("p (h d) -> p h d", h=4), scalar1=0.125),
                reads=[bres[6], Qzres], writes=[Qzres])
            for hh in range(4 * wb, 4 * wb + 4):
                S.add("pe", lambda e, hh=hh: e.transpose(out=bankbf(7)[:, (hh % 4) * 128:(hh % 4 + 1) * 128],
                                                         in_=Qz[:, hh, :], identity=ident),
                      reads=[Qzres] + CONST, writes=[bres[7]])
            for hh in range(4 * wb, 4 * wb + 4):
                copy_op("act" if hh % 2 else "dve", QTz[:, wb // 2, :, hh, :],
                        bankbf(7)[:, (hh % 4) * 128:(hh % 4 + 1) * 128].rearrange("p (s t) -> p s t", s=16),
                        [bres[7], QTzres], [QTzres])

        ATB = r2 + 24576
        Pb = [view(ATB + i * 512, 512, BF16) for i in range(2)]
        PTb = [view(ATB + 1024 + i * 512, 512, BF16) for i in range(2)]
        atm = [view(ATB + 2048 + i * 2048, 2048, BF16) for i in range(2)]
        Osm = view(ATB + 6144, 4096, BF16, "p (s u d) -> p s u d", s=16, u=2)
        Pres = [Res("P%d" % i) for i in range(2)]
        PTres = [Res("PT%d" % i) for i in range(2)]
        atmres = [Res("atm%d" % i) for i in range(2)]
        stres = {}

        def sres(q, b, h):
            k = (q, b, h)
            if k not in stres:
                stres[k] = Res("st%s" % (k,))
            return stres[k]

        att_n = {"n": 0}

        def softmax_head(sbank, b2, h, sink_ap, per_head_stats):
            n = att_n["n"]
            att_n["n"] += 1
            pb = n % 2
            mx = st_att[:, 0, b2, h:h + 1]
            ngm = st_att[:, 1, b2, h:h + 1]
            rsum = st_att[:, 2, b2, h:h + 1]
            R = [sres(q, b2, h) for q in range(6)]
            S.add("dve", lambda e: e.reduce_max(out=ngm, in_=banks[sbank][:, 0:258], axis=AX.X, negate=True),
                  reads=[bres[sbank]], writes=[R[1]])
            S.add("act", lambda e: e.activation(out=Pb[pb], in_=banks[sbank][:, 0:256], func=AF.Exp, bias=ngm,
                                                scale=1.0, accum_out=rsum),
                  reads=[bres[sbank], R[1]], writes=[Pres[pb], R[2]])
            if per_head_stats:
                es = st_att[:, 3, b2, h:h + 1]
                den = st_att[:, 4, b2, h:h + 1]
                rden = st_att[:, 5, b2, h:h + 1]
                S.add("act", lambda e: e.activation(out=es, in_=ngm, func=AF.Exp, bias=sink_ap, scale=1.0),
                      reads=[R[1]] + CONST, writes=[R[3]])
                S.add("dve", lambda e: e.tensor_tensor(out=den, in0=rsum, in1=es, op=ALU.add),
                      reads=[R[2], R[3]], writes=[R[4]])
                S.add("dve", lambda e: e.reciprocal(out=rden, in_=den), reads=[R[4]], writes=[R[5]])
            return pb

        def transpose_probs(pb):
            tb = 2 + pb
            for c in range(2):
                S.add("pe", lambda e, c=c: e.transpose(out=bankbf(tb)[:, c * 128:(c + 1) * 128],
                                                       in_=Pb[pb][:, c * 128:(c + 1) * 128], identity=ident),
                      reads=[Pres[pb]] + CONST, writes=[bres[tb]])
            copy_op("act", PTb[pb], bankbf(tb)[:, 0:256], [bres[tb]], [PTres[pb]])

        Osres = Res("Osm1")

        def sample_scores(s):
            sbank = s % 2
            S.add("pe", lambda e: e.matmul(banks[sbank][:, 256:258], lhsT=ident, rhs=sinkrep_b2,
                                           start=True, stop=True), reads=CONST, writes=[bres[sbank]])
            for c in range(2):
                S.add("pe", lambda e, c=c: e.matmul(
                    banks[sbank][:, 0:128], lhsT=QTz[:, c, s, :, :].rearrange("p h t -> p (h t)"), rhs=ckTb[:, c, s, :],
                    start=(c == 0), stop=False), reads=[QTzres, SAres], writes=[bres[sbank]])
            S.add("pe", lambda e: e.matmul(banks[sbank][:, 0:128], lhsT=ident, rhs=mSc, start=False, stop=True),
                  reads=CONST + [SAres], writes=[bres[sbank]])
            for c in range(2):
                S.add("pe", lambda e, c=c: e.matmul(
                    banks[sbank][:, 128:256], lhsT=QTz[:, c, s, :, :].rearrange("p h t -> p (h t)"), rhs=KT8[:, c, :],
                    start=(c == 0), stop=False), reads=[QTzres, KT8res], writes=[bres[sbank]])
            S.add("pe", lambda e: e.matmul(banks[sbank][:, 128:256], lhsT=ident, rhs=mSn[:, s, :], start=False, stop=True),
                  reads=CONST + [SAres], writes=[bres[sbank]])

        spb = {}

        def sample_E1(s):
            spb[s] = softmax_head(s % 2, s % 2, s, sinkrep_t[:, 0:1], True)

        def sample_V(s):
            b2 = s % 2
            pb = spb[s]
            ob = 4 + s % 2
            S.add("pe", lambda e, ob=ob, pb=pb, s=s: e.matmul(
                banks[ob][:, 0:256], lhsT=PTb[pb][:, 0:128], rhs=cvb[:, s, :], start=True, stop=False),
                reads=[PTres[pb], SAres], writes=[bres[ob]])
            S.add("pe", lambda e, ob=ob, pb=pb: e.matmul(
                banks[ob][:, 0:256], lhsT=PTb[pb][:, 128:256], rhs=Vb[:, 9, :], start=False, stop=True),
                reads=[PTres[pb], Vres[8]], writes=[bres[ob]])
            for g in range(4):
                S.add("dve", lambda e, ob=ob, g=g, s=s, b2=b2: e.tensor_scalar_mul(
                    out=Osm1[32 * g:32 * g + 32, s, :], in0=banks[ob][32 * g:32 * g + 32, 64 * g:64 * g + 64],
                    scalar1=st_att[32 * g:32 * g + 32, 5, b2, s:s + 1]),
                    reads=[bres[ob], sres(5, b2, s)], writes=[Osres])

        def gen_sample():
            sample_scores(0)
            sample_scores(1)
            sample_E1(0)
            sample_scores(2)
            sample_E1(1)
            transpose_probs(spb[0])
            for s in range(16):
                if s + 3 < 16:
                    sample_scores(s + 3)
                if s + 2 < 16:
                    sample_E1(s + 2)
                if s + 1 < 16:
                    transpose_probs(spb[s + 1])
                sample_V(s)
                yield

        qrest = gen_q_rest()
        for _ in gen_sample():
            try:
                next(qrest)
            except StopIteration:
                pass
        _drain(qrest)
        Osmres = Res("Osm")
        for u in range(2):
            S.add("dve", lambda e, u=u: e.tensor_copy(out=Osm[:, :, u, :], in_=Osm1), reads=[Osres, Osmres], writes=[Osmres])
        for s in range(16):
            tb = 6 + s % 2
            S.add("pe", lambda e, tb=tb, s=s: e.transpose(out=bankbf(tb)[:, 0:128],
                                                          in_=Osm[:, s, :, :].rearrange("p u d -> p (u d)"), identity=ident),
                  reads=[Osmres] + CONST, writes=[bres[tb]])
            for eh in range(2):
                srcv = bankbf(tb)[eh * 64:(eh + 1) * 64, 0:128].rearrange("p (j u t) -> p j u t", j=8, u=2)[:, :, eh, :]
                dstv = QT[eh * 64:(eh + 1) * 64, :, 1024 + s * 8:1024 + (s + 1) * 8]
                copy_op("act" if s % 2 else "dve", dstv, srcv, [bres[tb]], [QTres[8]])
        attnT = QT
        attnres = QTres

        sample_dead = [SAres, Qzres, QTzres, Osres]

        def prompt_scores(t, h):
            kvh = h // 4
            base = (h % 2) * 64
            ch = h // 2
            sbank = h % 2
            mk = mask0b if t == 0 else maskNb
            kcols = slice(t * 128, t * 128 + 256)
            prev_i = 9 if t == 0 else t - 1
            S.add("pe", lambda e: e.matmul(banks[sbank][:, 256:258], lhsT=ident, rhs=sinks_b2[:, 2 * h:2 * h + 2],
                                           start=True, stop=True), reads=CONST, writes=[bres[sbank]])
            S.add("pe", lambda e: e.matmul(
                banks[sbank][:, 0:256], lhsT=QT[base:base + 64, ch, tcols(t)], rhs=KTd[base:base + 64, kvh, kcols],
                start=True, stop=False),
                reads=[QTres[t], KTres[prev_i], KTres[t]], writes=[bres[sbank]])
            S.add("pe", lambda e: e.matmul(banks[sbank][:, 0:256], lhsT=ident, rhs=mk, start=False, stop=True),
                  reads=CONST, writes=[bres[sbank]])

        ATT_HEADS = [(t, h) for t in range(8) for h in range(16)]
        NH = len(ATT_HEADS)
        att_pb = {}

        def att_E1(idx):
            t, h = ATT_HEADS[idx]
            att_pb[idx] = softmax_head(h % 2, t % 2, h, sinks_t[:, h:h + 1], False)

        def att_E2(idx):
            pb = att_pb[idx]
            tb = 2 + pb
            for c in range(2):
                S.add("pe", lambda e, c=c: e.transpose(out=bankbf(tb)[:, c * 128:(c + 1) * 128],
                                                       in_=Pb[pb][:, c * 128:(c + 1) * 128], identity=ident),
                      reads=[Pres[pb]] + CONST, writes=[bres[tb]])
            copy_op("dve" if idx % 2 else "act", PTb[pb], bankbf(tb)[:, 0:256], [bres[tb]], [PTres[pb]])

        def att_V(idx):
            t, h = ATT_HEADS[idx]
            pb = att_pb[idx]
            b2 = t % 2
            ab = atm[b2]
            kvh = h // 4
            prev_i = 9 if t == 0 else t - 1
            ob = 4 + h // 8
            oc = (h % 8) * 64
            S.add("pe", lambda e: e.matmul(
                banks[ob][:, oc:oc + 64], lhsT=PTb[pb][:, 0:128], rhs=Vb[:, t, kvh * 64:(kvh + 1) * 64],
                start=True, stop=False), reads=[PTres[pb], Vres[prev_i]], writes=[bres[ob]])
            S.add("pe", lambda e: e.matmul(
                banks[ob][:, oc:oc + 64], lhsT=PTb[pb][:, 128:256], rhs=Vb[:, t + 1, kvh * 64:(kvh + 1) * 64],
                start=False, stop=True), reads=[PTres[pb], Vres[t]], writes=[bres[ob]])
            if h % 8 == 7:
                h0 = h - 7
                hs = slice(h0, h0 + 8)
                Rg = [sres(q, b2, hh) for q in (1, 2) for hh in range(h0, h0 + 8)]
                Wg = [sres(q, b2, hh) for q in (3, 4, 5) for hh in range(h0, h0 + 8)]
                S.add("dve", lambda e: e.tensor_tensor(out=st_att[:, 3, b2, hs], in0=st_att[:, 1, b2, hs],
                                                       in1=sinks_t[:, hs], op=ALU.add),
                      reads=Rg + CONST, writes=Wg)
                S.add("act", lambda e: e.activation(out=st_att[:, 3, b2, hs], in_=st_att[:, 3, b2, hs], func=AF.Exp),
                      reads=Wg, writes=Wg)
                S.add("dve", lambda e: e.tensor_tensor(out=st_att[:, 4, b2, hs], in0=st_att[:, 2, b2, hs],
                                                       in1=st_att[:, 3, b2, hs], op=ALU.add), reads=Rg + Wg, writes=Wg)
                S.add("dve", lambda e: e.reciprocal(out=st_att[:, 5, b2, hs], in_=st_att[:, 4, b2, hs]),
                      reads=Wg, writes=Wg)
                rdb = st_att[:, 5, b2, hs].unsqueeze(2).broadcast_to([128, 8, 64])
                S.add("dve", lambda e: e.tensor_tensor(
                    out=ab[:, h0 * 64:(h0 + 8) * 64].rearrange("p (h d) -> p h d", h=8),
                    in0=banks[ob][:, :].rearrange("p (h d) -> p h d", h=8), in1=rdb, op=ALU.mult),
                    reads=[bres[ob]] + Wg, writes=[atmres[b2]])
            if h == 15:
                for j in range(8):
                    S.add("pe", lambda e, j=j: e.transpose(out=bankbf(6)[:, j * 128:(j + 1) * 128],
                                                           in_=ab[:, j * 128:(j + 1) * 128], identity=ident),
                          reads=[atmres[b2]] + CONST, writes=[bres[6]])
                copy_op("act", QT[:, :, tcols(t)], bankbf(6)[:, 0:1024].rearrange("p (c t) -> p c t", c=8),
                        [bres[6]], [QTres[t]])

        def gen_att_prompt():
            prompt_scores(*ATT_HEADS[0])
            prompt_scores(*ATT_HEADS[1])
            att_E1(0)
            prompt_scores(*ATT_HEADS[2])
            att_E1(1)
            att_E2(0)
            for k in range(NH):
                if k + 3 < NH:
                    prompt_scores(*ATT_HEADS[k + 3])
                if k + 2 < NH:
                    att_E1(k + 2)
                if k + 1 < NH:
                    att_E2(k + 1)
                att_V(k)
                yield

        gvres = [Res("gvf%d" % i) for i in range(NT)]
        vlnb = view(r2, 18432, BF16, "p (t c) -> p t c", t=NT)
        gnb = view(r2 + 18432, 4096)
        vlnres = [Res("vln%d" % i) for i in range(NT)]
        gnres = Res("gnB")
        uTres = [Res("uT%d" % i) for i in range(NT)]
        lnres = [Res("ln0"), Res("ln1")]

        bnall = view(SMALL_EXTRA, 4 * NT * 4, F32, "p (t k) -> p t k", t=NT)
        bnst9 = view(SMALL_EXTRA + 160, 4 * NT * 12, F32, "p (t k) -> p t k", t=NT)

        def gen_gv():
            r2old = [xtres[0], xtres[1], KT8res, Kb8res] + kdres + kvfres
            S.add("sp", lambda e: e.dma_start(out=gnb, in_=gnB[:, :]), writes=[gnres] + r2old, dma_key="gn")
            for wb in range(4):
                off, rw = w_use(B_GV[wb])
                wt = v_k16(off)
                for t in range(NT):
                    for _ in gv_group(wb, t, wt, rw):
                        yield
                w_done(B_GV[wb])

        def gv_group(wb, t, wt, rw):
            bk = 7
            for kc in range(16):
                S.add("pe", lambda e, kc=kc: e.matmul(
                    banks[bk][:, 0:256], lhsT=hT[:, kc, tcols(t)], rhs=wt[:, kc, :], start=(kc == 0), stop=(kc == 15)),
                    reads=[hTres[t][kc], rw], writes=[bres[bk]])
                if kc % 4 == 3 and kc != 15:
                    yield
            S.add("dve", lambda e: e.tensor_copy(out=gvf[:, t, wb * 256:(wb + 1) * 256], in_=banks[bk][:, 0:256]),
                  reads=[bres[bk]], writes=[gvres[t]] + (sample_dead if wb == 0 else []))
            yield

        def u_group(wb, cl, tg, wt, rw):
            m = wb * 2 + cl
            bk, lo, hi = fm_group(wt, cl, 16, hT, hTr, tg, [rw], [0, 1, 2, 3, 4, 5, 6, 7])
            S.add("act", lambda e: e.activation(out=uT[:, m, lo:hi], in_=banks[bk][:, 0:384], func=AF.Gelu_apprx_tanh),
                  reads=[bres[bk]],
                  writes=[uTres[t] for t in tiles_of_tg(tg)]
                  + [gvres[t] for t in range((m * 2304) // 4096, ((m + 1) * 2304 - 1) // 4096 + 1)])

        def ln_part1():
            R = lnres[0]
            for t in range(NT):
                S.add("act", lambda e, t=t: e.activation(out=gvf[:, t, :], in_=gvf[:, t, :], func=AF.Gelu_apprx_tanh),
                      reads=[gvres[t]], writes=[gvres[t]])
                for c in range(2):
                    S.add("dve", lambda e, t=t, c=c: e.bn_stats(out=bnst9[:, t, c * 6:(c + 1) * 6],
                                                                in_=gvf[:, t, c * 512:(c + 1) * 512]),
                          reads=[gvres[t]], writes=[R])
                S.add("dve", lambda e, t=t: e.bn_aggr(out=bnall[:, t, 0:2], in_=bnst9[:, t, :]), reads=[R], writes=[R])

        def ln_part2():
            R = lnres[0]
            S.add("act", lambda e: e.activation(out=bnall[:, :, 2], in_=bnall[:, :, 1], func=AF.Sqrt, bias=eps_t[:, 0:1],
                                                scale=1.0), reads=[R, epsR], writes=[R])
            S.add("dve", lambda e: e.reciprocal(out=bnall[:, :, 3], in_=bnall[:, :, 2]), reads=[R], writes=[R])
            for t in range(NT):
                S.add("dve", lambda e, t=t: e.tensor_scalar(out=gvf[:, t, :], in0=gvf[:, t, :], scalar1=bnall[:, t, 0:1],
                                                            scalar2=bnall[:, t, 3:4], op0=ALU.subtract, op1=ALU.mult),
                      reads=[R, gvres[t]], writes=[gvres[t]])
                S.add("dve", lambda e, t=t: e.tensor_tensor(out=vlnb[:, t, :], in0=gvf[:, t, :], in1=gnb, op=ALU.mult),
                      reads=[gvres[t], gnres], writes=[vlnres[t]])
                if t == 8:
                    S.add("dve", lambda e, t=t: e.tensor_tensor(out=gvf[:, t, :], in0=gvf[:, t, :], in1=gnb, op=ALU.mult),
                          reads=[gvres[t], gnres, vlnres[t]], writes=[gvres[t]])
                    S.add("sp", lambda e, t=t: e.dma_start(out=sg_out[:, :], in_=gvf[:, t, :]), reads=[gvres[t]], dma_key="osg")

        _interleave(gen_att_prompt(), gen_gv(), k=1)
        ln_part1()
        ln_part2()
        for wb in range(4):
            off, rw = w_use(B_U[wb])
            wt = v_k16(off)
            for cl in range(2):
                for tg in range(3):
                    u_group(wb, cl, tg, wt, rw)
            w_done(B_U[wb])

        SPB = r2 + 22528
        wst = view(SPB, 4096, F32, "p (g t) -> p g t", g=8)
        wsm = view(SPB + 4096, 2048, BF16, "p (g t) -> p g t", g=8)
        wsms = view(SPB + 6144, 2048, BF16, "p (g t) -> p g t", g=8)
        msk = view(SPB + 8192, 512)
        bsb = view(SPB + 8704, 4096, F32, "p (g t) -> p g t", g=8)
        wres_sp = Res("wst")
        mres = Res("msk")
        wsmres = Res("wsm")
        bsres = Res("bsb")
        att_bufs = Pres + PTres + atmres + [Osmres]
        for which in range(2):
            srcw = wsT if which == 0 else wsTr
            srcm = trilT if which == 0 else blkm
            dstw = wsm if which == 0 else wsms
            S.add("sp", lambda e, srcw=srcw: e.dma_start(out=wst, in_=srcw.rearrange("g s t -> s g t")),
                  writes=[wres_sp] + (att_bufs if which == 0 else []), dma_key="wst")
            S.add("sp", lambda e, srcm=srcm: e.dma_start(out=msk, in_=srcm[:, :]),
                  writes=[mres] + (att_bufs if which == 0 else []), dma_key="msk")
            for g in range(8):
                S.add("dve", lambda e, g=g, dstw=dstw: e.tensor_tensor(out=dstw[:, g, :], in0=wst[:, g, :], in1=msk,
                                                                      op=ALU.mult),
                      reads=[wres_sp, mres], writes=[wsmres])
        S.add("sp", lambda e: e.dma_start(out=bsb, in_=bsB[:, :, :]), writes=[bsres] + att_bufs, dma_key="bsb")
        bsbs = wst
        S.add("sp", lambda e: e.dma_start(out=bsbs, in_=bsBs[:, :, :]), writes=[wres_sp], dma_key="wst")
        sptmp = [view(96256 + i * 2048, 2048) for i in range(2)]
        sptres = [Res("sptmp%d" % i) for i in range(2)]
        kv_dead = KTres + Vres
        for t in range(NT):
            wm = wsms if t == 8 else wsm
            bb = bsbs if t == 8 else bsb
            for half in range(2):
                n = t * 2 + half
                bk = n % 4
                for gl in range(4):
                    g = half * 4 + gl
                    S.add("pe", lambda e, bk=bk, gl=gl, g=g, t=t, wm=wm: e.matmul(
                        banks[bk][:, gl * 128:(gl + 1) * 128], lhsT=vlnb[:, t, g * 128:(g + 1) * 128], rhs=wm[:, g, :],
                        start=True, stop=True), reads=[vlnres[t], wsmres], writes=[bres[bk]])
                tp = sptmp[n % 2]
                S.add("dve", lambda e, bk=bk, tp=tp, bb=bb, half=half: e.tensor_tensor(
                    out=tp, in0=banks[bk][:, :], in1=bb[:, half * 4:(half + 1) * 4, :].rearrange("p g t -> p (g t)"),
                    op=ALU.add), reads=[bres[bk], bsres, wres_sp], writes=[sptres[n % 2]] + (kv_dead if n < 2 else []))
                S.add("pool" if n % 2 else "dve", lambda e, tp=tp, half=half, t=t: e.tensor_tensor(
                    out=uT[:, half * 4:(half + 1) * 4, tcols(t)], in0=uT[:, half * 4:(half + 1) * 4, tcols(t)],
                    in1=tp.rearrange("p (g t) -> p g t", g=4), op=ALU.mult),
                    reads=[sptres[n % 2], uTres[t]], writes=[uTres[t]])
        aT = uT
        aTres = uTres

        mixres = [Res("mix%d" % i) for i in range(NT)]
        sga = [view(VLN_OFF + i * 1536, 1536) for i in range(6)]
        sgb = [view(VLN_OFF + 9216 + i * 1536, 1536) for i in range(6)]
        sgares = [Res("sga%d" % i) for i in range(6)]
        sgbres = [Res("sgb%d" % i) for i in range(6)]
        R2old = vlnres + [gnres, wres_sp, mres, wsmres, bsres]
        gate_first = {"a": True, "m": True}
        ALLB = list(range(8))
        for i in range(8):
            ga, gb, pab = B_GATE[i]
            for (blk, dst, dres) in ((ga, sga, sgares), (gb, sgb, sgbres)):
                off, rw = w_use(blk)
                wt = v_k16(off)
                for cl in range(2):
                    for tg in range(3):
                        q = cl * 3 + tg
                        bk, lo, hi = fm_group(wt, cl, 16, hT, hTr, tg, [rw], ALLB)
                        old = (gvres + sptres) if gate_first["a"] else []
                        S.add("act", lambda e, q=q, bk=bk, dst=dst: e.activation(out=dst[q], in_=banks[bk][:, 0:384],
                                                                                  func=AF.Sigmoid),
                              reads=[bres[bk]], writes=[dres[q]] + old)
                gate_first["a"] = False
                w_done(blk)
            offp, rp = w_use(pab)
            wpa = view(offp, 4096, BF16, "p (k c) -> p k c", k=8)
            wpb = view(offp + 4096, 4096, BF16, "p (k c) -> p k c", k=8)
            for cl in range(2):
                m = i * 2 + cl
                for tg in range(3):
                    q = cl * 3 + tg
                    tl = tiles_of_tg(tg)
                    bka, lo, hi = fm_group(wpa, cl, 8, aT, lambda t, kc: aTres[t], tg, [rp], ALLB)
                    S.add("dve", lambda e, q=q, bka=bka: e.tensor_tensor(out=sga[q], in0=sga[q], in1=banks[bka][:, 0:384],
                                                                          op=ALU.mult),
                          reads=[sgares[q], bres[bka]], writes=[sgares[q]])
                    bkb, lo, hi = fm_group(wpb, cl, 8, attnT, lambda t, kc: attnres[t], tg, [rp], ALLB)
                    S.add("dve", lambda e, q=q, bkb=bkb: e.tensor_tensor(out=sgb[q], in0=sgb[q], in1=banks[bkb][:, 0:384],
                                                                          op=ALU.mult),
                          reads=[sgbres[q], bres[bkb]], writes=[sgbres[q]])
                    S.add("pool", lambda e, q=q, m=m, lo=lo, hi=hi: e.tensor_tensor(
                        out=mixT[:, m, lo:hi], in0=sga[q], in1=sgb[q], op=ALU.add),
                        reads=[sgares[q], sgbres[q]], writes=[mixres[t] for t in tl] + (R2old if gate_first["m"] else []))
                    gate_first["m"] = False
            w_done(pab)

        x1res = [Res("x1_%d" % i) for i in range(NT)]
        R1B_all = [r for l in hTres for r in l] + QTres + uTres + gvres + KTres + Vres + sptres + sgares + sgbres
        hT_all = [r for l in hTres for r in l]
        rest_all = QTres + uTres + gvres + KTres + Vres + sptres + sgares + sgbres
        for t in range(NT):
            guard = hT_all if t < 5 else (rest_all if t == 5 else [])
            S.add("sp", lambda e, t=t: e.dma_start(out=x1[:, t, :], in_=xc[t * 128:(t + 1) * 128, :]),
                  writes=[x1res[t]] + guard, dma_key="x1_%d" % (t % 3))
        hhres = [Res("hh%d" % i) for i in range(NT)]

        rms_state = {}

        def rms_a(t, n):
            b = n % 3
            ssr = Res("ss2")
            ss = st2[:, 2 * (n % 16):2 * (n % 16) + 1]
            rs = st2[:, 2 * (n % 16) + 1:2 * (n % 16) + 2]
            S.add("act", lambda e: e.activation(out=hb[b], in_=x1[:, t, :], func=AF.Square, accum_out=ss),
                  reads=[x1res[t]], writes=[hbres[b], ssr])
            rms_state[n] = (ss, rs, ssr)

        def rms_b1(n):
            ss, rs, ssr = rms_state[n]
            S.add("act", lambda e: e.activation(out=rs, in_=ss, func=AF.Sqrt, bias=eps_t[:, 0:1], scale=1.0 / D),
                  reads=[ssr, epsR], writes=[ssr])

        def rms_b(n):
            ss, rs, ssr = rms_state[n]
            S.add("dve", lambda e: e.reciprocal(out=rs, in_=rs), reads=[ssr], writes=[ssr])
            return rs, ssr

        def norm_stage1b(t, n):
            b = n % 3
            rs, ssr = rms_b(n)
            S.add("dve", lambda e: e.tensor_scalar_mul(out=hb[b], in0=x1[:, t, :], scalar1=rs),
                  reads=[ssr, x1res[t], hbres[b]], writes=[hbres[b]])

        def norm_stage2(t, n, gt, dstT, dres, tbanks):
            b = n % 3
            for half in range(2):
                bk = tbanks[(2 * n + half) % len(tbanks)]
                for j in range(8):
                    c = half * 8 + j
                    S.add("pe", lambda e, bk=bk, j=j, c=c: e.transpose(
                        out=bankbf(bk)[:, j * 128:(j + 1) * 128], in_=hb[b][:, c * 128:(c + 1) * 128], identity=ident),
                        reads=[hbres[b]] + CONST, writes=[bres[bk]])
                en_ = "act" if half == 0 else "dve"
                for j in range(8):
                    c = half * 8 + j
                    src = bankbf(bk)[:, j * 128:(j + 1) * 128]
                    dst = dstT[:, c, tcols(t)]
                    if en_ == "act":
                        S.add("act", lambda e, dst=dst, src=src, c=c: e.activation(
                            out=dst, in_=src, func=AF.Copy, scale=gt[:, c:c + 1]),
                            reads=[bres[bk]] + CONST, writes=[dres])
                    else:
                        S.add("dve", lambda e, dst=dst, src=src, c=c: e.tensor_scalar_mul(
                            out=dst, in0=src, scalar1=gt[:, c:c + 1]),
                            reads=[bres[bk]] + CONST, writes=[dres])

        wo_n = {"n": 0}

        def wo_group(cg, t, wt, rw):
            bk = wo_n["n"] % 4
            wo_n["n"] += 1
            for kc in range(16):
                S.add("pe", lambda e, kc=kc: e.matmul(
                    banks[bk][:, 0:256], lhsT=mixT[:, kc, tcols(t)], rhs=wt[:, kc, :], start=(kc == 0), stop=(kc == 15)),
                    reads=[mixres[t], rw], writes=[bres[bk]])
            S.add("dve", lambda e: e.tensor_tensor(
                out=x1[:, t, cg * 256:(cg + 1) * 256], in0=banks[bk][:, 0:256], in1=x1[:, t, cg * 256:(cg + 1) * 256],
                op=ALU.add), reads=[bres[bk], x1res[t]], writes=[x1res[t]])

        NCGO = 6
        for cg in range(NCGO):
            off, rw = w_use(B_WO[cg])
            wt = v_k16(off)
            for t in range(NT):
                wo_group(cg, t, wt, rw)
            w_done(B_WO[cg])
        tail_w = []
        for cg in range(NCGO, 8):
            off, rw = w_use(B_WO[cg])
            tail_w.append((cg, v_k16(off), rw))
        for t in range(NT + 1):
            if t < NT:
                for (cg, wt, rw) in tail_w:
                    wo_group(cg, t, wt, rw)
                rms_a(t, t)
                rms_b1(t)
            if 1 <= t <= NT:
                norm_stage1b(t - 1, t - 1)
            if 2 <= t:
                norm_stage2(t - 2, t - 2, g2t, hhT, hhres[t - 2], [4, 5, 6, 7])
        w_done(B_WO[7])

        def norm_drain():
            norm_stage2(NT - 1, NT - 1, g2t, hhT, hhres[NT - 1], [4, 5, 6, 7])

        actT = [view(r2 + i * 9216, 9216, BF16, "p (f t) -> p f t", f=4) for i in range(2)]
        actres = [[Res("act%d_%d" % (i, t)) for t in range(NT)] for i in range(2)]
        silt = [view(r2 + 18432 + i * 1536, 1536) for i in range(2)]
        silres = [Res("sil%d" % i) for i in range(2)]
        gfbv = view(r2 + 21504, 8192)
        gfres = Res("gfb")
        S.add("sp", lambda e: e.dma_start(out=gfbv, in_=gfB[:, :]), writes=[gfres] + mixres, dma_key="gf")
        ffn_n = {"n": 0}
        dn_n = {"n": 0}
        for gi in range(NG):
            gu, dn = B_FF[gi]
            ab = gi % 2
            for hf in range(2):
                offg, rg = w_use(gu[hf][0])
                offu, ru = w_use(gu[hf][1])
                wg_ = v_k16(offg)
                wu_ = v_k16(offu)
                cltg = [(cl, tg) for cl in range(2) for tg in range(3)]
                if gi == 0 and hf == 0:
                    cltg = [(0, 0), (0, 1), (1, 0), (1, 1), None, (0, 2), (1, 2)]
                for ct in cltg:
                    if ct is None:
                        norm_drain()
                        continue
                    cl, tg = ct
                    f = hf * 2 + cl
                    if True:
                        lo, hi = TG[tg]
                        n = ffn_n["n"]
                        ffn_n["n"] += 1
                        bg = (n % 2) * 2
                        bu = bg + 1
                        tl = tiles_of_tg(tg)
                        for (bk, wt, rr) in ((bg, wg_, rg), (bu, wu_, ru)):
                            for kc in range(16):
                                S.add("pe", lambda e, bk=bk, kc=kc, wt=wt, lo=lo, hi=hi, cl=cl: e.matmul(
                                    banks[bk][:, 0:384], lhsT=wt[:, kc, cl * 128:(cl + 1) * 128], rhs=hhT[:, kc, lo:hi],
                                    start=(kc == 0), stop=(kc == 15)),
                                    reads=[hhres[t] for t in tl] + [rr], writes=[bres[bk]])
                        sl = silt[n % 2]
                        old = mixres if n < 2 else []
                        S.add("act", lambda e, sl=sl, bg=bg: e.activation(out=sl, in_=banks[bg][:, 0:384], func=AF.Silu),
                              reads=[bres[bg]], writes=[silres[n % 2]] + old)
                        S.add("dve", lambda e, sl=sl, bu=bu, ab=ab, f=f, lo=lo, hi=hi: e.tensor_tensor(
                            out=actT[ab][:, f, lo:hi], in0=sl, in1=banks[bu][:, 0:384], op=ALU.mult),
                            reads=[silres[n % 2], bres[bu]], writes=[actres[ab][t] for t in tl] + old)
                w_done(gu[hf][1])
            last = (gi == NG - 1)
            dnw = []
            if last:
                for ch in range(2):
                    offd, rd_ = w_use(dn[ch])
                    dnw.append((view(offd, 8192, BF16, "p (f c) -> p f c", f=4), rd_))
            for ch_t in ([(ch, None) for ch in range(2)] if not last else [(ch, t) for t in range(NT) for ch in range(2)]):
                ch = ch_t[0]
                if not last:
                    offd, rd_ = w_use(dn[ch])
                    wd_ = view(offd, 8192, BF16, "p (f c) -> p f c", f=4)
                    tiles = range(NT)
                else:
                    wd_, rd_ = dnw[ch]
                    tiles = [ch_t[1]]
                for t in tiles:
                    for c2 in range(2):
                        bk = 4 + dn_n["n"] % 4
                        dn_n["n"] += 1
                        for f in range(4):
                            S.add("pe", lambda e, bk=bk, f=f, t=t, c2=c2, ab=ab, wd_=wd_: e.matmul(
                                banks[bk][:, 0:512], lhsT=actT[ab][:, f, tcols(t)], rhs=wd_[:, f, c2 * 512:(c2 + 1) * 512],
                                start=(f == 0), stop=(f == 3)), reads=[actres[ab][t], rd_], writes=[bres[bk]])
                        col = ch * 1024 + c2 * 512
                        S.add("dve", lambda e, bk=bk, t=t, col=col: e.tensor_tensor(
                            out=x1[:, t, col:col + 512], in0=banks[bk][:, 0:512], in1=x1[:, t, col:col + 512], op=ALU.add),
                            reads=[bres[bk], x1res[t]], writes=[x1res[t]])
                    if last and ch == 1:
                        rms_a(t, NT + t)
                        rms_b1(NT + t)
                    for tf in (([t - 1] if t >= 1 else []) + ([t] if t == NT - 1 else [])) if (last and ch == 1) else []:
                        rs, ssr = rms_b(NT + tf)
                        yA = Res("yA")
                        yB = Res("yB")
                        S.add("dve", lambda e, t=tf, rs=rs: e.scalar_tensor_tensor(
                            out=x1[:, t, 0:1024], in0=x1[:, t, 0:1024], scalar=rs, in1=gfbv[:, 0:1024],
                            op0=ALU.mult, op1=ALU.mult), reads=[ssr, x1res[tf], gfres], writes=[yA])
                        S.add("act", lambda e, t=tf, rs=rs: e.activation(out=x1[:, t, 1024:2048], in_=x1[:, t, 1024:2048],
                                                                         func=AF.Copy, scale=rs),
                              reads=[ssr, x1res[tf]], writes=[yB])
                        S.add("pool", lambda e, t=tf: e.tensor_tensor(out=x1[:, t, 1024:2048], in0=x1[:, t, 1024:2048],
                                                                      in1=gfbv[:, 1024:2048], op=ALU.mult),
                              reads=[yB, gfres], writes=[yB])
                        S.add("sp", lambda e, t=tf: e.dma_start(out=y_out[t * 128:(t + 1) * 128, :], in_=x1[:, t, :]),
                              reads=[yA, yB], writes=[x1res[tf]], dma_key="oy%d" % (tf % 3))
                if not last:
                    w_done(dn[ch])
            if last:
                w_done(dn[1])
        S.emit(nc, st)
    return nc


def _consts():
    i = np.arange(128)[:, None]
    j = np.arange(128)[None, :]
    prev = np.where(j >= i, 0.0, NEG).astype(np.float32)
    cur = np.where(j <= i, 0.0, NEG).astype(np.float32)
    maskN = np.concatenate([prev, cur], 1)
    mask0_first = np.concatenate([np.full((128, 128), NEG, np.float32), cur], 1)
    t_row = (np.arange(128) % 8)[:, None]
    maskSc = np.where(j >= t_row, 0.0, NEG).astype(np.float32)
    maskSn = np.full((128, 16, 128), NEG, np.float32)
    for s in range(16):
        for tp in range(8):
            maskSn[:, s, s * 8 + tp] = np.where(tp <= t_row[:, 0], 0.0, NEG)
    trilT = (i <= j).astype(np.float32)
    si, ti = np.arange(128)[:, None], np.arange(128)[None, :]
    blkm = ((si // 8 == ti // 8) & (si % 8 <= ti % 8)).astype(np.float32)
    ident = np.eye(128, dtype=np.float32)
    return dict(maskN=maskN, mask0_first=mask0_first, maskSc=maskSc, maskSn=maskSn, trilT=trilT, blkm=blkm, ident=ident)


_NC_CACHE = {}


def make_in_maps(x_prompt, x_sample, cache_k, cache_v, norm1_g, w_in, gmlp_norm_g, w_s, b_s, sinks,
                 w_pa, w_pb, w_o, norm2_g, w_ff_gate, w_ff_up, w_ff_down, final_g):
    f = lambda a: np.ascontiguousarray(np.asarray(a, dtype=np.float32))
    x_prompt, x_sample, cache_k, cache_v = f(x_prompt), f(x_sample), f(cache_k), f(cache_v)
    C = _consts()

    ws = f(w_s)[0]
    wsT = np.ascontiguousarray(ws.transpose(0, 2, 1))
    wsTr = np.ascontiguousarray(np.tile(wsT[:, 0:8, 0:8], (1, 16, 16)))
    bs = f(b_s)[0]
    bsB = np.ascontiguousarray(np.broadcast_to(bs[None], (128, 8, 128)))
    bsBs = np.ascontiguousarray(np.broadcast_to(np.tile(bs[:, 0:8], (1, 16))[None], (128, 8, 128)))
    sk = f(sinks)[0]
    shared = dict(
        w_in=f(w_in)[0], w_pa=f(w_pa)[0], w_pb=f(w_pb)[0], w_o=f(w_o)[0],
        w_g=f(w_ff_gate)[0], w_u=f(w_ff_up)[0], w_d=f(w_ff_down)[0],
        g1T=np.ascontiguousarray(f(norm1_g)[0].reshape(16, 128).T),
        g2T=np.ascontiguousarray(f(norm2_g)[0].reshape(16, 128).T),
        gfB=np.ascontiguousarray(np.broadcast_to(f(final_g)[None], (128, D))),
        gnB=np.ascontiguousarray(np.broadcast_to(f(gmlp_norm_g)[0][None], (128, 1024))),
        wsT=wsT, wsTr=wsTr, trilT=C["trilT"], blkm=C["blkm"], bsB=bsB, bsBs=bsBs,
        sinksB=np.ascontiguousarray(np.broadcast_to(sk[None], (128, 16))),
        sinkrep=np.ascontiguousarray(np.repeat(sk, 8)[:, None]),
        sinksB2=np.ascontiguousarray(np.broadcast_to(np.repeat(sk, 2)[None], (128, 32))),
        sinkrep2=np.ascontiguousarray(np.repeat(np.repeat(sk, 8)[:, None], 2, axis=1)),
        maskN=C["maskN"], maskSc=C["maskSc"], maskSn=C["maskSn"], ident=C["ident"],
    )
    in_maps = []
    for c in range(NCORES):
        b, half = c // 2, c % 2
        xs = x_sample[c * 16:(c + 1) * 16].reshape(128, D)
        xp = x_prompt[b, half * 1024:(half + 1) * 1024]
        if half == 1:
            prev = x_prompt[b, 896:1024]
            mask0 = C["maskN"]
        else:
            prev = np.zeros((128, D), np.float32)
            mask0 = C["mask0_first"]
        xcat = np.ascontiguousarray(np.concatenate([xp, xs, prev], 0))
        ckc = cache_k[0, c * 16:(c + 1) * 16].reshape(16, 128, 256)
        cvc = cache_v[0, c * 16:(c + 1) * 16].reshape(16, 128, 256)
        ckT = np.ascontiguousarray(ckc.reshape(16, 128, 2, 128).transpose(2, 3, 0, 1))
        m = dict(shared)
        m.update(xc=xcat, ckT=ckT, ck=np.ascontiguousarray(ckc), cv=np.ascontiguousarray(cvc), mask0=mask0)
        in_maps.append(m)
    return in_maps


def kernel(**inputs):
    in_maps = make_in_maps(**inputs)
    if "nc" not in _NC_CACHE:
        _NC_CACHE["nc"] = build_program()
    nc = _NC_CACHE["nc"]
    res = run_bass_kernel_spmd(nc, in_maps, core_ids=list(range(NCORES)))
    return assemble(res.results)


def assemble(R):
    y_prompt = np.empty((4, 2048, D), np.float32)
    y_sample = np.empty((128, 8, D), np.float32)
    pk = np.empty((1, 4, 128, 4, 64), np.float32)
    pv = np.empty((1, 4, 128, 4, 64), np.float32)
    skw = np.empty((1, 128, 128, 4, 64), np.float32)
    svw = np.empty((1, 128, 128, 4, 64), np.float32)
    sg = np.empty((1, 128, 8, 8, 128), np.float32)
    for c in range(NCORES):
        b, half = c // 2, c % 2
        y = R[c]["y"]
        y_prompt[b, half * 1024:(half + 1) * 1024] = y[0:1024]
        y_sample[c * 16:(c + 1) * 16] = y[1024:1152].reshape(16, 8, D)
        if half == 1:
            pk[0, b] = R[c]["kv7"][:, 0:256].reshape(128, 4, 64)
            pv[0, b] = R[c]["kv7"][:, 256:512].reshape(128, 4, 64)
        skw[0, c * 16:(c + 1) * 16] = R[c]["skw"].reshape(16, 128, 4, 64)
        svw[0, c * 16:(c + 1) * 16] = R[c]["svw"].reshape(16, 128, 4, 64)
        sg[0, c * 16:(c + 1) * 16] = R[c]["sg"].reshape(16, 8, 8, 128)
    return (y_prompt, y_sample, pk, pv, skw, svw, sg)
```

```python
import contextlib
import numpy as np
import concourse.bass as bass
import concourse.mybir as mybir
from concourse.bass_utils import run_bass_kernel_spmd

F32 = mybir.dt.float32
BF16 = mybir.dt.bfloat16
AF = mybir.ActivationFunctionType
ALU = mybir.AluOpType
AX = mybir.AxisListType

ENGS = ("pe", "act", "dve", "pool", "sp")

D = 2048
DFF = 5632
INW = 7680
NT = 9
TOK = NT * 128
TOKP = TOK + 128
EPS = 1e-6
NEG = -1e30
NCORES = 8


class Res:
    __slots__ = ("name", "last_w", "readers", "excl")

    def __init__(self, name, excl=False):
        self.name = name
        self.last_w = None
        self.readers = []
        self.excl = excl


class Op:
    __slots__ = ("eng", "fn", "deps", "signal", "is_dma", "key", "val", "idx")

    def __init__(self, eng, fn, is_dma, key):
        self.eng = eng
        self.fn = fn
        self.deps = []
        self.signal = False
        self.is_dma = is_dma
        self.key = key
        self.val = None
        self.idx = None


class Sched:
    def __init__(self):
        self.q = {e: [] for e in ENGS}
        self.dma_count = {}
        self.stopped = False

    def add(self, eng, fn, reads=(), writes=(), dma_key=None):
        if self.stopped:
            return None
        is_dma = dma_key is not None
        op = Op(eng, fn, is_dma, dma_key)
        deps = {}
        for r in reads:
            if r.last_w is not None:
                deps[id(r.last_w)] = r.last_w
            if r.excl:
                for rd in r.readers:
                    if rd.is_dma or rd.eng != eng:
                        deps[id(rd)] = rd
        for w in writes:
            if w.last_w is not None:
                deps[id(w.last_w)] = w.last_w
            for rd in w.readers:
                deps[id(rd)] = rd
        for d in deps.values():
            if d is op:
                continue
            if (not d.is_dma) and (not is_dma) and d.eng == "pe" and eng == "pe":
                continue
            op.deps.append(d)
            d.signal = True
        for r in reads:
            if not is_dma:
                r.readers = [x for x in r.readers if x.is_dma or x.eng != eng]
            r.readers.append(op)
        for w in writes:
            w.last_w = op
            w.readers = []
        if is_dma:
            c = self.dma_count.get(dma_key, 0) + 1
            self.dma_count[dma_key] = c
            op.val = 16 * c
        op.idx = len(self.q[eng])
        self.q[eng].append(op)
        return op

    def emit(self, nc, stack):
        for e in ENGS:
            c = 0
            for op in self.q[e]:
                if op.is_dma:
                    continue
                if op.signal:
                    c += 1
                    op.val = c
        sems = {}

        def sem_of(key):
            if key not in sems:
                sems[key] = stack.enter_context(nc.semaphore("s%d" % len(sems)))
            return sems[key]

        for e in ("pe", "act", "dve", "pool"):
            sem_of(("eng", e))
        for k in self.dma_count:
            sem_of(("dma", k))

        def tok(d):
            if d.is_dma:
                return ("dma", d.key), d.val
            return ("eng", d.eng), d.val

        block = stack.enter_context(nc.Block())
        engobj = {"pe": block.tensor, "act": block.scalar, "dve": block.vector,
                  "pool": block.gpsimd, "sp": block.sync}

        def run_queue(e, eng):
            waited = {}
            for op in self.q[e]:
                need = {}
                for d in op.deps:
                    k, v = tok(d)
                    if waited.get(k, 0) >= v:
                        continue
                    if need.get(k, 0) < v:
                        need[k] = v
                for k, v in need.items():
                    eng.wait_ge(sem_of(k), v)
                    waited[k] = v
                inst = op.fn(eng)
                if op.is_dma:
                    inst.then_inc(sem_of(("dma", op.key)), 16)
                elif op.signal:
                    inst.then_inc(sem_of(("eng", e)), 1)
            if e == "sp":
                for k, c in self.dma_count.items():
                    eng.wait_ge(sem_of(("dma", k)), 16 * c)

        for e in ENGS:
            def body(eng, e=e):
                run_queue(e, eng)
            engobj[e](body)


def _drain(gen):
    for _ in gen:
        pass


def _interleave(main, filler, k=1):
    fdone = False
    for _ in main:
        for _ in range(k):
            if not fdone:
                try:
                    next(filler)
                except StopIteration:
                    fdone = True
    if not fdone:
        _drain(filler)


def build_program(stop=None):
    nc = bass.Bass("TRN2", target_bir_lowering=False)

    def din(name, shape):
        return nc.dram_tensor(name, list(shape), F32, kind="ExternalInput").ap()

    def dout(name, shape):
        return nc.dram_tensor(name, list(shape), F32, kind="ExternalOutput").ap()

    xc = din("xc", [TOKP, D])
    ckT = din("ckT", [2, 128, 16, 128])
    ck = din("ck", [16, 128, 256])
    cv = din("cv", [16, 128, 256])
    w_in = din("w_in", [D, INW])
    w_pa = din("w_pa", [1024, D])
    w_pb = din("w_pb", [1024, D])
    w_o = din("w_o", [D, D])
    w_g = din("w_g", [D, DFF])
    w_u = din("w_u", [D, DFF])
    w_d = din("w_d", [DFF, D])
    g1T = din("g1T", [128, 16])
    g2T = din("g2T", [128, 16])
    gfB = din("gfB", [128, D])
    gnB = din("gnB", [128, 1024])
    wsT = din("wsT", [8, 128, 128])
    wsTr = din("wsTr", [8, 128, 128])
    trilT = din("trilT", [128, 128])
    blkm = din("blkm", [128, 128])
    bsB = din("bsB", [128, 8, 128])
    bsBs = din("bsBs", [128, 8, 128])
    sinksB = din("sinksB", [128, 16])
    sinkrep = din("sinkrep", [128, 1])
    maskN = din("maskN", [128, 256])
    mask0 = din("mask0", [128, 256])
    maskSc = din("maskSc", [128, 128])
    maskSn = din("maskSn", [128, 16, 128])
    identd = din("ident", [128, 128])
    sinksB2 = din("sinksB2", [128, 32])
    sinkrep2 = din("sinkrep2", [128, 2])

    y_out = dout("y", [TOK, D])
    kv7_out = dout("kv7", [128, 512])
    skw_out = dout("skw", [16, 128, 256])
    svw_out = dout("svw", [16, 128, 256])
    sg_out = dout("sg", [128, 1024])

    S = Sched()

    with contextlib.ExitStack() as st:
        R1 = 112640
        R2 = 36864
        NSLOT = 4
        WS = 8192
        SS = 12288
        SMALL = 7168
        TOTAL = R1 + R2 + NSLOT * WS + SS + SMALL
        arena = st.enter_context(nc.sbuf_tensor("arena", [128, TOTAL // 4], F32))

        def view(off, nbytes, dt=F32, pat=None, **kw):
            assert off % 4 == 0 and nbytes % 4 == 0
            ap = arena[:, off // 4:(off + nbytes) // 4]
            if dt is not F32:
                ap = ap.bitcast(dt)
            if pat is not None:
                ap = ap.rearrange(pat, **kw)
            return ap

        hT = view(0, 40960, BF16, "p (c t) -> p c t", c=16)
        QT = view(40960, 18432, BF16, "p (c t) -> p c t", c=8)
        uT = view(59392, 18432, BF16, "p (c t) -> p c t", c=8)
        VLN_OFF = 77824
        KTd = view(96256, 10240, BF16, "p (h t) -> p h t", h=4)
        Vb = view(106496, 5120, BF16, "p (b c) -> p b c", b=10)
        gvf = view(59392, 36864, F32, "p (t c) -> p t c", t=NT)
        x1 = view(0, 73728, F32, "p (t c) -> p t c", t=NT)
        hhT = view(73728, 36864, BF16, "p (c t) -> p c t", c=16)
        r2 = R1
        mixT = view(r2, 36864, BF16, "p (c t) -> p c t", c=16)
        w0 = R1 + R2
        wslot = [w0 + i * WS for i in range(NSLOT)]
        wres = [Res("w%d" % i) for i in range(NSLOT)]
        s0 = w0 + NSLOT * WS
        hb = [view(s0 + i * 4096, 4096, BF16) for i in range(3)]
        hbres = [Res("hb%d" % i) for i in range(3)]
        m0 = s0 + SS
        _sm = [m0]

        def small(nbytes, dt=F32, pat=None, **kw):
            off = _sm[0]
            _sm[0] += (nbytes + 31) // 32 * 32
            assert _sm[0] <= m0 + SMALL, "small region overflow"
            return view(off, nbytes, dt, pat, **kw)

        ident = small(256, BF16)
        maskNb = small(512, BF16)
        mask0b = small(512, BF16)
        g1t = small(64)
        g2t = small(64)
        sinks_t = small(64)
        sinkrep_t = small(4)
        sinks_b2 = small(64, BF16)
        sinkrep_b2 = small(4, BF16)
        statA = small(4 * 64)
        st_att = small(4 * 6 * 2 * 16, F32, "p (q b h) -> p q b h", q=6, b=2)
        bnst = small(4 * 12 * 2, F32, "p (b k) -> p b k", b=2)
        bnmv = small(4 * 4 * 2, F32, "p (b k) -> p b k", b=2)
        st2 = small(4 * 64)
        SMALL_EXTRA = _sm[0]
        _sm[0] += 160 + 448
        assert _sm[0] <= m0 + SMALL

        banks = [st.enter_context(nc.psum_tensor("bank%d" % i, [128, 512], F32)) for i in range(8)]
        bres = [Res("bank%d" % i, excl=True) for i in range(8)]

        def bankbf(i):
            return banks[i][:].bitcast(BF16)

        cnt = {"ev": 0}

        def tcols(i):
            return slice(i * 128, (i + 1) * 128)

        def kblk(i):
            return 0 if i == 9 else i + 1

        def ev_engine():
            cnt["ev"] += 1
            return "act" if cnt["ev"] % 2 else "dve"

        def copy_op(eng_name, out, in_, reads, writes, scale=None):
            if eng_name == "act":
                if scale is None:
                    S.add("act", lambda e: e.copy(out=out, in_=in_), reads=reads, writes=writes)
                else:
                    S.add("act", lambda e: e.mul(out=out, in_=in_, mul=scale), reads=reads, writes=writes)
            else:
                if scale is None:
                    S.add(eng_name, lambda e: e.tensor_copy(out=out, in_=in_), reads=reads, writes=writes)
                else:
                    S.add(eng_name, lambda e: e.tensor_scalar_mul(out=out, in0=in_, scalar1=scale),
                          reads=reads, writes=writes)

        wq = []
        wstate = {"issued": 0, "consumed": 0}

        def wblock(parts):
            wq.append(parts)
            return len(wq) - 1

        def w_issue_upto(n):
            while wstate["issued"] < min(n, len(wq)):
                b = wstate["issued"]
                s = b % NSLOT
                for (vf, src) in wq[b]:
                    dst = vf(wslot[s])
                    S.add("pool", lambda e, dst=dst, src=src: e.dma_start(out=dst, in_=src),
                          writes=[wres[s]], dma_key="w%d" % s)
                wstate["issued"] += 1

        def w_use(b):
            assert b < wstate["consumed"] + NSLOT, (b, wstate)
            w_issue_upto(wstate["consumed"] + NSLOT)
            return wslot[b % NSLOT], wres[b % NSLOT]

        def w_done(b):
            wstate["consumed"] = b + 1
            w_issue_upto(wstate["consumed"] + NSLOT)

        def v_k16(off):
            return view(off, 8192, BF16, "p (k c) -> p k c", k=16)

        def blk_k16(w, c0):
            return wblock([(v_k16, w[:, c0:c0 + 256].rearrange("(k p) c -> p k c", p=128))])

        B_K = blk_k16(w_in, 1024)
        B_V = blk_k16(w_in, 1280)
        B_Q = [blk_k16(w_in, 0 + 256 * i) for i in range(4)]
        B_GV = [blk_k16(w_in, 2560 + 256 * i) for i in range(4)]
        B_U = [blk_k16(w_in, 1536 + 256 * i) for i in range(4)]
        B_GATE = []
        for i in range(8):
            ga = blk_k16(w_in, 3584 + 256 * i)
            gb = blk_k16(w_in, 5632 + 256 * i)
            pab = wblock([
                (lambda off: view(off, 4096, BF16, "p (k c) -> p k c", k=8),
                 w_pa[:, 256 * i:256 * i + 256].rearrange("(k p) c -> p k c", p=128)),
                (lambda off: view(off + 4096, 4096, BF16, "p (k c) -> p k c", k=8),
                 w_pb[:, 256 * i:256 * i + 256].rearrange("(k p) c -> p k c", p=128)),
            ])
            B_GATE.append((ga, gb, pab))
        B_WO = [blk_k16(w_o, 256 * i) for i in range(8)]
        NG = 11
        B_FF = []
        for gi in range(NG):
            f0 = gi * 4
            gu = []
            for hf in range(2):
                c0 = (f0 + 2 * hf) * 128
                gu.append((blk_k16(w_g, c0), blk_k16(w_u, c0)))
            dn = []
            for ch in range(2):
                dn.append(wblock([(lambda off: view(off, 8192, BF16, "p (f c) -> p f c", f=4),
                                   w_d[f0 * 128:(f0 + 4) * 128, ch * 1024:(ch + 1) * 1024]
                                   .rearrange("(f p) c -> p f c", p=128))]))
            B_FF.append((gu, dn))

        def cload(dst, src, cast=False):
            if cast:
                S.add("pool", lambda e: e.dma_start(out=dst, in_=src), dma_key="constc")
            else:
                S.add("sp", lambda e: e.dma_start(out=dst, in_=src), dma_key="const")

        cload(ident, identd[:, :], cast=True)
        cload(maskNb, maskN[:, :], cast=True)
        cload(mask0b, mask0[:, :], cast=True)
        cload(g1t, g1T[:, :])
        cload(g2t, g2T[:, :])
        cload(sinks_t, sinksB[:, :])
        cload(sinkrep_t, sinkrep[:, :])
        cload(sinks_b2, sinksB2[:, :], cast=True)
        cload(sinkrep_b2, sinkrep2[:, :], cast=True)
        constA = Res("constA")
        constB = Res("constB")
        constA.last_w = [op for op in S.q["pool"] if op.key == "constc"][-1]
        constB.last_w = [op for op in S.q["sp"] if op.key == "const"][-1]
        CONST = [constA, constB]
        eps_t = small(4)
        epsR = Res("eps")
        S.add("pool", lambda e: e.memset(eps_t, EPS), writes=[epsR])

        xtv = [view(r2 + i * 8192, 8192) for i in range(2)]
        xtres = [Res("xt%d" % i) for i in range(2)]
        hTres = [[Res("hT%d_%d" % (i, c)) for c in range(16)] for i in range(10)]
        order = [9] + list(range(9))
        KVB = r2 + 16384
        kd = [view(KVB + i * 1024, 1024, BF16, "p (h u d) -> p h u d", h=4, u=2) for i in range(2)]
        kdres = [Res("kd%d" % i) for i in range(2)]
        kvf = [view(KVB + 2048 + i * 2048, 2048) for i in range(2)]
        kvfres = [Res("kvf%d" % i) for i in range(2)]
        Kb8 = view(KVB + 6144, 512, BF16)
        KT8 = view(KVB + 6656, 512, BF16, "p (c t) -> p c t", c=2)
        Kb8res = Res("Kb8")
        KT8res = Res("KT8")
        KTres = [Res("KT%d" % i) for i in range(10)]
        Vres = [Res("V%d" % i) for i in range(10)]

        def stageA1(n, i):
            b = n % 2
            r0 = TOK if i == 9 else i * 128
            S.add("sp", lambda e: e.dma_start(out=xtv[b], in_=xc[r0:r0 + 128, :]),
                  writes=[xtres[b]], dma_key="xt%d" % b)
            ssr = Res("ss")
            ss = statA[:, 2 * n:2 * n + 1]
            rs = statA[:, 2 * n + 1:2 * n + 2]
            S.add("act", lambda e: e.activation(out=hb[b], in_=xtv[b], func=AF.Square, accum_out=ss),
                  reads=[xtres[b]], writes=[hbres[b], ssr])
            S.add("act", lambda e: e.activation(out=rs, in_=ss, func=AF.Sqrt, bias=eps_t[:, 0:1], scale=1.0 / D),
                  reads=[ssr, epsR], writes=[ssr])
            S.add("dve", lambda e: e.reciprocal(out=rs, in_=rs), reads=[ssr], writes=[ssr])
            S.add("dve", lambda e: e.tensor_scalar_mul(out=hb[b], in0=xtv[b], scalar1=rs),
                  reads=[ssr, xtres[b], hbres[b]], writes=[hbres[b]])

        def stageA2(n, i):
            b = n % 2
            for half in range(2):
                bk = 4 + (2 * n + half) % 4
                for j in range(8):
                    c = half * 8 + j
                    S.add("pe", lambda e, j=j, c=c, bk=bk: e.transpose(
                        out=bankbf(bk)[:, j * 128:(j + 1) * 128], in_=hb[b][:, c * 128:(c + 1) * 128],
                        identity=ident), reads=[hbres[b]] + CONST, writes=[bres[bk]])
                en = "act" if half == 0 else "dve"
                for j in range(8):
                    c = half * 8 + j
                    src = bankbf(bk)[:, j * 128:(j + 1) * 128]
                    dst = hT[:, c, tcols(i)]
                    if en == "act":
                        S.add("act", lambda e, dst=dst, src=src, c=c: e.activation(
                            out=dst, in_=src, func=AF.Copy, scale=g1t[:, c:c + 1]),
                            reads=[bres[bk]] + CONST, writes=[hTres[i][c]])
                    else:
                        S.add("dve", lambda e, dst=dst, src=src, c=c: e.tensor_scalar_mul(
                            out=dst, in0=src, scalar1=g1t[:, c:c + 1]),
                            reads=[bres[bk]] + CONST, writes=[hTres[i][c]])

        offK, rK = w_use(B_K)
        offV, rV = w_use(B_V)
        wk = v_k16(offK)
        wv = v_k16(offV)
        sa = 59392
        Qz = view(sa, 4096, BF16, "p (h c) -> p h c", h=16)
        QTz = view(sa + 4096, 8192, BF16, "p (c s h t) -> p c s h t", c=2, s=16, h=16)
        ckTb = view(sa + 12288, 8192, BF16, "p (c s k) -> p c s k", c=2, s=16)
        cvb = view(sa + 20480, 8192, BF16, "p (s c) -> p s c", s=16)
        mSn = view(sa + 28672, 4096, BF16, "p (s k) -> p s k", s=16)
        mSc = view(sa + 32768, 256, BF16)
        Osm1 = view(sa + 33024, 2048, BF16, "p (s d) -> p s d", s=16)
        SAres = Res("sample_attn_bufs")
        Qzres = Res("Qz")
        QTzres = Res("QTz")
        S.add("pool", lambda e: e.memset(view(sa, 4096, BF16), 0.0), writes=[Qzres])
        S.add("pool", lambda e: e.memset(view(sa + 4096, 8192, BF16), 0.0), writes=[QTzres])
        S.add("pool", lambda e: e.dma_start(out=ckTb, in_=ckT.rearrange("c p s k -> p c s k")),
              writes=[SAres], dma_key="sa")
        S.add("pool", lambda e: e.dma_start(out=cvb, in_=cv.rearrange("s k c -> k s c")),
              writes=[SAres], dma_key="sa")
        S.add("pool", lambda e: e.dma_start(out=mSn, in_=maskSn[:, :, :]), writes=[SAres], dma_key="sa")
        S.add("pool", lambda e: e.dma_start(out=mSc, in_=maskSc[:, :]), writes=[SAres], dma_key="sa")


        def stageKV(n, i):
            bk = n % 2
            for kc in range(16):
                S.add("pe", lambda e, kc=kc: e.matmul(
                    banks[bk][:, 0:256], lhsT=hT[:, kc, tcols(i)], rhs=wk[:, kc, :], start=(kc == 0), stop=(kc == 15)),
                    reads=[hTres[i][kc], rK], writes=[bres[bk]])
            for kc in range(16):
                S.add("pe", lambda e, kc=kc: e.matmul(
                    banks[bk][:, 256:512], lhsT=hT[:, kc, tcols(i)], rhs=wv[:, kc, :], start=(kc == 0), stop=(kc == 15)),
                    reads=[hTres[i][kc], rV], writes=[bres[bk]])
            kdb = kd[n % 2]
            kdr = kdres[n % 2]
            kin = banks[bk][:, 0:256].rearrange("p (h d) -> p h d", h=4)
            S.add("dve", lambda e: e.tensor_copy(out=kdb[:, :, 0, :], in_=kin), reads=[bres[bk]], writes=[kdr])
            S.add("dve", lambda e: e.tensor_copy(out=kdb[:, :, 1, :], in_=kin), reads=[bres[bk], kdr], writes=[kdr])
            vdst = Vb[:, kblk(i), :]
            S.add("dve", lambda e: e.tensor_copy(out=vdst, in_=banks[bk][:, 256:512]), reads=[bres[bk]], writes=[Vres[i]])
            if i in (7, 8):
                kvb = kvf[i - 7]
                kvr = kvfres[i - 7]
                S.add("dve", lambda e: e.tensor_copy(out=kvb, in_=banks[bk][:, :]), reads=[bres[bk]], writes=[kvr])
                if i == 7:
                    S.add("sp", lambda e: e.dma_start(out=kv7_out[:, :], in_=kvb), reads=[kvr], dma_key="okv7")
                else:
                    for s in range(16):
                        S.add("sp", lambda e, s=s: e.dma_start(
                            out=skw_out[s, 120:128, :], in_=kvb[s * 8:(s + 1) * 8, 0:256]), reads=[kvr], dma_key="oskw")
                        S.add("sp", lambda e, s=s: e.dma_start(
                            out=svw_out[s, 120:128, :], in_=kvb[s * 8:(s + 1) * 8, 256:512]), reads=[kvr], dma_key="osvw")
                    S.add("dve", lambda e: e.tensor_copy(out=Kb8, in_=banks[bk][:, 0:256]), reads=[bres[bk]], writes=[Kb8res])
            tb = 2 + n % 2
            for h in range(4):
                S.add("pe", lambda e, h=h: e.transpose(
                    out=bankbf(tb)[:, h * 128:(h + 1) * 128],
                    in_=kdb[:, h, :, :].rearrange("p u d -> p (u d)"), identity=ident),
                    reads=[kdr] + CONST, writes=[bres[tb]])
            kdst = KTd[:, :, kblk(i) * 128:(kblk(i) + 1) * 128]
            S.add("act", lambda e: e.copy(out=kdst, in_=bankbf(tb)[:, 0:512].rearrange("p (h t) -> p h t", h=4)),
                  reads=[bres[tb]], writes=[KTres[i]])
            if i == 8:
                for c in range(2):
                    S.add("pe", lambda e, c=c: e.transpose(
                        out=bankbf(tb)[:, 512 + c * 128:512 + (c + 1) * 128], in_=Kb8[:, c * 128:(c + 1) * 128],
                        identity=ident), reads=[Kb8res] + CONST, writes=[bres[tb]])
                S.add("act", lambda e: e.copy(
                    out=KT8, in_=bankbf(tb)[:, 512:768].rearrange("p (c t) -> p c t", c=2)),
                    reads=[bres[tb]], writes=[KT8res])

        stageA1(0, order[0])
        for n, i in enumerate(order):
            if n + 1 < len(order):
                stageA1(n + 1, order[n + 1])
            stageA2(n, i)
            if n >= 1:
                stageKV(n - 1, order[n - 1])
        stageKV(len(order) - 1, order[-1])
        w_done(B_V)
        S.add("sp", lambda e: e.dma_start(out=skw_out[:, 0:120, :], in_=ck[:, 8:128, :]), dma_key="ockw")
        S.add("sp", lambda e: e.dma_start(out=svw_out[:, 0:120, :], in_=cv[:, 8:128, :]), dma_key="ocvw")

        TG = [(0, 384), (384, 768), (768, 1152)]
        QTres = [Res("QT%d" % i) for i in range(NT)]

        def tiles_of_tg(tg):
            return [tg * 3, tg * 3 + 1, tg * 3 + 2]

        fm_bank = {"n": 0}

        def fm_group(wt, cl, K, src, srcres_fn, tg, extra_reads, banklist):
            bk = banklist[fm_bank["n"] % len(banklist)]
            fm_bank["n"] += 1
            lo, hi = TG[tg]
            for kc in range(K):
                S.add("pe", lambda e, kc=kc: e.matmul(
                    banks[bk][:, 0:384], lhsT=wt[:, kc, cl * 128:(cl + 1) * 128], rhs=src[:, kc, lo:hi],
                    start=(kc == 0), stop=(kc == K - 1)),
                    reads=[srcres_fn(t, kc) for t in tiles_of_tg(tg)] + extra_reads, writes=[bres[bk]])
            return bk, lo, hi

        hTr = lambda t, kc: hTres[t][kc]
        q_w = []
        for wb in range(4):
            off, rw = w_use(B_Q[wb])
            q_w.append((v_k16(off), rw))

        def q_group(wb, cl, tg, banklist):
            wt, rw = q_w[wb]
            m = wb * 2 + cl
            bk, lo, hi = fm_group(wt, cl, 16, hT, hTr, tg, [rw], banklist)
            copy_op(ev_engine(), QT[:, m, lo:hi], banks[bk][:, 0:384], [bres[bk]],
                    [QTres[t] for t in tiles_of_tg(tg)], scale=0.125)

        def gen_q_rest():
            for tg in (1, 2):
                for wb in range(4):
                    for cl in range(2):
                        q_group(wb, cl, tg, [7])
                        if tg == 2 and cl == 1:
                            w_done(B_Q[wb])
                        yield

        for wb in range(4):
            wt, rw = q_w[wb]
            for cl in range(2):
                q_group(wb, cl, 0, [0, 1, 2, 3, 4, 5])
            for kc in range(16):
                S.add("pe", lambda e, kc=kc, wt=wt: e.matmul(
                    banks[6][:, 0:256], lhsT=hT[:, kc, tcols(8)], rhs=wt[:, kc, :], start=(kc == 0), stop=(kc == 15)),
                    reads=[hTres[8][kc], rw], writes=[bres[6]])
            eh = wb % 2
            S.add("dve", lambda e, wb=wb, eh=eh: e.tensor_scalar_mul(
                out=Qz[:, 4 * wb:4 * wb + 4, eh * 64:(eh + 1) * 64],
                in0=banks[6][:, 0:256].rearrange("p (h d) -> p h d", h=4), scalar1=0.125),
                reads=[bres[6], Qzres], writes=[Qzres])
            for hh in range(4 * wb, 4 * wb + 4):
                S.add("pe", lambda e, hh=hh: e.transpose(out=bankbf(7)[:, (hh % 4) * 128:(hh % 4 + 1) * 128],
                                                         in_=Qz[:, hh, :], identity=ident),
                      reads=[Qzres] + CONST, writes=[bres[7]])
            for hh in range(4 * wb, 4 * wb + 4):
                copy_op("act" if hh % 2 else "dve", QTz[:, wb // 2, :, hh, :],
                        bankbf(7)[:, (hh % 4) * 128:(hh % 4 + 1) * 128].rearrange("p (s t) -> p s t", s=16),
                        [bres[7], QTzres], [QTzres])

        ATB = r2 + 24576
        Pb = [view(ATB + i * 512, 512, BF16) for i in range(2)]
        PTb = [view(ATB + 1024 + i * 512, 512, BF16) for i in range(2)]
        atm = [view(ATB + 2048 + i * 2048, 2048, BF16) for i in range(2)]
        Osm = view(ATB + 6144, 4096, BF16, "p (s u d) -> p s u d", s=16, u=2)
        Pres = [Res("P%d" % i) for i in range(2)]
        PTres = [Res("PT%d" % i) for i in range(2)]
        atmres = [Res("atm%d" % i) for i in range(2)]
        stres = {}

        def sres(q, b, h):
            k = (q, b, h)
            if k not in stres:
                stres[k] = Res("st%s" % (k,))
            return stres[k]

        att_n = {"n": 0}

        def softmax_head(sbank, b2, h, sink_ap, per_head_stats):
            n = att_n["n"]
            att_n["n"] += 1
            pb = n % 2
            mx = st_att[:, 0, b2, h:h + 1]
            ngm = st_att[:, 1, b2, h:h + 1]
            rsum = st_att[:, 2, b2, h:h + 1]
            R = [sres(q, b2, h) for q in range(6)]
            S.add("dve", lambda e: e.reduce_max(out=ngm, in_=banks[sbank][:, 0:258], axis=AX.X, negate=True),
                  reads=[bres[sbank]], writes=[R[1]])
            S.add("act", lambda e: e.activation(out=Pb[pb], in_=banks[sbank][:, 0:256], func=AF.Exp, bias=ngm,
                                                scale=1.0, accum_out=rsum),
                  reads=[bres[sbank], R[1]], writes=[Pres[pb], R[2]])
            if per_head_stats:
                es = st_att[:, 3, b2, h:h + 1]
                den = st_att[:, 4, b2, h:h + 1]
                rden = st_att[:, 5, b2, h:h + 1]
                S.add("act", lambda e: e.activation(out=es, in_=ngm, func=AF.Exp, bias=sink_ap, scale=1.0),
                      reads=[R[1]] + CONST, writes=[R[3]])
                S.add("dve", lambda e: e.tensor_tensor(out=den, in0=rsum, in1=es, op=ALU.add),
                      reads=[R[2], R[3]], writes=[R[4]])
                S.add("dve", lambda e: e.reciprocal(out=rden, in_=den), reads=[R[4]], writes=[R[5]])
            return pb

        def transpose_probs(pb):
            tb = 2 + pb
            for c in range(2):
                S.add("pe", lambda e, c=c: e.transpose(out=bankbf(tb)[:, c * 128:(c + 1) * 128],
                                                       in_=Pb[pb][:, c * 128:(c + 1) * 128], identity=ident),
                      reads=[Pres[pb]] + CONST, writes=[bres[tb]])
            copy_op("act", PTb[pb], bankbf(tb)[:, 0:256], [bres[tb]], [PTres[pb]])

        Osres = Res("Osm1")

        def sample_scores(s):
            sbank = s % 2
            S.add("pe", lambda e: e.matmul(banks[sbank][:, 256:258], lhsT=ident, rhs=sinkrep_b2,
                                           start=True, stop=True), reads=CONST, writes=[bres[sbank]])
            for c in range(2):
                S.add("pe", lambda e, c=c: e.matmul(
                    banks[sbank][:, 0:128], lhsT=QTz[:, c, s, :, :].rearrange("p h t -> p (h t)"), rhs=ckTb[:, c, s, :],
                    start=(c == 0), stop=False), reads=[QTzres, SAres], writes=[bres[sbank]])
            S.add("pe", lambda e: e.matmul(banks[sbank][:, 0:128], lhsT=ident, rhs=mSc, start=False, stop=True),
                  reads=CONST + [SAres], writes=[bres[sbank]])
            for c in range(2):
                S.add("pe", lambda e, c=c: e.matmul(
                    banks[sbank][:, 128:256], lhsT=QTz[:, c, s, :, :].rearrange("p h t -> p (h t)"), rhs=KT8[:, c, :],
                    start=(c == 0), stop=False), reads=[QTzres, KT8res], writes=[bres[sbank]])
            S.add("pe", lambda e: e.matmul(banks[sbank][:, 128:256], lhsT=ident, rhs=mSn[:, s, :], start=False, stop=True),
                  reads=CONST + [SAres], writes=[bres[sbank]])

        spb = {}

        def sample_E1(s):
            spb[s] = softmax_head(s % 2, s % 2, s, sinkrep_t[:, 0:1], True)

        def sample_V(s):
            b2 = s % 2
            pb = spb[s]
            ob = 4 + s % 2
            S.add("pe", lambda e, ob=ob, pb=pb, s=s: e.matmul(
                banks[ob][:, 0:256], lhsT=PTb[pb][:, 0:128], rhs=cvb[:, s, :], start=True, stop=False),
                reads=[PTres[pb], SAres], writes=[bres[ob]])
            S.add("pe", lambda e, ob=ob, pb=pb: e.matmul(
                banks[ob][:, 0:256], lhsT=PTb[pb][:, 128:256], rhs=Vb[:, 9, :], start=False, stop=True),
                reads=[PTres[pb], Vres[8]], writes=[bres[ob]])
            for g in range(4):
                if g % 2 == 0:
                    S.add("dve", lambda e, ob=ob, g=g, s=s, b2=b2: e.tensor_scalar_mul(
                        out=Osm1[32 * g:32 * g + 32, s, :], in0=banks[ob][32 * g:32 * g + 32, 64 * g:64 * g + 64],
                        scalar1=st_att[32 * g:32 * g + 32, 5, b2, s:s + 1]),
                        reads=[bres[ob], sres(5, b2, s)], writes=[Osres])
                else:
                    S.add("act", lambda e, ob=ob, g=g, s=s, b2=b2: e.activation(
                        out=Osm1[32 * g:32 * g + 32, s, :], in_=banks[ob][32 * g:32 * g + 32, 64 * g:64 * g + 64],
                        func=AF.Copy, scale=st_att[32 * g:32 * g + 32, 5, b2, s:s + 1]),
                        reads=[bres[ob], sres(5, b2, s)], writes=[Osres])

        def gen_sample():
            sample_scores(0)
            sample_scores(1)
            sample_E1(0)
            sample_scores(2)
            sample_E1(1)
            transpose_probs(spb[0])
            for s in range(16):
                if s + 3 < 16:
                    sample_scores(s + 3)
                if s + 2 < 16:
                    sample_E1(s + 2)
                if s + 1 < 16:
                    transpose_probs(spb[s + 1])
                sample_V(s)
                yield

        qrest = gen_q_rest()
        for _ in gen_sample():
            try:
                next(qrest)
            except StopIteration:
                pass
        _drain(qrest)
        Osmres = Res("Osm")
        for u in range(2):
            S.add("dve", lambda e, u=u: e.tensor_copy(out=Osm[:, :, u, :], in_=Osm1), reads=[Osres, Osmres], writes=[Osmres])
        for s in range(16):
            tb = 6 + s % 2
            S.add("pe", lambda e, tb=tb, s=s: e.transpose(out=bankbf(tb)[:, 0:128],
                                                          in_=Osm[:, s, :, :].rearrange("p u d -> p (u d)"), identity=ident),
                  reads=[Osmres] + CONST, writes=[bres[tb]])
            for eh in range(2):
                srcv = bankbf(tb)[eh * 64:(eh + 1) * 64, 0:128].rearrange("p (j u t) -> p j u t", j=8, u=2)[:, :, eh, :]
                dstv = QT[eh * 64:(eh + 1) * 64, :, 1024 + s * 8:1024 + (s + 1) * 8]
                copy_op("act" if s % 2 else "dve", dstv, srcv, [bres[tb]], [QTres[8]])
        attnT = QT
        attnres = QTres

        sample_dead = [SAres, Qzres, QTzres, Osres]

        def prompt_scores(t, h):
            kvh = h // 4
            base = (h % 2) * 64
            ch = h // 2
            sbank = h % 2
            mk = mask0b if t == 0 else maskNb
            kcols = slice(t * 128, t * 128 + 256)
            prev_i = 9 if t == 0 else t - 1
            S.add("pe", lambda e: e.matmul(banks[sbank][:, 256:258], lhsT=ident, rhs=sinks_b2[:, 2 * h:2 * h + 2],
                                           start=True, stop=True), reads=CONST, writes=[bres[sbank]])
            S.add("pe", lambda e: e.matmul(
                banks[sbank][:, 0:256], lhsT=QT[base:base + 64, ch, tcols(t)], rhs=KTd[base:base + 64, kvh, kcols],
                start=True, stop=False),
                reads=[QTres[t], KTres[prev_i], KTres[t]], writes=[bres[sbank]])
            S.add("pe", lambda e: e.matmul(banks[sbank][:, 0:256], lhsT=ident, rhs=mk, start=False, stop=True),
                  reads=CONST, writes=[bres[sbank]])

        ATT_HEADS = [(t, h) for t in range(8) for h in range(16)]
        NH = len(ATT_HEADS)
        att_pb = {}

        def att_E1(idx):
            t, h = ATT_HEADS[idx]
            att_pb[idx] = softmax_head(h % 2, t % 2, h, sinks_t[:, h:h + 1], False)

        def att_E2(idx):
            pb = att_pb[idx]
            tb = 2 + pb
            for c in range(2):
                S.add("pe", lambda e, c=c: e.transpose(out=bankbf(tb)[:, c * 128:(c + 1) * 128],
                                                       in_=Pb[pb][:, c * 128:(c + 1) * 128], identity=ident),
                      reads=[Pres[pb]] + CONST, writes=[bres[tb]])
            copy_op("dve" if idx % 2 else "act", PTb[pb], bankbf(tb)[:, 0:256], [bres[tb]], [PTres[pb]])

        def att_V(idx):
            t, h = ATT_HEADS[idx]
            pb = att_pb[idx]
            b2 = t % 2
            ab = atm[b2]
            kvh = h // 4
            prev_i = 9 if t == 0 else t - 1
            ob = 4 + h // 8
            oc = (h % 8) * 64
            S.add("pe", lambda e: e.matmul(
                banks[ob][:, oc:oc + 64], lhsT=PTb[pb][:, 0:128], rhs=Vb[:, t, kvh * 64:(kvh + 1) * 64],
                start=True, stop=False), reads=[PTres[pb], Vres[prev_i]], writes=[bres[ob]])
            S.add("pe", lambda e: e.matmul(
                banks[ob][:, oc:oc + 64], lhsT=PTb[pb][:, 128:256], rhs=Vb[:, t + 1, kvh * 64:(kvh + 1) * 64],
                start=False, stop=True), reads=[PTres[pb], Vres[t]], writes=[bres[ob]])
            if h % 8 == 7:
                h0 = h - 7
                hs = slice(h0, h0 + 8)
                Rg = [sres(q, b2, hh) for q in (1, 2) for hh in range(h0, h0 + 8)]
                Wg = [sres(q, b2, hh) for q in (3, 4, 5) for hh in range(h0, h0 + 8)]
                S.add("dve", lambda e: e.tensor_tensor(out=st_att[:, 3, b2, hs], in0=st_att[:, 1, b2, hs],
                                                       in1=sinks_t[:, hs], op=ALU.add),
                      reads=Rg + CONST, writes=Wg)
                S.add("act", lambda e: e.activation(out=st_att[:, 3, b2, hs], in_=st_att[:, 3, b2, hs], func=AF.Exp),
                      reads=Wg, writes=Wg)
                S.add("dve", lambda e: e.tensor_tensor(out=st_att[:, 4, b2, hs], in0=st_att[:, 2, b2, hs],
                                                       in1=st_att[:, 3, b2, hs], op=ALU.add), reads=Rg + Wg, writes=Wg)
                S.add("dve", lambda e: e.reciprocal(out=st_att[:, 5, b2, hs], in_=st_att[:, 4, b2, hs]),
                      reads=Wg, writes=Wg)
                rdb = st_att[:, 5, b2, hs].unsqueeze(2).broadcast_to([128, 8, 64])
                S.add("dve", lambda e: e.tensor_tensor(
                    out=ab[:, h0 * 64:(h0 + 8) * 64].rearrange("p (h d) -> p h d", h=8),
                    in0=banks[ob][:, :].rearrange("p (h d) -> p h d", h=8), in1=rdb, op=ALU.mult),
                    reads=[bres[ob]] + Wg, writes=[atmres[b2]])
            if h == 15:
                for j in range(8):
                    S.add("pe", lambda e, j=j: e.transpose(out=bankbf(6)[:, j * 128:(j + 1) * 128],
                                                           in_=ab[:, j * 128:(j + 1) * 128], identity=ident),
                          reads=[atmres[b2]] + CONST, writes=[bres[6]])
                copy_op("act", QT[:, :, tcols(t)], bankbf(6)[:, 0:1024].rearrange("p (c t) -> p c t", c=8),
                        [bres[6]], [QTres[t]])

        def gen_att_prompt():
            prompt_scores(*ATT_HEADS[0])
            prompt_scores(*ATT_HEADS[1])
            att_E1(0)
            prompt_scores(*ATT_HEADS[2])
            att_E1(1)
            att_E2(0)
            for k in range(NH):
                if k + 3 < NH:
                    prompt_scores(*ATT_HEADS[k + 3])
                if k + 2 < NH:
                    att_E1(k + 2)
                if k + 1 < NH:
                    att_E2(k + 1)
                att_V(k)
                yield

        gvres = [Res("gvf%d" % i) for i in range(NT)]
        vlnb = view(r2, 18432, BF16, "p (t c) -> p t c", t=NT)
        gnb = view(r2 + 18432, 4096)
        vlnres = [Res("vln%d" % i) for i in range(NT)]
        gnres = Res("gnB")
        uTres = [Res("uT%d" % i) for i in range(NT)]
        lnres = [Res("ln0"), Res("ln1")]

        bnall = view(SMALL_EXTRA, 4 * NT * 4, F32, "p (t k) -> p t k", t=NT)
        bnst9 = view(SMALL_EXTRA + 160, 4 * NT * 12, F32, "p (t k) -> p t k", t=NT)

        def gen_gv():
            r2old = [xtres[0], xtres[1], KT8res, Kb8res] + kdres + kvfres
            S.add("sp", lambda e: e.dma_start(out=gnb, in_=gnB[:, :]), writes=[gnres] + r2old, dma_key="gn")
            for wb in range(4):
                off, rw = w_use(B_GV[wb])
                wt = v_k16(off)
                for t in range(NT):
                    for _ in gv_group(wb, t, wt, rw):
                        yield
                w_done(B_GV[wb])

        def gv_group(wb, t, wt, rw):
            bk = 7
            for kc in range(16):
                S.add("pe", lambda e, kc=kc: e.matmul(
                    banks[bk][:, 0:256], lhsT=hT[:, kc, tcols(t)], rhs=wt[:, kc, :], start=(kc == 0), stop=(kc == 15)),
                    reads=[hTres[t][kc], rw], writes=[bres[bk]])
                if kc % 4 == 3 and kc != 15:
                    yield
            S.add("dve", lambda e: e.tensor_copy(out=gvf[:, t, wb * 256:(wb + 1) * 256], in_=banks[bk][:, 0:256]),
                  reads=[bres[bk]], writes=[gvres[t]] + (sample_dead if wb == 0 else []))
            yield

        def u_group(wb, cl, tg, wt, rw):
            m = wb * 2 + cl
            bk, lo, hi = fm_group(wt, cl, 16, hT, hTr, tg, [rw], [0, 1, 2, 3, 4, 5, 6, 7])
            S.add("act", lambda e: e.activation(out=uT[:, m, lo:hi], in_=banks[bk][:, 0:384], func=AF.Gelu_apprx_tanh),
                  reads=[bres[bk]],
                  writes=[uTres[t] for t in tiles_of_tg(tg)]
                  + [gvres[t] for t in range((m * 2304) // 4096, ((m + 1) * 2304 - 1) // 4096 + 1)])

        def ln_part1():
            R = lnres[0]
            for t in range(NT):
                S.add("act", lambda e, t=t: e.activation(out=gvf[:, t, :], in_=gvf[:, t, :], func=AF.Gelu_apprx_tanh),
                      reads=[gvres[t]], writes=[gvres[t]])
                for c in range(2):
                    S.add("dve", lambda e, t=t, c=c: e.bn_stats(out=bnst9[:, t, c * 6:(c + 1) * 6],
                                                                in_=gvf[:, t, c * 512:(c + 1) * 512]),
                          reads=[gvres[t]], writes=[R])
                S.add("dve", lambda e, t=t: e.bn_aggr(out=bnall[:, t, 0:2], in_=bnst9[:, t, :]), reads=[R], writes=[R])

        def ln_part2():
            R = lnres[0]
            S.add("act", lambda e: e.activation(out=bnall[:, :, 2], in_=bnall[:, :, 1], func=AF.Sqrt, bias=eps_t[:, 0:1],
                                                scale=1.0), reads=[R, epsR], writes=[R])
            S.add("dve", lambda e: e.reciprocal(out=bnall[:, :, 3], in_=bnall[:, :, 2]), reads=[R], writes=[R])
            for t in range(NT):
                S.add("dve", lambda e, t=t: e.tensor_scalar(out=gvf[:, t, :], in0=gvf[:, t, :], scalar1=bnall[:, t, 0:1],
                                                            scalar2=bnall[:, t, 3:4], op0=ALU.subtract, op1=ALU.mult),
                      reads=[R, gvres[t]], writes=[gvres[t]])
                S.add("dve", lambda e, t=t: e.tensor_tensor(out=vlnb[:, t, :], in0=gvf[:, t, :], in1=gnb, op=ALU.mult),
                      reads=[gvres[t], gnres], writes=[vlnres[t]])
                if t == 8:
                    S.add("dve", lambda e, t=t: e.tensor_tensor(out=gvf[:, t, :], in0=gvf[:, t, :], in1=gnb, op=ALU.mult),
                          reads=[gvres[t], gnres, vlnres[t]], writes=[gvres[t]])
                    S.add("sp", lambda e, t=t: e.dma_start(out=sg_out[:, :], in_=gvf[:, t, :]), reads=[gvres[t]], dma_key="osg")

        _interleave(gen_att_prompt(), gen_gv(), k=1)
        ln_part1()
        ln_part2()
        for wb in range(4):
            off, rw = w_use(B_U[wb])
            wt = v_k16(off)
            for cl in range(2):
                for tg in range(3):
                    u_group(wb, cl, tg, wt, rw)
            w_done(B_U[wb])

        SPB = r2 + 22528
        wst = view(SPB, 4096, F32, "p (g t) -> p g t", g=8)
        wsm = view(SPB + 4096, 2048, BF16, "p (g t) -> p g t", g=8)
        wsms = view(SPB + 6144, 2048, BF16, "p (g t) -> p g t", g=8)
        msk = view(SPB + 8192, 512)
        bsb = view(SPB + 8704, 4096, F32, "p (g t) -> p g t", g=8)
        wres_sp = Res("wst")
        mres = Res("msk")
        wsmres = Res("wsm")
        bsres = Res("bsb")
        att_bufs = Pres + PTres + atmres + [Osmres]
        for which in range(2):
            srcw = wsT if which == 0 else wsTr
            srcm = trilT if which == 0 else blkm
            dstw = wsm if which == 0 else wsms
            S.add("sp", lambda e, srcw=srcw: e.dma_start(out=wst, in_=srcw.rearrange("g s t -> s g t")),
                  writes=[wres_sp] + (att_bufs if which == 0 else []), dma_key="wst")
            S.add("sp", lambda e, srcm=srcm: e.dma_start(out=msk, in_=srcm[:, :]),
                  writes=[mres] + (att_bufs if which == 0 else []), dma_key="msk")
            for g in range(8):
                S.add("dve", lambda e, g=g, dstw=dstw: e.tensor_tensor(out=dstw[:, g, :], in0=wst[:, g, :], in1=msk,
                                                                      op=ALU.mult),
                      reads=[wres_sp, mres], writes=[wsmres])
        S.add("sp", lambda e: e.dma_start(out=bsb, in_=bsB[:, :, :]), writes=[bsres] + att_bufs, dma_key="bsb")
        bsbs = wst
        S.add("sp", lambda e: e.dma_start(out=bsbs, in_=bsBs[:, :, :]), writes=[wres_sp], dma_key="wst")
        sptmp = [view(96256 + i * 2048, 2048) for i in range(2)]
        sptres = [Res("sptmp%d" % i) for i in range(2)]
        kv_dead = KTres + Vres
        def gen_sp():
          for t in range(NT):
            wm = wsms if t == 8 else wsm
            bb = bsbs if t == 8 else bsb
            for half in range(2):
                n = t * 2 + half
                bk = n % 2
                for gl in range(4):
                    g = half * 4 + gl
                    S.add("pe", lambda e, bk=bk, gl=gl, g=g, t=t, wm=wm: e.matmul(
                        banks[bk][:, gl * 128:(gl + 1) * 128], lhsT=vlnb[:, t, g * 128:(g + 1) * 128], rhs=wm[:, g, :],
                        start=True, stop=True), reads=[vlnres[t], wsmres], writes=[bres[bk]])
                tp = sptmp[n % 2]
                S.add("dve", lambda e, bk=bk, tp=tp, bb=bb, half=half: e.tensor_tensor(
                    out=tp, in0=banks[bk][:, :], in1=bb[:, half * 4:(half + 1) * 4, :].rearrange("p g t -> p (g t)"),
                    op=ALU.add), reads=[bres[bk], bsres, wres_sp], writes=[sptres[n % 2]] + (kv_dead if n < 2 else []))
                S.add("pool" if n % 2 else "dve", lambda e, tp=tp, half=half, t=t: e.tensor_tensor(
                    out=uT[:, half * 4:(half + 1) * 4, tcols(t)], in0=uT[:, half * 4:(half + 1) * 4, tcols(t)],
                    in1=tp.rearrange("p (g t) -> p g t", g=4), op=ALU.mult),
                    reads=[sptres[n % 2], uTres[t]], writes=[uTres[t]])
                yield

        sp_gen = gen_sp()
        aT = uT
        aTres = uTres

        mixres = [Res("mix%d" % i) for i in range(NT)]
        sga = [view(VLN_OFF + i * 1536, 1536) for i in range(6)]
        sgb = [view(VLN_OFF + 9216 + i * 1536, 1536) for i in range(6)]
        sgares = [Res("sga%d" % i) for i in range(6)]
        sgbres = [Res("sgb%d" % i) for i in range(6)]
        R2old = vlnres + [gnres, wres_sp, mres, wsmres, bsres]
        gate_first = {"a": True, "m": True}
        ALLB = list(range(8))
        for i in range(8):
            ga, gb, pab = B_GATE[i]
            for (blk, dst, dres) in ((ga, sga, sgares), (gb, sgb, sgbres)):
                off, rw = w_use(blk)
                wt = v_k16(off)
                for cl in range(2):
                    for tg in range(3):
                        q = cl * 3 + tg
                        if i == 0:
                            for _ in range(2):
                                try:
                                    next(sp_gen)
                                except StopIteration:
                                    pass
                        bk, lo, hi = fm_group(wt, cl, 16, hT, hTr, tg, [rw], ALLB if i else [2, 3, 4, 5, 6, 7])
                        old = (gvres + sptres) if gate_first["a"] else []
                        S.add("act", lambda e, q=q, bk=bk, dst=dst: e.activation(out=dst[q], in_=banks[bk][:, 0:384],
                                                                                  func=AF.Sigmoid),
                              reads=[bres[bk]], writes=[dres[q]] + old)
                gate_first["a"] = False
                w_done(blk)
            if i == 0:
                _drain(sp_gen)
            offp, rp = w_use(pab)
            wpa = view(offp, 4096, BF16, "p (k c) -> p k c", k=8)
            wpb = view(offp + 4096, 4096, BF16, "p (k c) -> p k c", k=8)
            for cl in range(2):
                m = i * 2 + cl
                for tg in range(3):
                    q = cl * 3 + tg
                    tl = tiles_of_tg(tg)
                    bka, lo, hi = fm_group(wpa, cl, 8, aT, lambda t, kc: aTres[t], tg, [rp], ALLB)
                    S.add("dve", lambda e, q=q, bka=bka: e.tensor_tensor(out=sga[q], in0=sga[q], in1=banks[bka][:, 0:384],
                                                                          op=ALU.mult),
                          reads=[sgares[q], bres[bka]], writes=[sgares[q]])
                    bkb, lo, hi = fm_group(wpb, cl, 8, attnT, lambda t, kc: attnres[t], tg, [rp], ALLB)
                    S.add("dve", lambda e, q=q, bkb=bkb: e.tensor_tensor(out=sgb[q], in0=sgb[q], in1=banks[bkb][:, 0:384],
                                                                          op=ALU.mult),
                          reads=[sgbres[q], bres[bkb]], writes=[sgbres[q]])
                    S.add("pool", lambda e, q=q, m=m, lo=lo, hi=hi: e.tensor_tensor(
                        out=mixT[:, m, lo:hi], in0=sga[q], in1=sgb[q], op=ALU.add),
                        reads=[sgares[q], sgbres[q]], writes=[mixres[t] for t in tl] + (R2old if gate_first["m"] else []))
                    gate_first["m"] = False
            w_done(pab)

        x1res = [Res("x1_%d" % i) for i in range(NT)]
        R1B_all = [r for l in hTres for r in l] + QTres + uTres + gvres + KTres + Vres + sptres + sgares + sgbres
        hT_all = [r for l in hTres for r in l]
        rest_all = QTres + uTres + gvres + KTres + Vres + sptres + sgares + sgbres
        for t in range(NT):
            guard = hT_all if t < 5 else (rest_all if t == 5 else [])
            S.add("sp", lambda e, t=t: e.dma_start(out=x1[:, t, :], in_=xc[t * 128:(t + 1) * 128, :]),
                  writes=[x1res[t]] + guard, dma_key="x1_%d" % (t % 3))
        hhres = [Res("hh%d" % i) for i in range(NT)]

        rms_state = {}

        def rms_a(t, n):
            b = n % 3
            ssr = Res("ss2")
            ss = st2[:, 2 * (n % 16):2 * (n % 16) + 1]
            rs = st2[:, 2 * (n % 16) + 1:2 * (n % 16) + 2]
            S.add("act", lambda e: e.activation(out=hb[b], in_=x1[:, t, :], func=AF.Square, accum_out=ss),
                  reads=[x1res[t]], writes=[hbres[b], ssr])
            rms_state[n] = (ss, rs, ssr)

        def rms_b1(n):
            ss, rs, ssr = rms_state[n]
            S.add("act", lambda e: e.activation(out=rs, in_=ss, func=AF.Sqrt, bias=eps_t[:, 0:1], scale=1.0 / D),
                  reads=[ssr, epsR], writes=[ssr])

        def rms_b(n):
            ss, rs, ssr = rms_state[n]
            S.add("dve", lambda e: e.reciprocal(out=rs, in_=rs), reads=[ssr], writes=[ssr])
            return rs, ssr

        def norm_stage1b(t, n):
            b = n % 3
            rs, ssr = rms_b(n)
            S.add("dve", lambda e: e.tensor_scalar_mul(out=hb[b], in0=x1[:, t, :], scalar1=rs),
                  reads=[ssr, x1res[t], hbres[b]], writes=[hbres[b]])

        def norm_stage2(t, n, gt, dstT, dres, tbanks):
            b = n % 3
            for half in range(2):
                bk = tbanks[(2 * n + half) % len(tbanks)]
                for j in range(8):
                    c = half * 8 + j
                    S.add("pe", lambda e, bk=bk, j=j, c=c: e.transpose(
                        out=bankbf(bk)[:, j * 128:(j + 1) * 128], in_=hb[b][:, c * 128:(c + 1) * 128], identity=ident),
                        reads=[hbres[b]] + CONST, writes=[bres[bk]])
                en_ = "act" if half == 0 else "dve"
                for j in range(8):
                    c = half * 8 + j
                    src = bankbf(bk)[:, j * 128:(j + 1) * 128]
                    dst = dstT[:, c, tcols(t)]
                    if en_ == "act":
                        S.add("act", lambda e, dst=dst, src=src, c=c: e.activation(
                            out=dst, in_=src, func=AF.Copy, scale=gt[:, c:c + 1]),
                            reads=[bres[bk]] + CONST, writes=[dres])
                    else:
                        S.add("dve", lambda e, dst=dst, src=src, c=c: e.tensor_scalar_mul(
                            out=dst, in0=src, scalar1=gt[:, c:c + 1]),
                            reads=[bres[bk]] + CONST, writes=[dres])

        wo_n = {"n": 0}

        def wo_group(cg, t, wt, rw):
            bk = wo_n["n"] % 4
            wo_n["n"] += 1
            for kc in range(16):
                S.add("pe", lambda e, kc=kc: e.matmul(
                    banks[bk][:, 0:256], lhsT=mixT[:, kc, tcols(t)], rhs=wt[:, kc, :], start=(kc == 0), stop=(kc == 15)),
                    reads=[mixres[t], rw], writes=[bres[bk]])
            S.add("dve", lambda e: e.tensor_tensor(
                out=x1[:, t, cg * 256:(cg + 1) * 256], in0=banks[bk][:, 0:256], in1=x1[:, t, cg * 256:(cg + 1) * 256],
                op=ALU.add), reads=[bres[bk], x1res[t]], writes=[x1res[t]])

        NCGO = 6
        for cg in range(NCGO):
            off, rw = w_use(B_WO[cg])
            wt = v_k16(off)
            for t in range(NT):
                wo_group(cg, t, wt, rw)
            w_done(B_WO[cg])
        tail_w = []
        for cg in range(NCGO, 8):
            off, rw = w_use(B_WO[cg])
            tail_w.append((cg, v_k16(off), rw))
        for t in range(NT + 1):
            if t < NT:
                for (cg, wt, rw) in tail_w:
                    wo_group(cg, t, wt, rw)
                rms_a(t, t)
                rms_b1(t)
            if 1 <= t <= NT:
                norm_stage1b(t - 1, t - 1)
            if 2 <= t:
                norm_stage2(t - 2, t - 2, g2t, hhT, hhres[t - 2], [4, 5, 6, 7])
        w_done(B_WO[7])

        def norm_drain():
            norm_stage2(NT - 1, NT - 1, g2t, hhT, hhres[NT - 1], [4, 5, 6, 7])

        actT = [view(r2 + i * 9216, 9216, BF16, "p (f t) -> p f t", f=4) for i in range(2)]
        actres = [[Res("act%d_%d" % (i, t)) for t in range(NT)] for i in range(2)]
        silt = [view(r2 + 18432 + i * 1536, 1536) for i in range(2)]
        silres = [Res("sil%d" % i) for i in range(2)]
        gfbv = view(r2 + 21504, 8192)
        gfres = Res("gfb")
        S.add("sp", lambda e: e.dma_start(out=gfbv, in_=gfB[:, :]), writes=[gfres] + mixres, dma_key="gf")
        ffn_n = {"n": 0}
        dn_n = {"n": 0}
        for gi in range(NG):
            gu, dn = B_FF[gi]
            ab = gi % 2
            for hf in range(2):
                offg, rg = w_use(gu[hf][0])
                offu, ru = w_use(gu[hf][1])
                wg_ = v_k16(offg)
                wu_ = v_k16(offu)
                cltg = [(cl, tg) for cl in range(2) for tg in range(3)]
                if gi == 0 and hf == 0:
                    cltg = [(0, 0), (0, 1), (1, 0), (1, 1), None, (0, 2), (1, 2)]
                for ct in cltg:
                    if ct is None:
                        norm_drain()
                        continue
                    cl, tg = ct
                    f = hf * 2 + cl
                    if True:
                        lo, hi = TG[tg]
                        n = ffn_n["n"]
                        ffn_n["n"] += 1
                        bg = (n % 2) * 2
                        bu = bg + 1
                        tl = tiles_of_tg(tg)
                        for (bk, wt, rr) in ((bg, wg_, rg), (bu, wu_, ru)):
                            for kc in range(16):
                                S.add("pe", lambda e, bk=bk, kc=kc, wt=wt, lo=lo, hi=hi, cl=cl: e.matmul(
                                    banks[bk][:, 0:384], lhsT=wt[:, kc, cl * 128:(cl + 1) * 128], rhs=hhT[:, kc, lo:hi],
                                    start=(kc == 0), stop=(kc == 15)),
                                    reads=[hhres[t] for t in tl] + [rr], writes=[bres[bk]])
                        sl = silt[n % 2]
                        old = mixres if n < 2 else []
                        S.add("act", lambda e, sl=sl, bg=bg: e.activation(out=sl, in_=banks[bg][:, 0:384], func=AF.Silu),
                              reads=[bres[bg]], writes=[silres[n % 2]] + old)
                        S.add("dve", lambda e, sl=sl, bu=bu, ab=ab, f=f, lo=lo, hi=hi: e.tensor_tensor(
                            out=actT[ab][:, f, lo:hi], in0=sl, in1=banks[bu][:, 0:384], op=ALU.mult),
                            reads=[silres[n % 2], bres[bu]], writes=[actres[ab][t] for t in tl] + old)
                w_done(gu[hf][1])
            last = (gi == NG - 1)
            dnw = []
            if last:
                for ch in range(2):
                    offd, rd_ = w_use(dn[ch])
                    dnw.append((view(offd, 8192, BF16, "p (f c) -> p f c", f=4), rd_))
            for ch_t in ([(ch, None) for ch in range(2)] if not last else [(ch, t) for t in range(NT) for ch in range(2)]):
                ch = ch_t[0]
                if not last:
                    offd, rd_ = w_use(dn[ch])
                    wd_ = view(offd, 8192, BF16, "p (f c) -> p f c", f=4)
                    tiles = range(NT)
                else:
                    wd_, rd_ = dnw[ch]
                    tiles = [ch_t[1]]
                for t in tiles:
                    for c2 in range(2):
                        bk = 4 + dn_n["n"] % 4
                        dn_n["n"] += 1
                        for f in range(4):
                            S.add("pe", lambda e, bk=bk, f=f, t=t, c2=c2, ab=ab, wd_=wd_: e.matmul(
                                banks[bk][:, 0:512], lhsT=actT[ab][:, f, tcols(t)], rhs=wd_[:, f, c2 * 512:(c2 + 1) * 512],
                                start=(f == 0), stop=(f == 3)), reads=[actres[ab][t], rd_], writes=[bres[bk]])
                        col = ch * 1024 + c2 * 512
                        S.add("dve", lambda e, bk=bk, t=t, col=col: e.tensor_tensor(
                            out=x1[:, t, col:col + 512], in0=banks[bk][:, 0:512], in1=x1[:, t, col:col + 512], op=ALU.add),
                            reads=[bres[bk], x1res[t]], writes=[x1res[t]])
                    if last and ch == 1:
                        rms_a(t, NT + t)
                        rms_b1(NT + t)
                    for tf in (([t - 1] if t >= 1 else []) + ([t] if t == NT - 1 else [])) if (last and ch == 1) else []:
                        rs, ssr = rms_b(NT + tf)
                        yA = Res("yA")
                        yB = Res("yB")
                        S.add("dve", lambda e, t=tf, rs=rs: e.scalar_tensor_tensor(
                            out=x1[:, t, 0:1024], in0=x1[:, t, 0:1024], scalar=rs, in1=gfbv[:, 0:1024],
                            op0=ALU.mult, op1=ALU.mult), reads=[ssr, x1res[tf], gfres], writes=[yA])
                        S.add("act", lambda e, t=tf, rs=rs: e.activation(out=x1[:, t, 1024:2048], in_=x1[:, t, 1024:2048],
                                                                         func=AF.Copy, scale=rs),
                              reads=[ssr, x1res[tf]], writes=[yB])
                        S.add("pool", lambda e, t=tf: e.tensor_tensor(out=x1[:, t, 1024:2048], in0=x1[:, t, 1024:2048],
                                                                      in1=gfbv[:, 1024:2048], op=ALU.mult),
                              reads=[yB, gfres], writes=[yB])
                        S.add("sp", lambda e, t=tf: e.dma_start(out=y_out[t * 128:(t + 1) * 128, :], in_=x1[:, t, :]),
                              reads=[yA, yB], writes=[x1res[tf]], dma_key="oy%d" % (tf % 3))
                if not last:
                    w_done(dn[ch])
            if last:
                w_done(dn[1])
        S.emit(nc, st)
    return nc


def _consts():
    i = np.arange(128)[:, None]
    j = np.arange(128)[None, :]
    prev = np.where(j >= i, 0.0, NEG).astype(np.float32)
    cur = np.where(j <= i, 0.0, NEG).astype(np.float32)
    maskN = np.concatenate([prev, cur], 1)
    mask0_first = np.concatenate([np.full((128, 128), NEG, np.float32), cur], 1)
    t_row = (np.arange(128) % 8)[:, None]
    maskSc = np.where(j >= t_row, 0.0, NEG).astype(np.float32)
    maskSn = np.full((128, 16, 128), NEG, np.float32)
    for s in range(16):
        for tp in range(8):
            maskSn[:, s, s * 8 + tp] = np.where(tp <= t_row[:, 0], 0.0, NEG)
    trilT = (i <= j).astype(np.float32)
    si, ti = np.arange(128)[:, None], np.arange(128)[None, :]
    blkm = ((si // 8 == ti // 8) & (si % 8 <= ti % 8)).astype(np.float32)
    ident = np.eye(128, dtype=np.float32)
    return dict(maskN=maskN, mask0_first=mask0_first, maskSc=maskSc, maskSn=maskSn, trilT=trilT, blkm=blkm, ident=ident)


_NC_CACHE = {}


def make_in_maps(x_prompt, x_sample, cache_k, cache_v, norm1_g, w_in, gmlp_norm_g, w_s, b_s, sinks,
                 w_pa, w_pb, w_o, norm2_g, w_ff_gate, w_ff_up, w_ff_down, final_g):
    f = lambda a: np.ascontiguousarray(np.asarray(a, dtype=np.float32))
    x_prompt, x_sample, cache_k, cache_v = f(x_prompt), f(x_sample), f(cache_k), f(cache_v)
    C = _consts()

    ws = f(w_s)[0]
    wsT = np.ascontiguousarray(ws.transpose(0, 2, 1))
    wsTr = np.ascontiguousarray(np.tile(wsT[:, 0:8, 0:8], (1, 16, 16)))
    bs = f(b_s)[0]
    bsB = np.ascontiguousarray(np.broadcast_to(bs[None], (128, 8, 128)))
    bsBs = np.ascontiguousarray(np.broadcast_to(np.tile(bs[:, 0:8], (1, 16))[None], (128, 8, 128)))
    sk = f(sinks)[0]
    shared = dict(
        w_in=f(w_in)[0], w_pa=f(w_pa)[0], w_pb=f(w_pb)[0], w_o=f(w_o)[0],
        w_g=f(w_ff_gate)[0], w_u=f(w_ff_up)[0], w_d=f(w_ff_down)[0],
        g1T=np.ascontiguousarray(f(norm1_g)[0].reshape(16, 128).T),
        g2T=np.ascontiguousarray(f(norm2_g)[0].reshape(16, 128).T),
        gfB=np.ascontiguousarray(np.broadcast_to(f(final_g)[None], (128, D))),
        gnB=np.ascontiguousarray(np.broadcast_to(f(gmlp_norm_g)[0][None], (128, 1024))),
        wsT=wsT, wsTr=wsTr, trilT=C["trilT"], blkm=C["blkm"], bsB=bsB, bsBs=bsBs,
        sinksB=np.ascontiguousarray(np.broadcast_to(sk[None], (128, 16))),
        sinkrep=np.ascontiguousarray(np.repeat(sk, 8)[:, None]),
        sinksB2=np.ascontiguousarray(np.broadcast_to(np.repeat(sk, 2)[None], (128, 32))),
        sinkrep2=np.ascontiguousarray(np.repeat(np.repeat(sk, 8)[:, None], 2, axis=1)),
        maskN=C["maskN"], maskSc=C["maskSc"], maskSn=C["maskSn"], ident=C["ident"],
    )
    in_maps = []
    for c in range(NCORES):
        b, half = c // 2, c % 2
        xs = x_sample[c * 16:(c + 1) * 16].reshape(128, D)
        xp = x_prompt[b, half * 1024:(half + 1) * 1024]
        if half == 1:
            prev = x_prompt[b, 896:1024]
            mask0 = C["maskN"]
        else:
            prev = np.zeros((128, D), np.float32)
            mask0 = C["mask0_first"]
        xcat = np.ascontiguousarray(np.concatenate([xp, xs, prev], 0))
        ckc = cache_k[0, c * 16:(c + 1) * 16].reshape(16, 128, 256)
        cvc = cache_v[0, c * 16:(c + 1) * 16].reshape(16, 128, 256)
        ckT = np.ascontiguousarray(ckc.reshape(16, 128, 2, 128).transpose(2, 3, 0, 1))
        m = dict(shared)
        m.update(xc=xcat, ckT=ckT, ck=np.ascontiguousarray(ckc), cv=np.ascontiguousarray(cvc), mask0=mask0)
        in_maps.append(m)
    return in_maps


def kernel(**inputs):
    in_maps = make_in_maps(**inputs)
    if "nc" not in _NC_CACHE:
        _NC_CACHE["nc"] = build_program()
    nc = _NC_CACHE["nc"]
    res = run_bass_kernel_spmd(nc, in_maps, core_ids=list(range(NCORES)))
    return assemble(res.results)


def assemble(R):
    y_prompt = np.empty((4, 2048, D), np.float32)
    y_sample = np.empty((128, 8, D), np.float32)
    pk = np.empty((1, 4, 128, 4, 64), np.float32)
    pv = np.empty((1, 4, 128, 4, 64), np.float32)
    skw = np.empty((1, 128, 128, 4, 64), np.float32)
    svw = np.empty((1, 128, 128, 4, 64), np.float32)
    sg = np.empty((1, 128, 8, 8, 128), np.float32)
    for c in range(NCORES):
        b, half = c // 2, c % 2
        y = R[c]["y"]
        y_prompt[b, half * 1024:(half + 1) * 1024] = y[0:1024]
        y_sample[c * 16:(c + 1) * 16] = y[1024:1152].reshape(16, 8, D)
        if half == 1:
            pk[0, b] = R[c]["kv7"][:, 0:256].reshape(128, 4, 64)
            pv[0, b] = R[c]["kv7"][:, 256:512].reshape(128, 4, 64)
        skw[0, c * 16:(c + 1) * 16] = R[c]["skw"].reshape(16, 128, 4, 64)
        svw[0, c * 16:(c + 1) * 16] = R[c]["svw"].reshape(16, 128, 4, 64)
        sg[0, c * 16:(c + 1) * 16] = R[c]["sg"].reshape(16, 8, 8, 128)
    return (y_prompt, y_sample, pk, pv, skw, svw, sg)
```
